# Optimizing a Trainium2 kernel written in Bass

```python
import math
import jax, jax.numpy as jnp
from jax import lax
import numpy as np

D_MODEL = 1024
BATCH = 4
SEQ = 8192
DEPTH = 2

N_A_LAYERS = DEPTH // 2
N_B_LAYERS = DEPTH - N_A_LAYERS

DN_HEADS = 8
DN_HEAD_K = 128
DN_HEAD_V = 128
DN_CONV = 4
DN_CHUNK = 64
DN_KEY_WIDTH = DN_HEADS * DN_HEAD_K
DN_VAL_WIDTH = DN_HEADS * DN_HEAD_V
DN_QKV_WIDTH = 2 * DN_KEY_WIDTH + DN_VAL_WIDTH
DN_IN_WIDTH = DN_QKV_WIDTH + DN_VAL_WIDTH + 2 * DN_HEADS

NSA_HEADS = 16
NSA_KV_GROUPS = 4
NSA_HEAD_DIM = 64
NSA_REP = NSA_HEADS // NSA_KV_GROUPS
NSA_WIDTH = NSA_HEADS * NSA_HEAD_DIM
N_BRANCH = 3
CMP_BLOCK = 32
CMP_STRIDE = 16
CMP_HIDDEN = 128
SEL_BLOCK = 64
SEL_TOPK = 16
WINDOW = 512
Q_BLOCK = 128
FORCED_SCORE = 1.0e4
NSA_IN_WIDTH = NSA_WIDTH + N_BRANCH * NSA_WIDTH + N_BRANCH * NSA_HEADS
SHARED_KV_WIDTH = N_BRANCH * 2 * NSA_KV_GROUPS * NSA_HEAD_DIM

ROPE_THETA = 10000.0
NORM_EPS = 1e-6
DEEPNORM_ALPHA = (2.0 * DEPTH) ** 0.25
DEEPNORM_BETA = (8.0 * DEPTH) ** -0.25

kernel_name = 'yoco_gdn_nsa_deepnorm_hybrid'


def layer_norm(x, w, b):
    xf = x.astype(jnp.float32)
    mu = jnp.mean(xf, axis=-1, keepdims=True)
    var = jnp.mean(jnp.square(xf - mu), axis=-1, keepdims=True)
    return ((xf - mu) * lax.rsqrt(var + NORM_EPS) * w.astype(jnp.float32) + b.astype(jnp.float32)).astype(x.dtype)


def rms_norm(x, w):
    xf = x.astype(jnp.float32)
    return xf * lax.rsqrt(jnp.mean(jnp.square(xf), axis=-1, keepdims=True) + NORM_EPS) * w.astype(jnp.float32)


def l2_normalize(x):
    xf = x.astype(jnp.float32)
    return xf * lax.rsqrt(jnp.sum(jnp.square(xf), axis=-1, keepdims=True) + NORM_EPS)


def rope(x, pos):
    half = x.shape[-1] // 2
    inv_freq = ROPE_THETA ** (-jnp.arange(half, dtype=jnp.float32) / half)
    ang = pos.astype(jnp.float32)[:, None] * inv_freq[None, :]
    cos, sin = jnp.cos(ang), jnp.sin(ang)
    x1 = x[..., :half].astype(jnp.float32)
    x2 = x[..., half:].astype(jnp.float32)
    out = jnp.concatenate([x1 * cos - x2 * sin, x2 * cos + x1 * sin], axis=-1)
    return out.astype(x.dtype)


def causal_short_conv(x, w):
    k_len = w.shape[0]
    t_len = x.shape[1]
    xp = jnp.pad(x, ((0, 0), (k_len - 1, 0), (0, 0)))
    y = xp[:, 0:t_len] * w[0]
    for i in range(1, k_len):
        y = y + xp[:, i:i + t_len] * w[i]
    return y


def gated_delta_rule_chunked(q, k, v, g, beta):
    f32 = jnp.float32
    b_sz, h_sz, t_len, dk = q.shape
    dv = v.shape[-1]
    c = DN_CHUNK
    n = t_len // c
    q = q.astype(f32).reshape(b_sz, h_sz, n, c, dk) * (dk ** -0.5)
    k = k.astype(f32).reshape(b_sz, h_sz, n, c, dk)
    v = v.astype(f32).reshape(b_sz, h_sz, n, c, dv)
    beta = beta.astype(f32).reshape(b_sz, h_sz, n, c, 1)
    gc = jnp.cumsum(g.astype(f32).reshape(b_sz, h_sz, n, c), axis=-1)
    idx = jnp.arange(c)
    incl = idx[:, None] >= idx[None, :]
    strict = idx[:, None] > idx[None, :]
    diff = gc[..., :, None] - gc[..., None, :]
    decay = jnp.where(incl, jnp.exp(jnp.where(incl, diff, 0.0)), 0.0)
    kb = k * beta
    low = jnp.where(strict, jnp.einsum('bhnid,bhnjd->bhnij', kb, k) * decay, 0.0)
    t_mat = low + jnp.eye(c, dtype=f32)
    rhs = jnp.concatenate([v * beta, kb * jnp.exp(gc)[..., None]], axis=-1)
    sol = lax.linalg.triangular_solve(t_mat, rhs, left_side=True, lower=True, unit_diagonal=True)
    u, w = sol[..., :dv], sol[..., dv:]
    attn = jnp.einsum('bhnid,bhnjd->bhnij', q, k) * decay
    q_g = q * jnp.exp(gc)[..., None]
    g_last = gc[..., -1]
    k_dec = k * jnp.exp(g_last[..., None] - gc)[..., None]
    xs = (jnp.moveaxis(q_g, 2, 0), jnp.moveaxis(w, 2, 0), jnp.moveaxis(u, 2, 0),
          jnp.moveaxis(attn, 2, 0), jnp.moveaxis(k_dec, 2, 0), jnp.moveaxis(jnp.exp(g_last), 2, 0))

    def step(state, inp):
        qg, wc, uc, ac, kd, gl = inp
        v_new = uc - jnp.einsum('bhcd,bhde->bhce', wc, state)
        o = jnp.einsum('bhcd,bhde->bhce', qg, state) + jnp.einsum('bhij,bhje->bhie', ac, v_new)
        state = state * gl[..., None, None] + jnp.einsum('bhcd,bhce->bhde', kd, v_new)
        return state, o

    s0 = jnp.zeros((b_sz, h_sz, dk, dv), f32)
    _, o = lax.scan(step, s0, xs)
    return jnp.moveaxis(o, 0, 2).reshape(b_sz, h_sz, t_len, dv)


def gated_deltanet_mixer(x, w_in, conv_w, a_log, dt_bias, norm_w, w_out):
    b_sz, t_len, _ = x.shape
    proj = x @ w_in
    qkv = jax.nn.silu(causal_short_conv(proj[..., :DN_QKV_WIDTH], conv_w))
    z = proj[..., DN_QKV_WIDTH:DN_QKV_WIDTH + DN_VAL_WIDTH]
    b_logit = proj[..., DN_QKV_WIDTH + DN_VAL_WIDTH:DN_QKV_WIDTH + DN_VAL_WIDTH + DN_HEADS]
    a_logit = proj[..., DN_QKV_WIDTH + DN_VAL_WIDTH + DN_HEADS:]
    q = l2_normalize(qkv[..., :DN_KEY_WIDTH].reshape(b_sz, t_len, DN_HEADS, DN_HEAD_K))
    k = l2_normalize(qkv[..., DN_KEY_WIDTH:2 * DN_KEY_WIDTH].reshape(b_sz, t_len, DN_HEADS, DN_HEAD_K))
    v = qkv[..., 2 * DN_KEY_WIDTH:].reshape(b_sz, t_len, DN_HEADS, DN_HEAD_V)
    beta = jax.nn.sigmoid(b_logit.astype(jnp.float32))
    g = -jnp.exp(a_log.astype(jnp.float32)) * jax.nn.softplus(
        a_logit.astype(jnp.float32) + dt_bias.astype(jnp.float32))
    o = gated_delta_rule_chunked(q.transpose(0, 2, 1, 3), k.transpose(0, 2, 1, 3), v.transpose(0, 2, 1, 3),
                                 g.transpose(0, 2, 1), beta.transpose(0, 2, 1))
    o = o.transpose(0, 2, 1, 3)
    zf = z.reshape(b_sz, t_len, DN_HEADS, DN_HEAD_V).astype(jnp.float32)
    o = (rms_norm(o, norm_w) * jax.nn.silu(zf)).astype(x.dtype)
    return o.reshape(b_sz, t_len, DN_VAL_WIDTH) @ w_out


def compress_blocks(x, pe, w1, w2):
    b_sz, g_sz, t_len, dh = x.shape
    n_chunk = t_len // CMP_STRIDE
    ratio = CMP_BLOCK // CMP_STRIDE
    n_cmp = n_chunk - ratio + 1
    chunks = x.reshape(b_sz, g_sz, n_chunk, CMP_STRIDE, dh)
    hid = None
    for m in range(ratio):
        part = chunks[:, :, m:m + n_cmp] + pe[m * CMP_STRIDE:(m + 1) * CMP_STRIDE]
        term = jnp.einsum('bgncd,cdh->bgnh', part, w1[m * CMP_STRIDE:(m + 1) * CMP_STRIDE])
        hid = term if hid is None else hid + term
    return jax.nn.silu(hid) @ w2


def nsa_shared_kv(h, w_kv, pe_k, pe_v, w1_k, w2_k, w1_v, w2_v):
    b_sz, t_len, _ = h.shape
    kv = (h @ w_kv).reshape(b_sz, t_len, 2 * N_BRANCH, NSA_KV_GROUPS, NSA_HEAD_DIM).transpose(2, 0, 3, 1, 4)
    pos = jnp.arange(t_len, dtype=jnp.int32)
    k_cmp = compress_blocks(kv[0], pe_k, w1_k, w2_k)
    n_cmp = k_cmp.shape[2]
    cmp_end = jnp.arange(n_cmp, dtype=jnp.int32) * CMP_STRIDE + CMP_BLOCK - 1
    k_cmp = rope(k_cmp, cmp_end)
    v_cmp = compress_blocks(kv[1], pe_v, w1_v, w2_v)
    n_sel = t_len // SEL_BLOCK
    k_sel = rope(kv[2], pos).reshape(b_sz, NSA_KV_GROUPS, n_sel, SEL_BLOCK, NSA_HEAD_DIM)
    v_sel = kv[3].reshape(b_sz, NSA_KV_GROUPS, n_sel, SEL_BLOCK, NSA_HEAD_DIM)
    pad = ((0, 0), (0, 0), (WINDOW, 0), (0, 0))
    k_win = jnp.pad(rope(kv[4], pos), pad)
    v_win = jnp.pad(kv[5], pad)
    return (k_cmp, v_cmp, k_sel, v_sel, k_win, v_win)


def cmp_to_sel_overlap(n_cmp, n_sel):
    start = np.arange(n_cmp)[:, None] * CMP_STRIDE
    bstart = np.arange(n_sel)[None, :] * SEL_BLOCK
    ov = np.clip(np.minimum(start + CMP_BLOCK, bstart + SEL_BLOCK) - np.maximum(start, bstart), 0, None)
    return (ov / CMP_BLOCK).astype(np.float32)


def masked_softmax(s, mask):
    s = jnp.where(mask, s.astype(jnp.float32), -1e30)
    return jnp.where(mask, jax.nn.softmax(s, axis=-1), 0.0)


def nsa_branch_attention(q, k_cmp, v_cmp, k_sel, v_sel, k_win, v_win):
    b_sz, g_sz, r_sz, t_len, dh = q.shape
    n_cmp = k_cmp.shape[2]
    n_sel = k_sel.shape[2]
    top_k = min(SEL_TOPK, n_sel)
    scale = dh ** -0.5
    cmp_end = jnp.arange(n_cmp, dtype=jnp.int32) * CMP_STRIDE + CMP_BLOCK - 1
    overlap = jnp.asarray(cmp_to_sel_overlap(n_cmp, n_sel))
    blk = jnp.arange(n_sel, dtype=jnp.int32)
    gather = jax.vmap(jax.vmap(lambda blocks, idx: blocks[idx]))

    def one_block(qb):
        s0 = qb * Q_BLOCK
        t = s0 + jnp.arange(Q_BLOCK, dtype=jnp.int32)
        qblk = lax.dynamic_slice_in_dim(q, s0, Q_BLOCK, axis=3)
        sc = jnp.einsum('bgrqd,bgnd->bgrqn', qblk, k_cmp) * scale
        p_cmp = masked_softmax(sc, cmp_end[None, :] <= t[:, None])
        o_cmp = jnp.einsum('bgrqn,bgnd->bgrqd', p_cmp, v_cmp)
        imp = jnp.einsum('bgrqn,ns->bgqs', p_cmp, overlap)
        cur = t // SEL_BLOCK
        forced = (blk[None, :] == 0) | (blk[None, :] == cur[:, None]) | (blk[None, :] == cur[:, None] - 1)
        visible = blk[None, :] * SEL_BLOCK <= t[:, None]
        score = jnp.where(visible, jnp.where(forced, FORCED_SCORE, imp), -1.0)
        _, idx = lax.top_k(score, top_k)
        ks = gather(k_sel, idx)
        vs = gather(v_sel, idx)
        ss = jnp.einsum('bgrqd,bgqnkd->bgrqnk', qblk, ks) * scale
        kpos = idx[..., None] * SEL_BLOCK + jnp.arange(SEL_BLOCK, dtype=jnp.int32)
        smask = (kpos <= t[:, None, None])[:, :, None]
        p_sel = masked_softmax(ss.reshape(b_sz, g_sz, r_sz, Q_BLOCK, -1),
                               smask.reshape(b_sz, g_sz, 1, Q_BLOCK, -1))
        o_sel = jnp.einsum('bgrqm,bgqmd->bgrqd', p_sel, vs.reshape(b_sz, g_sz, Q_BLOCK, -1, dh))
        kw = lax.dynamic_slice_in_dim(k_win, s0, WINDOW + Q_BLOCK, axis=2)
        vw = lax.dynamic_slice_in_dim(v_win, s0, WINDOW + Q_BLOCK, axis=2)
        wpos = s0 - WINDOW + jnp.arange(WINDOW + Q_BLOCK, dtype=jnp.int32)
        wmask = (wpos[None, :] <= t[:, None]) & (wpos[None, :] > t[:, None] - WINDOW) & (wpos[None, :] >= 0)
        sw = jnp.einsum('bgrqd,bgkd->bgrqk', qblk, kw) * scale
        p_win = masked_softmax(sw, wmask)
        o_win = jnp.einsum('bgrqk,bgkd->bgrqd', p_win, vw)
        return (o_cmp, o_sel, o_win)

    o_cmp, o_sel, o_win = lax.map(one_block, jnp.arange(t_len // Q_BLOCK, dtype=jnp.int32))

    def to_bthd(o):
        return o.transpose(1, 0, 4, 2, 3, 5).reshape(b_sz, t_len, g_sz * r_sz, dh)

    return (to_bthd(o_cmp), to_bthd(o_sel), to_bthd(o_win))


def nsa_mixer(x, shared, w_in, w_out):
    k_cmp, v_cmp, k_sel, v_sel, k_win, v_win = shared
    b_sz, t_len, _ = x.shape
    proj = x @ w_in
    q = proj[..., :NSA_WIDTH].reshape(b_sz, t_len, NSA_HEADS, NSA_HEAD_DIM).transpose(0, 2, 1, 3)
    q = rope(q, jnp.arange(t_len, dtype=jnp.int32)).reshape(b_sz, NSA_KV_GROUPS, NSA_REP, t_len, NSA_HEAD_DIM)
    z = proj[..., NSA_WIDTH:NSA_WIDTH * (1 + N_BRANCH)].reshape(b_sz, t_len, N_BRANCH, NSA_HEADS, NSA_HEAD_DIM)
    gates = jax.nn.sigmoid(proj[..., NSA_WIDTH * (1 + N_BRANCH):].astype(jnp.float32)).reshape(
        b_sz, t_len, N_BRANCH, NSA_HEADS)
    o_cmp, o_sel, o_win = nsa_branch_attention(q, k_cmp, v_cmp, k_sel, v_sel, k_win, v_win)
    o = jnp.stack([o_cmp, o_sel, o_win], axis=2).astype(jnp.float32)
    o = jnp.sum(gates[..., None] * o * jax.nn.silu(z.astype(jnp.float32)), axis=2).astype(x.dtype)
    return o.reshape(b_sz, t_len, NSA_WIDTH) @ w_out


def deepnorm_residual(x, y, w, b):
    return layer_norm(DEEPNORM_ALPHA * x + y, w, b)


def setup_inputs(seed: int = 0) -> dict:
    key = jax.random.key(seed)
    ks = jax.random.split(key, 24)
    nrm = jax.random.normal
    f32 = jnp.float32
    x = nrm(ks[0], (BATCH, SEQ, D_MODEL), f32)
    a_w_in = nrm(ks[1], (N_A_LAYERS, D_MODEL, DN_IN_WIDTH), f32) * D_MODEL ** -0.5
    a_conv_w = nrm(ks[2], (N_A_LAYERS, DN_CONV, DN_QKV_WIDTH), f32) * DN_CONV ** -0.5
    a_a_log = jnp.log(jax.random.uniform(ks[3], (N_A_LAYERS, DN_HEADS), f32, 1.0, 16.0))
    dt = jnp.exp(jax.random.uniform(ks[4], (N_A_LAYERS, DN_HEADS), f32, math.log(1e-3), math.log(1e-1)))
    a_dt_bias = dt + jnp.log(-jnp.expm1(-dt))
    a_norm_w = 1.0 + 0.02 * nrm(ks[5], (N_A_LAYERS, DN_HEAD_V), f32)
    a_w_out = nrm(ks[6], (N_A_LAYERS, DN_VAL_WIDTH, D_MODEL), f32) * DN_VAL_WIDTH ** -0.5 * DEEPNORM_BETA
    a_ln_w = 1.0 + 0.02 * nrm(ks[7], (N_A_LAYERS, D_MODEL), f32)
    a_ln_b = 0.02 * nrm(ks[8], (N_A_LAYERS, D_MODEL), f32)
    s_w_kv = nrm(ks[9], (D_MODEL, SHARED_KV_WIDTH), f32) * D_MODEL ** -0.5
    s_pe_k = 0.1 * nrm(ks[10], (CMP_BLOCK, NSA_HEAD_DIM), f32)
    s_pe_v = 0.1 * nrm(ks[11], (CMP_BLOCK, NSA_HEAD_DIM), f32)
    s_w1_k = nrm(ks[12], (CMP_BLOCK, NSA_HEAD_DIM, CMP_HIDDEN), f32) * (CMP_BLOCK * NSA_HEAD_DIM) ** -0.5
    s_w2_k = nrm(ks[13], (CMP_HIDDEN, NSA_HEAD_DIM), f32) * CMP_HIDDEN ** -0.5
    s_w1_v = nrm(ks[14], (CMP_BLOCK, NSA_HEAD_DIM, CMP_HIDDEN), f32) * (CMP_BLOCK * NSA_HEAD_DIM) ** -0.5
    s_w2_v = nrm(ks[15], (CMP_HIDDEN, NSA_HEAD_DIM), f32) * CMP_HIDDEN ** -0.5
    b_w_in = nrm(ks[16], (N_B_LAYERS, D_MODEL, NSA_IN_WIDTH), f32) * D_MODEL ** -0.5
    b_w_out = nrm(ks[17], (N_B_LAYERS, NSA_WIDTH, D_MODEL), f32) * NSA_WIDTH ** -0.5 * DEEPNORM_BETA
    b_ln_w = 1.0 + 0.02 * nrm(ks[18], (N_B_LAYERS, D_MODEL), f32)
    b_ln_b = 0.02 * nrm(ks[19], (N_B_LAYERS, D_MODEL), f32)
    return {'x': x, 'a_w_in': a_w_in, 'a_conv_w': a_conv_w, 'a_a_log': a_a_log, 'a_dt_bias': a_dt_bias,
            'a_norm_w': a_norm_w, 'a_w_out': a_w_out, 'a_ln_w': a_ln_w, 'a_ln_b': a_ln_b,
            's_w_kv': s_w_kv, 's_pe_k': s_pe_k, 's_pe_v': s_pe_v, 's_w1_k': s_w1_k, 's_w2_k': s_w2_k,
            's_w1_v': s_w1_v, 's_w2_v': s_w2_v,
            'b_w_in': b_w_in, 'b_w_out': b_w_out, 'b_ln_w': b_ln_w, 'b_ln_b': b_ln_b}


def reference(x, a_w_in, a_conv_w, a_a_log, a_dt_bias, a_norm_w, a_w_out, a_ln_w, a_ln_b,
              s_w_kv, s_pe_k, s_pe_v, s_w1_k, s_w2_k, s_w1_v, s_w2_v,
              b_w_in, b_w_out, b_ln_w, b_ln_b):
    shared = None
    for layer in range(DEPTH):
        if layer < N_A_LAYERS:
            i = layer
            y = gated_deltanet_mixer(x, a_w_in[i], a_conv_w[i], a_a_log[i], a_dt_bias[i], a_norm_w[i], a_w_out[i])
            x = deepnorm_residual(x, y, a_ln_w[i], a_ln_b[i])
            if layer == N_A_LAYERS - 1:
                shared = nsa_shared_kv(x, s_w_kv, s_pe_k, s_pe_v, s_w1_k, s_w2_k, s_w1_v, s_w2_v)
        else:
            j = layer - N_A_LAYERS
            y = nsa_mixer(x, shared, b_w_in[j], b_w_out[j])
            x = deepnorm_residual(x, y, b_ln_w[j], b_ln_b[j])
    return x
```

```python
import math
from contextlib import ExitStack

import numpy as np
import concourse.bass as bass
import concourse.mybir as mybir
from concourse.bass_utils import run_bass_kernel_spmd

F32 = mybir.dt.float32
BF16 = mybir.dt.bfloat16
AF = mybir.ActivationFunctionType
ALU = mybir.AluOpType
AX = mybir.AxisListType

ENGS = ('sync', 'gpsimd', 'scalar', 'vector', 'tensor')

T = 8192
D = 1024
NH = 8
EPS = 1e-6
ALPHA = 4.0 ** 0.25


class Sched:
    def __init__(self, nc, csems, dsems):
        self.nc = nc
        self.csem = csems
        self.dsems = dsems
        self.ops = {e: [] for e in ENGS}
        self.cnt = {e: 0 for e in ENGS}
        self.dcount = [0] * len(dsems)
        nd = len(dsems)
        self.dpool = {'sync': list(range(0, nd - 4)), 'gpsimd': list(range(nd - 4, nd))}
        self.dptr = {'sync': 0, 'gpsimd': 0}
        self.lastw = {}
        self.readers = {}
        self.waited = {e: {} for e in ENGS}
        self.nops = 0

    def _sem(self, sk):
        return self.csem[sk[1]] if sk[0] == 'c' else self.dsems[sk[1]]

    kmap = {}

    def _expand(self, keys):
        out = []
        for k in keys:
            if isinstance(k, str):
                k = self.kmap.get(k, k)
            if isinstance(k, tuple) and len(k) == 2 and k[0] == 'PB':
                out.append(('PB', k[1], 0))
                out.append(('PB', k[1], 1))
            else:
                out.append(k)
        return out

    def op(self, eng, fn, reads=(), writes=(), dma=False):
        need = {}
        reads = self._expand(reads)
        writes = self._expand(writes)

        def want(tok):
            sk, val, src = tok
            if sk[0] == 'c' and src == eng and eng == 'tensor':
                return
            if need.get(sk, 0) < val:
                need[sk] = val

        for k in reads:
            t = self.lastw.get(k)
            if t is not None:
                want(t)
        for k in writes:
            t = self.lastw.get(k)
            if t is not None:
                want(t)
            for t in self.readers.get(k, ()):
                want(t)
        if dma:
            pool = self.dpool[eng]
            i = pool[self.dptr[eng] % len(pool)]
            self.dptr[eng] += 1
            if self.dcount[i] > 0:
                want((('d', i), 16 * self.dcount[i], None))
            self.dcount[i] += 1
            tok = (('d', i), 16 * self.dcount[i], eng)
            inc = 16
        else:
            self.cnt[eng] += 1
            tok = (('c', eng), self.cnt[eng], eng)
            inc = 1
        w = self.waited[eng]
        waits = []
        for sk, val in need.items():
            if w.get(sk, 0) < val:
                w[sk] = val
                waits.append((self._sem(sk), val))
        self.ops[eng].append((waits, fn, self._sem(tok[0]), inc))
        for k in writes:
            self.lastw[k] = tok
            self.readers[k] = []
        for k in reads:
            lst = self.readers.setdefault(k, [])
            if len(lst) < 64:
                lst.append(tok)
            else:
                d = {}
                for t in lst + [tok]:
                    if d.get(t[0], (0,))[0] < t[1]:
                        d[t[0]] = (t[1], t[2])
                self.readers[k] = [(sk, v[0], v[1]) for sk, v in d.items()]
        self.nops += 1
        return tok

    def wait_all(self, eng, keys):
        need = {}
        for k in keys:
            t = self.lastw.get(k)
            if t is not None:
                sk, val, src = t
                if need.get(sk, 0) < val:
                    need[sk] = val
        waits = [(self._sem(sk), val) for sk, val in need.items()]
        self.ops[eng].append((waits, None, None, 0))

    def drain_dmas(self, eng):
        waits = [(self.dsems[i], 16 * self.dcount[i]) for i in range(len(self.dsems)) if self.dcount[i] > 0]
        self.ops[eng].append((waits, None, None, 0))
        for i in range(len(self.dsems)):
            self.waited[eng][('d', i)] = 16 * self.dcount[i]

    def emit(self):
        nc = self.nc
        with nc.Block() as block:
            for e in ENGS:
                ops = self.ops[e]
                if not ops:
                    continue

                def body(engine, ops=ops):
                    for waits, fn, sem, inc in ops:
                        for s, v in waits:
                            engine.wait_ge(s, v)
                        if fn is not None:
                            ins = fn(engine)
                            ins.then_inc(sem, inc)

                getattr(block, e)(body)
        self.ops = {e: [] for e in ENGS}


def host_consts():
    c = {}
    half = 32
    inv = (np.float32(10000.0) ** (-(np.arange(half, dtype=np.float32) / np.float32(half)))).astype(np.float32)
    pos = np.arange(T, dtype=np.float32)
    ang = (pos[:, None] * inv[None, :]).astype(np.float32)
    c['rope_cs'] = np.concatenate([np.cos(ang), np.sin(ang)], axis=1).astype(np.float32)
    pc = (np.arange(512, dtype=np.float32) * 16 + 31).astype(np.float32)
    angc = (pc[None, :] * inv[:, None]).astype(np.float32)
    cosF = np.concatenate([np.cos(angc), np.cos(angc)], axis=0)
    sinF = np.concatenate([-np.sin(angc), np.sin(angc)], axis=0)
    c['cmp_cs'] = np.concatenate([cosF, sinF], axis=1).astype(np.float32)
    st = np.arange(512)[:, None] * 16
    bs = np.arange(128)[None, :] * 64
    ov = np.clip(np.minimum(st + 32, bs + 64) - np.maximum(st, bs), 0, None) / 32.0
    ov[511] = 0.0
    c['ovm'] = ov.astype(np.float32)
    n = np.arange(128)[:, None, None]
    dl = np.arange(17)[None, :, None]
    i = np.arange(128)[None, None, :]
    c['cmpmask'] = (16 * n + 31 - i <= 128 * dl).astype(np.float32)
    kk = np.arange(128)[:, None]
    qq = np.arange(128)[None, :]
    c['cwmask'] = np.stack([(kk <= qq), (kk > qq)], axis=1).astype(np.float32)
    bt = np.zeros((64, 128, 128), np.float32)
    for qb in range(64):
        t = qb * 128 + np.arange(128)
        cur = t // 64
        blk = np.arange(128)[None, :]
        forced = (blk == 0) | (blk == cur[:, None]) | (blk == cur[:, None] - 1)
        vis = blk * 64 <= t[:, None]
        bt[qb] = np.where(vis, np.where(forced, 1.0e4, 0.0), -1.0)
    c['btab'] = bt
    c['ident'] = np.eye(128, dtype=np.float32)
    k = np.arange(64)
    tri = (k[:, None] <= k[None, :]).astype(np.float32)
    sup = (k[:, None] > k[None, :]).astype(np.float32)
    negT = np.where(k[None, :] < k[:, None], -1e30, 0.0).astype(np.float32)
    offd = (k[:, None] != k[None, :]).astype(np.float32)
    eye = np.eye(64, dtype=np.float32)
    c64 = np.concatenate([
        tri, -tri, sup, np.ones((64, 128), np.float32), -np.ones((64, 64), np.float32),
        np.tile(negT[:, None, :], (1, 8, 1)).reshape(64, 512), offd,
    ], axis=1)
    c['c64'] = np.ascontiguousarray(c64)
    return c


def slot_qb(p, j):
    first = (p == (j % 2))
    return 2 * j if first else 2 * j + 1


def core_consts(p, hc):
    import ml_dtypes
    c = {}
    bw = np.zeros((128, 2, 2), np.float32)
    for e in range(2):
        first = (p == e)
        bw[:, e, 0] = 1.0 if first else 0.0
        bw[:, e, 1] = 0.0 if first else 1.0
    c['bw'] = bw
    qbs = [slot_qb(p, j) for j in range(32)]
    c['rope_q'] = np.concatenate([hc['rope_cs'][qb * 128:(qb + 1) * 128] for qb in qbs], axis=0)
    c['btab_c'] = np.stack([hc['btab'][qb] for qb in qbs], axis=0)
    n = np.arange(128)[:, None, None, None]
    nt = np.arange(4)[None, None, :, None]
    i = np.arange(128)[None, None, None, :]
    qbv = np.array(qbs)[None, :, None, None]
    c['cmask_c'] = (16 * (128 * nt + n) + 31 <= 128 * qbv + i).astype(ml_dtypes.bfloat16)
    kk = np.arange(128)[:, None]
    qq = np.arange(128)[None, :]
    caus = (kk <= qq).astype(np.float32)
    win = (kk > qq).astype(np.float32)
    one = np.ones((128, 128), np.float32)
    zero = np.zeros((128, 128), np.float32)
    dm = np.zeros((128, 2, 2, 128), np.float32)
    wm = np.zeros((128, 2, 6, 128), np.float32)
    for e in range(2):
        first = (p == e)
        dl = [caus, zero] if first else [one, caus]
        wl = [win, one, one, one, caus, zero] if first else [zero, win, one, one, one, caus]
        for a, m_ in enumerate(dl):
            dm[:, e, a, :] = m_
        for a, m_ in enumerate(wl):
            wm[:, e, a, :] = m_
    c['dmask'] = dm
    c['wmask'] = wm
    return c


C64_OFF = {}
_o = 0
for _n, _w in [('tri', 64), ('ntri', 64), ('sup', 64), ('ones', 128), ('nones', 64),
               ('negT8', 512), ('offd', 64)]:
    C64_OFF[_n] = (_o, _o + _w)
    _o += _w
C64_W = _o


class Builder:
    def __init__(self, phases=('p1',), dbg=None, ntiles1=None, nqb4=None, dbg4=None):
        self.nqb4 = nqb4
        self.dbg4 = dbg4
        self.phases = phases
        self.dbg = dbg or {}
        self.ntiles1 = ntiles1
        self.nc = bass.Bass("TRN2", target_bir_lowering=False)
        self.es = ExitStack()
        self.inputs = {}
        self.outputs = {}

    def din(self, name, shape, dt=F32):
        t = self.nc.dram_tensor(name, list(shape), dt, kind="ExternalInput")
        self.inputs[name] = t
        return t.ap()

    def dscratch(self, name, shape, dt):
        if name in self.dbg:
            t = self.nc.dram_tensor(name, list(shape), dt, kind="ExternalOutput")
            self.outputs[name] = t
        else:
            t = self.nc.dram_tensor(name, list(shape), dt)
        return t.ap()

    def dout(self, name, shape, dt):
        t = self.nc.dram_tensor(name, list(shape), dt, kind="ExternalOutput")
        self.outputs[name] = t
        return t.ap()

    def sb(self, name, shape, dt):
        return self.pes.enter_context(self.nc.sbuf_tensor(name, list(shape), dt))

    def ps(self, name, shape, dt):
        return self.es.enter_context(self.nc.psum_tensor(name, list(shape), dt))

    def V(self, fn, r=(), w=()):
        self.S.op('vector', fn, r, w)

    def A(self, fn, r=(), w=()):
        self.S.op('scalar', fn, r, w)

    def G(self, fn, r=(), w=()):
        self.S.op('gpsimd', fn, r, w)

    def PE(self, fn, r=(), w=()):
        self.S.op('tensor', fn, r, w)

    def DMA(self, out, in_, r=(), w=(), eng='sync', **kw):
        self.S.op(eng, lambda e: e.dma_start(out=out, in_=in_, **kw), r, w, dma=True)

    def mm(self, out, lhsT, rhs, start, stop, r, w):
        self.S.op('tensor', lambda e: e.matmul(out, lhsT=lhsT, rhs=rhs, start=start, stop=stop), r, w)

    def tr(self, out, in_, ident, r, w):
        self.S.op('tensor', lambda e: e.transpose(out=out, in_=in_, identity=ident), r, w)

    def act(self, out, in_, func, r, w, **kw):
        self.S.op('scalar', lambda e: e.activation(out=out, in_=in_, func=func, **kw), r, w)

    def tt(self, eng, out, in0, in1, op, r, w):
        self.S.op(eng, lambda e: e.tensor_tensor(out=out, in0=in0, in1=in1, op=op), r, w)

    def cp(self, eng, out, in_, r, w):
        if eng == 'scalar':
            self.S.op(eng, lambda e: e.copy(out=out, in_=in_), r, w)
        else:
            self.S.op(eng, lambda e: e.tensor_copy(out=out, in_=in_), r, w)

    def build(self):
        nc = self.nc
        es = self.es
        with es:
            csems = {e: es.enter_context(nc.semaphore("c_" + e)) for e in ENGS}
            dsems = [es.enter_context(nc.semaphore("d%d" % i)) for i in range(12)]
            self.S = Sched(nc, csems, dsems)
            self.pes = es
            self.PB = [es.enter_context(nc.psum_tensor("pb%d" % i, [128, 512], F32)) for i in range(8)]
            self.setup_common()
            if 'p1' in self.phases:
                with ExitStack() as pes:
                    self.pes = pes
                    self.phase1()
                    self.S.drain_dmas('sync')
                    self.S.emit()
            for ph in ('p2', 'p3', 'p4', 'p5'):
                if ph in self.phases:
                    with ExitStack() as pes:
                        self.pes = pes
                        getattr(self, 'phase' + ph[1])()
                        self.S.drain_dmas('sync')
                        self.S.emit()
            self.pes = es
            self.finish()
            self.S.emit()
        return nc

    def finish(self):
        if not self.outputs:
            o = self.dout("out", [T // 2, D], F32)
            self.DMA(o[0:128, 0:128], self.ident[:], r=['ident'], w=['dummy_out'])
            self.final_keys.append('dummy_out')
        self.S.wait_all('sync', list(self.final_keys))

    def setup_common(self):
        self.final_keys = []
        x = self.din("x", [T, D])
        self.x = x
        self.c_ident = self.din("ident", [128, 128])
        self.c_c64 = self.din("c64", [64, C64_W])
        self.ident = self.sb("ident_sb", [128, 128], F32)
        self.identb = self.sb("identb_sb", [128, 128], BF16)
        self.c64 = self.sb("c64_sb", [64, C64_W], F32)
        self.DMA(self.ident[:], self.c_ident, w=['ident'])
        self.DMA(self.c64[:], self.c_c64, w=['c64'])
        self.cp('vector', self.identb[:], self.ident[:], ['ident'], ['identb'])
        self.ones128 = self.sb("ones128", [128, 128], F32)
        self.V(lambda e: e.memset(self.ones128[:], 1.0), w=['ones128'])
        if 'p1' in self.phases:
            self.h1_d = self.dscratch("h1_d", [T, D], F32)
            self.h1T_d = self.dscratch("h1T_d", [128, 8, T], BF16)
        else:
            self.h1_d = self.din("h1_d", [T, D], F32)
            self.h1T_d = self.din("h1T_d", [128, 8, T], BF16)
        self.kselT_d = self.dscratch("kselT_d", [4, 64, T], BF16)
        self.kwinT_d = self.dscratch("kwinT_d", [4, 64, T], BF16)
        self.kcsT_d = self.dscratch("kcsT_d", [4, 64, T], BF16)
        self.vcsT_d = self.dscratch("vcsT_d", [4, 64, T], BF16)
        self.vsel_d = self.dscratch("vsel_d", [4, T, 65], BF16)
        self.vwin_d = self.dscratch("vwin_d", [4, T, 65], BF16)
        self.qT_d = self.dscratch("qT_d", [4, 32, 64, 512], BF16)
        self.gz_d = self.dscratch("gz_d", [T // 2, 3, 1024], BF16)
        self.og_d = self.dscratch("og_d", [T // 2, 1024], BF16)
        if self.dbg4 is not None:
            self.d4 = {'imp': self.dout('dbg_imp', [128, 128], F32), 'sel': self.dout('dbg_sel', [128, 128], BF16),
                       'ocmp': self.dout('dbg_ocmp', [128, 4, 64], F32), 'osel': self.dout('dbg_osel', [128, 4, 64], F32),
                       'owin': self.dout('dbg_owin', [128, 4, 64], F32)}
            self.final_keys.append('dbg4')
        self.kcmpT = self.sb("kcmpT", [64, 4, 512], BF16)
        self.vcaug = self.sb("vcaug", [128, 4, 4, 193], BF16)

    def c64v(self, name, heads=False):
        a, b = C64_OFF[name]
        v = self.c64[:, a:b]
        if heads:
            v = v.rearrange("p (h i) -> p h i", h=8)
        return v

    def phase1(self):
        nc = self.nc
        S = self.S
        PB = self.PB
        TT = 256
        NCH = 4
        NT = T // TT if self.ntiles1 is None else self.ntiles1
        a_w_in = self.din("a_w_in", [1024, 4112])
        a_conv_w = self.din("a_conv_w", [4, 3072])
        a_a_log = self.din("a_a_log", [1, 8])
        a_dt_bias = self.din("a_dt_bias", [1, 8])
        a_norm_w = self.din("a_norm_w", [128, 1])
        a_w_out = self.din("a_w_out", [1024, 1024])
        a_ln_w = self.din("a_ln_w", [1, 1024])
        a_ln_b = self.din("a_ln_b", [1, 1024])

        w_in = self.sb("w_in_sb", [128, 8, 4112], BF16)
        qkT = self.sb("qkT", [128, 2, 8, TT], F32)
        qT = qkT[:, 0]
        kT = qkT[:, 1]
        w_out = qkT[:].rearrange("p a h t -> p (a h t)").bitcast(BF16).rearrange("p (c n) -> p c n", c=8)
        w_out_d = self.dscratch("w_out_bf_d", [128, 8, 1024], BF16)
        for kc in range(8):
            self.DMA(w_in[:, kc, :], a_w_in[kc * 128:(kc + 1) * 128, :], w=['w_in'], eng='gpsimd')
        self.DMA(w_out, a_w_out.rearrange("(c p) n -> p c n", p=128), w=['qT', 'kT'], eng='gpsimd')
        self.DMA(w_out_d, w_out, r=['qT', 'kT'], w=['w_out_d'])
        xs = [self.sb("xs0", [128, 2, 1024], F32)]
        vcur2 = [xs[0][0:64, i, :].rearrange("p (h d) -> p h d", h=8) for i in range(2)]
        cw4 = xs[0][0:4].rearrange("p s d -> p (s d)")
        convw = self.sb("convw", [128, 24, 4], F32)
        pcw = PB[0][:, 0:96].rearrange("p (b i) -> p b i", i=4)
        for part in range(2):
            nb_ = 16 if part == 0 else 8
            self.DMA(cw4[:, 0:nb_ * 128], a_conv_w[:, part * 2048:part * 2048 + nb_ * 128], w=['xs'])
            for b in range(nb_):
                self.tr(pcw[:, part * 16 + b, :], cw4[:, b * 128:(b + 1) * 128], self.ident[0:4, 0:4], ['xs', 'ident'], [('PB', 0)])
        self.cp('vector', convw[:], pcw, [('PB', 0)], ['convw'])
        normw = self.sb("normw", [128, 1], F32)
        self.DMA(normw[:], a_norm_w, w=['normw'])
        lnw_b = self.sb("lnw_b", [128, 1024], F32)
        lnb_b = self.sb("lnb_b", [128, 1024], F32)
        self.DMA(lnw_b[:], a_ln_w.partition_broadcast(128), w=['lnw_b'])
        self.DMA(lnb_b[:], a_ln_b.partition_broadcast(128), w=['lnb_b'])
        dtb = self.sb("dtb", [64, 8], F32)
        alog = self.sb("alog", [64, 8], F32)
        negA = self.sb("negA", [64, 8], F32)
        self.DMA(dtb[:], a_dt_bias.partition_broadcast(64), w=['dtb'])
        self.DMA(alog[:], a_a_log.partition_broadcast(64), w=['alog'])
        self.act(negA[:], alog[:], AF.Exp, ['alog'], ['negA'])
        self.V(lambda e: e.tensor_scalar(out=negA[:], in0=negA[:], scalar1=-1.0, scalar2=None, op0=ALU.mult), ['negA'], ['negA'])

        halo = self.sb("halo", [128, 24, 3], F32)
        self.V(lambda e: e.memset(halo[:], 0.0), w=['halo'])
        Sst = self.sb("Sst", [128, 8, 128], F32)
        self.V(lambda e: e.memset(Sst[:], 0.0), w=[('Sst', 0), ('Sst', 1)])

        xT = self.sb("xT", [128, 8, TT], BF16)
        pre = [self.sb("pre%d" % i, [128, TT + 3], F32) for i in range(2)]
        yb = [self.sb("yb%d" % i, [128, TT], F32) for i in range(2)]
        sb4 = [self.sb("s%d" % i, [128, TT], F32) for i in range(4)]
        sq = [self.sb("sq%d" % i, [128, TT], F32) for i in range(2)]
        lnb = [self.sb("lnt%d" % i, [128, TT], F32) for i in range(2)]
        ktokp = self.sb("ktok", [128, 2, 8, 128], F32)
        vtokp = self.sb("vtok", [128, 2, 8, 128], F32)

        def ktok_c(c):
            return ktokp[(c // 2) * 64:(c // 2) * 64 + 64, c % 2]

        def vtok_c(c):
            return vtokp[(c // 2) * 64:(c // 2) * 64 + 64, c % 2]
        szT = self.sb("szT", [128, 8, TT], BF16)
        ogT = self.sb("ogT", [128, 8, TT], BF16)
        beta = self.sb("beta", [64, NCH, 8], F32)
        nbeta = self.sb("nbeta", [64, NCH, 8], F32)
        gt = self.sb("gt", [64, NCH, 8], F32)
        xg = self.sb("xg", [64, NCH, 8], F32)

        decT = self.sb("decT", [64, 8, 64], F32)
        eg2 = [self.sb("eg%d" % i, [128, 24], F32) for i in range(2)]
        tmpA = self.sb("tmpA", [64, 8, 64], F32)
        G1 = tmpA
        Mb = [self.sb("Mb%d" % i, [64, 8, 64], BF16) for i in range(2)]
        MTb = [self.sb("MTb%d" % i, [64, 8, 64], BF16) for i in range(2)]
        Pb = [self.sb("Pb%d" % i, [64, 8, 64], BF16) for i in range(2)]
        XTf2 = [self.sb("XTf%d" % i, [64, 8, 64], F32) for i in range(2)]
        attT2 = [self.sb("attT%d" % i, [64, 8, 64], F32) for i in range(2)]
        kdec2 = [self.sb("kdec%d" % i, [64, 8, 128], F32) for i in range(2)]
        Rp = self.sb("Rp", [64, 4, 128], F32)
        qs = self.sb("qs", [64, 4, 128], F32)
        vnew = self.sb("vnew", [64, 4, 128], F32)
        oc = self.sb("oc", [64, 8, 128], F32)
        oss = self.sb("oss", [64, 8], F32)
        orr = self.sb("orr", [64, 8], F32)
        onb = self.sb("onb", [64, 8, 128], BF16)
        t2 = self.sb("t2", [128, 4, 128], F32)
        rr = self.sb("rr", [128, 1024], F32)
        h1s = rr
        xn = rr
        h1T = self.sb("h1T", [128, 8, 128], BF16)
        bst = self.sb("bst", [128, 2, 6], F32)
        mv = self.sb("mv", [128, 2], F32)
        lrs = self.sb("lrs", [128, 1], F32)
        rstd = self.sb("rstd", [128, 1], F32)
        nb = self.sb("nb", [128, 1], F32)

        tri = self.c64v('tri')
        ntri = self.c64v('ntri')
        sup = self.c64v('sup')
        ones64 = self.c64v('ones')
        negT8 = self.c64v('negT8')
        offd8 = self.c64v('offd').unsqueeze(1).to_broadcast([64, 8, 64])
        eye8 = self.ident[0:64, 0:64].unsqueeze(1).to_broadcast([64, 8, 64])
        tri8 = tri.unsqueeze(1).to_broadcast([64, 8, 64])
        id64 = self.ident[0:64, 0:64]
        idb64 = self.identb[0:64, 0:64]
        lnqs = math.log(128.0 ** -0.5)

        dbg = self.dbg
        if 'dbg_qT' in dbg:
            d_qT = self.dout('dbg_qT', [128, 8, TT], F32)
            d_kT = self.dout('dbg_kT', [128, 8, TT], F32)
            d_vtok = self.dout('dbg_vtok', [128, 2, 8, 128], F32)
            d_ktok = self.dout('dbg_ktok', [128, 2, 8, 128], F32)
            d_beta = self.dout('dbg_beta', [64, NCH, 8], F32)
            d_g = self.dout('dbg_g', [64, NCH, 8], F32)
            d_oc = self.dout('dbg_oc', [NT * NCH, 64, 8, 128], F32)
            d_decT = self.dout('dbg_decT', [64, 8, 64], F32)
            d_XT = self.dout('dbg_XT', [64, 8, 64], F32)
            d_M = self.dout('dbg_M', [64, 8, 64], F32)
            d_ogT = self.dout('dbg_ogT', [128, 8, TT], BF16)
            d_oss = self.dout('dbg_oss', [64, 8], F32)
            d_orr = self.dout('dbg_orr', [64, 8], F32)
            d_mv = self.dout('dbg_mv', [128, 2], F32)
            d_rstd = self.dout('dbg_rstd', [128, 1], F32)
            d_bst = self.dout('dbg_bst', [128, 12], F32)
            d_rr = self.dout('dbg_rr', [2, 128, 1024], F32)
            d_wout = self.dout('dbg_wout', [128, 8, 1024], BF16)

        def bank(i, shape=None, dt=None):
            v = PB[i][:]
            if dt is not None:
                v = v.bitcast(dt)
            return v

        for t in range(NT):
            t0 = t * TT
            xt = xs[0]
            kx = 'xs'
            self.DMA(xt[:], self.x[t0:t0 + TT, :].rearrange("(s p) d -> p s d", p=128), w=[kx])
            pTf = PB[2][:].rearrange("p (c n) -> p c n", c=4)
            for s in range(2):
                for g4 in range(2):
                    for k4 in range(4):
                        kc = g4 * 4 + k4
                        self.tr(pTf[:, k4, :], xt[:, s, kc * 128:(kc + 1) * 128], self.ident[:], [kx, 'ident'], [('PB', 2)])
                    self.cp('scalar', xT[:, g4 * 4:g4 * 4 + 4, s * 128:(s + 1) * 128], pTf, [('PB', 2)], ['xT'])
            ppb = [0, 1, 6, 7]

            def s1_mm(blk):
                pbk = ppb[blk % 4]
                pp = PB[pbk][:, 0:TT]
                for kc in range(8):
                    self.mm(pp, w_in[:, kc, blk * 128:(blk + 1) * 128], xT[:, kc, :], kc == 0, kc == 7, ['w_in', 'xT'], [('PB', pbk)])

            def bufs(blk):
                pb = blk % 2
                sp = blk % 4
                return pb, sp, pre[pb], ('pre', pb), yb[pb], ('y', pb), sb4[sp], ('s', sp)

            def a1(blk):
                pbk = ppb[blk % 4]
                pp = PB[pbk][:, 0:TT]
                kp = ('PB', pbk)
                if blk >= 24:
                    self.act(szT[:, blk - 24, :], pp, AF.Silu, [kp], ['szT'])
                    return
                pb, sp, pr, kpr, y, ky, s_, ks = bufs(blk)
                self.cp('scalar', pr[:, 3:TT + 3], pp, [kp], [kpr])
                self.cp('gpsimd', pr[:, 0:3], halo[:, blk, :], ['halo'], [kpr])

            def a2(blk):
                if blk >= 24:
                    return
                pb, sp, pr, kpr, y, ky, s_, ks = bufs(blk)
                self.V(lambda e, pr=pr, y=y, blk=blk: e.tensor_scalar(out=y[:], in0=pr[:, 0:TT], scalar1=convw[:, blk, 0:1], scalar2=None, op0=ALU.mult),
                       [kpr, 'convw'], [ky])
                for i in range(1, 4):
                    self.V(lambda e, pr=pr, y=y, blk=blk, i=i: e.scalar_tensor_tensor(out=y[:], in0=pr[:, i:i + TT], scalar=convw[:, blk, i:i + 1], in1=y[:], op0=ALU.mult, op1=ALU.add),
                           [kpr, 'convw', ky], [ky])
                self.cp('gpsimd', halo[:, blk, :], pr[:, TT:TT + 3], [kpr], ['halo'])

            def a3(blk):
                if blk >= 24:
                    return
                pb, sp, pr, kpr, y, ky, s_, ks = bufs(blk)
                self.act(s_[:], y[:], AF.Silu, [ky], [ks])

            def b45(blk):
                if blk >= 16:
                    return
                pb, sp, pr, kpr, y, ky, s_, ks = bufs(blk)
                q_ = sq[pb]
                ksq = ('sq', pb)
                self.tt('gpsimd', q_[:], s_[:], s_[:], ALU.mult, [ks], [ksq])
                psb = 3 if pb == 0 else 5
                self.mm(PB[psb][:, 0:TT], self.ones128[:], q_[:], True, True, ['ones128', ksq], [('PB', psb)])

            def b6(blk):
                if blk >= 16:
                    return
                pb = blk % 2
                psb = 3 if pb == 0 else 5
                l_ = lnb[pb]
                kl = ('lnt', pb)
                self.act(l_[:], PB[psb][:, 0:TT], AF.Ln, [('PB', psb)], [kl], bias=EPS)
                self.act(l_[:], l_[:], AF.Exp, [kl], [kl], scale=-0.5, bias=(lnqs if blk < 8 else 0.0))

            def b78(blk):
                if blk >= 24:
                    return
                pb, sp, pr, kpr, y, ky, s_, ks = bufs(blk)
                h = blk % 8
                pkb = 4 if pb == 0 else 2
                pk = PB[pkb][0:64, :].rearrange("p (c d) -> p c d", c=NCH)
                if blk < 16:
                    l_ = lnb[pb]
                    kl = ('lnt', pb)
                    dest = qT if blk < 8 else kT
                    kd = 'qT' if blk < 8 else 'kT'
                    self.tt('vector', dest[:, h, :], s_[:], l_[:], ALU.mult, [ks, kl], [kd])
                    if blk >= 8:
                        for c in range(NCH):
                            self.tr(pk[:, c, :], kT[:, h, c * 64:(c + 1) * 64], self.ident[:], ['kT', 'ident'], [('PB', pkb)])
                        for c2 in range(2):
                            self.cp('scalar', ktokp[c2 * 64:c2 * 64 + 64, :, h, :], pk[:, c2 * 2:c2 * 2 + 2, :], [('PB', pkb)], ['ktok'])
                else:
                    for c in range(NCH):
                        self.tr(pk[:, c, :], s_[:, c * 64:(c + 1) * 64], self.ident[:], [ks, 'ident'], [('PB', pkb)])
                    for c2 in range(2):
                        self.cp('scalar', vtokp[c2 * 64:c2 * 64 + 64, :, h, :], pk[:, c2 * 2:c2 * 2 + 2, :], [('PB', pkb)], ['vtok'])

            groups = [(2 * i, 2 * i + 1) for i in range(16)]
            NG = len(groups)

            def both(fn, g):
                fn(g[0])
                fn(g[1])

            both(s1_mm, groups[0])
            both(s1_mm, groups[1])
            both(a1, groups[0])
            both(a2, groups[0])
            both(a3, groups[0])
            for gi in range(NG):
                g = groups[gi]
                gn = groups[gi + 1] if gi + 1 < NG else None
                if gi + 2 < NG:
                    both(s1_mm, groups[gi + 2])
                if gn:
                    both(a1, gn)
                both(b45, g)
                if gn:
                    both(a2, gn)
                both(b6, g)
                if gn:
                    both(a3, gn)
                both(b78, g)
            pL = PB[5][0:64, 0:NCH * 16].rearrange("p (c n) -> p c n", c=NCH)
            for c in range(NCH):
                for kc in range(8):
                    self.mm(pL[:, c, :], xT[:, kc, c * 64:(c + 1) * 64], w_in[:, kc, 4096:4112], kc == 0, kc == 7, ['xT', 'w_in'], [('PB', 5)])
            self.act(beta[:], pL[:, :, 0:8], AF.Sigmoid, [('PB', 5)], ['beta'])
            self.tt('vector', xg[:], pL[:, :, 8:16], dtb[:].unsqueeze(1).to_broadcast([64, NCH, 8]), ALU.add, [('PB', 5), 'dtb'], ['xg'])
            self.act(xg[:], xg[:], AF.Exp, ['xg'], ['xg'])
            self.act(xg[:], xg[:], AF.Ln, ['xg'], ['xg'], bias=1.0)
            self.tt('vector', gt[:], xg[:], negA[:].unsqueeze(1).to_broadcast([64, NCH, 8]), ALU.mult, ['xg', 'negA'], ['gt'])
            self.V(lambda e: e.tensor_scalar(out=nbeta[:], in0=beta[:], scalar1=-1.0, scalar2=None, op0=ALU.mult), ['beta'], ['nbeta'])
            if 'dbg_qT' in dbg and t == 0:
                self.DMA(d_qT, qT[:], ['qT'], ['dbg'])
                self.DMA(d_kT, kT[:], ['kT'], ['dbg'])
                self.DMA(d_vtok, vtokp[:], ['vtok'], ['dbg'])
                self.DMA(d_ktok, ktokp[:], ['ktok'], ['dbg'])
                self.DMA(d_beta, beta[:], ['beta'], ['dbg'])
                self.DMA(d_g, gt[:], ['gt'], ['dbg'])
                self.final_keys.append('dbg')

            def prep(c):
                cp_ = c % 2
                cs = slice(c * 64, (c + 1) * 64)
                attT = attT2[cp_]
                G2 = attT
                eg = eg2[cp_]
                kdec = kdec2[cp_]
                XTf = XTf2[cp_]
                kat = ('attT', cp_)
                keg = ('eg', cp_)
                gcv = gt[:, c, :]
                gb = gcv.unsqueeze(2).to_broadcast([64, 8, 64])
                self.cp('vector', G1[:], gb, ['gt'], ['tmpA'])
                self.tt('vector', G2[:], tri8, gb, ALU.mult, ['gt', 'c64'], [kat])
                pD = PB[7][0:64, :]
                kD = ('PB', 7)
                self.mm(pD, ones64[:, 0:64], G2[:].rearrange("p h i -> p (h i)"), True, False, ['c64', kat], [kD])
                self.mm(pD, ntri, G1[:].rearrange("p h i -> p (h i)"), False, False, ['c64', 'tmpA'], [kD])
                self.mm(pD, id64, negT8, False, True, ['c64', 'ident'], [kD])
                self.act(decT[:].rearrange("p h i -> p (h i)"), pD, AF.Exp, [kD], ['decT'])
                pG = PB[3]
                kG = ('PB', 3)
                self.mm(pG[0:64, 0:8], tri, gcv, True, True, ['c64', 'gt'], [kG])
                self.mm(pG[0:64, 8:16], sup, gcv, True, True, ['c64', 'gt'], [kG])
                self.mm(pG[:, 16:24], ones64, gcv, True, True, ['c64', 'gt'], [kG])
                self.act(eg[0:64, 0:16], pG[0:64, 0:16], AF.Exp, [kG], [keg])
                self.act(eg[:, 16:24], pG[:, 16:24], AF.Exp, [kG], [keg])
                edec = eg[0:64, 8:16]
                yield
                pA = PB[4][0:64, :].rearrange("p (h i) -> p h i", h=8)
                pQK = PB[5][0:64, :].rearrange("p (h i) -> p h i", h=8)
                for h in range(8):
                    self.mm(pA[:, h, :], kT[:, h, cs], kT[:, h, cs], True, True, ['kT'], [('PB', 4)])
                for h in range(8):
                    self.mm(pQK[:, h, :], kT[:, h, cs], qT[:, h, cs], True, True, ['kT', 'qT'], [('PB', 5)])
                M = Mb[0]
                MT = MTb[0]
                P = Pb[0]
                self.tt('vector', tmpA[:], pA, decT[:], ALU.mult, [('PB', 4), 'decT'], ['tmpA'])
                self.tt('vector', tmpA[:], tmpA[:], beta[:, c, :].unsqueeze(2).to_broadcast([64, 8, 64]), ALU.mult, ['tmpA', 'beta'], ['tmpA'])
                self.tt('gpsimd', M[:], tmpA[:], offd8, ALU.mult, ['tmpA', 'c64'], [('M', 0, 0), ('M', 0, 1)])
                yield
                self.tt('vector', attT[:], pQK, decT[:], ALU.mult, [('PB', 5), 'decT'], [kat])
                pMT = PB[6][0:64, :].bitcast(BF16)[:, 0:512].rearrange("p (h i) -> p h i", h=8)
                for h in range(8):
                    self.tr(pMT[:, h, :], M[:, h, :], idb64, [('M', 0, 0), ('M', 0, 1), 'identb'], [('PB', 6)])
                self.cp('scalar', MT[:], pMT, [('PB', 6)], [('MT', 0, 0), ('MT', 0, 1)])
                self.tt('vector', P[:], eye8, M[:], ALU.subtract, ['ident', ('M', 0, 0), ('M', 0, 1)], [('P', 0, 0), ('P', 0, 1)])
                self.cp('gpsimd', kdec[:], ktok_c(c), ['ktok'], [('kdec', cp_)])
                self.tt('gpsimd', kdec[:], kdec[:], edec.unsqueeze(2).to_broadcast([64, 8, 128]), ALU.mult, [('kdec', cp_), keg], [('kdec', cp_)])
                self.cp('gpsimd', vcur2[cp_], vtok_c(c), ['vtok'], ['xs'])
                yield
                cur = 0
                ibank = [(3, 4), (5, 6)]
                for lvl in range(1, 6):
                    nxt = 1 - cur
                    M, MT, P = Mb[cur], MTb[cur], Pb[cur]
                    M2, M2T, P2 = Mb[nxt], MTb[nxt], Pb[nxt]
                    for grp in range(2):
                        gs = slice(grp * 4, grp * 4 + 4)
                        b1, b2_ = ibank[grp]
                        pM2T = PB[b1][0:64, 0:256].rearrange("p (h i) -> p h i", h=4)
                        pM2 = PB[b2_][0:64, 0:256].rearrange("p (h i) -> p h i", h=4)
                        kin = [('M', cur, grp), ('MT', cur, grp)]
                        for hh in range(4):
                            h = grp * 4 + hh
                            self.mm(pM2T[:, hh, :], M[:, h, :], MT[:, h, :], True, True, kin, [('PB', b1)])
                        self.cp('scalar', M2T[:, gs, :], pM2T, [('PB', b1)], [('MT', nxt, grp)])
                        if lvl < 5:
                            for hh in range(4):
                                h = grp * 4 + hh
                                self.mm(pM2[:, hh, :], MT[:, h, :], M[:, h, :], True, True, kin, [('PB', b2_)])
                            self.cp('vector', M2[:, gs, :], pM2, [('PB', b2_)], [('M', nxt, grp)])
                    yield
                    for grp in range(2):
                        gs = slice(grp * 4, grp * 4 + 4)
                        b1, b2_ = ibank[grp]
                        pPP = PB[b1][0:64, 0:256].rearrange("p (h i) -> p h i", h=4)
                        for hh in range(4):
                            h = grp * 4 + hh
                            self.mm(pPP[:, hh, :], M2T[:, h, :], P[:, h, :], True, True, [('MT', nxt, grp), ('P', cur, grp)], [('PB', b1)])
                        if lvl == 5:
                            self.tt('vector', XTf[:, gs, :], P[:, gs, :], pPP, ALU.add, [('P', cur, grp), ('PB', b1)], [('XT', cp_, grp)])
                        else:
                            self.tt('vector', P2[:, gs, :], P[:, gs, :], pPP, ALU.add, [('P', cur, grp), ('PB', b1)], [('P', nxt, grp)])
                    cur = nxt
                    yield
                if 'dbg_qT' in dbg and t == 0 and c == 0:
                    self.DMA(d_decT, decT[:], ['decT'], ['dbg'])
                    self.DMA(d_XT, XTf[:], [('XT', cp_, 0), ('XT', cp_, 1)], ['dbg'])

            def scan_out(c):
                cp_ = c % 2
                cs = slice(c * 64, (c + 1) * 64)
                attT = attT2[cp_]
                eg = eg2[cp_]
                kdec = kdec2[cp_]
                osq = kdec
                XT = XTf2[cp_]
                vcur = vcur2[cp_]
                kat = ('attT', cp_)
                keg = ('eg', cp_)
                egc = eg[0:64, 0:8]
                glb = eg[:, 16:24]
                pKS = PB[0][0:64, :].rearrange("p (h d) -> p h d", h=4)
                pQS = PB[1][0:64, :].rearrange("p (h d) -> p h d", h=4)
                pS = PB[2][:, :].rearrange("p (h d) -> p h d", h=4)
                kX, kY, kZ = ('PB', 0), ('PB', 1), ('PB', 2)
                for half in range(2):
                    hs = slice(half * 4, half * 4 + 4)
                    egb = egc[:, hs].unsqueeze(2).to_broadcast([64, 4, 128])
                    for hh in range(4):
                        h = half * 4 + hh
                        self.mm(pKS[:, hh, :], kT[:, h, cs], Sst[:, h, :], True, True, ['kT', ('Sst', half)], [kX])
                    for hh in range(4):
                        h = half * 4 + hh
                        self.mm(pQS[:, hh, :], qT[:, h, cs], Sst[:, h, :], True, True, ['qT', ('Sst', half)], [kY])
                    self.tt('vector', Rp[:], pKS, egb, ALU.mult, [kX, keg], ['Rp'])
                    self.tt('vector', Rp[:], Rp[:], vcur[:, hs, :], ALU.subtract, ['Rp', 'xs'], ['Rp'])
                    self.tt('vector', qs[:], pQS, egb, ALU.mult, [kY, keg], ['qs'])
                    yield
                    pVN = pKS
                    for hh in range(4):
                        h = half * 4 + hh
                        self.mm(pVN[:, hh, :], XT[:, h, :], Rp[:, hh, :], True, True, [('XT', cp_, half), 'Rp'], [kX])
                    self.tt('vector', vnew[:], pVN, nbeta[:, c, hs].unsqueeze(2).to_broadcast([64, 4, 128]), ALU.mult, [kX, 'nbeta'], ['vnew'])
                    yield
                    pAV = pQS
                    for hh in range(4):
                        h = half * 4 + hh
                        self.mm(pAV[:, hh, :], attT[:, h, :], vnew[:, hh, :], True, True, [kat, 'vnew'], [kY])
                    self.tt('vector', oc[:, hs, :], pAV, qs[:], ALU.add, [kY, 'qs'], [('oc', half)])
                    for hh in range(4):
                        h = half * 4 + hh
                        self.mm(pS[:, hh, :], kdec[:, h, :], vnew[:, hh, :], True, True, [('kdec', cp_), 'vnew'], [kZ])
                    self.tt('gpsimd', t2[:], Sst[:, hs, :], glb[:, hs].unsqueeze(2).to_broadcast([128, 4, 128]), ALU.mult, [('Sst', half), keg], ['t2'])
                    self.tt('vector', Sst[:, hs, :], t2[:], pS, ALU.add, ['t2', kZ], [('Sst', half)])
                    yield
                if 'dbg_qT' in dbg:
                    self.DMA(d_oc[t * NCH + c], oc[:], [('oc', 0), ('oc', 1)], ['dbg'])
                self.tt('gpsimd', osq[:], oc[:], oc[:], ALU.mult, [('oc', 0), ('oc', 1)], [('kdec', cp_)])
                self.V(lambda e: e.tensor_reduce(out=oss[:], in_=osq[:], axis=AX.X, op=ALU.add), [('kdec', cp_)], ['oss'])
                self.act(orr[:], oss[:], AF.Ln, ['oss'], ['orr'], scale=1.0 / 128.0, bias=EPS)
                self.act(orr[:], orr[:], AF.Exp, ['orr'], ['orr'], scale=-0.5)
                yield
                self.tt('vector', onb[:], oc[:], orr[:].unsqueeze(2).to_broadcast([64, 8, 128]), ALU.mult, [('oc', 0), ('oc', 1), 'orr'], ['onb'])
                pOT = PB[2][:].bitcast(BF16)[:, 0:512].rearrange("p (h i) -> p h i", h=8)
                for h in range(8):
                    self.tr(pOT[:, h, :], onb[:, h, :], idb64, ['onb', 'identb'], [('PB', 2)])
                self.V(lambda e, cs=cs, pOT=pOT: e.scalar_tensor_tensor(out=ogT[:, :, cs], in0=pOT, scalar=normw[:, 0:1], in1=szT[:, :, cs], op0=ALU.mult, op1=ALU.mult),
                       [('PB', 2), 'normw', 'szT'], ['ogT'])
                yield

            for c in range(NCH + 1):
                gens = []
                if c < NCH:
                    gens.append(prep(c))
                if c >= 1:
                    gens.append(scan_out(c - 1))
                while gens:
                    for g_ in list(gens):
                        try:
                            next(g_)
                        except StopIteration:
                            gens.remove(g_)
            self.DMA(w_out, w_out_d, r=['w_out_d'], w=['qT', 'kT'])
            if 'dbg_qT' in dbg and t == 0:
                self.DMA(d_ogT, ogT[:], ['ogT'], ['dbg'])
                self.DMA(d_wout, w_out, ['qT', 'kT'], ['dbg'])
            for s in range(2):
                pY = [PB[0][:], PB[1][:]]
                for half in range(2):
                    for h in range(8):
                        self.mm(pY[half], ogT[:, h, s * 128:(s + 1) * 128], w_out[:, h, half * 512:(half + 1) * 512], h == 0, h == 7, ['ogT', 'qT', 'kT'], [('PB', half)])
                self.DMA(rr[:], self.x[t0 + s * 128:t0 + (s + 1) * 128, :], w=['rr'])
                for half in range(2):
                    self.V(lambda e, half=half: e.scalar_tensor_tensor(out=rr[:, half * 512:(half + 1) * 512], in0=rr[:, half * 512:(half + 1) * 512], scalar=ALPHA, in1=pY[half], op0=ALU.mult, op1=ALU.add),
                           ['rr', ('PB', half)], ['rr'])
                    self.V(lambda e, half=half: e.bn_stats(out=bst[:, half, :], in_=rr[:, half * 512:(half + 1) * 512]), ['rr'], ['bst'])
                if 'dbg_qT' in dbg and t == 0:
                    self.DMA(d_rr[s], rr[:], ['rr'], ['dbg'])
                self.V(lambda e: e.bn_aggr(out=mv[:], in_=bst[:].rearrange("p a b -> p (a b)")), ['bst'], ['mv'])
                self.act(lrs[:], mv[:, 1:2], AF.Ln, ['mv'], ['lrs'], bias=EPS)
                self.act(rstd[:], lrs[:], AF.Exp, ['lrs'], ['rstd'], scale=-0.5)
                self.V(lambda e: e.tensor_scalar(out=nb[:], in0=mv[:, 0:1], scalar1=rstd[:, 0:1], scalar2=-1.0, op0=ALU.mult, op1=ALU.mult), ['mv', 'rstd'], ['nb'])
                if 'dbg_qT' in dbg and t == 0 and s == 0:
                    self.DMA(d_mv, mv[:], ['mv'], ['dbg'])
                    self.DMA(d_rstd, rstd[:], ['rstd'], ['dbg'])
                    self.DMA(d_bst, bst[:].rearrange("p a b -> p (a b)"), ['bst'], ['dbg'])
                self.act(xn[:], rr[:], AF.Identity, ['rr', 'rstd', 'nb'], ['rr'], scale=rstd[:, 0:1], bias=nb[:, 0:1])
                self.tt('gpsimd', xn[:], xn[:], lnw_b[:], ALU.mult, ['rr', 'lnw_b'], ['rr'])
                self.tt('gpsimd', h1s[:], xn[:], lnb_b[:], ALU.add, ['rr', 'lnb_b'], ['rr'])
                r0 = t0 + s * 128
                self.DMA(self.h1_d[r0:r0 + 128, :], h1s[:], ['rr'], [('h1_d', t, s)])
                self.final_keys.append(('h1_d', t, s))
                pTf = PB[2][:].rearrange("p (c n) -> p c n", c=4)
                for g4 in range(2):
                    for k4 in range(4):
                        kc = g4 * 4 + k4
                        self.tr(pTf[:, k4, :], h1s[:, kc * 128:(kc + 1) * 128], self.ident[:], ['rr', 'ident'], [('PB', 2)])
                    self.cp('scalar', h1T[:, g4 * 4:g4 * 4 + 4, :], pTf, [('PB', 2)], ['h1T'])
                self.DMA(self.h1T_d[:, :, r0:r0 + 128], h1T[:], ['h1T'], [('h1T_d', t, s)])
                self.final_keys.append(('h1T_d', t, s))

    def rope_tm(self, out4, x4, cs, nh, t1, t2, kx, kout):
        cosb = cs[:, 0:32].unsqueeze(1).unsqueeze(1).to_broadcast([128, nh, 2, 32])
        sinb = cs[:, 32:64].unsqueeze(1).to_broadcast([128, nh, 32])
        self.tt('vector', t1, x4, cosb, ALU.mult, [kx, 'cs'], ['rt1'])
        self.tt('gpsimd', t2[:, :, 0, :], x4[:, :, 1, :], sinb, ALU.mult, [kx, 'cs'], ['rt2'])
        self.tt('gpsimd', t2[:, :, 1, :], x4[:, :, 0, :], sinb, ALU.mult, [kx, 'cs'], ['rt2'])
        self.tt('vector', out4[:, :, 0, :], t1[:, :, 0, :], t2[:, :, 0, :], ALU.subtract, ['rt1', 'rt2'], [kout])
        self.tt('vector', out4[:, :, 1, :], t1[:, :, 1, :], t2[:, :, 1, :], ALU.add, ['rt1', 'rt2'], [kout])

    def phase2(self):
        PB = self.PB
        s_w_kv = self.din("s_w_kv", [1024, 1536])
        b_w_in = self.din("b_w_in", [1024, 4144])
        rope_cs = self.din("rope_cs", [T, 64])
        rope_q = self.din("rope_q", [T // 2, 64])
        bw_d = self.din("bw", [128, 2, 2])
        bw = self.sb("p2bw", [128, 2, 2], F32)
        self.DMA(bw[:], bw_d, w=['bw'])
        hA = self.sb("p2hA", [128, 8, 128], BF16)
        hB = self.sb("p2hB", [128, 8, 128], BF16)
        wkv = self.sb("wkv", [128, 8, 1536], BF16)
        wq = self.sb("wq", [128, 8, 1024], BF16)
        wz = self.sb("wz", [128, 8, 3072], BF16)
        wg = self.sb("wg", [128, 8, 48], BF16)
        for kc in range(8):
            rs = slice(kc * 128, (kc + 1) * 128)
            self.DMA(wkv[:, kc, :], s_w_kv[rs, :], w=['wkv'], eng='gpsimd')
            self.DMA(wq[:, kc, :], b_w_in[rs, 0:1024], w=['wq'], eng='gpsimd')
            self.DMA(wz[:, kc, :], b_w_in[rs, 1024:4096], w=['wz'], eng='gpsimd')
            self.DMA(wg[:, kc, :], b_w_in[rs, 4096:4144], w=['wg'], eng='gpsimd')
        h1T = [self.sb("p2h1T%d" % i, [128, 8, 128], BF16) for i in range(2)]
        cs = [self.sb("p2cs%d" % i, [128, 64], F32) for i in range(2)]
        kvs2 = [self.sb("kvs%d" % i, [128, 1536], F32) for i in range(2)]
        qs2_ = [self.sb("qs_%d" % i, [128, 1024], F32) for i in range(2)]
        t12 = [self.sb("rt1%d" % i, [128, 1024], F32) for i in range(2)]
        t22 = [self.sb("rt2%d" % i, [128, 1024], F32) for i in range(2)]
        krb2 = [self.sb("krb%d" % i, [128, 4, 256], BF16) for i in range(2)]
        qrb2 = [self.sb("qrb%d" % i, [128, 1024], BF16) for i in range(2)]
        vst2 = [self.sb("vst%d" % i, [128, 2, 4, 65], BF16) for i in range(2)]
        kT42 = [self.sb("kT4%d" % i, [64, 16, 128], BF16) for i in range(2)]
        qTt2 = [self.sb("qTt%d" % i, [64, 16, 128], BF16) for i in range(2)]
        zs2 = [self.sb("zs%d" % i, [128, 1024], F32) for i in range(2)]
        gts2 = [self.sb("gts%d" % i, [128, 48], F32) for i in range(2)]
        gz2 = [self.sb("gz%d" % i, [128, 3, 1024], BF16) for i in range(2)]
        for i in range(2):
            self.V(lambda e, i=i: e.memset(vst2[i][:], 1.0), w=[('vst', i)])
        P2KEYS = ['kvs', 'qs_', 'rt1', 'rt2', 'krb', 'qrb', 'vst', 'kT4', 'qTt', 'zs', 'gts', 'gz']

        for qb in range(T // 128):
            par = qb % 2
            self.S.kmap = {k: (k, par) for k in P2KEYS}
            kvs, t1, t2, krb, vst, kT4 = kvs2[par], t12[par], t22[par], krb2[par], vst2[par], kT42[par]
            t0 = qb * 128
            hT = h1T[qb % 2]
            kh = ('p2h1T', qb % 2)
            c_ = cs[qb % 2]
            self.DMA(hT[:], self.h1T_d[:, :, t0:t0 + 128], r=['h1T_all'], w=[kh])
            self.DMA(c_[:], rope_cs[t0:t0 + 128, :], w=['cs'])
            for j in range(3):
                for kc in range(8):
                    self.mm(PB[j][:], hT[:, kc, :], wkv[:, kc, j * 512:(j + 1) * 512], kc == 0, kc == 7, [kh, 'wkv'], [('PB', j)])
                self.cp('scalar', kvs[:, j * 512:(j + 1) * 512], PB[j][:], [('PB', j)], ['kvs'])
            for i, c0 in enumerate((512, 1024)):
                x4 = kvs[:, c0:c0 + 256].rearrange("p (g a d) -> p g a d", g=4, a=2)
                o4 = krb[:, i, :].rearrange("p (g a d) -> p g a d", g=4, a=2)
                self.rope_tm(o4, x4, c_, 4, t1[:, 0:256].rearrange("p (g a d) -> p g a d", g=4, a=2),
                             t2[:, 0:256].rearrange("p (g a d) -> p g a d", g=4, a=2), 'kvs', 'krb')
            self.cp('gpsimd', krb[:, 2, :], kvs[:, 0:256], ['kvs'], ['krb'])
            self.cp('gpsimd', krb[:, 3, :], kvs[:, 256:512], ['kvs'], ['krb'])
            self.cp('vector', vst[:, 0, :, 0:64], kvs[:, 768:1024].rearrange("p (g d) -> p g d", g=4), ['kvs'], ['vst'])
            self.cp('vector', vst[:, 1, :, 0:64], kvs[:, 1280:1536].rearrange("p (g d) -> p g d", g=4), ['kvs'], ['vst'])
            pk = [PB[3][0:64, :].bitcast(BF16).rearrange("p (a t) -> p a t", a=8), PB[4][0:64, :].bitcast(BF16).rearrange("p (a t) -> p a t", a=8)]
            for i in range(4):
                for g in range(4):
                    a = i * 4 + g
                    self.tr(pk[a // 8][:, a % 8, :], krb[:, i, g * 64:(g + 1) * 64], self.identb[:], ['krb', 'identb'], [('PB', 3 + a // 8)])
            self.cp('scalar', kT4[:, 0:8, :], pk[0], [('PB', 3)], ['kT4'])
            self.cp('scalar', kT4[:, 8:16, :], pk[1], [('PB', 4)], ['kT4'])
            for i, dst in enumerate((self.kselT_d, self.kwinT_d, self.kcsT_d, self.vcsT_d)):
                self.DMA(dst[:, :, t0:t0 + 128].rearrange("g d t -> d g t"), kT4[:, i * 4:(i + 1) * 4, :], r=['kT4'], w=[('kvT_d', qb, i)])
            self.DMA(self.vsel_d[:, t0:t0 + 128, :].rearrange("g p c -> p g c"), vst[:, 0], r=['vst'], w=[('vsel_d', qb)])
            self.DMA(self.vwin_d[:, t0:t0 + 128, :].rearrange("g p c -> p g c"), vst[:, 1], r=['vst'], w=[('vwin_d', qb)])
        for slot in range(T // 256):
            par = slot % 2
            self.S.kmap = {k: (k, par) for k in P2KEYS}
            qs_, t1, t2, qrb, qTt, zs, gts, gz = qs2_[par], t12[par], t22[par], qrb2[par], qTt2[par], zs2[par], gts2[par], gz2[par]
            e2 = slot % 2
            qb = slot
            t0 = slot * 128
            hT = h1T[slot % 2]
            kh = ('p2h1T', slot % 2)
            c_ = cs[slot % 2]
            self.DMA(hA[:], self.h1T_d[:, :, (2 * slot) * 128:(2 * slot + 1) * 128], w=['p2hA'])
            self.DMA(hB[:], self.h1T_d[:, :, (2 * slot + 1) * 128:(2 * slot + 2) * 128], w=['p2hB'])
            self.DMA(c_[:], rope_q[t0:t0 + 128, :], w=['cs'])
            self.V(lambda e, hT=hT, e2=e2: e.tensor_scalar(out=hT[:], in0=hA[:], scalar1=bw[:, e2, 0:1], scalar2=None, op0=ALU.mult), ['p2hA', 'bw'], [kh])
            self.V(lambda e, hT=hT, e2=e2: e.scalar_tensor_tensor(out=hT[:], in0=hB[:], scalar=bw[:, e2, 1:2], in1=hT[:], op0=ALU.mult, op1=ALU.add), ['p2hB', 'bw', kh], [kh])
            for j in range(2):
                for kc in range(8):
                    self.mm(PB[5 + j][:], hT[:, kc, :], wq[:, kc, j * 512:(j + 1) * 512], kc == 0, kc == 7, [kh, 'wq'], [('PB', 5 + j)])
                self.S.op('scalar', lambda e, j=j, qs_=qs_: e.mul(out=qs_[:, j * 512:(j + 1) * 512], in_=PB[5 + j][:], mul=0.125), [('PB', 5 + j)], ['qs_'])
            v16 = "p (g a d) -> p g a d"
            self.rope_tm(qrb[:].rearrange(v16, g=16, a=2), qs_[:].rearrange(v16, g=16, a=2), c_, 16,
                         t1[:].rearrange(v16, g=16, a=2), t2[:].rearrange(v16, g=16, a=2), 'qs_', 'qrb')
            for hh in range(16):
                self.tr(pk[hh // 8][:, hh % 8, :], qrb[:, hh * 64:(hh + 1) * 64], self.identb[:], ['qrb', 'identb'], [('PB', 3 + hh // 8)])
            self.cp('scalar', qTt[:, 0:8, :], pk[0], [('PB', 3)], ['qTt'])
            self.cp('scalar', qTt[:, 8:16, :], pk[1], [('PB', 4)], ['qTt'])
            self.DMA(self.qT_d[:, qb].rearrange("g d (h t) -> d g h t", h=4), qTt[:].rearrange("d (g h) t -> d g h t", g=4), r=['qTt'], w=[('qT_d', qb)])
            pg = PB[7][:, 0:48]
            for kc in range(8):
                self.mm(pg, hT[:, kc, :], wg[:, kc, :], kc == 0, kc == 7, [kh, 'wg'], [('PB', 7)])
            self.act(gts[:], pg, AF.Sigmoid, [('PB', 7)], ['gts'])
            for br in range(3):
                for j in range(2):
                    pz = PB[j][:]
                    for kc in range(8):
                        self.mm(pz, hT[:, kc, :], wz[:, kc, br * 1024 + j * 512:br * 1024 + (j + 1) * 512], kc == 0, kc == 7, [kh, 'wz'], [('PB', j)])
                    self.act(zs[:, j * 512:(j + 1) * 512], pz, AF.Silu, [('PB', j)], ['zs'])
                self.tt('vector' if br != 1 else 'gpsimd', gz[:, br, :].rearrange("p (h d) -> p h d", h=16), zs[:].rearrange("p (h d) -> p h d", h=16),
                        gts[:, br * 16:(br + 1) * 16].unsqueeze(2).to_broadcast([128, 16, 64]), ALU.mult, ['zs', 'gts'], ['gz'])
            self.DMA(self.gz_d[t0:t0 + 128], gz[:], r=['gz'], w=[('gz_d', qb)])

        self.S.kmap = {}

    def phase3(self):
        PB = self.PB
        s_pe = [self.din("s_pe_k", [32, 64]), self.din("s_pe_v", [32, 64])]
        s_w1 = [self.din("s_w1_k", [32, 64, 128]), self.din("s_w1_v", [32, 64, 128])]
        s_w2 = [self.din("s_w2_k", [128, 64]), self.din("s_w2_v", [128, 64])]
        cmp_cs = self.din("cmp_cs", [64, 1024])
        ovm = self.din("ovm", [512, 128])
        w1 = [self.sb("w1_%d" % i, [64, 32, 128], BF16) for i in range(2)]
        w2 = [self.sb("w2_%d" % i, [128, 64], BF16) for i in range(2)]
        w2s = self.sb("w2s", [128, 64], BF16)
        pe32 = self.sb("pe32", [32, 2, 64], F32)
        peT = self.sb("peT", [64, 2, 32], BF16)
        bias = self.sb("cbias", [128, 2], F32)
        ccs = self.sb("ccs", [64, 1024], F32)
        src = self.sb("csrc", [64, T], BF16)
        hs = self.sb("chs", [128, 512], BF16)
        kx = self.sb("ckx", [64, 512], F32)
        kxs = self.sb("ckxs", [64, 512], F32)
        self.DMA(ccs[:], cmp_cs, w=['ccs'])
        for i in range(2):
            self.DMA(w1[i][:], s_w1[i].rearrange("c d h -> d c h"), w=[('w1', i)], eng='gpsimd')
            self.DMA(w2[i][:], s_w2[i], w=[('w2', i)], eng='gpsimd')
            self.DMA(pe32[:, i, :], s_pe[i], w=['pe32'])
        self.cp('vector', w2s[:, 0:32], w2[0][:, 32:64], [('w2', 0)], ['w2s'])
        self.cp('vector', w2s[:, 32:64], w2[0][:, 0:32], [('w2', 0)], ['w2s'])
        for i in range(2):
            pT = PB[0][0:64, i * 32:(i + 1) * 32]
            self.tr(pT, pe32[:, i, :], self.ident[0:32, 0:32], ['pe32', 'ident'], [('PB', 0)])
            self.cp('vector', peT[:, i, :], pT, [('PB', 0)], ['peT'])
        for i in range(2):
            pb_ = PB[1][:, i:i + 1]
            for c in range(32):
                self.mm(pb_, w1[i][:, c, :], peT[:, i, c:c + 1], c == 0, c == 31, [('w1', i), 'peT'], [('PB', 1)])
            self.cp('vector', bias[:, i:i + 1], pb_, [('PB', 1)], ['cbias'])
        self.V(lambda e: e.memset(self.vcaug[:, :, :, 64:65], 1.0), w=['vcaug'])
        for g in range(4):
            self.DMA(self.vcaug[:, g, :, 65:193], ovm.rearrange("(n p) s -> p n s", p=128), w=['vcaug'], eng='gpsimd')
        self.V(lambda e: e.memset(hs[:, 511:512], 0.0), w=['chs'])
        for i in range(2):
            srcd = self.kcsT_d if i == 0 else self.vcsT_d
            for g in range(4):
                self.DMA(src[:], srcd[g], r=['kvT_all'], w=['csrc'])
                s3 = src[:].rearrange("p (n r) -> p n r", r=16)
                ph = PB[2][:, 0:511]
                for c in range(32):
                    rhs = s3[:, 0:511, c] if c < 16 else s3[:, 1:512, c - 16]
                    self.mm(ph, w1[i][:, c, :], rhs, c == 0, c == 31, [('w1', i), 'csrc'], [('PB', 2)])
                self.act(hs[:, 0:511], ph, AF.Silu, [('PB', 2), 'cbias'], ['chs'], bias=bias[:, i:i + 1])
                if i == 0:
                    pk = PB[3][0:64, :]
                    pks = PB[4][0:64, :]
                    self.mm(pk, w2[0][:], hs[:], True, True, [('w2', 0), 'chs'], [('PB', 3)])
                    self.mm(pks, w2s[:], hs[:], True, True, ['w2s', 'chs'], [('PB', 4)])
                    self.tt('vector', kx[:], pk, ccs[:, 0:512], ALU.mult, [('PB', 3), 'ccs'], ['ckx'])
                    self.tt('vector', kxs[:], pks, ccs[:, 512:1024], ALU.mult, [('PB', 4), 'ccs'], ['ckxs'])
                    self.tt('vector', self.kcmpT[:, g, :], kx[:], kxs[:], ALU.add, ['ckx', 'ckxs'], ['kcmpT'])
                else:
                    pv = PB[5][:, 0:256].rearrange("p (n d) -> p n d", n=4)
                    for nt in range(4):
                        self.mm(pv[:, nt, :], hs[:, nt * 128:(nt + 1) * 128], w2[1][:], True, True, ['chs', ('w2', 1)], [('PB', 5)])
                    self.cp('vector', self.vcaug[:, g, :, 0:64], pv, [('PB', 5)], ['vcaug'])
        if 'dbg_kcmpT' in self.dbg:
            d1 = self.dout('dbg_kcmpT', [64, 4, 512], BF16)
            d2 = self.dout('dbg_vcaug', [128, 4, 4, 193], BF16)
            self.DMA(d1, self.kcmpT[:], ['kcmpT'], ['dbgk'])
            self.DMA(d2, self.vcaug[:], ['vcaug'], ['dbgk'])

    def phase4(self):
        PB = self.PB
        NQB = T // 256 if self.nqb4 is None else self.nqb4
        cmask_d = self.din("cmask_c", [128, 32, 4, 128], BF16)
        dmask_d = self.din("dmask", [128, 2, 2, 128])
        wmask_d = self.din("wmask", [128, 2, 6, 128])
        btab = self.din("btab_c", [32, 128, 128])
        cmk = [self.sb("cmk%d" % i, [128, 4, 128], BF16) for i in range(2)]
        dmk = self.sb("dmk", [128, 2, 2, 128], BF16)
        wmk = self.sb("wmk", [128, 2, 6, 128], BF16)
        self.DMA(dmk[:], dmask_d, w=['dmk'], eng='gpsimd')
        self.DMA(wmk[:], wmask_d, w=['wmk'], eng='gpsimd')
        kselT = self.sb("kselT", [64, T], BF16)
        kwinT = self.sb("kwinT", [64, T], BF16)
        vsel = self.sb("vsel", [128, 64, 65], BF16)
        vwin = self.sb("vwin", [128, 64, 65], BF16)
        qTb = [self.sb("qTb%d" % i, [64, 512], BF16) for i in range(2)]
        Btb = [self.sb("Btb%d" % i, [128, 128], F32) for i in range(2)]
        gzb = [self.sb("gzb%d" % i, [128, 3, 256], BF16) for i in range(2)]
        Eb = [self.sb("Eb%d" % i, [128, 4, 128], BF16) for i in range(3)]
        Pb_ = [self.sb("Pb_%d" % i, [128, 4, 128], BF16) for i in range(2)]
        rden = self.sb("rden", [128, 3, 4], F32)
        imp = self.sb("imp", [128, 128], F32)
        score = self.sb("score", [128, 128], F32)
        sc2 = self.sb("sc2", [128, 128], F32)
        m8 = self.sb("m8", [128, 16], F32)
        selb = self.sb("selb", [128, 128], BF16)
        selx = self.sb("selx", [128, 128, 64], BF16)
        tmp = self.sb("etmp", [128, 4, 64], F32)
        tmp2 = self.sb("etmp2", [128, 4, 64], F32)
        acc = self.sb("eacc", [128, 4, 64], F32)
        ogt = [self.sb("ogt%d" % i, [128, 256], BF16) for i in range(2)]
        ecnt = [0]
        pcnt = [0]
        scnt = [0]

        def qk_exp(kT_tile, kkeys, qT, kq):
            i = scnt[0] % 2
            scnt[0] += 1
            pS = PB[i][:]
            self.mm(pS, kT_tile, qT[:], True, True, list(kkeys) + [kq], [('PB', i)])
            j = ecnt[0] % 3
            ecnt[0] += 1
            E = Eb[j]
            self.act(E[:].rearrange("p h q -> p (h q)"), pS, AF.Exp, [('PB', i)], [('E', j)])
            return E, ('E', j)

        def loads(g, qb):
            t0 = qb * 128
            b2 = qb % 2
            self.DMA(qTb[b2][:], self.qT_d[g, qb], w=[('qTb', b2)])
            self.DMA(Btb[b2][:], btab[qb], w=[('Btb', b2)])
            self.DMA(gzb[b2][:], self.gz_d[t0:t0 + 128, :, g * 256:(g + 1) * 256], w=[('gzb', b2)])
            self.DMA(cmk[b2][:], cmask_d[:, qb], w=[('cmk', b2)])

        def make_items(g, qb):
            items = []
            t0 = qb * 128
            b2 = qb % 2
            qT = qTb[b2]
            kq = ('qTb', b2)
            Bt = Btb[b2]
            gz = gzb[b2]
            e2 = qb % 2
            qbm = 2 * qb + 1
            ntmax = (8 * qbm + 6) // 128
            pc = [PB[3][:, 0:386].rearrange("p (h c) -> p h c", h=2), PB[4][:, 0:386].rearrange("p (h c) -> p h c", h=2)]

            def cmp_post():
                for hb in range(2):
                    self.V(lambda e, hb=hb: e.tensor_scalar(out=rden[:, 0, hb * 2:hb * 2 + 2], in0=pc[hb][:, :, 64], scalar1=1e-30, scalar2=None, op0=ALU.max),
                           [('PB', 3 + hb)], ['rden0'])
                self.V(lambda e: e.reciprocal(out=rden[:, 0, :], in_=rden[:, 0, :]), ['rden0'], ['rden0'])
                for h in range(4):
                    src = pc[h // 2][:, h % 2, 65:193]
                    if h == 0:
                        self.V(lambda e, src=src: e.tensor_scalar(out=imp[:], in0=src, scalar1=rden[:, 0, 0:1], scalar2=None, op0=ALU.mult), [('PB', 3), 'rden0'], ['imp'])
                    else:
                        self.V(lambda e, src=src, h=h: e.scalar_tensor_tensor(out=imp[:], in0=src, scalar=rden[:, 0, h:h + 1], in1=imp[:], op0=ALU.mult, op1=ALU.add),
                               [('PB', 3 + h // 2), 'rden0', 'imp'], ['imp'])
                self.tt('vector', score[:], imp[:], Bt[:], ALU.add, ['imp', ('Btb', b2)], ['score'])
                self.V(lambda e: e.max(out=m8[:, 0:8], in_=score[:]), ['score'], ['m8'])
                self.V(lambda e: e.match_replace(out=sc2[:], in_to_replace=m8[:, 0:8], in_values=score[:], imm_value=-1e9), ['score', 'm8'], ['sc2'])
                self.V(lambda e: e.max(out=m8[:, 8:16], in_=sc2[:]), ['sc2'], ['m8'])
                nbk = 2 * (qbm + 1)
                self.V(lambda e: e.tensor_scalar(out=selx[:, 0:nbk, :], in0=score[:, 0:nbk].unsqueeze(2).to_broadcast([128, nbk, 64]), scalar1=m8[:, 15:16], scalar2=None, op0=ALU.is_ge),
                       ['score', 'm8'], ['selx'])
                for hb in range(2):
                    self.tt('vector', tmp[:, hb * 2:hb * 2 + 2, :], pc[hb][:, :, 0:64], rden[:, 0, hb * 2:hb * 2 + 2].unsqueeze(2).to_broadcast([128, 2, 64]), ALU.mult,
                            [('PB', 3 + hb), 'rden0'], ['etmp'])
                self.tt('gpsimd', acc[:], tmp[:], gz[:, 0, :].rearrange("p (h d) -> p h d", h=4), ALU.mult, ['etmp', ('gzb', b2)], ['eacc'])
                if self.dbg4 is not None and g == 0 and qb == self.dbg4:
                    self.V(lambda e: e.tensor_scalar(out=selb[:], in0=score[:], scalar1=m8[:, 15:16], scalar2=None, op0=ALU.is_ge), ['score', 'm8'], ['selb'])
                    self.DMA(self.d4['imp'], imp[:], ['imp'], ['dbg4'])
                    self.DMA(self.d4['sel'], selb[:], ['selb'], ['dbg4'])
                    self.DMA(self.d4['ocmp'], tmp[:], ['etmp'], ['dbg4'])

            for nt in range(ntmax + 1):
                it = {'mdep': False, 'M': None}

                def A(it=it, nt=nt):
                    it['E'], it['kE'] = qk_exp(self.kcmpT[:, g, nt * 128:(nt + 1) * 128], ['kcmpT'], qT, kq)

                def B(it=it, nt=nt):
                    E, kE = it['E'], it['kE']
                    self.tt('vector', E[:], E[:], cmk[b2][:, nt, :].unsqueeze(1).to_broadcast([128, 4, 128]), ALU.mult, [kE, ('cmk', b2)], [kE])
                    for h in range(4):
                        self.mm(pc[h // 2][:, h % 2, :], E[:, h, :], self.vcaug[:, g, nt, :], nt == 0 and h % 2 == 0, nt == ntmax and h % 2 == 1, [kE, 'vcaug'], [('PB', 3 + h // 2)])
                    if nt == ntmax:
                        cmp_post()
                it['A'], it['B'] = A, B
                items.append(it)

            for br in (1, 2):
                pacc = PB[4 + br][:, 0:260].rearrange("p (h c) -> p h c", h=4)
                kacc = ('PB', 4 + br)
                if br == 1:
                    kts = list(range(0, qbm + 1))
                    kT_, kkey, V_, vkey = kselT, 'kselT', vsel, 'vsel'
                else:
                    kts = [kt for kt in range(qbm - 5, qbm + 1) if kt >= 0]
                    kT_, kkey, V_, vkey = kwinT, 'kwinT', vwin, 'vwin'

                def br_post(br=br, pacc=pacc, kacc=kacc):
                    kr = 'rden%d' % br
                    self.V(lambda e: e.reciprocal(out=rden[:, br, :], in_=pacc[:, :, 64]), [kacc], [kr])
                    self.tt('vector', tmp[:], pacc[:, :, 0:64], rden[:, br, :].unsqueeze(2).to_broadcast([128, 4, 64]), ALU.mult, [kacc, kr], ['etmp'])
                    if self.dbg4 is not None and g == 0 and qb == self.dbg4:
                        self.DMA(self.d4['osel' if br == 1 else 'owin'], tmp[:], ['etmp'], ['dbg4'])
                    self.tt('gpsimd', tmp2[:], tmp[:], gz[:, br, :].rearrange("p (h d) -> p h d", h=4), ALU.mult, ['etmp', ('gzb', b2)], ['etmp2'])
                    if br == 1:
                        self.tt('gpsimd', acc[:], acc[:], tmp2[:], ALU.add, ['eacc', 'etmp2'], ['eacc'])
                    else:
                        og = ogt[b2]
                        self.tt('gpsimd', og[:].rearrange("p (h d) -> p h d", h=4), acc[:], tmp2[:], ALU.add, ['eacc', 'etmp2'], [('ogt', b2)])
                        self.DMA(self.og_d[t0:t0 + 128, g * 256:(g + 1) * 256], og[:], r=[('ogt', b2)], w=[('og_d', g, qb)])

                for kt in kts:
                    it = {'mdep': (br == 1 and kt == kts[0]), 'M': None}

                    def A(it=it, kt=kt, kT_=kT_, kkey=kkey):
                        it['E'], it['kE'] = qk_exp(kT_[:, kt * 128:(kt + 1) * 128], [kkey], qT, kq)

                    def M(it=it, kt=kt):
                        pM = PB[2 if kt % 2 == 0 else 7][:].bitcast(BF16)[:, 0:128]
                        kM = ('PB', 2 if kt % 2 == 0 else 7)
                        self.tr(pM, selx[:, 2 * kt:2 * kt + 2, :].rearrange("p a k -> p (a k)"), self.identb[:], ['selx', 'identb'], [kM])
                        it['pM'], it['kM'] = pM, kM

                    def B(it=it, kt=kt, br=br, kts=kts, pacc=pacc, kacc=kacc, V_=V_, vkey=vkey, br_post=br_post):
                        E, kE = it['E'], it['kE']
                        if br == 1:
                            ip = pcnt[0] % 2
                            pcnt[0] += 1
                            P = Pb_[ip]
                            kP = ('P4', ip)
                            self.tt('vector', P[:], E[:], it['pM'].unsqueeze(1).to_broadcast([128, 4, 128]), ALU.mult, [kE, it['kM']], [kP])
                            if kt >= qbm - 1:
                                self.tt('gpsimd', P[:], P[:], dmk[:, e2, kt - (qbm - 1), :].unsqueeze(1).to_broadcast([128, 4, 128]), ALU.mult, [kP, 'dmk'], [kP])
                        else:
                            P, kP = E, kE
                            wi = kt - (qbm - 5)
                            if wi not in (2, 3):
                                self.tt('gpsimd', P[:], P[:], wmk[:, e2, wi, :].unsqueeze(1).to_broadcast([128, 4, 128]), ALU.mult, [kP, 'wmk'], [kP])
                        for h in range(4):
                            self.mm(pacc[:, h, :], P[:, h, :], V_[:, kt, :], kt == kts[0] and h == 0, kt == kts[-1] and h == 3, [kP, vkey], [kacc])
                        if kt == kts[-1]:
                            br_post()
                    it['A'], it['B'] = A, B
                    if br == 1:
                        it['M'] = M
                    items.append(it)
            return items

        for g in range(4):
            self.DMA(kselT[:], self.kselT_d[g], w=['kselT'])
            self.DMA(kwinT[:], self.kwinT_d[g], w=['kwinT'])
            self.DMA(vsel[:], self.vsel_d[g].rearrange("(n p) c -> p n c", p=128), w=['vsel'])
            self.DMA(vwin[:], self.vwin_d[g].rearrange("(n p) c -> p n c", p=128), w=['vwin'])
            loads(g, 0)
            items = []
            for qb in range(NQB):
                if qb + 1 < NQB:
                    items.append({'load': (g, qb + 1)})
                items += make_items(g, qb)
            work = [it for it in items if 'load' not in it]
            pos = 0
            load_at = {}
            for it in items:
                if 'load' in it:
                    load_at.setdefault(pos, []).append(it['load'])
                else:
                    pos += 1
            n = len(work)
            done_loads = set()

            def do_loads(upto):
                for p_ in sorted(load_at):
                    if p_ <= upto and p_ not in done_loads:
                        done_loads.add(p_)
                        for l in load_at[p_]:
                            loads(*l)

            do_loads(0)
            for j in range(min(2, n)):
                work[j]['A']()
            if n > 0 and work[0]['M'] is not None:
                work[0]['M']()
            for i in range(n):
                do_loads(i)
                if i + 2 < n:
                    work[i + 2]['A']()
                nxt = work[i + 1] if i + 1 < n else None
                if nxt is not None and nxt['M'] is not None and not nxt['mdep']:
                    nxt['M']()
                work[i]['B']()
                if nxt is not None and nxt['M'] is not None and nxt['mdep']:
                    nxt['M']()

    def phase5(self):
        PB = self.PB
        NQB = T // 256 if self.nqb4 is None else self.nqb4
        bw_d = self.din("bw", [128, 2, 2]) if 'bw' not in self.inputs else self.inputs['bw'].ap()
        bw = self.sb("p5bw", [128, 2, 2], F32)
        self.DMA(bw[:], bw_d, w=['p5bw'])
        hB = [self.sb("p5hB%d" % i, [128, 1024], F32) for i in range(2)]
        b_w_out = self.din("b_w_out", [1024, 1024])
        b_ln_w = self.din("b_ln_w", [1, 1024])
        b_ln_b = self.din("b_ln_b", [1, 1024])
        out = self.dout("out", [T // 2, D], F32)
        w_out = self.sb("p5wout", [128, 8, 1024], BF16)
        self.DMA(w_out[:], b_w_out.rearrange("(c p) n -> p c n", p=128), w=['p5wout'], eng='gpsimd')
        lnw_b = self.sb("p5lnw", [128, 1024], F32)
        lnb_b = self.sb("p5lnb", [128, 1024], F32)
        self.DMA(lnw_b[:], b_ln_w.partition_broadcast(128), w=['p5lnw'])
        self.DMA(lnb_b[:], b_ln_b.partition_broadcast(128), w=['p5lnb'])
        ogs = [self.sb("p5og%d" % i, [128, 1024], BF16) for i in range(2)]
        h1s = [self.sb("p5h1%d" % i, [128, 1024], F32) for i in range(2)]
        ogT = self.sb("p5ogT", [128, 8, 128], BF16)
        rr = self.sb("p5rr", [128, 1024], F32)
        xo = [self.sb("p5xo%d" % i, [128, 1024], F32) for i in range(2)]
        bst = self.sb("p5bst", [128, 2, 6], F32)
        mv = self.sb("p5mv", [128, 2], F32)
        lrs = self.sb("p5lrs", [128, 1], F32)
        rstd = self.sb("p5rstd", [128, 1], F32)
        nb = self.sb("p5nb", [128, 1], F32)
        for qb in range(NQB):
            t0 = qb * 128
            b2 = qb % 2
            og = ogs[b2]
            h1 = h1s[b2]
            xn = xo[b2]
            self.DMA(og[:], self.og_d[t0:t0 + 128, :], w=[('p5og', b2)])
            e2 = qb % 2
            hb_ = hB[b2]
            self.DMA(h1[:], self.h1_d[(2 * qb) * 128:(2 * qb + 1) * 128, :], w=[('p5h1', b2)])
            self.DMA(hb_[:], self.h1_d[(2 * qb + 1) * 128:(2 * qb + 2) * 128, :], w=[('p5hB', b2)])
            self.V(lambda e, h1=h1, e2=e2: e.tensor_scalar(out=h1[:], in0=h1[:], scalar1=bw[:, e2, 0:1], scalar2=None, op0=ALU.mult), [('p5h1', b2), 'p5bw'], [('p5h1', b2)])
            self.V(lambda e, h1=h1, hb_=hb_, e2=e2: e.scalar_tensor_tensor(out=h1[:], in0=hb_[:], scalar=bw[:, e2, 1:2], in1=h1[:], op0=ALU.mult, op1=ALU.add),
                   [('p5hB', b2), 'p5bw', ('p5h1', b2)], [('p5h1', b2)])
            pTb = PB[2][:].bitcast(BF16).rearrange("p (c n) -> p c n", c=8)
            for kc in range(8):
                self.tr(pTb[:, kc, :], og[:, kc * 128:(kc + 1) * 128], self.identb[:], [('p5og', b2), 'identb'], [('PB', 2)])
            self.cp('scalar', ogT[:], pTb, [('PB', 2)], ['p5ogT'])
            pY = [PB[0][:], PB[1][:]]
            for half in range(2):
                for kc in range(8):
                    self.mm(pY[half], ogT[:, kc, :], w_out[:, kc, half * 512:(half + 1) * 512], kc == 0, kc == 7, ['p5ogT', 'p5wout'], [('PB', half)])
            for half in range(2):
                self.V(lambda e, half=half, h1=h1: e.scalar_tensor_tensor(out=rr[:, half * 512:(half + 1) * 512], in0=h1[:, half * 512:(half + 1) * 512], scalar=ALPHA, in1=pY[half], op0=ALU.mult, op1=ALU.add),
                       [('p5h1', b2), ('PB', half)], ['p5rr'])
                self.V(lambda e, half=half: e.bn_stats(out=bst[:, half, :], in_=rr[:, half * 512:(half + 1) * 512]), ['p5rr'], ['p5bst'])
            self.V(lambda e: e.bn_aggr(out=mv[:], in_=bst[:].rearrange("p a b -> p (a b)")), ['p5bst'], ['p5mv'])
            self.act(lrs[:], mv[:, 1:2], AF.Ln, ['p5mv'], ['p5lrs'], bias=EPS)
            self.act(rstd[:], lrs[:], AF.Exp, ['p5lrs'], ['p5rstd'], scale=-0.5)
            self.V(lambda e: e.tensor_scalar(out=nb[:], in0=mv[:, 0:1], scalar1=rstd[:, 0:1], scalar2=-1.0, op0=ALU.mult, op1=ALU.mult), ['p5mv', 'p5rstd'], ['p5nb'])
            self.act(xn[:], rr[:], AF.Identity, ['p5rr', 'p5rstd', 'p5nb'], [('p5xo', b2)], scale=rstd[:, 0:1], bias=nb[:, 0:1])
            self.tt('gpsimd', xn[:], xn[:], lnw_b[:], ALU.mult, [('p5xo', b2), 'p5lnw'], [('p5xo', b2)])
            self.tt('gpsimd', xn[:], xn[:], lnb_b[:], ALU.add, [('p5xo', b2), 'p5lnb'], [('p5xo', b2)])
            self.DMA(out[t0:t0 + 128, :], xn[:], r=[('p5xo', b2)], w=[('out', qb)])
            self.final_keys.append(('out', qb))


def _in_maps(b, inputs):
    hc = host_consts()
    maps = []
    for core in range(8):
        bi = core // 2
        m = dict(hc)
        m.update(core_consts(core % 2, hc))
        m['x'] = inputs['x'][bi]
        m['a_w_in'] = inputs['a_w_in'][0]
        m['a_conv_w'] = inputs['a_conv_w'][0]
        m['a_a_log'] = inputs['a_a_log'].reshape(1, 8)
        m['a_dt_bias'] = inputs['a_dt_bias'].reshape(1, 8)
        m['a_norm_w'] = inputs['a_norm_w'].reshape(128, 1)
        m['a_w_out'] = inputs['a_w_out'][0]
        m['a_ln_w'] = inputs['a_ln_w'].reshape(1, 1024)
        m['a_ln_b'] = inputs['a_ln_b'].reshape(1, 1024)
        for k in ('s_w_kv', 's_pe_k', 's_pe_v', 's_w1_k', 's_w2_k', 's_w1_v', 's_w2_v'):
            m[k] = inputs[k]
        m['b_w_in'] = inputs['b_w_in'][0]
        m['b_w_out'] = inputs['b_w_out'][0]
        m['b_ln_w'] = inputs['b_ln_w'].reshape(1, 1024)
        m['b_ln_b'] = inputs['b_ln_b'].reshape(1, 1024)
        maps.append({k: np.ascontiguousarray(v if k == 'cmask_c' else np.asarray(v, dtype=np.float32)) for k, v in m.items() if k in b.inputs})
    return maps


def kernel(**inputs):
    inputs = {k: np.asarray(v) for k, v in inputs.items()}
    import os
    ph = tuple(os.environ.get('KPHASES', 'p1,p2,p3,p4,p5').split(','))
    b = Builder(phases=ph)
    nc = b.build()
    maps = _in_maps(b, inputs)
    res = run_bass_kernel_spmd(nc, maps, core_ids=list(range(8)))
    if 'out' not in b.outputs:
        return np.zeros((4, T, D), np.float32)
    out = np.zeros((4, T, D), np.float32)
    for core in range(8):
        bi, p = core // 2, core % 2
        o = np.asarray(res.results[core]['out'], dtype=np.float32)
        for j in range(32):
            qb = slot_qb(p, j)
            out[bi, qb * 128:(qb + 1) * 128] = o[j * 128:(j + 1) * 128]
    return out
```

```python
import math
from contextlib import ExitStack

import numpy as np
import concourse.bass as bass
import concourse.mybir as mybir
from concourse.bass_utils import run_bass_kernel_spmd

F32 = mybir.dt.float32
BF16 = mybir.dt.bfloat16
AF = mybir.ActivationFunctionType
ALU = mybir.AluOpType
AX = mybir.AxisListType

ENGS = ('sync', 'gpsimd', 'scalar', 'vector', 'tensor')

T = 8192
D = 1024
NH = 8
EPS = 1e-6
ALPHA = 4.0 ** 0.25


class Sched:
    def __init__(self, nc, csems, dsems):
        self.nc = nc
        self.csem = csems
        self.dsems = dsems
        self.ops = {e: [] for e in ENGS}
        self.cnt = {e: 0 for e in ENGS}
        self.dcount = [0] * len(dsems)
        nd = len(dsems)
        self.dpool = {'sync': list(range(0, nd - 4)), 'gpsimd': list(range(nd - 4, nd))}
        self.dptr = {'sync': 0, 'gpsimd': 0}
        self.lastw = {}
        self.readers = {}
        self.waited = {e: {} for e in ENGS}
        self.nops = 0

    def _sem(self, sk):
        return self.csem[sk[1]] if sk[0] == 'c' else self.dsems[sk[1]]

    kmap = {}

    def _expand(self, keys):
        out = []
        for k in keys:
            if isinstance(k, str):
                k = self.kmap.get(k, k)
            if isinstance(k, tuple) and len(k) == 2 and k[0] == 'PB':
                out.append(('PB', k[1], 0))
                out.append(('PB', k[1], 1))
            else:
                out.append(k)
        return out

    def op(self, eng, fn, reads=(), writes=(), dma=False):
        need = {}
        reads = self._expand(reads)
        writes = self._expand(writes)

        def want(tok):
            sk, val, src = tok
            if sk[0] == 'c' and src == eng and eng == 'tensor':
                return
            if need.get(sk, 0) < val:
                need[sk] = val

        for k in reads:
            t = self.lastw.get(k)
            if t is not None:
                want(t)
        for k in writes:
            t = self.lastw.get(k)
            if t is not None:
                want(t)
            for t in self.readers.get(k, ()):
                want(t)
        if dma:
            pool = self.dpool[eng]
            i = pool[self.dptr[eng] % len(pool)]
            self.dptr[eng] += 1
            if self.dcount[i] > 0:
                want((('d', i), 16 * self.dcount[i], None))
            self.dcount[i] += 1
            tok = (('d', i), 16 * self.dcount[i], eng)
            inc = 16
        else:
            self.cnt[eng] += 1
            tok = (('c', eng), self.cnt[eng], eng)
            inc = 1
        w = self.waited[eng]
        waits = []
        for sk, val in need.items():
            if w.get(sk, 0) < val:
                w[sk] = val
                waits.append((self._sem(sk), val))
        self.ops[eng].append((waits, fn, self._sem(tok[0]), inc))
        for k in writes:
            self.lastw[k] = tok
            self.readers[k] = []
        for k in reads:
            lst = self.readers.setdefault(k, [])
            if len(lst) < 64:
                lst.append(tok)
            else:
                d = {}
                for t in lst + [tok]:
                    if d.get(t[0], (0,))[0] < t[1]:
                        d[t[0]] = (t[1], t[2])
                self.readers[k] = [(sk, v[0], v[1]) for sk, v in d.items()]
        self.nops += 1
        return tok

    def wait_all(self, eng, keys):
        need = {}
        for k in keys:
            t = self.lastw.get(k)
            if t is not None:
                sk, val, src = t
                if need.get(sk, 0) < val:
                    need[sk] = val
        waits = [(self._sem(sk), val) for sk, val in need.items()]
        self.ops[eng].append((waits, None, None, 0))

    def drain_dmas(self, eng):
        waits = [(self.dsems[i], 16 * self.dcount[i]) for i in range(len(self.dsems)) if self.dcount[i] > 0]
        self.ops[eng].append((waits, None, None, 0))
        for i in range(len(self.dsems)):
            self.waited[eng][('d', i)] = 16 * self.dcount[i]

    def emit(self):
        nc = self.nc
        with nc.Block() as block:
            for e in ENGS:
                ops = self.ops[e]
                if not ops:
                    continue

                def body(engine, ops=ops):
                    for waits, fn, sem, inc in ops:
                        for s, v in waits:
                            engine.wait_ge(s, v)
                        if fn is not None:
                            ins = fn(engine)
                            ins.then_inc(sem, inc)

                getattr(block, e)(body)
        self.ops = {e: [] for e in ENGS}


def host_consts():
    c = {}
    half = 32
    inv = (np.float32(10000.0) ** (-(np.arange(half, dtype=np.float32) / np.float32(half)))).astype(np.float32)
    pos = np.arange(T, dtype=np.float32)
    ang = (pos[:, None] * inv[None, :]).astype(np.float32)
    c['rope_cs'] = np.concatenate([np.cos(ang), np.sin(ang)], axis=1).astype(np.float32)
    pc = (np.arange(512, dtype=np.float32) * 16 + 31).astype(np.float32)
    angc = (pc[None, :] * inv[:, None]).astype(np.float32)
    cosF = np.concatenate([np.cos(angc), np.cos(angc)], axis=0)
    sinF = np.concatenate([-np.sin(angc), np.sin(angc)], axis=0)
    c['cmp_cs'] = np.concatenate([cosF, sinF], axis=1).astype(np.float32)
    st = np.arange(512)[:, None] * 16
    bs = np.arange(128)[None, :] * 64
    ov = np.clip(np.minimum(st + 32, bs + 64) - np.maximum(st, bs), 0, None) / 32.0
    ov[511] = 0.0
    c['ovm'] = ov.astype(np.float32)
    n = np.arange(128)[:, None, None]
    dl = np.arange(17)[None, :, None]
    i = np.arange(128)[None, None, :]
    c['cmpmask'] = (16 * n + 31 - i <= 128 * dl).astype(np.float32)
    kk = np.arange(128)[:, None]
    qq = np.arange(128)[None, :]
    c['cwmask'] = np.stack([(kk <= qq), (kk > qq)], axis=1).astype(np.float32)
    bt = np.zeros((64, 128, 128), np.float32)
    for qb in range(64):
        t = qb * 128 + np.arange(128)
        cur = t // 64
        blk = np.arange(128)[None, :]
        forced = (blk == 0) | (blk == cur[:, None]) | (blk == cur[:, None] - 1)
        vis = blk * 64 <= t[:, None]
        bt[qb] = np.where(vis, np.where(forced, 1.0e4, 0.0), -1.0)
    c['btab'] = bt
    c['ident'] = np.eye(128, dtype=np.float32)
    k = np.arange(64)
    tri = (k[:, None] <= k[None, :]).astype(np.float32)
    sup = (k[:, None] > k[None, :]).astype(np.float32)
    negT = np.where(k[None, :] < k[:, None], -1e30, 0.0).astype(np.float32)
    offd = (k[:, None] != k[None, :]).astype(np.float32)
    eye = np.eye(64, dtype=np.float32)
    c64 = np.concatenate([
        tri, -tri, sup, np.ones((64, 128), np.float32), -np.ones((64, 64), np.float32),
        np.tile(negT[:, None, :], (1, 8, 1)).reshape(64, 512), offd,
    ], axis=1)
    c['c64'] = np.ascontiguousarray(c64)
    return c


def slot_qb(p, j):
    first = (p == (j % 2))
    return 2 * j if first else 2 * j + 1


def core_consts(p, hc):
    import ml_dtypes
    c = {}
    bw = np.zeros((128, 2, 2), np.float32)
    for e in range(2):
        first = (p == e)
        bw[:, e, 0] = 1.0 if first else 0.0
        bw[:, e, 1] = 0.0 if first else 1.0
    c['bw'] = bw
    qbs = [slot_qb(p, j) for j in range(32)]
    c['rope_q'] = np.concatenate([hc['rope_cs'][qb * 128:(qb + 1) * 128] for qb in qbs], axis=0)
    c['btab_c'] = np.stack([hc['btab'][qb] for qb in qbs], axis=0)
    n = np.arange(128)[:, None, None, None]
    nt = np.arange(4)[None, None, :, None]
    i = np.arange(128)[None, None, None, :]
    qbv = np.array(qbs)[None, :, None, None]
    c['cmask_c'] = (16 * (128 * nt + n) + 31 <= 128 * qbv + i).astype(ml_dtypes.bfloat16)
    kk = np.arange(128)[:, None]
    qq = np.arange(128)[None, :]
    caus = (kk <= qq).astype(np.float32)
    win = (kk > qq).astype(np.float32)
    one = np.ones((128, 128), np.float32)
    zero = np.zeros((128, 128), np.float32)
    dm = np.zeros((128, 2, 2, 128), np.float32)
    wm = np.zeros((128, 2, 6, 128), np.float32)
    for e in range(2):
        first = (p == e)
        dl = [caus, zero] if first else [one, caus]
        wl = [win, one, one, one, caus, zero] if first else [zero, win, one, one, one, caus]
        for a, m_ in enumerate(dl):
            dm[:, e, a, :] = m_
        for a, m_ in enumerate(wl):
            wm[:, e, a, :] = m_
    c['dmask'] = dm
    c['wmask'] = wm
    return c


C64_OFF = {}
_o = 0
for _n, _w in [('tri', 64), ('ntri', 64), ('sup', 64), ('ones', 128), ('nones', 64),
               ('negT8', 512), ('offd', 64)]:
    C64_OFF[_n] = (_o, _o + _w)
    _o += _w
C64_W = _o


class Builder:
    def __init__(self, phases=('p1',), dbg=None, ntiles1=None, nqb4=None, dbg4=None):
        self.nqb4 = nqb4
        self.dbg4 = dbg4
        self.phases = phases
        self.dbg = dbg or {}
        self.ntiles1 = ntiles1
        self.nc = bass.Bass("TRN2", target_bir_lowering=False)
        self.es = ExitStack()
        self.inputs = {}
        self.outputs = {}

    def din(self, name, shape, dt=F32):
        t = self.nc.dram_tensor(name, list(shape), dt, kind="ExternalInput")
        self.inputs[name] = t
        return t.ap()

    def dscratch(self, name, shape, dt):
        if name in self.dbg:
            t = self.nc.dram_tensor(name, list(shape), dt, kind="ExternalOutput")
            self.outputs[name] = t
        else:
            t = self.nc.dram_tensor(name, list(shape), dt)
        return t.ap()

    def dout(self, name, shape, dt):
        t = self.nc.dram_tensor(name, list(shape), dt, kind="ExternalOutput")
        self.outputs[name] = t
        return t.ap()

    def sb(self, name, shape, dt):
        return self.pes.enter_context(self.nc.sbuf_tensor(name, list(shape), dt))

    def ps(self, name, shape, dt):
        return self.es.enter_context(self.nc.psum_tensor(name, list(shape), dt))

    def V(self, fn, r=(), w=()):
        self.S.op('vector', fn, r, w)

    def A(self, fn, r=(), w=()):
        self.S.op('scalar', fn, r, w)

    def G(self, fn, r=(), w=()):
        self.S.op('gpsimd', fn, r, w)

    def PE(self, fn, r=(), w=()):
        self.S.op('tensor', fn, r, w)

    def DMA(self, out, in_, r=(), w=(), eng='sync', **kw):
        self.S.op(eng, lambda e: e.dma_start(out=out, in_=in_, **kw), r, w, dma=True)

    def mm(self, out, lhsT, rhs, start, stop, r, w):
        self.S.op('tensor', lambda e: e.matmul(out, lhsT=lhsT, rhs=rhs, start=start, stop=stop), r, w)

    def tr(self, out, in_, ident, r, w):
        self.S.op('tensor', lambda e: e.transpose(out=out, in_=in_, identity=ident), r, w)

    def act(self, out, in_, func, r, w, **kw):
        self.S.op('scalar', lambda e: e.activation(out=out, in_=in_, func=func, **kw), r, w)

    def tt(self, eng, out, in0, in1, op, r, w):
        self.S.op(eng, lambda e: e.tensor_tensor(out=out, in0=in0, in1=in1, op=op), r, w)

    def cp(self, eng, out, in_, r, w):
        if eng == 'scalar':
            self.S.op(eng, lambda e: e.copy(out=out, in_=in_), r, w)
        else:
            self.S.op(eng, lambda e: e.tensor_copy(out=out, in_=in_), r, w)

    def build(self):
        nc = self.nc
        es = self.es
        with es:
            csems = {e: es.enter_context(nc.semaphore("c_" + e)) for e in ENGS}
            dsems = [es.enter_context(nc.semaphore("d%d" % i)) for i in range(12)]
            self.S = Sched(nc, csems, dsems)
            self.pes = es
            self.PB = [es.enter_context(nc.psum_tensor("pb%d" % i, [128, 512], F32)) for i in range(8)]
            self.setup_common()
            if 'p1' in self.phases:
                with ExitStack() as pes:
                    self.pes = pes
                    self.phase1()
                    self.S.drain_dmas('sync')
                    self.S.emit()
            for ph in ('p2', 'p3', 'p4', 'p5'):
                if ph in self.phases:
                    with ExitStack() as pes:
                        self.pes = pes
                        getattr(self, 'phase' + ph[1])()
                        self.S.drain_dmas('sync')
                        self.S.emit()
            self.pes = es
            self.finish()
            self.S.emit()
        return nc

    def finish(self):
        if not self.outputs:
            o = self.dout("out", [T // 2, D], F32)
            self.DMA(o[0:128, 0:128], self.ident[:], r=['ident'], w=['dummy_out'])
            self.final_keys.append('dummy_out')
        self.S.wait_all('sync', list(self.final_keys))

    def setup_common(self):
        self.final_keys = []
        x = self.din("x", [T, D])
        self.x = x
        self.c_ident = self.din("ident", [128, 128])
        self.c_c64 = self.din("c64", [64, C64_W])
        self.ident = self.sb("ident_sb", [128, 128], F32)
        self.identb = self.sb("identb_sb", [128, 128], BF16)
        self.c64 = self.sb("c64_sb", [64, C64_W], F32)
        self.DMA(self.ident[:], self.c_ident, w=['ident'])
        self.DMA(self.c64[:], self.c_c64, w=['c64'])
        self.cp('vector', self.identb[:], self.ident[:], ['ident'], ['identb'])
        self.ones128 = self.sb("ones128", [128, 128], F32)
        self.V(lambda e: e.memset(self.ones128[:], 1.0), w=['ones128'])
        if 'p1' in self.phases:
            self.h1_d = self.dscratch("h1_d", [T, D], F32)
            self.h1T_d = self.dscratch("h1T_d", [128, 8, T], BF16)
        else:
            self.h1_d = self.din("h1_d", [T, D], F32)
            self.h1T_d = self.din("h1T_d", [128, 8, T], BF16)
        self.kselT_d = self.dscratch("kselT_d", [4, 64, T], BF16)
        self.kwinT_d = self.dscratch("kwinT_d", [4, 64, T], BF16)
        self.kcsT_d = self.dscratch("kcsT_d", [4, 64, T], BF16)
        self.vcsT_d = self.dscratch("vcsT_d", [4, 64, T], BF16)
        self.vsel_d = self.dscratch("vsel_d", [4, T, 65], BF16)
        self.vwin_d = self.dscratch("vwin_d", [4, T, 65], BF16)
        self.qT_d = self.dscratch("qT_d", [4, 32, 64, 512], BF16)
        self.gz_d = self.dscratch("gz_d", [T // 2, 3, 1024], BF16)
        self.og_d = self.dscratch("og_d", [T // 2, 1024], BF16)
        if self.dbg4 is not None:
            self.d4 = {'imp': self.dout('dbg_imp', [128, 128], F32), 'sel': self.dout('dbg_sel', [128, 128], BF16),
                       'ocmp': self.dout('dbg_ocmp', [128, 4, 64], F32), 'osel': self.dout('dbg_osel', [128, 4, 64], F32),
                       'owin': self.dout('dbg_owin', [128, 4, 64], F32)}
            self.final_keys.append('dbg4')
        self.kcmpT = self.sb("kcmpT", [64, 4, 512], BF16)
        self.vcaug = self.sb("vcaug", [128, 4, 4, 193], BF16)

    def c64v(self, name, heads=False):
        a, b = C64_OFF[name]
        v = self.c64[:, a:b]
        if heads:
            v = v.rearrange("p (h i) -> p h i", h=8)
        return v

    def phase1(self):
        nc = self.nc
        S = self.S
        PB = self.PB
        TT = 256
        NCH = 4
        NT = T // TT if self.ntiles1 is None else self.ntiles1
        a_w_in = self.din("a_w_in", [1024, 4112])
        a_conv_w = self.din("a_conv_w", [4, 3072])
        a_a_log = self.din("a_a_log", [1, 8])
        a_dt_bias = self.din("a_dt_bias", [1, 8])
        a_norm_w = self.din("a_norm_w", [128, 1])
        a_w_out = self.din("a_w_out", [1024, 1024])
        a_ln_w = self.din("a_ln_w", [1, 1024])
        a_ln_b = self.din("a_ln_b", [1, 1024])

        w_in = self.sb("w_in_sb", [128, 8, 4112], BF16)
        qkT = self.sb("qkT", [128, 2, 8, TT], F32)
        qT = qkT[:, 0]
        kT = qkT[:, 1]
        w_out = qkT[:].rearrange("p a h t -> p (a h t)").bitcast(BF16).rearrange("p (c n) -> p c n", c=8)
        w_out_d = self.dscratch("w_out_bf_d", [128, 8, 1024], BF16)
        for kc in range(8):
            self.DMA(w_in[:, kc, :], a_w_in[kc * 128:(kc + 1) * 128, :], w=['w_in'], eng='gpsimd')
        self.DMA(w_out, a_w_out.rearrange("(c p) n -> p c n", p=128), w=['qT', 'kT'], eng='gpsimd')
        self.DMA(w_out_d, w_out, r=['qT', 'kT'], w=['w_out_d'])
        xs = [self.sb("xs0", [128, 2, 1024], F32)]
        vcur2 = [xs[0][0:64, i, :].rearrange("p (h d) -> p h d", h=8) for i in range(2)]
        cw4 = xs[0][0:4].rearrange("p s d -> p (s d)")
        convw = self.sb("convw", [128, 24, 4], F32)
        pcw = PB[0][:, 0:96].rearrange("p (b i) -> p b i", i=4)
        for part in range(2):
            nb_ = 16 if part == 0 else 8
            self.DMA(cw4[:, 0:nb_ * 128], a_conv_w[:, part * 2048:part * 2048 + nb_ * 128], w=['xs'])
            for b in range(nb_):
                self.tr(pcw[:, part * 16 + b, :], cw4[:, b * 128:(b + 1) * 128], self.ident[0:4, 0:4], ['xs', 'ident'], [('PB', 0)])
        self.cp('vector', convw[:], pcw, [('PB', 0)], ['convw'])
        normw = self.sb("normw", [128, 1], F32)
        self.DMA(normw[:], a_norm_w, w=['normw'])
        lnw_b = self.sb("lnw_b", [128, 1024], F32)
        lnb_b = self.sb("lnb_b", [128, 1024], F32)
        self.DMA(lnw_b[:], a_ln_w.partition_broadcast(128), w=['lnw_b'])
        self.DMA(lnb_b[:], a_ln_b.partition_broadcast(128), w=['lnb_b'])
        dtb = self.sb("dtb", [64, 8], F32)
        alog = self.sb("alog", [64, 8], F32)
        negA = self.sb("negA", [64, 8], F32)
        self.DMA(dtb[:], a_dt_bias.partition_broadcast(64), w=['dtb'])
        self.DMA(alog[:], a_a_log.partition_broadcast(64), w=['alog'])
        self.act(negA[:], alog[:], AF.Exp, ['alog'], ['negA'])
        self.V(lambda e: e.tensor_scalar(out=negA[:], in0=negA[:], scalar1=-1.0, scalar2=None, op0=ALU.mult), ['negA'], ['negA'])

        halo = self.sb("halo", [128, 24, 3], F32)
        self.V(lambda e: e.memset(halo[:], 0.0), w=['halo'])
        Sst = self.sb("Sst", [128, 8, 128], F32)
        self.V(lambda e: e.memset(Sst[:], 0.0), w=[('Sst', 0), ('Sst', 1)])

        xT = self.sb("xT", [128, 8, TT], BF16)
        pre = [self.sb("pre%d" % i, [128, TT + 3], F32) for i in range(2)]
        yb = [self.sb("yb%d" % i, [128, TT], F32) for i in range(2)]
        sb4 = [self.sb("s%d" % i, [128, TT], F32) for i in range(4)]
        sq = [self.sb("sq%d" % i, [128, TT], F32) for i in range(2)]
        lnb = [self.sb("lnt%d" % i, [128, TT], F32) for i in range(2)]
        ktokp = self.sb("ktok", [128, 2, 8, 128], F32)
        vtokp = self.sb("vtok", [128, 2, 8, 128], F32)

        def ktok_c(c):
            return ktokp[(c // 2) * 64:(c // 2) * 64 + 64, c % 2]

        def vtok_c(c):
            return vtokp[(c // 2) * 64:(c // 2) * 64 + 64, c % 2]
        szT = self.sb("szT", [128, 8, TT], BF16)
        ogT = self.sb("ogT", [128, 8, TT], BF16)
        beta = self.sb("beta", [64, NCH, 8], F32)
        nbeta = self.sb("nbeta", [64, NCH, 8], F32)
        gt = self.sb("gt", [64, NCH, 8], F32)
        xg = self.sb("xg", [64, NCH, 8], F32)

        decT = self.sb("decT", [64, 8, 64], F32)
        eg2 = [self.sb("eg%d" % i, [128, 24], F32) for i in range(2)]
        tmpA = self.sb("tmpA", [64, 8, 64], F32)
        G1 = tmpA
        Mb = [self.sb("Mb%d" % i, [64, 8, 64], BF16) for i in range(2)]
        MTb = [self.sb("MTb%d" % i, [64, 8, 64], BF16) for i in range(2)]
        Pb = [self.sb("Pb%d" % i, [64, 8, 64], BF16) for i in range(2)]
        XTf2 = [self.sb("XTf%d" % i, [64, 8, 64], F32) for i in range(2)]
        attT2 = [self.sb("attT%d" % i, [64, 8, 64], F32) for i in range(2)]
        kdec2 = [self.sb("kdec%d" % i, [64, 8, 128], F32) for i in range(2)]
        Rp = self.sb("Rp", [64, 4, 128], F32)
        qs = self.sb("qs", [64, 4, 128], F32)
        vnew = self.sb("vnew", [64, 4, 128], F32)
        oc = self.sb("oc", [64, 8, 128], F32)
        oss = self.sb("oss", [64, 8], F32)
        orr = self.sb("orr", [64, 8], F32)
        onb = self.sb("onb", [64, 8, 128], BF16)
        t2 = self.sb("t2", [128, 4, 128], F32)
        rr = self.sb("rr", [128, 1024], F32)
        h1s = rr
        xn = rr
        h1T = self.sb("h1T", [128, 8, 128], BF16)
        bst = self.sb("bst", [128, 2, 6], F32)
        mv = self.sb("mv", [128, 2], F32)
        lrs = self.sb("lrs", [128, 1], F32)
        rstd = self.sb("rstd", [128, 1], F32)
        nb = self.sb("nb", [128, 1], F32)

        tri = self.c64v('tri')
        ntri = self.c64v('ntri')
        sup = self.c64v('sup')
        ones64 = self.c64v('ones')
        negT8 = self.c64v('negT8')
        offd8 = self.c64v('offd').unsqueeze(1).to_broadcast([64, 8, 64])
        eye8 = self.ident[0:64, 0:64].unsqueeze(1).to_broadcast([64, 8, 64])
        tri8 = tri.unsqueeze(1).to_broadcast([64, 8, 64])
        id64 = self.ident[0:64, 0:64]
        idb64 = self.identb[0:64, 0:64]
        lnqs = math.log(128.0 ** -0.5)

        dbg = self.dbg
        if 'dbg_qT' in dbg:
            d_qT = self.dout('dbg_qT', [128, 8, TT], F32)
            d_kT = self.dout('dbg_kT', [128, 8, TT], F32)
            d_vtok = self.dout('dbg_vtok', [128, 2, 8, 128], F32)
            d_ktok = self.dout('dbg_ktok', [128, 2, 8, 128], F32)
            d_beta = self.dout('dbg_beta', [64, NCH, 8], F32)
            d_g = self.dout('dbg_g', [64, NCH, 8], F32)
            d_oc = self.dout('dbg_oc', [NT * NCH, 64, 8, 128], F32)
            d_decT = self.dout('dbg_decT', [64, 8, 64], F32)
            d_XT = self.dout('dbg_XT', [64, 8, 64], F32)
            d_M = self.dout('dbg_M', [64, 8, 64], F32)
            d_ogT = self.dout('dbg_ogT', [128, 8, TT], BF16)
            d_oss = self.dout('dbg_oss', [64, 8], F32)
            d_orr = self.dout('dbg_orr', [64, 8], F32)
            d_mv = self.dout('dbg_mv', [128, 2], F32)
            d_rstd = self.dout('dbg_rstd', [128, 1], F32)
            d_bst = self.dout('dbg_bst', [128, 12], F32)
            d_rr = self.dout('dbg_rr', [2, 128, 1024], F32)
            d_wout = self.dout('dbg_wout', [128, 8, 1024], BF16)

        def bank(i, shape=None, dt=None):
            v = PB[i][:]
            if dt is not None:
                v = v.bitcast(dt)
            return v

        for t in range(NT):
            t0 = t * TT
            xt = xs[0]
            kx = 'xs'
            self.DMA(xt[:], self.x[t0:t0 + TT, :].rearrange("(s p) d -> p s d", p=128), w=[kx])
            pTf = PB[2][:].rearrange("p (c n) -> p c n", c=4)
            for s in range(2):
                for g4 in range(2):
                    for k4 in range(4):
                        kc = g4 * 4 + k4
                        self.tr(pTf[:, k4, :], xt[:, s, kc * 128:(kc + 1) * 128], self.ident[:], [kx, 'ident'], [('PB', 2)])
                    self.cp('scalar', xT[:, g4 * 4:g4 * 4 + 4, s * 128:(s + 1) * 128], pTf, [('PB', 2)], ['xT'])
            ppb = [0, 1, 6, 7]

            def s1_mm(blk):
                pbk = ppb[blk % 4]
                pp = PB[pbk][:, 0:TT]
                for kc in range(8):
                    self.mm(pp, w_in[:, kc, blk * 128:(blk + 1) * 128], xT[:, kc, :], kc == 0, kc == 7, ['w_in', 'xT'], [('PB', pbk)])

            def bufs(blk):
                pb = blk % 2
                sp = blk % 4
                return pb, sp, pre[pb], ('pre', pb), yb[pb], ('y', pb), sb4[sp], ('s', sp)

            def a1(blk):
                pbk = ppb[blk % 4]
                pp = PB[pbk][:, 0:TT]
                kp = ('PB', pbk)
                if blk >= 24:
                    self.act(szT[:, blk - 24, :], pp, AF.Silu, [kp], ['szT'])
                    return
                pb, sp, pr, kpr, y, ky, s_, ks = bufs(blk)
                self.cp('scalar', pr[:, 3:TT + 3], pp, [kp], [kpr])
                self.cp('gpsimd', pr[:, 0:3], halo[:, blk, :], ['halo'], [kpr])

            def a2(blk):
                if blk >= 24:
                    return
                pb, sp, pr, kpr, y, ky, s_, ks = bufs(blk)
                self.V(lambda e, pr=pr, y=y, blk=blk: e.tensor_scalar(out=y[:], in0=pr[:, 0:TT], scalar1=convw[:, blk, 0:1], scalar2=None, op0=ALU.mult),
                       [kpr, 'convw'], [ky])
                for i in range(1, 4):
                    self.V(lambda e, pr=pr, y=y, blk=blk, i=i: e.scalar_tensor_tensor(out=y[:], in0=pr[:, i:i + TT], scalar=convw[:, blk, i:i + 1], in1=y[:], op0=ALU.mult, op1=ALU.add),
                           [kpr, 'convw', ky], [ky])
                self.cp('gpsimd', halo[:, blk, :], pr[:, TT:TT + 3], [kpr], ['halo'])

            def a3(blk):
                if blk >= 24:
                    return
                pb, sp, pr, kpr, y, ky, s_, ks = bufs(blk)
                self.act(s_[:], y[:], AF.Silu, [ky], [ks])

            def b45(blk):
                if blk >= 16:
                    return
                pb, sp, pr, kpr, y, ky, s_, ks = bufs(blk)
                q_ = sq[pb]
                ksq = ('sq', pb)
                self.tt('gpsimd', q_[:], s_[:], s_[:], ALU.mult, [ks], [ksq])
                psb = 3 if pb == 0 else 5
                self.mm(PB[psb][:, 0:TT], self.ones128[:], q_[:], True, True, ['ones128', ksq], [('PB', psb)])

            def b6(blk):
                if blk >= 16:
                    return
                pb = blk % 2
                psb = 3 if pb == 0 else 5
                l_ = lnb[pb]
                kl = ('lnt', pb)
                self.act(l_[:], PB[psb][:, 0:TT], AF.Ln, [('PB', psb)], [kl], bias=EPS)
                self.act(l_[:], l_[:], AF.Exp, [kl], [kl], scale=-0.5, bias=(lnqs if blk < 8 else 0.0))

            def b78(blk):
                if blk >= 24:
                    return
                pb, sp, pr, kpr, y, ky, s_, ks = bufs(blk)
                h = blk % 8
                pkb = 4 if pb == 0 else 2
                pk = PB[pkb][0:64, :].rearrange("p (c d) -> p c d", c=NCH)
                if blk < 16:
                    l_ = lnb[pb]
                    kl = ('lnt', pb)
                    dest = qT if blk < 8 else kT
                    kd = 'qT' if blk < 8 else 'kT'
                    self.tt('vector', dest[:, h, :], s_[:], l_[:], ALU.mult, [ks, kl], [kd])
                    if blk >= 8:
                        for c in range(NCH):
                            self.tr(pk[:, c, :], kT[:, h, c * 64:(c + 1) * 64], self.ident[:], ['kT', 'ident'], [('PB', pkb)])
                        for c2 in range(2):
                            self.cp('scalar', ktokp[c2 * 64:c2 * 64 + 64, :, h, :], pk[:, c2 * 2:c2 * 2 + 2, :], [('PB', pkb)], ['ktok'])
                else:
                    for c in range(NCH):
                        self.tr(pk[:, c, :], s_[:, c * 64:(c + 1) * 64], self.ident[:], [ks, 'ident'], [('PB', pkb)])
                    for c2 in range(2):
                        self.cp('scalar', vtokp[c2 * 64:c2 * 64 + 64, :, h, :], pk[:, c2 * 2:c2 * 2 + 2, :], [('PB', pkb)], ['vtok'])

            groups = [(2 * i, 2 * i + 1) for i in range(16)]
            NG = len(groups)

            def both(fn, g):
                fn(g[0])
                fn(g[1])

            both(s1_mm, groups[0])
            both(s1_mm, groups[1])
            both(a1, groups[0])
            both(a2, groups[0])
            both(a3, groups[0])
            for gi in range(NG):
                g = groups[gi]
                gn = groups[gi + 1] if gi + 1 < NG else None
                if gi + 2 < NG:
                    both(s1_mm, groups[gi + 2])
                if gn:
                    both(a1, gn)
                both(b45, g)
                if gn:
                    both(a2, gn)
                both(b6, g)
                if gn:
                    both(a3, gn)
                both(b78, g)
            pL = PB[5][0:64, 0:NCH * 16].rearrange("p (c n) -> p c n", c=NCH)
            for c in range(NCH):
                for kc in range(8):
                    self.mm(pL[:, c, :], xT[:, kc, c * 64:(c + 1) * 64], w_in[:, kc, 4096:4112], kc == 0, kc == 7, ['xT', 'w_in'], [('PB', 5)])
            self.act(beta[:], pL[:, :, 0:8], AF.Sigmoid, [('PB', 5)], ['beta'])
            self.tt('vector', xg[:], pL[:, :, 8:16], dtb[:].unsqueeze(1).to_broadcast([64, NCH, 8]), ALU.add, [('PB', 5), 'dtb'], ['xg'])
            self.act(xg[:], xg[:], AF.Exp, ['xg'], ['xg'])
            self.act(xg[:], xg[:], AF.Ln, ['xg'], ['xg'], bias=1.0)
            self.tt('vector', gt[:], xg[:], negA[:].unsqueeze(1).to_broadcast([64, NCH, 8]), ALU.mult, ['xg', 'negA'], ['gt'])
            self.V(lambda e: e.tensor_scalar(out=nbeta[:], in0=beta[:], scalar1=-1.0, scalar2=None, op0=ALU.mult), ['beta'], ['nbeta'])
            if 'dbg_qT' in dbg and t == 0:
                self.DMA(d_qT, qT[:], ['qT'], ['dbg'])
                self.DMA(d_kT, kT[:], ['kT'], ['dbg'])
                self.DMA(d_vtok, vtokp[:], ['vtok'], ['dbg'])
                self.DMA(d_ktok, ktokp[:], ['ktok'], ['dbg'])
                self.DMA(d_beta, beta[:], ['beta'], ['dbg'])
                self.DMA(d_g, gt[:], ['gt'], ['dbg'])
                self.final_keys.append('dbg')

            def prep(c):
                cp_ = c % 2
                cs = slice(c * 64, (c + 1) * 64)
                attT = attT2[cp_]
                G2 = attT
                eg = eg2[cp_]
                kdec = kdec2[cp_]
                XTf = XTf2[cp_]
                kat = ('attT', cp_)
                keg = ('eg', cp_)
                gcv = gt[:, c, :]
                gb = gcv.unsqueeze(2).to_broadcast([64, 8, 64])
                self.cp('vector', G1[:], gb, ['gt'], ['tmpA'])
                self.tt('vector', G2[:], tri8, gb, ALU.mult, ['gt', 'c64'], [kat])
                pD = PB[7][0:64, :]
                kD = ('PB', 7)
                self.mm(pD, ones64[:, 0:64], G2[:].rearrange("p h i -> p (h i)"), True, False, ['c64', kat], [kD])
                self.mm(pD, ntri, G1[:].rearrange("p h i -> p (h i)"), False, False, ['c64', 'tmpA'], [kD])
                self.mm(pD, id64, negT8, False, True, ['c64', 'ident'], [kD])
                self.act(decT[:].rearrange("p h i -> p (h i)"), pD, AF.Exp, [kD], ['decT'])
                pG = PB[3]
                kG = ('PB', 3)
                self.mm(pG[0:64, 0:8], tri, gcv, True, True, ['c64', 'gt'], [kG])
                self.mm(pG[0:64, 8:16], sup, gcv, True, True, ['c64', 'gt'], [kG])
                self.mm(pG[:, 16:24], ones64, gcv, True, True, ['c64', 'gt'], [kG])
                self.act(eg[0:64, 0:16], pG[0:64, 0:16], AF.Exp, [kG], [keg])
                self.act(eg[:, 16:24], pG[:, 16:24], AF.Exp, [kG], [keg])
                edec = eg[0:64, 8:16]
                yield
                pA = PB[4][0:64, :].rearrange("p (h i) -> p h i", h=8)
                pQK = PB[5][0:64, :].rearrange("p (h i) -> p h i", h=8)
                for h in range(8):
                    self.mm(pA[:, h, :], kT[:, h, cs], kT[:, h, cs], True, True, ['kT'], [('PB', 4)])
                for h in range(8):
                    self.mm(pQK[:, h, :], kT[:, h, cs], qT[:, h, cs], True, True, ['kT', 'qT'], [('PB', 5)])
                M = Mb[0]
                MT = MTb[0]
                P = Pb[0]
                self.tt('vector', tmpA[:], pA, decT[:], ALU.mult, [('PB', 4), 'decT'], ['tmpA'])
                self.tt('vector', tmpA[:], tmpA[:], beta[:, c, :].unsqueeze(2).to_broadcast([64, 8, 64]), ALU.mult, ['tmpA', 'beta'], ['tmpA'])
                self.tt('gpsimd', M[:], tmpA[:], offd8, ALU.mult, ['tmpA', 'c64'], [('M', 0, 0), ('M', 0, 1)])
                yield
                self.tt('vector', attT[:], pQK, decT[:], ALU.mult, [('PB', 5), 'decT'], [kat])
                pMT = PB[6][0:64, :].bitcast(BF16)[:, 0:512].rearrange("p (h i) -> p h i", h=8)
                for h in range(8):
                    self.tr(pMT[:, h, :], M[:, h, :], idb64, [('M', 0, 0), ('M', 0, 1), 'identb'], [('PB', 6)])
                self.cp('scalar', MT[:], pMT, [('PB', 6)], [('MT', 0, 0), ('MT', 0, 1)])
                self.tt('vector', P[:], eye8, M[:], ALU.subtract, ['ident', ('M', 0, 0), ('M', 0, 1)], [('P', 0, 0), ('P', 0, 1)])
                self.cp('gpsimd', kdec[:], ktok_c(c), ['ktok'], [('kdec', cp_)])
                self.tt('gpsimd', kdec[:], kdec[:], edec.unsqueeze(2).to_broadcast([64, 8, 128]), ALU.mult, [('kdec', cp_), keg], [('kdec', cp_)])
                self.cp('gpsimd', vcur2[cp_], vtok_c(c), ['vtok'], ['xs'])
                yield
                cur = 0
                ibank = [(3, 4), (5, 6)]
                for lvl in range(1, 6):
                    nxt = 1 - cur
                    M, MT, P = Mb[cur], MTb[cur], Pb[cur]
                    M2, M2T, P2 = Mb[nxt], MTb[nxt], Pb[nxt]
                    for grp in range(2):
                        gs = slice(grp * 4, grp * 4 + 4)
                        b1, b2_ = ibank[grp]
                        pM2T = PB[b1][0:64, 0:256].rearrange("p (h i) -> p h i", h=4)
                        pM2 = PB[b2_][0:64, 0:256].rearrange("p (h i) -> p h i", h=4)
                        kin = [('M', cur, grp), ('MT', cur, grp)]
                        for hh in range(4):
                            h = grp * 4 + hh
                            self.mm(pM2T[:, hh, :], M[:, h, :], MT[:, h, :], True, True, kin, [('PB', b1)])
                        self.cp('scalar', M2T[:, gs, :], pM2T, [('PB', b1)], [('MT', nxt, grp)])
                        if lvl < 5:
                            for hh in range(4):
                                h = grp * 4 + hh
                                self.mm(pM2[:, hh, :], MT[:, h, :], M[:, h, :], True, True, kin, [('PB', b2_)])
                            self.cp('vector', M2[:, gs, :], pM2, [('PB', b2_)], [('M', nxt, grp)])
                    yield
                    for grp in range(2):
                        gs = slice(grp * 4, grp * 4 + 4)
                        b1, b2_ = ibank[grp]
                        pPP = PB[b1][0:64, 0:256].rearrange("p (h i) -> p h i", h=4)
                        for hh in range(4):
                            h = grp * 4 + hh
                            self.mm(pPP[:, hh, :], M2T[:, h, :], P[:, h, :], True, True, [('MT', nxt, grp), ('P', cur, grp)], [('PB', b1)])
                        if lvl == 5:
                            self.tt('vector', XTf[:, gs, :], P[:, gs, :], pPP, ALU.add, [('P', cur, grp), ('PB', b1)], [('XT', cp_, grp)])
                        else:
                            self.tt('vector', P2[:, gs, :], P[:, gs, :], pPP, ALU.add, [('P', cur, grp), ('PB', b1)], [('P', nxt, grp)])
                    cur = nxt
                    yield
                if 'dbg_qT' in dbg and t == 0 and c == 0:
                    self.DMA(d_decT, decT[:], ['decT'], ['dbg'])
                    self.DMA(d_XT, XTf[:], [('XT', cp_, 0), ('XT', cp_, 1)], ['dbg'])

            def scan_out(c):
                cp_ = c % 2
                cs = slice(c * 64, (c + 1) * 64)
                attT = attT2[cp_]
                eg = eg2[cp_]
                kdec = kdec2[cp_]
                osq = kdec
                XT = XTf2[cp_]
                vcur = vcur2[cp_]
                kat = ('attT', cp_)
                keg = ('eg', cp_)
                egc = eg[0:64, 0:8]
                glb = eg[:, 16:24]
                pKS = PB[0][0:64, :].rearrange("p (h d) -> p h d", h=4)
                pQS = PB[1][0:64, :].rearrange("p (h d) -> p h d", h=4)
                pS = PB[2][:, :].rearrange("p (h d) -> p h d", h=4)
                kX, kY, kZ = ('PB', 0), ('PB', 1), ('PB', 2)
                for half in range(2):
                    hs = slice(half * 4, half * 4 + 4)
                    egb = egc[:, hs].unsqueeze(2).to_broadcast([64, 4, 128])
                    for hh in range(4):
                        h = half * 4 + hh
                        self.mm(pKS[:, hh, :], kT[:, h, cs], Sst[:, h, :], True, True, ['kT', ('Sst', half)], [kX])
                    for hh in range(4):
                        h = half * 4 + hh
                        self.mm(pQS[:, hh, :], qT[:, h, cs], Sst[:, h, :], True, True, ['qT', ('Sst', half)], [kY])
                    self.tt('vector', Rp[:], pKS, egb, ALU.mult, [kX, keg], ['Rp'])
                    self.tt('vector', Rp[:], Rp[:], vcur[:, hs, :], ALU.subtract, ['Rp', 'xs'], ['Rp'])
                    self.tt('vector', qs[:], pQS, egb, ALU.mult, [kY, keg], ['qs'])
                    yield
                    pVN = pKS
                    for hh in range(4):
                        h = half * 4 + hh
                        self.mm(pVN[:, hh, :], XT[:, h, :], Rp[:, hh, :], True, True, [('XT', cp_, half), 'Rp'], [kX])
                    self.tt('vector', vnew[:], pVN, nbeta[:, c, hs].unsqueeze(2).to_broadcast([64, 4, 128]), ALU.mult, [kX, 'nbeta'], ['vnew'])
                    yield
                    pAV = pQS
                    for hh in range(4):
                        h = half * 4 + hh
                        self.mm(pAV[:, hh, :], attT[:, h, :], vnew[:, hh, :], True, True, [kat, 'vnew'], [kY])
                    self.tt('vector', oc[:, hs, :], pAV, qs[:], ALU.add, [kY, 'qs'], [('oc', half)])
                    for hh in range(4):
                        h = half * 4 + hh
                        self.mm(pS[:, hh, :], kdec[:, h, :], vnew[:, hh, :], True, True, [('kdec', cp_), 'vnew'], [kZ])
                    self.tt('gpsimd', t2[:], Sst[:, hs, :], glb[:, hs].unsqueeze(2).to_broadcast([128, 4, 128]), ALU.mult, [('Sst', half), keg], ['t2'])
                    self.tt('vector', Sst[:, hs, :], t2[:], pS, ALU.add, ['t2', kZ], [('Sst', half)])
                    yield
                if 'dbg_qT' in dbg:
                    self.DMA(d_oc[t * NCH + c], oc[:], [('oc', 0), ('oc', 1)], ['dbg'])
                self.tt('gpsimd', osq[:], oc[:], oc[:], ALU.mult, [('oc', 0), ('oc', 1)], [('kdec', cp_)])
                self.V(lambda e: e.tensor_reduce(out=oss[:], in_=osq[:], axis=AX.X, op=ALU.add), [('kdec', cp_)], ['oss'])
                self.act(orr[:], oss[:], AF.Ln, ['oss'], ['orr'], scale=1.0 / 128.0, bias=EPS)
                self.act(orr[:], orr[:], AF.Exp, ['orr'], ['orr'], scale=-0.5)
                yield
                self.tt('vector', onb[:], oc[:], orr[:].unsqueeze(2).to_broadcast([64, 8, 128]), ALU.mult, [('oc', 0), ('oc', 1), 'orr'], ['onb'])
                pOT = PB[2][:].bitcast(BF16)[:, 0:512].rearrange("p (h i) -> p h i", h=8)
                for h in range(8):
                    self.tr(pOT[:, h, :], onb[:, h, :], idb64, ['onb', 'identb'], [('PB', 2)])
                self.V(lambda e, cs=cs, pOT=pOT: e.scalar_tensor_tensor(out=ogT[:, :, cs], in0=pOT, scalar=normw[:, 0:1], in1=szT[:, :, cs], op0=ALU.mult, op1=ALU.mult),
                       [('PB', 2), 'normw', 'szT'], ['ogT'])
                yield

            for c in range(NCH + 1):
                gens = []
                if c < NCH:
                    gens.append(prep(c))
                if c >= 1:
                    gens.append(scan_out(c - 1))
                while gens:
                    for g_ in list(gens):
                        try:
                            next(g_)
                        except StopIteration:
                            gens.remove(g_)
            self.DMA(w_out, w_out_d, r=['w_out_d'], w=['qT', 'kT'])
            if 'dbg_qT' in dbg and t == 0:
                self.DMA(d_ogT, ogT[:], ['ogT'], ['dbg'])
                self.DMA(d_wout, w_out, ['qT', 'kT'], ['dbg'])
            for s in range(2):
                pY = [PB[0][:], PB[1][:]]
                for half in range(2):
                    for h in range(8):
                        self.mm(pY[half], ogT[:, h, s * 128:(s + 1) * 128], w_out[:, h, half * 512:(half + 1) * 512], h == 0, h == 7, ['ogT', 'qT', 'kT'], [('PB', half)])
                self.DMA(rr[:], self.x[t0 + s * 128:t0 + (s + 1) * 128, :], w=['rr'])
                for half in range(2):
                    self.V(lambda e, half=half: e.scalar_tensor_tensor(out=rr[:, half * 512:(half + 1) * 512], in0=rr[:, half * 512:(half + 1) * 512], scalar=ALPHA, in1=pY[half], op0=ALU.mult, op1=ALU.add),
                           ['rr', ('PB', half)], ['rr'])
                    self.V(lambda e, half=half: e.bn_stats(out=bst[:, half, :], in_=rr[:, half * 512:(half + 1) * 512]), ['rr'], ['bst'])
                if 'dbg_qT' in dbg and t == 0:
                    self.DMA(d_rr[s], rr[:], ['rr'], ['dbg'])
                self.V(lambda e: e.bn_aggr(out=mv[:], in_=bst[:].rearrange("p a b -> p (a b)")), ['bst'], ['mv'])
                self.act(lrs[:], mv[:, 1:2], AF.Ln, ['mv'], ['lrs'], bias=EPS)
                self.act(rstd[:], lrs[:], AF.Exp, ['lrs'], ['rstd'], scale=-0.5)
                self.V(lambda e: e.tensor_scalar(out=nb[:], in0=mv[:, 0:1], scalar1=rstd[:, 0:1], scalar2=-1.0, op0=ALU.mult, op1=ALU.mult), ['mv', 'rstd'], ['nb'])
                if 'dbg_qT' in dbg and t == 0 and s == 0:
                    self.DMA(d_mv, mv[:], ['mv'], ['dbg'])
                    self.DMA(d_rstd, rstd[:], ['rstd'], ['dbg'])
                    self.DMA(d_bst, bst[:].rearrange("p a b -> p (a b)"), ['bst'], ['dbg'])
                self.act(xn[:], rr[:], AF.Identity, ['rr', 'rstd', 'nb'], ['rr'], scale=rstd[:, 0:1], bias=nb[:, 0:1])
                self.tt('gpsimd', xn[:], xn[:], lnw_b[:], ALU.mult, ['rr', 'lnw_b'], ['rr'])
                self.tt('gpsimd', h1s[:], xn[:], lnb_b[:], ALU.add, ['rr', 'lnb_b'], ['rr'])
                r0 = t0 + s * 128
                self.DMA(self.h1_d[r0:r0 + 128, :], h1s[:], ['rr'], [('h1_d', t, s)])
                self.final_keys.append(('h1_d', t, s))
                pTf = PB[2][:].rearrange("p (c n) -> p c n", c=4)
                for g4 in range(2):
                    for k4 in range(4):
                        kc = g4 * 4 + k4
                        self.tr(pTf[:, k4, :], h1s[:, kc * 128:(kc + 1) * 128], self.ident[:], ['rr', 'ident'], [('PB', 2)])
                    self.cp('scalar', h1T[:, g4 * 4:g4 * 4 + 4, :], pTf, [('PB', 2)], ['h1T'])
                self.DMA(self.h1T_d[:, :, r0:r0 + 128], h1T[:], ['h1T'], [('h1T_d', t, s)])
                self.final_keys.append(('h1T_d', t, s))

    def rope_tm(self, out4, x4, cs, nh, t1, t2, kx, kout):
        cosb = cs[:, 0:32].unsqueeze(1).unsqueeze(1).to_broadcast([128, nh, 2, 32])
        sinb = cs[:, 32:64].unsqueeze(1).to_broadcast([128, nh, 32])
        self.tt('vector', t1, x4, cosb, ALU.mult, [kx, 'cs'], ['rt1'])
        self.tt('gpsimd', t2[:, :, 0, :], x4[:, :, 1, :], sinb, ALU.mult, [kx, 'cs'], ['rt2'])
        self.tt('gpsimd', t2[:, :, 1, :], x4[:, :, 0, :], sinb, ALU.mult, [kx, 'cs'], ['rt2'])
        self.tt('vector', out4[:, :, 0, :], t1[:, :, 0, :], t2[:, :, 0, :], ALU.subtract, ['rt1', 'rt2'], [kout])
        self.tt('vector', out4[:, :, 1, :], t1[:, :, 1, :], t2[:, :, 1, :], ALU.add, ['rt1', 'rt2'], [kout])

    def phase2(self):
        PB = self.PB
        s_w_kv = self.din("s_w_kv", [1024, 1536])
        b_w_in = self.din("b_w_in", [1024, 4144])
        rope_cs = self.din("rope_cs", [T, 64])
        rope_q = self.din("rope_q", [T // 2, 64])
        bw_d = self.din("bw", [128, 2, 2])
        bw = self.sb("p2bw", [128, 2, 2], F32)
        self.DMA(bw[:], bw_d, w=['bw'])
        hA = self.sb("p2hA", [128, 8, 128], BF16)
        hB = self.sb("p2hB", [128, 8, 128], BF16)
        wkv = self.sb("wkv", [128, 8, 1536], BF16)
        wq = self.sb("wq", [128, 8, 1024], BF16)
        wz = self.sb("wz", [128, 8, 3072], BF16)
        wg = self.sb("wg", [128, 8, 48], BF16)
        for kc in range(8):
            rs = slice(kc * 128, (kc + 1) * 128)
            self.DMA(wkv[:, kc, :], s_w_kv[rs, :], w=['wkv'], eng='gpsimd')
            self.DMA(wq[:, kc, :], b_w_in[rs, 0:1024], w=['wq'], eng='gpsimd')
            self.DMA(wz[:, kc, :], b_w_in[rs, 1024:4096], w=['wz'], eng='gpsimd')
            self.DMA(wg[:, kc, :], b_w_in[rs, 4096:4144], w=['wg'], eng='gpsimd')
        h1T = [self.sb("p2h1T%d" % i, [128, 8, 128], BF16) for i in range(2)]
        cs = [self.sb("p2cs%d" % i, [128, 64], F32) for i in range(2)]
        kvs2 = [self.sb("kvs%d" % i, [128, 1536], F32) for i in range(2)]
        qs2_ = [self.sb("qs_%d" % i, [128, 1024], F32) for i in range(2)]
        t12 = [self.sb("rt1%d" % i, [128, 1024], F32) for i in range(2)]
        t22 = [self.sb("rt2%d" % i, [128, 1024], F32) for i in range(2)]
        krb2 = [self.sb("krb%d" % i, [128, 4, 256], BF16) for i in range(2)]
        qrb2 = [self.sb("qrb%d" % i, [128, 1024], BF16) for i in range(2)]
        vst2 = [self.sb("vst%d" % i, [128, 2, 4, 65], BF16) for i in range(2)]
        kT42 = [self.sb("kT4%d" % i, [64, 16, 128], BF16) for i in range(2)]
        qTt2 = [self.sb("qTt%d" % i, [64, 16, 128], BF16) for i in range(2)]
        zs3 = [self.sb("zs%d" % i, [128, 1024], F32) for i in range(3)]
        gts2 = [self.sb("gts%d" % i, [128, 48], F32) for i in range(2)]
        gz2 = [self.sb("gz%d" % i, [128, 3, 1024], BF16) for i in range(2)]
        for i in range(2):
            self.V(lambda e, i=i: e.memset(vst2[i][:], 1.0), w=[('vst', i)])
        P2KEYS = ['kvs', 'qs_', 'rt1', 'rt2', 'krb', 'qrb', 'vst', 'kT4', 'qTt', 'zs', 'gts', 'gz']

        pk = [PB[3][0:64, :].bitcast(BF16).rearrange("p (a t) -> p a t", a=8), PB[4][0:64, :].bitcast(BF16).rearrange("p (a t) -> p a t", a=8)]

        def kvA(qb):
            par = qb % 2
            self.S.kmap = {k: (k, par) for k in P2KEYS}
            kvs = kvs2[par]
            t0 = qb * 128
            hT = h1T[par]
            kh = ('p2h1T', par)
            self.DMA(hT[:], self.h1T_d[:, :, t0:t0 + 128], r=['h1T_all'], w=[kh])
            self.DMA(cs[par][:], rope_cs[t0:t0 + 128, :], w=[('cs', par)])
            for j in range(3):
                for kc in range(8):
                    self.mm(PB[j][:], hT[:, kc, :], wkv[:, kc, j * 512:(j + 1) * 512], kc == 0, kc == 7, [kh, 'wkv'], [('PB', j)])
                self.cp('scalar', kvs[:, j * 512:(j + 1) * 512], PB[j][:], [('PB', j)], ['kvs'])

        def kvB(qb):
            par = qb % 2
            self.S.kmap = {k: (k, par) for k in P2KEYS}
            self.S.kmap['cs'] = ('cs', par)
            kvs, t1, t2, krb, vst, kT4 = kvs2[par], t12[par], t22[par], krb2[par], vst2[par], kT42[par]
            t0 = qb * 128
            c_ = cs[par]
            for i, c0 in enumerate((512, 1024)):
                x4 = kvs[:, c0:c0 + 256].rearrange("p (g a d) -> p g a d", g=4, a=2)
                o4 = krb[:, i, :].rearrange("p (g a d) -> p g a d", g=4, a=2)
                self.rope_tm(o4, x4, c_, 4, t1[:, 0:256].rearrange("p (g a d) -> p g a d", g=4, a=2),
                             t2[:, 0:256].rearrange("p (g a d) -> p g a d", g=4, a=2), 'kvs', 'krb')
            self.cp('gpsimd', krb[:, 2, :], kvs[:, 0:256], ['kvs'], ['krb'])
            self.cp('gpsimd', krb[:, 3, :], kvs[:, 256:512], ['kvs'], ['krb'])
            self.cp('vector', vst[:, 0, :, 0:64], kvs[:, 768:1024].rearrange("p (g d) -> p g d", g=4), ['kvs'], ['vst'])
            self.cp('vector', vst[:, 1, :, 0:64], kvs[:, 1280:1536].rearrange("p (g d) -> p g d", g=4), ['kvs'], ['vst'])
            for i in range(4):
                for g in range(4):
                    a = i * 4 + g
                    self.tr(pk[a // 8][:, a % 8, :], krb[:, i, g * 64:(g + 1) * 64], self.identb[:], ['krb', 'identb'], [('PB', 3 + a // 8)])
            self.cp('scalar', kT4[:, 0:8, :], pk[0], [('PB', 3)], ['kT4'])
            self.cp('scalar', kT4[:, 8:16, :], pk[1], [('PB', 4)], ['kT4'])
            for i, dst in enumerate((self.kselT_d, self.kwinT_d, self.kcsT_d, self.vcsT_d)):
                self.DMA(dst[:, :, t0:t0 + 128].rearrange("g d t -> d g t"), kT4[:, i * 4:(i + 1) * 4, :], r=['kT4'], w=[('kvT_d', qb, i)])
            self.DMA(self.vsel_d[:, t0:t0 + 128, :].rearrange("g p c -> p g c"), vst[:, 0], r=['vst'], w=[('vsel_d', qb)])
            self.DMA(self.vwin_d[:, t0:t0 + 128, :].rearrange("g p c -> p g c"), vst[:, 1], r=['vst'], w=[('vwin_d', qb)])

        NKV = T // 128
        kvA(0)
        for qb in range(NKV):
            if qb + 1 < NKV:
                kvA(qb + 1)
            kvB(qb)

        def qA(slot):
            par = slot % 2
            self.S.kmap = {k: (k, par) for k in P2KEYS}
            qs_, gts, gz = qs2_[par], gts2[par], gz2[par]
            e2 = slot % 2
            t0 = slot * 128
            hT = h1T[par]
            kh = ('p2h1T', par)
            self.DMA(hA[:], self.h1T_d[:, :, (2 * slot) * 128:(2 * slot + 1) * 128], w=['p2hA'])
            self.DMA(hB[:], self.h1T_d[:, :, (2 * slot + 1) * 128:(2 * slot + 2) * 128], w=['p2hB'])
            self.DMA(cs[par][:], rope_q[t0:t0 + 128, :], w=[('cs', par)])
            self.V(lambda e, hT=hT, e2=e2: e.tensor_scalar(out=hT[:], in0=hA[:], scalar1=bw[:, e2, 0:1], scalar2=None, op0=ALU.mult), ['p2hA', 'bw'], [kh])
            self.V(lambda e, hT=hT, e2=e2: e.scalar_tensor_tensor(out=hT[:], in0=hB[:], scalar=bw[:, e2, 1:2], in1=hT[:], op0=ALU.mult, op1=ALU.add), ['p2hB', 'bw', kh], [kh])
            for j in range(2):
                for kc in range(8):
                    self.mm(PB[5 + j][:], hT[:, kc, :], wq[:, kc, j * 512:(j + 1) * 512], kc == 0, kc == 7, [kh, 'wq'], [('PB', 5 + j)])
                self.S.op('scalar', lambda e, j=j, qs_=qs_: e.mul(out=qs_[:, j * 512:(j + 1) * 512], in_=PB[5 + j][:], mul=0.125), [('PB', 5 + j)], ['qs_'])
            pg = PB[7][:, 0:48]
            for kc in range(8):
                self.mm(pg, hT[:, kc, :], wg[:, kc, :], kc == 0, kc == 7, [kh, 'wg'], [('PB', 7)])
            self.act(gts[:], pg, AF.Sigmoid, [('PB', 7)], ['gts'])
            for br in range(3):
                zs = zs3[br]
                for j in range(2):
                    pz = PB[j][:]
                    for kc in range(8):
                        self.mm(pz, hT[:, kc, :], wz[:, kc, br * 1024 + j * 512:br * 1024 + (j + 1) * 512], kc == 0, kc == 7, [kh, 'wz'], [('PB', j)])
                    self.act(zs[:, j * 512:(j + 1) * 512], pz, AF.Silu, [('PB', j)], [('zs3', br)])
                self.tt('vector' if br != 1 else 'gpsimd', gz[:, br, :].rearrange("p (h d) -> p h d", h=16), zs[:].rearrange("p (h d) -> p h d", h=16),
                        gts[:, br * 16:(br + 1) * 16].unsqueeze(2).to_broadcast([128, 16, 64]), ALU.mult, [('zs3', br), 'gts'], ['gz'])
            self.DMA(self.gz_d[t0:t0 + 128], gz[:], r=['gz'], w=[('gz_d', slot)])

        def qB(slot):
            par = slot % 2
            self.S.kmap = {k: (k, par) for k in P2KEYS}
            self.S.kmap['cs'] = ('cs', par)
            qs_, t1, t2, qrb, qTt = qs2_[par], t12[par], t22[par], qrb2[par], qTt2[par]
            c_ = cs[par]
            v16 = "p (g a d) -> p g a d"
            self.rope_tm(qrb[:].rearrange(v16, g=16, a=2), qs_[:].rearrange(v16, g=16, a=2), c_, 16,
                         t1[:].rearrange(v16, g=16, a=2), t2[:].rearrange(v16, g=16, a=2), 'qs_', 'qrb')
            for hh in range(16):
                self.tr(pk[hh // 8][:, hh % 8, :], qrb[:, hh * 64:(hh + 1) * 64], self.identb[:], ['qrb', 'identb'], [('PB', 3 + hh // 8)])
            self.cp('scalar', qTt[:, 0:8, :], pk[0], [('PB', 3)], ['qTt'])
            self.cp('scalar', qTt[:, 8:16, :], pk[1], [('PB', 4)], ['qTt'])
            self.DMA(self.qT_d[:, slot].rearrange("g d (h t) -> d g h t", h=4), qTt[:].rearrange("d (g h) t -> d g h t", g=4), r=['qTt'], w=[('qT_d', slot)])

        NSL = T // 256
        qA(0)
        for slot in range(NSL):
            if slot + 1 < NSL:
                qA(slot + 1)
            qB(slot)

        self.S.kmap = {}

    def phase3(self):
        PB = self.PB
        s_pe = [self.din("s_pe_k", [32, 64]), self.din("s_pe_v", [32, 64])]
        s_w1 = [self.din("s_w1_k", [32, 64, 128]), self.din("s_w1_v", [32, 64, 128])]
        s_w2 = [self.din("s_w2_k", [128, 64]), self.din("s_w2_v", [128, 64])]
        cmp_cs = self.din("cmp_cs", [64, 1024])
        ovm = self.din("ovm", [512, 128])
        w1 = [self.sb("w1_%d" % i, [64, 32, 128], BF16) for i in range(2)]
        w2 = [self.sb("w2_%d" % i, [128, 64], BF16) for i in range(2)]
        w2s = self.sb("w2s", [128, 64], BF16)
        pe32 = self.sb("pe32", [32, 2, 64], F32)
        peT = self.sb("peT", [64, 2, 32], BF16)
        bias = self.sb("cbias", [128, 2], F32)
        ccs = self.sb("ccs", [64, 1024], F32)
        src = self.sb("csrc", [64, T], BF16)
        hs = self.sb("chs", [128, 512], BF16)
        kx = self.sb("ckx", [64, 512], F32)
        kxs = self.sb("ckxs", [64, 512], F32)
        self.DMA(ccs[:], cmp_cs, w=['ccs'])
        for i in range(2):
            self.DMA(w1[i][:], s_w1[i].rearrange("c d h -> d c h"), w=[('w1', i)], eng='gpsimd')
            self.DMA(w2[i][:], s_w2[i], w=[('w2', i)], eng='gpsimd')
            self.DMA(pe32[:, i, :], s_pe[i], w=['pe32'])
        self.cp('vector', w2s[:, 0:32], w2[0][:, 32:64], [('w2', 0)], ['w2s'])
        self.cp('vector', w2s[:, 32:64], w2[0][:, 0:32], [('w2', 0)], ['w2s'])
        for i in range(2):
            pT = PB[0][0:64, i * 32:(i + 1) * 32]
            self.tr(pT, pe32[:, i, :], self.ident[0:32, 0:32], ['pe32', 'ident'], [('PB', 0)])
            self.cp('vector', peT[:, i, :], pT, [('PB', 0)], ['peT'])
        for i in range(2):
            pb_ = PB[1][:, i:i + 1]
            for c in range(32):
                self.mm(pb_, w1[i][:, c, :], peT[:, i, c:c + 1], c == 0, c == 31, [('w1', i), 'peT'], [('PB', 1)])
            self.cp('vector', bias[:, i:i + 1], pb_, [('PB', 1)], ['cbias'])
        self.V(lambda e: e.memset(self.vcaug[:, :, :, 64:65], 1.0), w=['vcaug'])
        for g in range(4):
            self.DMA(self.vcaug[:, g, :, 65:193], ovm.rearrange("(n p) s -> p n s", p=128), w=['vcaug'], eng='gpsimd')
        self.V(lambda e: e.memset(hs[:, 511:512], 0.0), w=['chs'])
        for i in range(2):
            srcd = self.kcsT_d if i == 0 else self.vcsT_d
            for g in range(4):
                self.DMA(src[:], srcd[g], r=['kvT_all'], w=['csrc'])
                s3 = src[:].rearrange("p (n r) -> p n r", r=16)
                ph = PB[2][:, 0:511]
                for c in range(32):
                    rhs = s3[:, 0:511, c] if c < 16 else s3[:, 1:512, c - 16]
                    self.mm(ph, w1[i][:, c, :], rhs, c == 0, c == 31, [('w1', i), 'csrc'], [('PB', 2)])
                self.act(hs[:, 0:511], ph, AF.Silu, [('PB', 2), 'cbias'], ['chs'], bias=bias[:, i:i + 1])
                if i == 0:
                    pk = PB[3][0:64, :]
                    pks = PB[4][0:64, :]
                    self.mm(pk, w2[0][:], hs[:], True, True, [('w2', 0), 'chs'], [('PB', 3)])
                    self.mm(pks, w2s[:], hs[:], True, True, ['w2s', 'chs'], [('PB', 4)])
                    self.tt('vector', kx[:], pk, ccs[:, 0:512], ALU.mult, [('PB', 3), 'ccs'], ['ckx'])
                    self.tt('vector', kxs[:], pks, ccs[:, 512:1024], ALU.mult, [('PB', 4), 'ccs'], ['ckxs'])
                    self.tt('vector', self.kcmpT[:, g, :], kx[:], kxs[:], ALU.add, ['ckx', 'ckxs'], ['kcmpT'])
                else:
                    pv = PB[5][:, 0:256].rearrange("p (n d) -> p n d", n=4)
                    for nt in range(4):
                        self.mm(pv[:, nt, :], hs[:, nt * 128:(nt + 1) * 128], w2[1][:], True, True, ['chs', ('w2', 1)], [('PB', 5)])
                    self.cp('vector', self.vcaug[:, g, :, 0:64], pv, [('PB', 5)], ['vcaug'])
        if 'dbg_kcmpT' in self.dbg:
            d1 = self.dout('dbg_kcmpT', [64, 4, 512], BF16)
            d2 = self.dout('dbg_vcaug', [128, 4, 4, 193], BF16)
            self.DMA(d1, self.kcmpT[:], ['kcmpT'], ['dbgk'])
            self.DMA(d2, self.vcaug[:], ['vcaug'], ['dbgk'])

    def phase4(self):
        PB = self.PB
        NQB = T // 256 if self.nqb4 is None else self.nqb4
        cmask_d = self.din("cmask_c", [128, 32, 4, 128], BF16)
        dmask_d = self.din("dmask", [128, 2, 2, 128])
        wmask_d = self.din("wmask", [128, 2, 6, 128])
        btab = self.din("btab_c", [32, 128, 128])
        cmk = [self.sb("cmk%d" % i, [128, 4, 128], BF16) for i in range(2)]
        dmk = self.sb("dmk", [128, 2, 2, 128], BF16)
        wmk = self.sb("wmk", [128, 2, 6, 128], BF16)
        self.DMA(dmk[:], dmask_d, w=['dmk'], eng='gpsimd')
        self.DMA(wmk[:], wmask_d, w=['wmk'], eng='gpsimd')
        kselT = self.sb("kselT", [64, T], BF16)
        kwinT = self.sb("kwinT", [64, T], BF16)
        vsel = self.sb("vsel", [128, 64, 65], BF16)
        vwin = self.sb("vwin", [128, 64, 65], BF16)
        qTb = [self.sb("qTb%d" % i, [64, 512], BF16) for i in range(2)]
        Btb = [self.sb("Btb%d" % i, [128, 128], F32) for i in range(2)]
        gzb = [self.sb("gzb%d" % i, [128, 3, 256], BF16) for i in range(2)]
        Eb = [self.sb("Eb%d" % i, [128, 4, 128], BF16) for i in range(3)]
        Pb_ = [self.sb("Pb_%d" % i, [128, 4, 128], BF16) for i in range(2)]
        rden = self.sb("rden", [128, 3, 4], F32)
        imp = self.sb("imp", [128, 128], F32)
        score = self.sb("score", [128, 128], F32)
        sc2 = self.sb("sc2", [128, 128], F32)
        m8 = self.sb("m8", [128, 16], F32)
        selb = self.sb("selb", [128, 128], BF16)
        selx = self.sb("selx", [128, 128, 64], BF16)
        tmp = self.sb("etmp", [128, 4, 64], F32)
        tmp2 = self.sb("etmp2", [128, 4, 64], F32)
        acc = self.sb("eacc", [128, 4, 64], F32)
        ogt = [self.sb("ogt%d" % i, [128, 256], BF16) for i in range(2)]
        ecnt = [0]
        pcnt = [0]
        scnt = [0]

        def qk_exp(kT_tile, kkeys, qT, kq):
            i = scnt[0] % 2
            scnt[0] += 1
            pS = PB[i][:]
            self.mm(pS, kT_tile, qT[:], True, True, list(kkeys) + [kq], [('PB', i)])
            j = ecnt[0] % 3
            ecnt[0] += 1
            E = Eb[j]
            self.act(E[:].rearrange("p h q -> p (h q)"), pS, AF.Exp, [('PB', i)], [('E', j)])
            return E, ('E', j)

        def loads(g, qb):
            t0 = qb * 128
            b2 = qb % 2
            self.DMA(qTb[b2][:], self.qT_d[g, qb], w=[('qTb', b2)])
            self.DMA(Btb[b2][:], btab[qb], w=[('Btb', b2)])
            self.DMA(gzb[b2][:], self.gz_d[t0:t0 + 128, :, g * 256:(g + 1) * 256], w=[('gzb', b2)])
            self.DMA(cmk[b2][:], cmask_d[:, qb], w=[('cmk', b2)])

        def make_items(g, qb):
            items = []
            t0 = qb * 128
            b2 = qb % 2
            qT = qTb[b2]
            kq = ('qTb', b2)
            Bt = Btb[b2]
            gz = gzb[b2]
            e2 = qb % 2
            qbm = 2 * qb + 1
            ntmax = (8 * qbm + 6) // 128
            pc = [PB[3][:, 0:386].rearrange("p (h c) -> p h c", h=2), PB[4][:, 0:386].rearrange("p (h c) -> p h c", h=2)]

            def cmp_post():
                for hb in range(2):
                    self.V(lambda e, hb=hb: e.tensor_scalar(out=rden[:, 0, hb * 2:hb * 2 + 2], in0=pc[hb][:, :, 64], scalar1=1e-30, scalar2=None, op0=ALU.max),
                           [('PB', 3 + hb)], ['rden0'])
                self.V(lambda e: e.reciprocal(out=rden[:, 0, :], in_=rden[:, 0, :]), ['rden0'], ['rden0'])
                for h in range(4):
                    src = pc[h // 2][:, h % 2, 65:193]
                    if h == 0:
                        self.V(lambda e, src=src: e.tensor_scalar(out=imp[:], in0=src, scalar1=rden[:, 0, 0:1], scalar2=None, op0=ALU.mult), [('PB', 3), 'rden0'], ['imp'])
                    else:
                        self.V(lambda e, src=src, h=h: e.scalar_tensor_tensor(out=imp[:], in0=src, scalar=rden[:, 0, h:h + 1], in1=imp[:], op0=ALU.mult, op1=ALU.add),
                               [('PB', 3 + h // 2), 'rden0', 'imp'], ['imp'])
                self.tt('vector', score[:], imp[:], Bt[:], ALU.add, ['imp', ('Btb', b2)], ['score'])
                self.V(lambda e: e.max(out=m8[:, 0:8], in_=score[:]), ['score'], ['m8'])
                self.V(lambda e: e.match_replace(out=sc2[:], in_to_replace=m8[:, 0:8], in_values=score[:], imm_value=-1e9), ['score', 'm8'], ['sc2'])
                self.V(lambda e: e.max(out=m8[:, 8:16], in_=sc2[:]), ['sc2'], ['m8'])
                nbk = 2 * (qbm + 1)
                self.V(lambda e: e.tensor_scalar(out=selx[:, 0:nbk, :], in0=score[:, 0:nbk].unsqueeze(2).to_broadcast([128, nbk, 64]), scalar1=m8[:, 15:16], scalar2=None, op0=ALU.is_ge),
                       ['score', 'm8'], ['selx'])
                for hb in range(2):
                    self.tt('vector', tmp[:, hb * 2:hb * 2 + 2, :], pc[hb][:, :, 0:64], rden[:, 0, hb * 2:hb * 2 + 2].unsqueeze(2).to_broadcast([128, 2, 64]), ALU.mult,
                            [('PB', 3 + hb), 'rden0'], ['etmp'])
                self.tt('gpsimd', acc[:], tmp[:], gz[:, 0, :].rearrange("p (h d) -> p h d", h=4), ALU.mult, ['etmp', ('gzb', b2)], ['eacc'])
                if self.dbg4 is not None and g == 0 and qb == self.dbg4:
                    self.V(lambda e: e.tensor_scalar(out=selb[:], in0=score[:], scalar1=m8[:, 15:16], scalar2=None, op0=ALU.is_ge), ['score', 'm8'], ['selb'])
                    self.DMA(self.d4['imp'], imp[:], ['imp'], ['dbg4'])
                    self.DMA(self.d4['sel'], selb[:], ['selb'], ['dbg4'])
                    self.DMA(self.d4['ocmp'], tmp[:], ['etmp'], ['dbg4'])

            for nt in range(ntmax + 1):
                it = {'mdep': False, 'M': None}

                def A(it=it, nt=nt):
                    it['E'], it['kE'] = qk_exp(self.kcmpT[:, g, nt * 128:(nt + 1) * 128], ['kcmpT'], qT, kq)

                def B(it=it, nt=nt):
                    E, kE = it['E'], it['kE']
                    self.tt('vector', E[:], E[:], cmk[b2][:, nt, :].unsqueeze(1).to_broadcast([128, 4, 128]), ALU.mult, [kE, ('cmk', b2)], [kE])
                    for h in range(4):
                        self.mm(pc[h // 2][:, h % 2, :], E[:, h, :], self.vcaug[:, g, nt, :], nt == 0 and h % 2 == 0, nt == ntmax and h % 2 == 1, [kE, 'vcaug'], [('PB', 3 + h // 2)])
                    if nt == ntmax:
                        cmp_post()
                it['A'], it['B'] = A, B
                items.append(it)

            for br in (1, 2):
                pacc = PB[4 + br][:, 0:260].rearrange("p (h c) -> p h c", h=4)
                kacc = ('PB', 4 + br)
                if br == 1:
                    kts = list(range(0, qbm + 1))
                    kT_, kkey, V_, vkey = kselT, 'kselT', vsel, 'vsel'
                else:
                    kts = [kt for kt in range(qbm - 5, qbm + 1) if kt >= 0]
                    kT_, kkey, V_, vkey = kwinT, 'kwinT', vwin, 'vwin'

                def br_post(br=br, pacc=pacc, kacc=kacc):
                    kr = 'rden%d' % br
                    self.V(lambda e: e.reciprocal(out=rden[:, br, :], in_=pacc[:, :, 64]), [kacc], [kr])
                    self.tt('vector', tmp[:], pacc[:, :, 0:64], rden[:, br, :].unsqueeze(2).to_broadcast([128, 4, 64]), ALU.mult, [kacc, kr], ['etmp'])
                    if self.dbg4 is not None and g == 0 and qb == self.dbg4:
                        self.DMA(self.d4['osel' if br == 1 else 'owin'], tmp[:], ['etmp'], ['dbg4'])
                    self.tt('gpsimd', tmp2[:], tmp[:], gz[:, br, :].rearrange("p (h d) -> p h d", h=4), ALU.mult, ['etmp', ('gzb', b2)], ['etmp2'])
                    if br == 1:
                        self.tt('gpsimd', acc[:], acc[:], tmp2[:], ALU.add, ['eacc', 'etmp2'], ['eacc'])
                    else:
                        og = ogt[b2]
                        self.tt('gpsimd', og[:].rearrange("p (h d) -> p h d", h=4), acc[:], tmp2[:], ALU.add, ['eacc', 'etmp2'], [('ogt', b2)])
                        self.DMA(self.og_d[t0:t0 + 128, g * 256:(g + 1) * 256], og[:], r=[('ogt', b2)], w=[('og_d', g, qb)])

                for kt in kts:
                    it = {'mdep': (br == 1 and kt == kts[0]), 'M': None}

                    def A(it=it, kt=kt, kT_=kT_, kkey=kkey):
                        it['E'], it['kE'] = qk_exp(kT_[:, kt * 128:(kt + 1) * 128], [kkey], qT, kq)

                    def M(it=it, kt=kt):
                        pM = PB[2 if kt % 2 == 0 else 7][:].bitcast(BF16)[:, 0:128]
                        kM = ('PB', 2 if kt % 2 == 0 else 7)
                        self.tr(pM, selx[:, 2 * kt:2 * kt + 2, :].rearrange("p a k -> p (a k)"), self.identb[:], ['selx', 'identb'], [kM])
                        it['pM'], it['kM'] = pM, kM

                    def B(it=it, kt=kt, br=br, kts=kts, pacc=pacc, kacc=kacc, V_=V_, vkey=vkey, br_post=br_post):
                        E, kE = it['E'], it['kE']
                        if br == 1:
                            ip = pcnt[0] % 2
                            pcnt[0] += 1
                            P = Pb_[ip]
                            kP = ('P4', ip)
                            self.tt('vector', P[:], E[:], it['pM'].unsqueeze(1).to_broadcast([128, 4, 128]), ALU.mult, [kE, it['kM']], [kP])
                            if kt >= qbm - 1:
                                self.tt('gpsimd', P[:], P[:], dmk[:, e2, kt - (qbm - 1), :].unsqueeze(1).to_broadcast([128, 4, 128]), ALU.mult, [kP, 'dmk'], [kP])
                        else:
                            P, kP = E, kE
                            wi = kt - (qbm - 5)
                            if wi not in (2, 3):
                                self.tt('gpsimd', P[:], P[:], wmk[:, e2, wi, :].unsqueeze(1).to_broadcast([128, 4, 128]), ALU.mult, [kP, 'wmk'], [kP])
                        for h in range(4):
                            self.mm(pacc[:, h, :], P[:, h, :], V_[:, kt, :], kt == kts[0] and h == 0, kt == kts[-1] and h == 3, [kP, vkey], [kacc])
                        if kt == kts[-1]:
                            br_post()
                    it['A'], it['B'] = A, B
                    if br == 1:
                        it['M'] = M
                    items.append(it)
            return items

        for g in range(4):
            self.DMA(kselT[:], self.kselT_d[g], w=['kselT'])
            self.DMA(kwinT[:], self.kwinT_d[g], w=['kwinT'])
            self.DMA(vsel[:], self.vsel_d[g].rearrange("(n p) c -> p n c", p=128), w=['vsel'])
            self.DMA(vwin[:], self.vwin_d[g].rearrange("(n p) c -> p n c", p=128), w=['vwin'])
            loads(g, 0)
            items = []
            for qb in range(NQB):
                if qb + 1 < NQB:
                    items.append({'load': (g, qb + 1)})
                items += make_items(g, qb)
            work = [it for it in items if 'load' not in it]
            pos = 0
            load_at = {}
            for it in items:
                if 'load' in it:
                    load_at.setdefault(pos, []).append(it['load'])
                else:
                    pos += 1
            n = len(work)
            done_loads = set()

            def do_loads(upto):
                for p_ in sorted(load_at):
                    if p_ <= upto and p_ not in done_loads:
                        done_loads.add(p_)
                        for l in load_at[p_]:
                            loads(*l)

            do_loads(0)
            for j in range(min(2, n)):
                work[j]['A']()
            if n > 0 and work[0]['M'] is not None:
                work[0]['M']()
            for i in range(n):
                do_loads(i)
                if i + 2 < n:
                    work[i + 2]['A']()
                nxt = work[i + 1] if i + 1 < n else None
                if nxt is not None and nxt['M'] is not None and not nxt['mdep']:
                    nxt['M']()
                work[i]['B']()
                if nxt is not None and nxt['M'] is not None and nxt['mdep']:
                    nxt['M']()

    def phase5(self):
        PB = self.PB
        NQB = T // 256 if self.nqb4 is None else self.nqb4
        bw_d = self.din("bw", [128, 2, 2]) if 'bw' not in self.inputs else self.inputs['bw'].ap()
        bw = self.sb("p5bw", [128, 2, 2], F32)
        self.DMA(bw[:], bw_d, w=['p5bw'])
        hB = [self.sb("p5hB%d" % i, [128, 1024], F32) for i in range(2)]
        b_w_out = self.din("b_w_out", [1024, 1024])
        b_ln_w = self.din("b_ln_w", [1, 1024])
        b_ln_b = self.din("b_ln_b", [1, 1024])
        out = self.dout("out", [T // 2, D], F32)
        w_out = self.sb("p5wout", [128, 8, 1024], BF16)
        self.DMA(w_out[:], b_w_out.rearrange("(c p) n -> p c n", p=128), w=['p5wout'], eng='gpsimd')
        lnw_b = self.sb("p5lnw", [128, 1024], F32)
        lnb_b = self.sb("p5lnb", [128, 1024], F32)
        self.DMA(lnw_b[:], b_ln_w.partition_broadcast(128), w=['p5lnw'])
        self.DMA(lnb_b[:], b_ln_b.partition_broadcast(128), w=['p5lnb'])
        ogs = [self.sb("p5og%d" % i, [128, 1024], BF16) for i in range(2)]
        h1s = [self.sb("p5h1%d" % i, [128, 1024], F32) for i in range(2)]
        ogT = self.sb("p5ogT", [128, 8, 128], BF16)
        rr = self.sb("p5rr", [128, 1024], F32)
        xo = [self.sb("p5xo%d" % i, [128, 1024], F32) for i in range(2)]
        bst = self.sb("p5bst", [128, 2, 6], F32)
        mv = self.sb("p5mv", [128, 2], F32)
        lrs = self.sb("p5lrs", [128, 1], F32)
        rstd = self.sb("p5rstd", [128, 1], F32)
        nb = self.sb("p5nb", [128, 1], F32)
        for qb in range(NQB):
            t0 = qb * 128
            b2 = qb % 2
            og = ogs[b2]
            h1 = h1s[b2]
            xn = xo[b2]
            self.DMA(og[:], self.og_d[t0:t0 + 128, :], w=[('p5og', b2)])
            e2 = qb % 2
            hb_ = hB[b2]
            self.DMA(h1[:], self.h1_d[(2 * qb) * 128:(2 * qb + 1) * 128, :], w=[('p5h1', b2)])
            self.DMA(hb_[:], self.h1_d[(2 * qb + 1) * 128:(2 * qb + 2) * 128, :], w=[('p5hB', b2)])
            self.V(lambda e, h1=h1, e2=e2: e.tensor_scalar(out=h1[:], in0=h1[:], scalar1=bw[:, e2, 0:1], scalar2=None, op0=ALU.mult), [('p5h1', b2), 'p5bw'], [('p5h1', b2)])
            self.V(lambda e, h1=h1, hb_=hb_, e2=e2: e.scalar_tensor_tensor(out=h1[:], in0=hb_[:], scalar=bw[:, e2, 1:2], in1=h1[:], op0=ALU.mult, op1=ALU.add),
                   [('p5hB', b2), 'p5bw', ('p5h1', b2)], [('p5h1', b2)])
            pTb = PB[2][:].bitcast(BF16).rearrange("p (c n) -> p c n", c=8)
            for kc in range(8):
                self.tr(pTb[:, kc, :], og[:, kc * 128:(kc + 1) * 128], self.identb[:], [('p5og', b2), 'identb'], [('PB', 2)])
            self.cp('scalar', ogT[:], pTb, [('PB', 2)], ['p5ogT'])
            pY = [PB[0][:], PB[1][:]]
            for half in range(2):
                for kc in range(8):
                    self.mm(pY[half], ogT[:, kc, :], w_out[:, kc, half * 512:(half + 1) * 512], kc == 0, kc == 7, ['p5ogT', 'p5wout'], [('PB', half)])
            for half in range(2):
                self.V(lambda e, half=half, h1=h1: e.scalar_tensor_tensor(out=rr[:, half * 512:(half + 1) * 512], in0=h1[:, half * 512:(half + 1) * 512], scalar=ALPHA, in1=pY[half], op0=ALU.mult, op1=ALU.add),
                       [('p5h1', b2), ('PB', half)], ['p5rr'])
                self.V(lambda e, half=half: e.bn_stats(out=bst[:, half, :], in_=rr[:, half * 512:(half + 1) * 512]), ['p5rr'], ['p5bst'])
            self.V(lambda e: e.bn_aggr(out=mv[:], in_=bst[:].rearrange("p a b -> p (a b)")), ['p5bst'], ['p5mv'])
            self.act(lrs[:], mv[:, 1:2], AF.Ln, ['p5mv'], ['p5lrs'], bias=EPS)
            self.act(rstd[:], lrs[:], AF.Exp, ['p5lrs'], ['p5rstd'], scale=-0.5)
            self.V(lambda e: e.tensor_scalar(out=nb[:], in0=mv[:, 0:1], scalar1=rstd[:, 0:1], scalar2=-1.0, op0=ALU.mult, op1=ALU.mult), ['p5mv', 'p5rstd'], ['p5nb'])
            self.act(xn[:], rr[:], AF.Identity, ['p5rr', 'p5rstd', 'p5nb'], [('p5xo', b2)], scale=rstd[:, 0:1], bias=nb[:, 0:1])
            self.tt('gpsimd', xn[:], xn[:], lnw_b[:], ALU.mult, [('p5xo', b2), 'p5lnw'], [('p5xo', b2)])
            self.tt('gpsimd', xn[:], xn[:], lnb_b[:], ALU.add, [('p5xo', b2), 'p5lnb'], [('p5xo', b2)])
            self.DMA(out[t0:t0 + 128, :], xn[:], r=[('p5xo', b2)], w=[('out', qb)])
            self.final_keys.append(('out', qb))


def _in_maps(b, inputs):
    hc = host_consts()
    maps = []
    for core in range(8):
        bi = core // 2
        m = dict(hc)
        m.update(core_consts(core % 2, hc))
        m['x'] = inputs['x'][bi]
        m['a_w_in'] = inputs['a_w_in'][0]
        m['a_conv_w'] = inputs['a_conv_w'][0]
        m['a_a_log'] = inputs['a_a_log'].reshape(1, 8)
        m['a_dt_bias'] = inputs['a_dt_bias'].reshape(1, 8)
        m['a_norm_w'] = inputs['a_norm_w'].reshape(128, 1)
        m['a_w_out'] = inputs['a_w_out'][0]
        m['a_ln_w'] = inputs['a_ln_w'].reshape(1, 1024)
        m['a_ln_b'] = inputs['a_ln_b'].reshape(1, 1024)
        for k in ('s_w_kv', 's_pe_k', 's_pe_v', 's_w1_k', 's_w2_k', 's_w1_v', 's_w2_v'):
            m[k] = inputs[k]
        m['b_w_in'] = inputs['b_w_in'][0]
        m['b_w_out'] = inputs['b_w_out'][0]
        m['b_ln_w'] = inputs['b_ln_w'].reshape(1, 1024)
        m['b_ln_b'] = inputs['b_ln_b'].reshape(1, 1024)
        maps.append({k: np.ascontiguousarray(v if k == 'cmask_c' else np.asarray(v, dtype=np.float32)) for k, v in m.items() if k in b.inputs})
    return maps


def kernel(**inputs):
    inputs = {k: np.asarray(v) for k, v in inputs.items()}
    import os
    ph = tuple(os.environ.get('KPHASES', 'p1,p2,p3,p4,p5').split(','))
    b = Builder(phases=ph)
    nc = b.build()
    maps = _in_maps(b, inputs)
    res = run_bass_kernel_spmd(nc, maps, core_ids=list(range(8)))
    if 'out' not in b.outputs:
        return np.zeros((4, T, D), np.float32)
    out = np.zeros((4, T, D), np.float32)
    for core in range(8):
        bi, p = core // 2, core % 2
        o = np.asarray(res.results[core]['out'], dtype=np.float32)
        for j in range(32):
            qb = slot_qb(p, j)
            out[bi, qb * 128:(qb + 1) * 128] = o[j * 128:(j + 1) * 128]
    return out
```

```python
import math
from contextlib import ExitStack

import numpy as np
import concourse.bass as bass
import concourse.mybir as mybir
from concourse.bass_utils import run_bass_kernel_spmd

F32 = mybir.dt.float32
BF16 = mybir.dt.bfloat16
AF = mybir.ActivationFunctionType
ALU = mybir.AluOpType
AX = mybir.AxisListType

ENGS = ('sync', 'gpsimd', 'scalar', 'vector', 'tensor')

T = 8192
D = 1024
NH = 8
EPS = 1e-6
ALPHA = 4.0 ** 0.25


class Sched:
    def __init__(self, nc, csems, dsems):
        self.nc = nc
        self.csem = csems
        self.dsems = dsems
        self.ops = {e: [] for e in ENGS}
        self.cnt = {e: 0 for e in ENGS}
        self.dcount = [0] * len(dsems)
        nd = len(dsems)
        self.dpool = {'sync': list(range(0, nd - 4)), 'gpsimd': list(range(nd - 4, nd))}
        self.dptr = {'sync': 0, 'gpsimd': 0}
        self.lastw = {}
        self.readers = {}
        self.waited = {e: {} for e in ENGS}
        self.nops = 0

    def _sem(self, sk):
        return self.csem[sk[1]] if sk[0] == 'c' else self.dsems[sk[1]]

    kmap = {}

    def _expand(self, keys):
        out = []
        for k in keys:
            if isinstance(k, str):
                k = self.kmap.get(k, k)
            if isinstance(k, tuple) and len(k) == 2 and k[0] == 'PB':
                out.append(('PB', k[1], 0))
                out.append(('PB', k[1], 1))
            else:
                out.append(k)
        return out

    def op(self, eng, fn, reads=(), writes=(), dma=False):
        need = {}
        reads = self._expand(reads)
        writes = self._expand(writes)

        def want(tok):
            sk, val, src = tok
            if sk[0] == 'c' and src == eng and eng == 'tensor':
                return
            if need.get(sk, 0) < val:
                need[sk] = val

        for k in reads:
            t = self.lastw.get(k)
            if t is not None:
                want(t)
        for k in writes:
            t = self.lastw.get(k)
            if t is not None:
                want(t)
            for t in self.readers.get(k, ()):
                want(t)
        if dma:
            pool = self.dpool[eng]
            i = pool[self.dptr[eng] % len(pool)]
            self.dptr[eng] += 1
            if self.dcount[i] > 0:
                want((('d', i), 16 * self.dcount[i], None))
            self.dcount[i] += 1
            tok = (('d', i), 16 * self.dcount[i], eng)
            inc = 16
        else:
            self.cnt[eng] += 1
            tok = (('c', eng), self.cnt[eng], eng)
            inc = 1
        w = self.waited[eng]
        waits = []
        for sk, val in need.items():
            if w.get(sk, 0) < val:
                w[sk] = val
                waits.append((self._sem(sk), val))
        self.ops[eng].append((waits, fn, self._sem(tok[0]), inc))
        for k in writes:
            self.lastw[k] = tok
            self.readers[k] = []
        for k in reads:
            lst = self.readers.setdefault(k, [])
            if len(lst) < 64:
                lst.append(tok)
            else:
                d = {}
                for t in lst + [tok]:
                    if d.get(t[0], (0,))[0] < t[1]:
                        d[t[0]] = (t[1], t[2])
                self.readers[k] = [(sk, v[0], v[1]) for sk, v in d.items()]
        self.nops += 1
        return tok

    def wait_all(self, eng, keys):
        need = {}
        for k in keys:
            t = self.lastw.get(k)
            if t is not None:
                sk, val, src = t
                if need.get(sk, 0) < val:
                    need[sk] = val
        waits = [(self._sem(sk), val) for sk, val in need.items()]
        self.ops[eng].append((waits, None, None, 0))

    def drain_dmas(self, eng):
        waits = [(self.dsems[i], 16 * self.dcount[i]) for i in range(len(self.dsems)) if self.dcount[i] > 0]
        self.ops[eng].append((waits, None, None, 0))
        for i in range(len(self.dsems)):
            self.waited[eng][('d', i)] = 16 * self.dcount[i]

    def emit(self):
        nc = self.nc
        with nc.Block() as block:
            for e in ENGS:
                ops = self.ops[e]
                if not ops:
                    continue

                def body(engine, ops=ops):
                    for waits, fn, sem, inc in ops:
                        for s, v in waits:
                            engine.wait_ge(s, v)
                        if fn is not None:
                            ins = fn(engine)
                            ins.then_inc(sem, inc)

                getattr(block, e)(body)
        self.ops = {e: [] for e in ENGS}


def host_consts():
    c = {}
    half = 32
    inv = (np.float32(10000.0) ** (-(np.arange(half, dtype=np.float32) / np.float32(half)))).astype(np.float32)
    pos = np.arange(T, dtype=np.float32)
    ang = (pos[:, None] * inv[None, :]).astype(np.float32)
    c['rope_cs'] = np.concatenate([np.cos(ang), np.sin(ang)], axis=1).astype(np.float32)
    pc = (np.arange(512, dtype=np.float32) * 16 + 31).astype(np.float32)
    angc = (pc[None, :] * inv[:, None]).astype(np.float32)
    cosF = np.concatenate([np.cos(angc), np.cos(angc)], axis=0)
    sinF = np.concatenate([-np.sin(angc), np.sin(angc)], axis=0)
    c['cmp_cs'] = np.concatenate([cosF, sinF], axis=1).astype(np.float32)
    st = np.arange(512)[:, None] * 16
    bs = np.arange(128)[None, :] * 64
    ov = np.clip(np.minimum(st + 32, bs + 64) - np.maximum(st, bs), 0, None) / 32.0
    ov[511] = 0.0
    c['ovm'] = ov.astype(np.float32)
    n = np.arange(128)[:, None, None]
    dl = np.arange(17)[None, :, None]
    i = np.arange(128)[None, None, :]
    c['cmpmask'] = (16 * n + 31 - i <= 128 * dl).astype(np.float32)
    kk = np.arange(128)[:, None]
    qq = np.arange(128)[None, :]
    c['cwmask'] = np.stack([(kk <= qq), (kk > qq)], axis=1).astype(np.float32)
    bt = np.zeros((64, 128, 128), np.float32)
    for qb in range(64):
        t = qb * 128 + np.arange(128)
        cur = t // 64
        blk = np.arange(128)[None, :]
        forced = (blk == 0) | (blk == cur[:, None]) | (blk == cur[:, None] - 1)
        vis = blk * 64 <= t[:, None]
        bt[qb] = np.where(vis, np.where(forced, 1.0e4, 0.0), -1.0)
    c['btab'] = bt
    c['ident'] = np.eye(128, dtype=np.float32)
    k = np.arange(64)
    tri = (k[:, None] <= k[None, :]).astype(np.float32)
    sup = (k[:, None] > k[None, :]).astype(np.float32)
    negT = np.where(k[None, :] < k[:, None], -1e30, 0.0).astype(np.float32)
    offd = (k[:, None] != k[None, :]).astype(np.float32)
    eye = np.eye(64, dtype=np.float32)
    c64 = np.concatenate([
        tri, -tri, sup, np.ones((64, 128), np.float32), -np.ones((64, 64), np.float32),
        np.tile(negT[:, None, :], (1, 8, 1)).reshape(64, 512), offd,
    ], axis=1)
    c['c64'] = np.ascontiguousarray(c64)
    return c


def slot_qb(p, j):
    first = (p == (j % 2))
    return 2 * j if first else 2 * j + 1


def core_consts(p, hc):
    import ml_dtypes
    c = {}
    bw = np.zeros((128, 2, 2), np.float32)
    for e in range(2):
        first = (p == e)
        bw[:, e, 0] = 1.0 if first else 0.0
        bw[:, e, 1] = 0.0 if first else 1.0
    c['bw'] = bw
    qbs = [slot_qb(p, j) for j in range(32)]
    c['rope_q'] = np.concatenate([hc['rope_cs'][qb * 128:(qb + 1) * 128] for qb in qbs], axis=0)
    c['btab_c'] = np.stack([hc['btab'][qb] for qb in qbs], axis=0)
    n = np.arange(128)[:, None, None, None]
    nt = np.arange(4)[None, None, :, None]
    i = np.arange(128)[None, None, None, :]
    qbv = np.array(qbs)[None, :, None, None]
    c['cmask_c'] = (16 * (128 * nt + n) + 31 <= 128 * qbv + i).astype(ml_dtypes.bfloat16)
    kk = np.arange(128)[:, None]
    qq = np.arange(128)[None, :]
    caus = (kk <= qq).astype(np.float32)
    win = (kk > qq).astype(np.float32)
    one = np.ones((128, 128), np.float32)
    zero = np.zeros((128, 128), np.float32)
    dm = np.zeros((128, 2, 2, 128), np.float32)
    wm = np.zeros((128, 2, 6, 128), np.float32)
    for e in range(2):
        first = (p == e)
        dl = [caus, zero] if first else [one, caus]
        wl = [win, one, one, one, caus, zero] if first else [zero, win, one, one, one, caus]
        for a, m_ in enumerate(dl):
            dm[:, e, a, :] = m_
        for a, m_ in enumerate(wl):
            wm[:, e, a, :] = m_
    c['dmask'] = dm
    c['wmask'] = wm
    return c


C64_OFF = {}
_o = 0
for _n, _w in [('tri', 64), ('ntri', 64), ('sup', 64), ('ones', 128), ('nones', 64),
               ('negT8', 512), ('offd', 64)]:
    C64_OFF[_n] = (_o, _o + _w)
    _o += _w
C64_W = _o


class Builder:
    def __init__(self, phases=('p1',), dbg=None, ntiles1=None, nqb4=None, dbg4=None):
        self.nqb4 = nqb4
        self.dbg4 = dbg4
        self.phases = phases
        self.dbg = dbg or {}
        self.ntiles1 = ntiles1
        self.nc = bass.Bass("TRN2", target_bir_lowering=False)
        self.es = ExitStack()
        self.inputs = {}
        self.outputs = {}

    def din(self, name, shape, dt=F32):
        t = self.nc.dram_tensor(name, list(shape), dt, kind="ExternalInput")
        self.inputs[name] = t
        return t.ap()

    def dscratch(self, name, shape, dt):
        if name in self.dbg:
            t = self.nc.dram_tensor(name, list(shape), dt, kind="ExternalOutput")
            self.outputs[name] = t
        else:
            t = self.nc.dram_tensor(name, list(shape), dt)
        return t.ap()

    def dout(self, name, shape, dt):
        t = self.nc.dram_tensor(name, list(shape), dt, kind="ExternalOutput")
        self.outputs[name] = t
        return t.ap()

    def sb(self, name, shape, dt):
        return self.pes.enter_context(self.nc.sbuf_tensor(name, list(shape), dt))

    def ps(self, name, shape, dt):
        return self.es.enter_context(self.nc.psum_tensor(name, list(shape), dt))

    def V(self, fn, r=(), w=()):
        self.S.op('vector', fn, r, w)

    def A(self, fn, r=(), w=()):
        self.S.op('scalar', fn, r, w)

    def G(self, fn, r=(), w=()):
        self.S.op('gpsimd', fn, r, w)

    def PE(self, fn, r=(), w=()):
        self.S.op('tensor', fn, r, w)

    def DMA(self, out, in_, r=(), w=(), eng='sync', **kw):
        self.S.op(eng, lambda e: e.dma_start(out=out, in_=in_, **kw), r, w, dma=True)

    def mm(self, out, lhsT, rhs, start, stop, r, w):
        self.S.op('tensor', lambda e: e.matmul(out, lhsT=lhsT, rhs=rhs, start=start, stop=stop), r, w)

    def tr(self, out, in_, ident, r, w):
        self.S.op('tensor', lambda e: e.transpose(out=out, in_=in_, identity=ident), r, w)

    def act(self, out, in_, func, r, w, **kw):
        self.S.op('scalar', lambda e: e.activation(out=out, in_=in_, func=func, **kw), r, w)

    def tt(self, eng, out, in0, in1, op, r, w):
        self.S.op(eng, lambda e: e.tensor_tensor(out=out, in0=in0, in1=in1, op=op), r, w)

    def cp(self, eng, out, in_, r, w):
        if eng == 'scalar':
            self.S.op(eng, lambda e: e.copy(out=out, in_=in_), r, w)
        else:
            self.S.op(eng, lambda e: e.tensor_copy(out=out, in_=in_), r, w)

    def build(self):
        nc = self.nc
        es = self.es
        with es:
            csems = {e: es.enter_context(nc.semaphore("c_" + e)) for e in ENGS}
            dsems = [es.enter_context(nc.semaphore("d%d" % i)) for i in range(12)]
            self.S = Sched(nc, csems, dsems)
            self.pes = es
            self.PB = [es.enter_context(nc.psum_tensor("pb%d" % i, [128, 512], F32)) for i in range(8)]
            self.setup_common()
            if 'p1' in self.phases:
                with ExitStack() as pes:
                    self.pes = pes
                    self.phase1()
                    self.S.drain_dmas('sync')
                    self.S.emit()
            for ph in ('p2', 'p3', 'p4', 'p5'):
                if ph in self.phases:
                    with ExitStack() as pes:
                        self.pes = pes
                        getattr(self, 'phase' + ph[1])()
                        self.S.drain_dmas('sync')
                        self.S.emit()
            self.pes = es
            self.finish()
            self.S.emit()
        return nc

    def finish(self):
        if not self.outputs:
            o = self.dout("out", [T // 2, D], F32)
            self.DMA(o[0:128, 0:128], self.ident[:], r=['ident'], w=['dummy_out'])
            self.final_keys.append('dummy_out')
        self.S.wait_all('sync', list(self.final_keys))

    def setup_common(self):
        self.final_keys = []
        x = self.din("x", [T, D])
        self.x = x
        self.c_ident = self.din("ident", [128, 128])
        self.c_c64 = self.din("c64", [64, C64_W])
        self.ident = self.sb("ident_sb", [128, 128], F32)
        self.identb = self.sb("identb_sb", [128, 128], BF16)
        self.c64 = self.sb("c64_sb", [64, C64_W], F32)
        self.DMA(self.ident[:], self.c_ident, w=['ident'])
        self.DMA(self.c64[:], self.c_c64, w=['c64'])
        self.cp('vector', self.identb[:], self.ident[:], ['ident'], ['identb'])
        self.ones128 = self.sb("ones128", [128, 128], F32)
        self.V(lambda e: e.memset(self.ones128[:], 1.0), w=['ones128'])
        if 'p1' in self.phases:
            self.h1_d = self.dscratch("h1_d", [T, D], F32)
            self.h1T_d = self.dscratch("h1T_d", [128, 8, T], BF16)
        else:
            self.h1_d = self.din("h1_d", [T, D], F32)
            self.h1T_d = self.din("h1T_d", [128, 8, T], BF16)
        self.kselT_d = self.dscratch("kselT_d", [4, 64, T], BF16)
        self.kwinT_d = self.dscratch("kwinT_d", [4, 64, T], BF16)
        self.kcsT_d = self.dscratch("kcsT_d", [4, 64, T], BF16)
        self.vcsT_d = self.dscratch("vcsT_d", [4, 64, T], BF16)
        self.vsel_d = self.dscratch("vsel_d", [4, 128, T // 128, 65], BF16)
        self.vwin_d = self.dscratch("vwin_d", [4, 128, T // 128, 65], BF16)
        self.qT_d = self.dscratch("qT_d", [4, 32, 64, 512], BF16)
        self.gz_d = self.dscratch("gz_d", [T // 2, 3, 1024], BF16)
        self.og_d = self.dscratch("og_d", [T // 2, 1024], BF16)
        if self.dbg4 is not None:
            self.d4 = {'imp': self.dout('dbg_imp', [128, 128], F32), 'sel': self.dout('dbg_sel', [128, 128], BF16),
                       'ocmp': self.dout('dbg_ocmp', [128, 4, 64], F32), 'osel': self.dout('dbg_osel', [128, 4, 64], F32),
                       'owin': self.dout('dbg_owin', [128, 4, 64], F32)}
            self.final_keys.append('dbg4')
        self.kcmpT = self.sb("kcmpT", [64, 4, 512], BF16)
        self.vcaug = self.sb("vcaug", [128, 4, 4, 193], BF16)

    def c64v(self, name, heads=False):
        a, b = C64_OFF[name]
        v = self.c64[:, a:b]
        if heads:
            v = v.rearrange("p (h i) -> p h i", h=8)
        return v

    def phase1(self):
        nc = self.nc
        S = self.S
        PB = self.PB
        TT = 256
        NCH = 4
        NT = T // TT if self.ntiles1 is None else self.ntiles1
        a_w_in = self.din("a_w_in", [1024, 4112])
        a_conv_w = self.din("a_conv_w", [4, 3072])
        a_a_log = self.din("a_a_log", [1, 8])
        a_dt_bias = self.din("a_dt_bias", [1, 8])
        a_norm_w = self.din("a_norm_w", [128, 1])
        a_w_out = self.din("a_w_out", [1024, 1024])
        a_ln_w = self.din("a_ln_w", [1, 1024])
        a_ln_b = self.din("a_ln_b", [1, 1024])

        w_in = self.sb("w_in_sb", [128, 8, 4112], BF16)
        qkT = self.sb("qkT", [128, 2, 8, TT], F32)
        qT = qkT[:, 0]
        kT = qkT[:, 1]
        w_out = qkT[:].rearrange("p a h t -> p (a h t)").bitcast(BF16).rearrange("p (c n) -> p c n", c=8)
        w_out_d = self.dscratch("w_out_bf_d", [128, 8, 1024], BF16)
        for kc in range(8):
            self.DMA(w_in[:, kc, :], a_w_in[kc * 128:(kc + 1) * 128, :], w=['w_in'], eng='gpsimd')
        self.DMA(w_out, a_w_out.rearrange("(c p) n -> p c n", p=128), w=['qT', 'kT'], eng='gpsimd')
        self.DMA(w_out_d, w_out, r=['qT', 'kT'], w=['w_out_d'])
        xs = [self.sb("xs0", [128, 2, 1024], F32)]
        vcur2 = [xs[0][0:64, i, :].rearrange("p (h d) -> p h d", h=8) for i in range(2)]
        cw4 = xs[0][0:4].rearrange("p s d -> p (s d)")
        convw = self.sb("convw", [128, 24, 4], F32)
        pcw = PB[0][:, 0:96].rearrange("p (b i) -> p b i", i=4)
        for part in range(2):
            nb_ = 16 if part == 0 else 8
            self.DMA(cw4[:, 0:nb_ * 128], a_conv_w[:, part * 2048:part * 2048 + nb_ * 128], w=['xs'])
            for b in range(nb_):
                self.tr(pcw[:, part * 16 + b, :], cw4[:, b * 128:(b + 1) * 128], self.ident[0:4, 0:4], ['xs', 'ident'], [('PB', 0)])
        self.cp('vector', convw[:], pcw, [('PB', 0)], ['convw'])
        normw = self.sb("normw", [128, 1], F32)
        self.DMA(normw[:], a_norm_w, w=['normw'])
        lnw_b = self.sb("lnw_b", [128, 1024], F32)
        lnb_b = self.sb("lnb_b", [128, 1024], F32)
        self.DMA(lnw_b[:], a_ln_w.partition_broadcast(128), w=['lnw_b'])
        self.DMA(lnb_b[:], a_ln_b.partition_broadcast(128), w=['lnb_b'])
        dtb = self.sb("dtb", [64, 8], F32)
        alog = self.sb("alog", [64, 8], F32)
        negA = self.sb("negA", [64, 8], F32)
        self.DMA(dtb[:], a_dt_bias.partition_broadcast(64), w=['dtb'])
        self.DMA(alog[:], a_a_log.partition_broadcast(64), w=['alog'])
        self.act(negA[:], alog[:], AF.Exp, ['alog'], ['negA'])
        self.V(lambda e: e.tensor_scalar(out=negA[:], in0=negA[:], scalar1=-1.0, scalar2=None, op0=ALU.mult), ['negA'], ['negA'])

        halo = self.sb("halo", [128, 24, 3], F32)
        self.V(lambda e: e.memset(halo[:], 0.0), w=['halo'])
        Sst = self.sb("Sst", [128, 8, 128], F32)
        self.V(lambda e: e.memset(Sst[:], 0.0), w=[('Sst', 0), ('Sst', 1)])

        xT = self.sb("xT", [128, 8, TT], BF16)
        pre = [self.sb("pre%d" % i, [128, TT + 3], F32) for i in range(2)]
        yb = [self.sb("yb%d" % i, [128, TT], F32) for i in range(2)]
        sb4 = [self.sb("s%d" % i, [128, TT], F32) for i in range(4)]
        sq = [self.sb("sq%d" % i, [128, TT], F32) for i in range(2)]
        lnb = [self.sb("lnt%d" % i, [128, TT], F32) for i in range(2)]
        ktokp = self.sb("ktok", [128, 2, 8, 128], F32)
        vtokp = self.sb("vtok", [128, 2, 8, 128], F32)

        def ktok_c(c):
            return ktokp[(c // 2) * 64:(c // 2) * 64 + 64, c % 2]

        def vtok_c(c):
            return vtokp[(c // 2) * 64:(c // 2) * 64 + 64, c % 2]
        szT = self.sb("szT", [128, 8, TT], BF16)
        ogT = self.sb("ogT", [128, 8, TT], BF16)
        beta = self.sb("beta", [64, NCH, 8], F32)
        nbeta = self.sb("nbeta", [64, NCH, 8], F32)
        gt = self.sb("gt", [64, NCH, 8], F32)
        xg = self.sb("xg", [64, NCH, 8], F32)

        decT = self.sb("decT", [64, 8, 64], F32)
        eg2 = [self.sb("eg%d" % i, [128, 24], F32) for i in range(2)]
        tmpA = self.sb("tmpA", [64, 8, 64], F32)
        G1 = tmpA
        Mb = [self.sb("Mb%d" % i, [64, 8, 64], BF16) for i in range(2)]
        MTb = [self.sb("MTb%d" % i, [64, 8, 64], BF16) for i in range(2)]
        Pb = [self.sb("Pb%d" % i, [64, 8, 64], BF16) for i in range(2)]
        XTf2 = [self.sb("XTf%d" % i, [64, 8, 64], F32) for i in range(2)]
        attT2 = [self.sb("attT%d" % i, [64, 8, 64], F32) for i in range(2)]
        kdec2 = [self.sb("kdec%d" % i, [64, 8, 128], F32) for i in range(2)]
        Rp = self.sb("Rp", [64, 4, 128], F32)
        qs = self.sb("qs", [64, 4, 128], F32)
        vnew = self.sb("vnew", [64, 4, 128], F32)
        oc = self.sb("oc", [64, 8, 128], F32)
        oss = self.sb("oss", [64, 8], F32)
        orr = self.sb("orr", [64, 8], F32)
        onb = self.sb("onb", [64, 8, 128], BF16)
        t2 = self.sb("t2", [128, 4, 128], F32)
        rr = self.sb("rr", [128, 1024], F32)
        h1s = rr
        xn = rr
        h1T = self.sb("h1T", [128, 8, 128], BF16)
        bst = self.sb("bst", [128, 2, 6], F32)
        mv = self.sb("mv", [128, 2], F32)
        lrs = self.sb("lrs", [128, 1], F32)
        rstd = self.sb("rstd", [128, 1], F32)
        nb = self.sb("nb", [128, 1], F32)

        tri = self.c64v('tri')
        ntri = self.c64v('ntri')
        sup = self.c64v('sup')
        ones64 = self.c64v('ones')
        negT8 = self.c64v('negT8')
        offd8 = self.c64v('offd').unsqueeze(1).to_broadcast([64, 8, 64])
        eye8 = self.ident[0:64, 0:64].unsqueeze(1).to_broadcast([64, 8, 64])
        tri8 = tri.unsqueeze(1).to_broadcast([64, 8, 64])
        id64 = self.ident[0:64, 0:64]
        idb64 = self.identb[0:64, 0:64]
        lnqs = math.log(128.0 ** -0.5)

        dbg = self.dbg
        if 'dbg_qT' in dbg:
            d_qT = self.dout('dbg_qT', [128, 8, TT], F32)
            d_kT = self.dout('dbg_kT', [128, 8, TT], F32)
            d_vtok = self.dout('dbg_vtok', [128, 2, 8, 128], F32)
            d_ktok = self.dout('dbg_ktok', [128, 2, 8, 128], F32)
            d_beta = self.dout('dbg_beta', [64, NCH, 8], F32)
            d_g = self.dout('dbg_g', [64, NCH, 8], F32)
            d_oc = self.dout('dbg_oc', [NT * NCH, 64, 8, 128], F32)
            d_decT = self.dout('dbg_decT', [64, 8, 64], F32)
            d_XT = self.dout('dbg_XT', [64, 8, 64], F32)
            d_M = self.dout('dbg_M', [64, 8, 64], F32)
            d_ogT = self.dout('dbg_ogT', [128, 8, TT], BF16)
            d_oss = self.dout('dbg_oss', [64, 8], F32)
            d_orr = self.dout('dbg_orr', [64, 8], F32)
            d_mv = self.dout('dbg_mv', [128, 2], F32)
            d_rstd = self.dout('dbg_rstd', [128, 1], F32)
            d_bst = self.dout('dbg_bst', [128, 12], F32)
            d_rr = self.dout('dbg_rr', [2, 128, 1024], F32)
            d_wout = self.dout('dbg_wout', [128, 8, 1024], BF16)

        def bank(i, shape=None, dt=None):
            v = PB[i][:]
            if dt is not None:
                v = v.bitcast(dt)
            return v

        for t in range(NT):
            t0 = t * TT
            xt = xs[0]
            kx = 'xs'
            self.DMA(xt[:], self.x[t0:t0 + TT, :].rearrange("(s p) d -> p s d", p=128), w=[kx])
            pTf = PB[2][:].rearrange("p (c n) -> p c n", c=4)
            for s in range(2):
                for g4 in range(2):
                    for k4 in range(4):
                        kc = g4 * 4 + k4
                        self.tr(pTf[:, k4, :], xt[:, s, kc * 128:(kc + 1) * 128], self.ident[:], [kx, 'ident'], [('PB', 2)])
                    self.cp('scalar', xT[:, g4 * 4:g4 * 4 + 4, s * 128:(s + 1) * 128], pTf, [('PB', 2)], ['xT'])
            ppb = [0, 1, 6, 7]

            def s1_mm(blk):
                pbk = ppb[blk % 4]
                pp = PB[pbk][:, 0:TT]
                for kc in range(8):
                    self.mm(pp, w_in[:, kc, blk * 128:(blk + 1) * 128], xT[:, kc, :], kc == 0, kc == 7, ['w_in', 'xT'], [('PB', pbk)])

            def bufs(blk):
                pb = blk % 2
                sp = blk % 4
                return pb, sp, pre[pb], ('pre', pb), yb[pb], ('y', pb), sb4[sp], ('s', sp)

            def a1(blk):
                pbk = ppb[blk % 4]
                pp = PB[pbk][:, 0:TT]
                kp = ('PB', pbk)
                if blk >= 24:
                    self.act(szT[:, blk - 24, :], pp, AF.Silu, [kp], ['szT'])
                    return
                pb, sp, pr, kpr, y, ky, s_, ks = bufs(blk)
                self.cp('scalar', pr[:, 3:TT + 3], pp, [kp], [kpr])
                self.cp('gpsimd', pr[:, 0:3], halo[:, blk, :], ['halo'], [kpr])

            def a2(blk):
                if blk >= 24:
                    return
                pb, sp, pr, kpr, y, ky, s_, ks = bufs(blk)
                self.V(lambda e, pr=pr, y=y, blk=blk: e.tensor_scalar(out=y[:], in0=pr[:, 0:TT], scalar1=convw[:, blk, 0:1], scalar2=None, op0=ALU.mult),
                       [kpr, 'convw'], [ky])
                for i in range(1, 4):
                    self.V(lambda e, pr=pr, y=y, blk=blk, i=i: e.scalar_tensor_tensor(out=y[:], in0=pr[:, i:i + TT], scalar=convw[:, blk, i:i + 1], in1=y[:], op0=ALU.mult, op1=ALU.add),
                           [kpr, 'convw', ky], [ky])
                self.cp('gpsimd', halo[:, blk, :], pr[:, TT:TT + 3], [kpr], ['halo'])

            def a3(blk):
                if blk >= 24:
                    return
                pb, sp, pr, kpr, y, ky, s_, ks = bufs(blk)
                self.act(s_[:], y[:], AF.Silu, [ky], [ks])

            def b45(blk):
                if blk >= 16:
                    return
                pb, sp, pr, kpr, y, ky, s_, ks = bufs(blk)
                q_ = sq[pb]
                ksq = ('sq', pb)
                self.tt('gpsimd', q_[:], s_[:], s_[:], ALU.mult, [ks], [ksq])
                psb = 3 if pb == 0 else 5
                self.mm(PB[psb][:, 0:TT], self.ones128[:], q_[:], True, True, ['ones128', ksq], [('PB', psb)])

            def b6(blk):
                if blk >= 16:
                    return
                pb = blk % 2
                psb = 3 if pb == 0 else 5
                l_ = lnb[pb]
                kl = ('lnt', pb)
                self.act(l_[:], PB[psb][:, 0:TT], AF.Ln, [('PB', psb)], [kl], bias=EPS)
                self.act(l_[:], l_[:], AF.Exp, [kl], [kl], scale=-0.5, bias=(lnqs if blk < 8 else 0.0))

            def b78(blk):
                if blk >= 24:
                    return
                pb, sp, pr, kpr, y, ky, s_, ks = bufs(blk)
                h = blk % 8
                pkb = 4 if pb == 0 else 2
                pk = PB[pkb][0:64, :].rearrange("p (c d) -> p c d", c=NCH)
                if blk < 16:
                    l_ = lnb[pb]
                    kl = ('lnt', pb)
                    dest = qT if blk < 8 else kT
                    kd = 'qT' if blk < 8 else 'kT'
                    self.tt('vector', dest[:, h, :], s_[:], l_[:], ALU.mult, [ks, kl], [kd])
                    if blk >= 8:
                        for c in range(NCH):
                            self.tr(pk[:, c, :], kT[:, h, c * 64:(c + 1) * 64], self.ident[:], ['kT', 'ident'], [('PB', pkb)])
                        for c2 in range(2):
                            self.cp('scalar', ktokp[c2 * 64:c2 * 64 + 64, :, h, :], pk[:, c2 * 2:c2 * 2 + 2, :], [('PB', pkb)], ['ktok'])
                else:
                    for c in range(NCH):
                        self.tr(pk[:, c, :], s_[:, c * 64:(c + 1) * 64], self.ident[:], [ks, 'ident'], [('PB', pkb)])
                    for c2 in range(2):
                        self.cp('scalar', vtokp[c2 * 64:c2 * 64 + 64, :, h, :], pk[:, c2 * 2:c2 * 2 + 2, :], [('PB', pkb)], ['vtok'])

            groups = [(2 * i, 2 * i + 1) for i in range(16)]
            NG = len(groups)

            def both(fn, g):
                fn(g[0])
                fn(g[1])

            both(s1_mm, groups[0])
            both(s1_mm, groups[1])
            both(a1, groups[0])
            both(a2, groups[0])
            both(a3, groups[0])
            for gi in range(NG):
                g = groups[gi]
                gn = groups[gi + 1] if gi + 1 < NG else None
                if gi + 2 < NG:
                    both(s1_mm, groups[gi + 2])
                if gn:
                    both(a1, gn)
                both(b45, g)
                if gn:
                    both(a2, gn)
                both(b6, g)
                if gn:
                    both(a3, gn)
                both(b78, g)
            pL = PB[5][0:64, 0:NCH * 16].rearrange("p (c n) -> p c n", c=NCH)
            for c in range(NCH):
                for kc in range(8):
                    self.mm(pL[:, c, :], xT[:, kc, c * 64:(c + 1) * 64], w_in[:, kc, 4096:4112], kc == 0, kc == 7, ['xT', 'w_in'], [('PB', 5)])
            self.act(beta[:], pL[:, :, 0:8], AF.Sigmoid, [('PB', 5)], ['beta'])
            self.tt('vector', xg[:], pL[:, :, 8:16], dtb[:].unsqueeze(1).to_broadcast([64, NCH, 8]), ALU.add, [('PB', 5), 'dtb'], ['xg'])
            self.act(xg[:], xg[:], AF.Exp, ['xg'], ['xg'])
            self.act(xg[:], xg[:], AF.Ln, ['xg'], ['xg'], bias=1.0)
            self.tt('vector', gt[:], xg[:], negA[:].unsqueeze(1).to_broadcast([64, NCH, 8]), ALU.mult, ['xg', 'negA'], ['gt'])
            self.V(lambda e: e.tensor_scalar(out=nbeta[:], in0=beta[:], scalar1=-1.0, scalar2=None, op0=ALU.mult), ['beta'], ['nbeta'])
            if 'dbg_qT' in dbg and t == 0:
                self.DMA(d_qT, qT[:], ['qT'], ['dbg'])
                self.DMA(d_kT, kT[:], ['kT'], ['dbg'])
                self.DMA(d_vtok, vtokp[:], ['vtok'], ['dbg'])
                self.DMA(d_ktok, ktokp[:], ['ktok'], ['dbg'])
                self.DMA(d_beta, beta[:], ['beta'], ['dbg'])
                self.DMA(d_g, gt[:], ['gt'], ['dbg'])
                self.final_keys.append('dbg')

            def prep(c):
                cp_ = c % 2
                cs = slice(c * 64, (c + 1) * 64)
                attT = attT2[cp_]
                G2 = attT
                eg = eg2[cp_]
                kdec = kdec2[cp_]
                XTf = XTf2[cp_]
                kat = ('attT', cp_)
                keg = ('eg', cp_)
                gcv = gt[:, c, :]
                gb = gcv.unsqueeze(2).to_broadcast([64, 8, 64])
                self.cp('vector', G1[:], gb, ['gt'], ['tmpA'])
                self.tt('vector', G2[:], tri8, gb, ALU.mult, ['gt', 'c64'], [kat])
                pD = PB[7][0:64, :]
                kD = ('PB', 7)
                self.mm(pD, ones64[:, 0:64], G2[:].rearrange("p h i -> p (h i)"), True, False, ['c64', kat], [kD])
                self.mm(pD, ntri, G1[:].rearrange("p h i -> p (h i)"), False, False, ['c64', 'tmpA'], [kD])
                self.mm(pD, id64, negT8, False, True, ['c64', 'ident'], [kD])
                self.act(decT[:].rearrange("p h i -> p (h i)"), pD, AF.Exp, [kD], ['decT'])
                pG = PB[3]
                kG = ('PB', 3)
                self.mm(pG[0:64, 0:8], tri, gcv, True, True, ['c64', 'gt'], [kG])
                self.mm(pG[0:64, 8:16], sup, gcv, True, True, ['c64', 'gt'], [kG])
                self.mm(pG[:, 16:24], ones64, gcv, True, True, ['c64', 'gt'], [kG])
                self.act(eg[0:64, 0:16], pG[0:64, 0:16], AF.Exp, [kG], [keg])
                self.act(eg[:, 16:24], pG[:, 16:24], AF.Exp, [kG], [keg])
                edec = eg[0:64, 8:16]
                yield
                pA = PB[4][0:64, :].rearrange("p (h i) -> p h i", h=8)
                pQK = PB[5][0:64, :].rearrange("p (h i) -> p h i", h=8)
                for h in range(8):
                    self.mm(pA[:, h, :], kT[:, h, cs], kT[:, h, cs], True, True, ['kT'], [('PB', 4)])
                for h in range(8):
                    self.mm(pQK[:, h, :], kT[:, h, cs], qT[:, h, cs], True, True, ['kT', 'qT'], [('PB', 5)])
                M = Mb[0]
                MT = MTb[0]
                P = Pb[0]
                self.tt('vector', tmpA[:], pA, decT[:], ALU.mult, [('PB', 4), 'decT'], ['tmpA'])
                self.tt('vector', tmpA[:], tmpA[:], beta[:, c, :].unsqueeze(2).to_broadcast([64, 8, 64]), ALU.mult, ['tmpA', 'beta'], ['tmpA'])
                self.tt('gpsimd', M[:], tmpA[:], offd8, ALU.mult, ['tmpA', 'c64'], [('M', 0, 0), ('M', 0, 1)])
                yield
                self.tt('vector', attT[:], pQK, decT[:], ALU.mult, [('PB', 5), 'decT'], [kat])
                pMT = PB[6][0:64, :].bitcast(BF16)[:, 0:512].rearrange("p (h i) -> p h i", h=8)
                for h in range(8):
                    self.tr(pMT[:, h, :], M[:, h, :], idb64, [('M', 0, 0), ('M', 0, 1), 'identb'], [('PB', 6)])
                self.cp('scalar', MT[:], pMT, [('PB', 6)], [('MT', 0, 0), ('MT', 0, 1)])
                self.tt('vector', P[:], eye8, M[:], ALU.subtract, ['ident', ('M', 0, 0), ('M', 0, 1)], [('P', 0, 0), ('P', 0, 1)])
                self.cp('gpsimd', kdec[:], ktok_c(c), ['ktok'], [('kdec', cp_)])
                self.tt('gpsimd', kdec[:], kdec[:], edec.unsqueeze(2).to_broadcast([64, 8, 128]), ALU.mult, [('kdec', cp_), keg], [('kdec', cp_)])
                self.cp('gpsimd', vcur2[cp_], vtok_c(c), ['vtok'], ['xs'])
                yield
                cur = 0
                ibank = [(3, 4), (5, 6)]
                for lvl in range(1, 6):
                    nxt = 1 - cur
                    M, MT, P = Mb[cur], MTb[cur], Pb[cur]
                    M2, M2T, P2 = Mb[nxt], MTb[nxt], Pb[nxt]
                    for grp in range(2):
                        gs = slice(grp * 4, grp * 4 + 4)
                        b1, b2_ = ibank[grp]
                        pM2T = PB[b1][0:64, 0:256].rearrange("p (h i) -> p h i", h=4)
                        pM2 = PB[b2_][0:64, 0:256].rearrange("p (h i) -> p h i", h=4)
                        kin = [('M', cur, grp), ('MT', cur, grp)]
                        for hh in range(4):
                            h = grp * 4 + hh
                            self.mm(pM2T[:, hh, :], M[:, h, :], MT[:, h, :], True, True, kin, [('PB', b1)])
                        self.cp('scalar', M2T[:, gs, :], pM2T, [('PB', b1)], [('MT', nxt, grp)])
                        if lvl < 5:
                            for hh in range(4):
                                h = grp * 4 + hh
                                self.mm(pM2[:, hh, :], MT[:, h, :], M[:, h, :], True, True, kin, [('PB', b2_)])
                            self.cp('vector', M2[:, gs, :], pM2, [('PB', b2_)], [('M', nxt, grp)])
                    yield
                    for grp in range(2):
                        gs = slice(grp * 4, grp * 4 + 4)
                        b1, b2_ = ibank[grp]
                        pPP = PB[b1][0:64, 0:256].rearrange("p (h i) -> p h i", h=4)
                        for hh in range(4):
                            h = grp * 4 + hh
                            self.mm(pPP[:, hh, :], M2T[:, h, :], P[:, h, :], True, True, [('MT', nxt, grp), ('P', cur, grp)], [('PB', b1)])
                        if lvl == 5:
                            self.tt('vector', XTf[:, gs, :], P[:, gs, :], pPP, ALU.add, [('P', cur, grp), ('PB', b1)], [('XT', cp_, grp)])
                        else:
                            self.tt('vector', P2[:, gs, :], P[:, gs, :], pPP, ALU.add, [('P', cur, grp), ('PB', b1)], [('P', nxt, grp)])
                    cur = nxt
                    yield
                if 'dbg_qT' in dbg and t == 0 and c == 0:
                    self.DMA(d_decT, decT[:], ['decT'], ['dbg'])
                    self.DMA(d_XT, XTf[:], [('XT', cp_, 0), ('XT', cp_, 1)], ['dbg'])

            def scan_out(c):
                cp_ = c % 2
                cs = slice(c * 64, (c + 1) * 64)
                attT = attT2[cp_]
                eg = eg2[cp_]
                kdec = kdec2[cp_]
                osq = kdec
                XT = XTf2[cp_]
                vcur = vcur2[cp_]
                kat = ('attT', cp_)
                keg = ('eg', cp_)
                egc = eg[0:64, 0:8]
                glb = eg[:, 16:24]
                pKS = PB[0][0:64, :].rearrange("p (h d) -> p h d", h=4)
                pQS = PB[1][0:64, :].rearrange("p (h d) -> p h d", h=4)
                pS = PB[2][:, :].rearrange("p (h d) -> p h d", h=4)
                kX, kY, kZ = ('PB', 0), ('PB', 1), ('PB', 2)
                for half in range(2):
                    hs = slice(half * 4, half * 4 + 4)
                    egb = egc[:, hs].unsqueeze(2).to_broadcast([64, 4, 128])
                    for hh in range(4):
                        h = half * 4 + hh
                        self.mm(pKS[:, hh, :], kT[:, h, cs], Sst[:, h, :], True, True, ['kT', ('Sst', half)], [kX])
                    for hh in range(4):
                        h = half * 4 + hh
                        self.mm(pQS[:, hh, :], qT[:, h, cs], Sst[:, h, :], True, True, ['qT', ('Sst', half)], [kY])
                    self.tt('vector', Rp[:], pKS, egb, ALU.mult, [kX, keg], ['Rp'])
                    self.tt('vector', Rp[:], Rp[:], vcur[:, hs, :], ALU.subtract, ['Rp', 'xs'], ['Rp'])
                    self.tt('vector', qs[:], pQS, egb, ALU.mult, [kY, keg], ['qs'])
                    yield
                    pVN = pKS
                    for hh in range(4):
                        h = half * 4 + hh
                        self.mm(pVN[:, hh, :], XT[:, h, :], Rp[:, hh, :], True, True, [('XT', cp_, half), 'Rp'], [kX])
                    self.tt('vector', vnew[:], pVN, nbeta[:, c, hs].unsqueeze(2).to_broadcast([64, 4, 128]), ALU.mult, [kX, 'nbeta'], ['vnew'])
                    yield
                    pAV = pQS
                    for hh in range(4):
                        h = half * 4 + hh
                        self.mm(pAV[:, hh, :], attT[:, h, :], vnew[:, hh, :], True, True, [kat, 'vnew'], [kY])
                    self.tt('vector', oc[:, hs, :], pAV, qs[:], ALU.add, [kY, 'qs'], [('oc', half)])
                    for hh in range(4):
                        h = half * 4 + hh
                        self.mm(pS[:, hh, :], kdec[:, h, :], vnew[:, hh, :], True, True, [('kdec', cp_), 'vnew'], [kZ])
                    self.tt('gpsimd', t2[:], Sst[:, hs, :], glb[:, hs].unsqueeze(2).to_broadcast([128, 4, 128]), ALU.mult, [('Sst', half), keg], ['t2'])
                    self.tt('vector', Sst[:, hs, :], t2[:], pS, ALU.add, ['t2', kZ], [('Sst', half)])
                    yield
                if 'dbg_qT' in dbg:
                    self.DMA(d_oc[t * NCH + c], oc[:], [('oc', 0), ('oc', 1)], ['dbg'])
                self.tt('gpsimd', osq[:], oc[:], oc[:], ALU.mult, [('oc', 0), ('oc', 1)], [('kdec', cp_)])
                self.V(lambda e: e.tensor_reduce(out=oss[:], in_=osq[:], axis=AX.X, op=ALU.add), [('kdec', cp_)], ['oss'])
                self.act(orr[:], oss[:], AF.Ln, ['oss'], ['orr'], scale=1.0 / 128.0, bias=EPS)
                self.act(orr[:], orr[:], AF.Exp, ['orr'], ['orr'], scale=-0.5)
                yield
                self.tt('vector', onb[:], oc[:], orr[:].unsqueeze(2).to_broadcast([64, 8, 128]), ALU.mult, [('oc', 0), ('oc', 1), 'orr'], ['onb'])
                pOT = PB[2][:].bitcast(BF16)[:, 0:512].rearrange("p (h i) -> p h i", h=8)
                for h in range(8):
                    self.tr(pOT[:, h, :], onb[:, h, :], idb64, ['onb', 'identb'], [('PB', 2)])
                self.V(lambda e, cs=cs, pOT=pOT: e.scalar_tensor_tensor(out=ogT[:, :, cs], in0=pOT, scalar=normw[:, 0:1], in1=szT[:, :, cs], op0=ALU.mult, op1=ALU.mult),
                       [('PB', 2), 'normw', 'szT'], ['ogT'])
                yield

            for c in range(NCH + 1):
                gens = []
                if c < NCH:
                    gens.append(prep(c))
                if c >= 1:
                    gens.append(scan_out(c - 1))
                while gens:
                    for g_ in list(gens):
                        try:
                            next(g_)
                        except StopIteration:
                            gens.remove(g_)
            self.DMA(w_out, w_out_d, r=['w_out_d'], w=['qT', 'kT'])
            if 'dbg_qT' in dbg and t == 0:
                self.DMA(d_ogT, ogT[:], ['ogT'], ['dbg'])
                self.DMA(d_wout, w_out, ['qT', 'kT'], ['dbg'])
            for s in range(2):
                pY = [PB[0][:], PB[1][:]]
                for half in range(2):
                    for h in range(8):
                        self.mm(pY[half], ogT[:, h, s * 128:(s + 1) * 128], w_out[:, h, half * 512:(half + 1) * 512], h == 0, h == 7, ['ogT', 'qT', 'kT'], [('PB', half)])
                self.DMA(rr[:], self.x[t0 + s * 128:t0 + (s + 1) * 128, :], w=['rr'])
                for half in range(2):
                    self.V(lambda e, half=half: e.scalar_tensor_tensor(out=rr[:, half * 512:(half + 1) * 512], in0=rr[:, half * 512:(half + 1) * 512], scalar=ALPHA, in1=pY[half], op0=ALU.mult, op1=ALU.add),
                           ['rr', ('PB', half)], ['rr'])
                    self.V(lambda e, half=half: e.bn_stats(out=bst[:, half, :], in_=rr[:, half * 512:(half + 1) * 512]), ['rr'], ['bst'])
                if 'dbg_qT' in dbg and t == 0:
                    self.DMA(d_rr[s], rr[:], ['rr'], ['dbg'])
                self.V(lambda e: e.bn_aggr(out=mv[:], in_=bst[:].rearrange("p a b -> p (a b)")), ['bst'], ['mv'])
                self.act(lrs[:], mv[:, 1:2], AF.Ln, ['mv'], ['lrs'], bias=EPS)
                self.act(rstd[:], lrs[:], AF.Exp, ['lrs'], ['rstd'], scale=-0.5)
                self.V(lambda e: e.tensor_scalar(out=nb[:], in0=mv[:, 0:1], scalar1=rstd[:, 0:1], scalar2=-1.0, op0=ALU.mult, op1=ALU.mult), ['mv', 'rstd'], ['nb'])
                if 'dbg_qT' in dbg and t == 0 and s == 0:
                    self.DMA(d_mv, mv[:], ['mv'], ['dbg'])
                    self.DMA(d_rstd, rstd[:], ['rstd'], ['dbg'])
                    self.DMA(d_bst, bst[:].rearrange("p a b -> p (a b)"), ['bst'], ['dbg'])
                self.act(xn[:], rr[:], AF.Identity, ['rr', 'rstd', 'nb'], ['rr'], scale=rstd[:, 0:1], bias=nb[:, 0:1])
                self.tt('gpsimd', xn[:], xn[:], lnw_b[:], ALU.mult, ['rr', 'lnw_b'], ['rr'])
                self.tt('gpsimd', h1s[:], xn[:], lnb_b[:], ALU.add, ['rr', 'lnb_b'], ['rr'])
                r0 = t0 + s * 128
                self.DMA(self.h1_d[r0:r0 + 128, :], h1s[:], ['rr'], [('h1_d', t, s)])
                self.final_keys.append(('h1_d', t, s))
                pTf = PB[2][:].rearrange("p (c n) -> p c n", c=4)
                for g4 in range(2):
                    for k4 in range(4):
                        kc = g4 * 4 + k4
                        self.tr(pTf[:, k4, :], h1s[:, kc * 128:(kc + 1) * 128], self.ident[:], ['rr', 'ident'], [('PB', 2)])
                    self.cp('scalar', h1T[:, g4 * 4:g4 * 4 + 4, :], pTf, [('PB', 2)], ['h1T'])
                self.DMA(self.h1T_d[:, :, r0:r0 + 128], h1T[:], ['h1T'], [('h1T_d', t, s)])
                self.final_keys.append(('h1T_d', t, s))

    def rope_tm(self, out4, x4, cs, nh, t1, t2, kx, kout):
        cosb = cs[:, 0:32].unsqueeze(1).unsqueeze(1).to_broadcast([128, nh, 2, 32])
        sinb = cs[:, 32:64].unsqueeze(1).to_broadcast([128, nh, 32])
        self.tt('vector', t1, x4, cosb, ALU.mult, [kx, 'cs'], ['rt1'])
        self.tt('gpsimd', t2[:, :, 0, :], x4[:, :, 1, :], sinb, ALU.mult, [kx, 'cs'], ['rt2'])
        self.tt('gpsimd', t2[:, :, 1, :], x4[:, :, 0, :], sinb, ALU.mult, [kx, 'cs'], ['rt2'])
        self.tt('vector', out4[:, :, 0, :], t1[:, :, 0, :], t2[:, :, 0, :], ALU.subtract, ['rt1', 'rt2'], [kout])
        self.tt('vector', out4[:, :, 1, :], t1[:, :, 1, :], t2[:, :, 1, :], ALU.add, ['rt1', 'rt2'], [kout])

    def phase2(self):
        PB = self.PB
        s_w_kv = self.din("s_w_kv", [1024, 1536])
        b_w_in = self.din("b_w_in", [1024, 4144])
        rope_cs = self.din("rope_cs", [T, 64])
        rope_q = self.din("rope_q", [T // 2, 64])
        bw_d = self.din("bw", [128, 2, 2])
        bw = self.sb("p2bw", [128, 2, 2], F32)
        self.DMA(bw[:], bw_d, w=['bw'])
        hA = self.sb("p2hA", [128, 8, 128], BF16)
        hB = self.sb("p2hB", [128, 8, 128], BF16)
        wkv = self.sb("wkv", [128, 8, 1536], BF16)
        wq = self.sb("wq", [128, 8, 1024], BF16)
        wz = self.sb("wz", [128, 8, 3072], BF16)
        wg = self.sb("wg", [128, 8, 48], BF16)
        for kc in range(8):
            rs = slice(kc * 128, (kc + 1) * 128)
            self.DMA(wkv[:, kc, :], s_w_kv[rs, :], w=['wkv'], eng='gpsimd')
            self.DMA(wq[:, kc, :], b_w_in[rs, 0:1024], w=['wq'], eng='gpsimd')
            self.DMA(wz[:, kc, :], b_w_in[rs, 1024:4096], w=['wz'], eng='gpsimd')
            self.DMA(wg[:, kc, :], b_w_in[rs, 4096:4144], w=['wg'], eng='gpsimd')
        h1T = [self.sb("p2h1T%d" % i, [128, 8, 128], BF16) for i in range(2)]
        cs = [self.sb("p2cs%d" % i, [128, 64], F32) for i in range(2)]
        kvs2 = [self.sb("kvs%d" % i, [128, 1536], F32) for i in range(2)]
        qs2_ = [self.sb("qs_%d" % i, [128, 1024], F32) for i in range(2)]
        t12 = [self.sb("rt1%d" % i, [128, 1024], F32) for i in range(2)]
        t22 = [self.sb("rt2%d" % i, [128, 1024], F32) for i in range(2)]
        krb2 = [self.sb("krb%d" % i, [128, 4, 256], BF16) for i in range(2)]
        qrb2 = [self.sb("qrb%d" % i, [128, 1024], BF16) for i in range(2)]
        vst2 = [self.sb("vst%d" % i, [128, 2, 4, 65], BF16) for i in range(2)]
        kT42 = [self.sb("kT4%d" % i, [64, 16, 128], BF16) for i in range(2)]
        qTt2 = [self.sb("qTt%d" % i, [64, 16, 128], BF16) for i in range(2)]
        zs3 = [self.sb("zs%d" % i, [128, 1024], F32) for i in range(3)]
        gts2 = [self.sb("gts%d" % i, [128, 48], F32) for i in range(2)]
        gz2 = [self.sb("gz%d" % i, [128, 3, 1024], BF16) for i in range(2)]
        for i in range(2):
            self.V(lambda e, i=i: e.memset(vst2[i][:], 1.0), w=[('vst', i)])
        P2KEYS = ['kvs', 'qs_', 'rt1', 'rt2', 'krb', 'qrb', 'vst', 'kT4', 'qTt', 'zs', 'gts', 'gz']

        pk = [PB[3][0:64, :].bitcast(BF16).rearrange("p (a t) -> p a t", a=8), PB[4][0:64, :].bitcast(BF16).rearrange("p (a t) -> p a t", a=8)]

        def kvA(qb):
            par = qb % 2
            self.S.kmap = {k: (k, par) for k in P2KEYS}
            kvs = kvs2[par]
            t0 = qb * 128
            hT = h1T[par]
            kh = ('p2h1T', par)
            self.DMA(hT[:], self.h1T_d[:, :, t0:t0 + 128], r=['h1T_all'], w=[kh])
            self.DMA(cs[par][:], rope_cs[t0:t0 + 128, :], w=[('cs', par)])
            for j in range(3):
                for kc in range(8):
                    self.mm(PB[j][:], hT[:, kc, :], wkv[:, kc, j * 512:(j + 1) * 512], kc == 0, kc == 7, [kh, 'wkv'], [('PB', j)])
                self.cp('scalar', kvs[:, j * 512:(j + 1) * 512], PB[j][:], [('PB', j)], ['kvs'])

        def kvB(qb):
            par = qb % 2
            self.S.kmap = {k: (k, par) for k in P2KEYS}
            self.S.kmap['cs'] = ('cs', par)
            kvs, t1, t2, krb, vst, kT4 = kvs2[par], t12[par], t22[par], krb2[par], vst2[par], kT42[par]
            t0 = qb * 128
            c_ = cs[par]
            for i, c0 in enumerate((512, 1024)):
                x4 = kvs[:, c0:c0 + 256].rearrange("p (g a d) -> p g a d", g=4, a=2)
                o4 = krb[:, i, :].rearrange("p (g a d) -> p g a d", g=4, a=2)
                self.rope_tm(o4, x4, c_, 4, t1[:, 0:256].rearrange("p (g a d) -> p g a d", g=4, a=2),
                             t2[:, 0:256].rearrange("p (g a d) -> p g a d", g=4, a=2), 'kvs', 'krb')
            self.cp('gpsimd', krb[:, 2, :], kvs[:, 0:256], ['kvs'], ['krb'])
            self.cp('gpsimd', krb[:, 3, :], kvs[:, 256:512], ['kvs'], ['krb'])
            self.cp('vector', vst[:, 0, :, 0:64], kvs[:, 768:1024].rearrange("p (g d) -> p g d", g=4), ['kvs'], ['vst'])
            self.cp('vector', vst[:, 1, :, 0:64], kvs[:, 1280:1536].rearrange("p (g d) -> p g d", g=4), ['kvs'], ['vst'])
            for i in range(4):
                for g in range(4):
                    a = i * 4 + g
                    self.tr(pk[a // 8][:, a % 8, :], krb[:, i, g * 64:(g + 1) * 64], self.identb[:], ['krb', 'identb'], [('PB', 3 + a // 8)])
            self.cp('scalar', kT4[:, 0:8, :], pk[0], [('PB', 3)], ['kT4'])
            self.cp('scalar', kT4[:, 8:16, :], pk[1], [('PB', 4)], ['kT4'])
            for i, dst in enumerate((self.kselT_d, self.kwinT_d, self.kcsT_d, self.vcsT_d)):
                self.DMA(dst[:, :, t0:t0 + 128].rearrange("g d t -> d g t"), kT4[:, i * 4:(i + 1) * 4, :], r=['kT4'], w=[('kvT_d', qb, i)])
            self.DMA(self.vsel_d[:, :, qb, :].rearrange("g p c -> p g c"), vst[:, 0], r=['vst'], w=[('vsel_d', qb)])
            self.DMA(self.vwin_d[:, :, qb, :].rearrange("g p c -> p g c"), vst[:, 1], r=['vst'], w=[('vwin_d', qb)])

        NKV = T // 128
        kvA(0)
        for qb in range(NKV):
            if qb + 1 < NKV:
                kvA(qb + 1)
            kvB(qb)

        def qA(slot):
            par = slot % 2
            self.S.kmap = {k: (k, par) for k in P2KEYS}
            qs_, gts, gz = qs2_[par], gts2[par], gz2[par]
            e2 = slot % 2
            t0 = slot * 128
            hT = h1T[par]
            kh = ('p2h1T', par)
            self.DMA(hA[:], self.h1T_d[:, :, (2 * slot) * 128:(2 * slot + 1) * 128], w=['p2hA'])
            self.DMA(hB[:], self.h1T_d[:, :, (2 * slot + 1) * 128:(2 * slot + 2) * 128], w=['p2hB'])
            self.DMA(cs[par][:], rope_q[t0:t0 + 128, :], w=[('cs', par)])
            self.V(lambda e, hT=hT, e2=e2: e.tensor_scalar(out=hT[:], in0=hA[:], scalar1=bw[:, e2, 0:1], scalar2=None, op0=ALU.mult), ['p2hA', 'bw'], [kh])
            self.V(lambda e, hT=hT, e2=e2: e.scalar_tensor_tensor(out=hT[:], in0=hB[:], scalar=bw[:, e2, 1:2], in1=hT[:], op0=ALU.mult, op1=ALU.add), ['p2hB', 'bw', kh], [kh])
            for j in range(2):
                for kc in range(8):
                    self.mm(PB[5 + j][:], hT[:, kc, :], wq[:, kc, j * 512:(j + 1) * 512], kc == 0, kc == 7, [kh, 'wq'], [('PB', 5 + j)])
                self.S.op('scalar', lambda e, j=j, qs_=qs_: e.mul(out=qs_[:, j * 512:(j + 1) * 512], in_=PB[5 + j][:], mul=0.125), [('PB', 5 + j)], ['qs_'])
            pg = PB[7][:, 0:48]
            for kc in range(8):
                self.mm(pg, hT[:, kc, :], wg[:, kc, :], kc == 0, kc == 7, [kh, 'wg'], [('PB', 7)])
            self.act(gts[:], pg, AF.Sigmoid, [('PB', 7)], ['gts'])
            for br in range(3):
                zs = zs3[br]
                for j in range(2):
                    pz = PB[j][:]
                    for kc in range(8):
                        self.mm(pz, hT[:, kc, :], wz[:, kc, br * 1024 + j * 512:br * 1024 + (j + 1) * 512], kc == 0, kc == 7, [kh, 'wz'], [('PB', j)])
                    self.act(zs[:, j * 512:(j + 1) * 512], pz, AF.Silu, [('PB', j)], [('zs3', br)])
                self.tt('vector' if br != 1 else 'gpsimd', gz[:, br, :].rearrange("p (h d) -> p h d", h=16), zs[:].rearrange("p (h d) -> p h d", h=16),
                        gts[:, br * 16:(br + 1) * 16].unsqueeze(2).to_broadcast([128, 16, 64]), ALU.mult, [('zs3', br), 'gts'], ['gz'])
            self.DMA(self.gz_d[t0:t0 + 128], gz[:], r=['gz'], w=[('gz_d', slot)])

        def qB(slot):
            par = slot % 2
            self.S.kmap = {k: (k, par) for k in P2KEYS}
            self.S.kmap['cs'] = ('cs', par)
            qs_, t1, t2, qrb, qTt = qs2_[par], t12[par], t22[par], qrb2[par], qTt2[par]
            c_ = cs[par]
            v16 = "p (g a d) -> p g a d"
            self.rope_tm(qrb[:].rearrange(v16, g=16, a=2), qs_[:].rearrange(v16, g=16, a=2), c_, 16,
                         t1[:].rearrange(v16, g=16, a=2), t2[:].rearrange(v16, g=16, a=2), 'qs_', 'qrb')
            for hh in range(16):
                self.tr(pk[hh // 8][:, hh % 8, :], qrb[:, hh * 64:(hh + 1) * 64], self.identb[:], ['qrb', 'identb'], [('PB', 3 + hh // 8)])
            self.cp('scalar', qTt[:, 0:8, :], pk[0], [('PB', 3)], ['qTt'])
            self.cp('scalar', qTt[:, 8:16, :], pk[1], [('PB', 4)], ['qTt'])
            self.DMA(self.qT_d[:, slot].rearrange("g d (h t) -> d g h t", h=4), qTt[:].rearrange("d (g h) t -> d g h t", g=4), r=['qTt'], w=[('qT_d', slot)])

        NSL = T // 256
        qA(0)
        for slot in range(NSL):
            if slot + 1 < NSL:
                qA(slot + 1)
            qB(slot)

        self.S.kmap = {}

    def phase3(self):
        PB = self.PB
        s_pe = [self.din("s_pe_k", [32, 64]), self.din("s_pe_v", [32, 64])]
        s_w1 = [self.din("s_w1_k", [32, 64, 128]), self.din("s_w1_v", [32, 64, 128])]
        s_w2 = [self.din("s_w2_k", [128, 64]), self.din("s_w2_v", [128, 64])]
        cmp_cs = self.din("cmp_cs", [64, 1024])
        ovm = self.din("ovm", [512, 128])
        w1 = [self.sb("w1_%d" % i, [64, 32, 128], BF16) for i in range(2)]
        w2 = [self.sb("w2_%d" % i, [128, 64], BF16) for i in range(2)]
        w2s = self.sb("w2s", [128, 64], BF16)
        pe32 = self.sb("pe32", [32, 2, 64], F32)
        peT = self.sb("peT", [64, 2, 32], BF16)
        bias = self.sb("cbias", [128, 2], F32)
        ccs = self.sb("ccs", [64, 1024], F32)
        src = self.sb("csrc", [64, T], BF16)
        hs = self.sb("chs", [128, 512], BF16)
        kx = self.sb("ckx", [64, 512], F32)
        kxs = self.sb("ckxs", [64, 512], F32)
        self.DMA(ccs[:], cmp_cs, w=['ccs'])
        for i in range(2):
            self.DMA(w1[i][:], s_w1[i].rearrange("c d h -> d c h"), w=[('w1', i)], eng='gpsimd')
            self.DMA(w2[i][:], s_w2[i], w=[('w2', i)], eng='gpsimd')
            self.DMA(pe32[:, i, :], s_pe[i], w=['pe32'])
        self.cp('vector', w2s[:, 0:32], w2[0][:, 32:64], [('w2', 0)], ['w2s'])
        self.cp('vector', w2s[:, 32:64], w2[0][:, 0:32], [('w2', 0)], ['w2s'])
        for i in range(2):
            pT = PB[0][0:64, i * 32:(i + 1) * 32]
            self.tr(pT, pe32[:, i, :], self.ident[0:32, 0:32], ['pe32', 'ident'], [('PB', 0)])
            self.cp('vector', peT[:, i, :], pT, [('PB', 0)], ['peT'])
        for i in range(2):
            pb_ = PB[1][:, i:i + 1]
            for c in range(32):
                self.mm(pb_, w1[i][:, c, :], peT[:, i, c:c + 1], c == 0, c == 31, [('w1', i), 'peT'], [('PB', 1)])
            self.cp('vector', bias[:, i:i + 1], pb_, [('PB', 1)], ['cbias'])
        self.V(lambda e: e.memset(self.vcaug[:, :, :, 64:65], 1.0), w=['vcaug'])
        for g in range(4):
            self.DMA(self.vcaug[:, g, :, 65:193], ovm.rearrange("(n p) s -> p n s", p=128), w=['vcaug'], eng='gpsimd')
        self.V(lambda e: e.memset(hs[:, 511:512], 0.0), w=['chs'])
        for i in range(2):
            srcd = self.kcsT_d if i == 0 else self.vcsT_d
            for g in range(4):
                self.DMA(src[:], srcd[g], r=['kvT_all'], w=['csrc'])
                s3 = src[:].rearrange("p (n r) -> p n r", r=16)
                ph = PB[2][:, 0:511]
                for c in range(32):
                    rhs = s3[:, 0:511, c] if c < 16 else s3[:, 1:512, c - 16]
                    self.mm(ph, w1[i][:, c, :], rhs, c == 0, c == 31, [('w1', i), 'csrc'], [('PB', 2)])
                self.act(hs[:, 0:511], ph, AF.Silu, [('PB', 2), 'cbias'], ['chs'], bias=bias[:, i:i + 1])
                if i == 0:
                    pk = PB[3][0:64, :]
                    pks = PB[4][0:64, :]
                    self.mm(pk, w2[0][:], hs[:], True, True, [('w2', 0), 'chs'], [('PB', 3)])
                    self.mm(pks, w2s[:], hs[:], True, True, ['w2s', 'chs'], [('PB', 4)])
                    self.tt('vector', kx[:], pk, ccs[:, 0:512], ALU.mult, [('PB', 3), 'ccs'], ['ckx'])
                    self.tt('vector', kxs[:], pks, ccs[:, 512:1024], ALU.mult, [('PB', 4), 'ccs'], ['ckxs'])
                    self.tt('vector', self.kcmpT[:, g, :], kx[:], kxs[:], ALU.add, ['ckx', 'ckxs'], ['kcmpT'])
                else:
                    pv = PB[5][:, 0:256].rearrange("p (n d) -> p n d", n=4)
                    for nt in range(4):
                        self.mm(pv[:, nt, :], hs[:, nt * 128:(nt + 1) * 128], w2[1][:], True, True, ['chs', ('w2', 1)], [('PB', 5)])
                    self.cp('vector', self.vcaug[:, g, :, 0:64], pv, [('PB', 5)], ['vcaug'])
        if 'dbg_kcmpT' in self.dbg:
            d1 = self.dout('dbg_kcmpT', [64, 4, 512], BF16)
            d2 = self.dout('dbg_vcaug', [128, 4, 4, 193], BF16)
            self.DMA(d1, self.kcmpT[:], ['kcmpT'], ['dbgk'])
            self.DMA(d2, self.vcaug[:], ['vcaug'], ['dbgk'])

    def phase4(self):
        PB = self.PB
        NQB = T // 256 if self.nqb4 is None else self.nqb4
        cmask_d = self.din("cmask_c", [128, 32, 4, 128], BF16)
        dmask_d = self.din("dmask", [128, 2, 2, 128])
        wmask_d = self.din("wmask", [128, 2, 6, 128])
        btab = self.din("btab_c", [32, 128, 128])
        cmk = [self.sb("cmk%d" % i, [128, 4, 128], BF16) for i in range(2)]
        dmk = self.sb("dmk", [128, 2, 2, 128], BF16)
        wmk = self.sb("wmk", [128, 2, 6, 128], BF16)
        self.DMA(dmk[:], dmask_d, w=['dmk'], eng='gpsimd')
        self.DMA(wmk[:], wmask_d, w=['wmk'], eng='gpsimd')
        kselT = self.sb("kselT", [64, T], BF16)
        kwinT = self.sb("kwinT", [64, T], BF16)
        vsel = self.sb("vsel", [128, 64, 65], BF16)
        vwin = self.sb("vwin", [128, 64, 65], BF16)
        qTb = [self.sb("qTb%d" % i, [64, 512], BF16) for i in range(2)]
        Btb = [self.sb("Btb%d" % i, [128, 128], F32) for i in range(2)]
        gzb = [self.sb("gzb%d" % i, [128, 3, 256], BF16) for i in range(2)]
        Eb = [self.sb("Eb%d" % i, [128, 4, 128], BF16) for i in range(3)]
        Pb_ = [self.sb("Pb_%d" % i, [128, 4, 128], BF16) for i in range(2)]
        rden = self.sb("rden", [128, 3, 4], F32)
        imp = self.sb("imp", [128, 128], F32)
        score = self.sb("score", [128, 128], F32)
        sc2 = self.sb("sc2", [128, 128], F32)
        m8 = self.sb("m8", [128, 16], F32)
        selb = self.sb("selb", [128, 128], BF16)
        selx = self.sb("selx", [128, 128, 64], BF16)
        tmp = self.sb("etmp", [128, 4, 64], F32)
        tmp2 = self.sb("etmp2", [128, 4, 64], F32)
        acc = self.sb("eacc", [128, 4, 64], F32)
        ogt = [self.sb("ogt%d" % i, [128, 256], BF16) for i in range(2)]
        ecnt = [0]
        pcnt = [0]
        scnt = [0]

        def qk_exp(kT_tile, kkeys, qT, kq):
            i = scnt[0] % 2
            scnt[0] += 1
            pS = PB[i][:]
            self.mm(pS, kT_tile, qT[:], True, True, list(kkeys) + [kq], [('PB', i)])
            j = ecnt[0] % 3
            ecnt[0] += 1
            E = Eb[j]
            self.act(E[:].rearrange("p h q -> p (h q)"), pS, AF.Exp, [('PB', i)], [('E', j)])
            return E, ('E', j)

        def loads(g, qb):
            t0 = qb * 128
            b2 = qb % 2
            self.DMA(qTb[b2][:], self.qT_d[g, qb], w=[('qTb', b2)])
            self.DMA(Btb[b2][:], btab[qb], w=[('Btb', b2)])
            self.DMA(gzb[b2][:], self.gz_d[t0:t0 + 128, :, g * 256:(g + 1) * 256], w=[('gzb', b2)])
            self.DMA(cmk[b2][:], cmask_d[:, qb], w=[('cmk', b2)])

        def make_items(g, qb):
            items = []
            t0 = qb * 128
            b2 = qb % 2
            qT = qTb[b2]
            kq = ('qTb', b2)
            Bt = Btb[b2]
            gz = gzb[b2]
            e2 = qb % 2
            qbm = 2 * qb + 1
            ntmax = (8 * qbm + 6) // 128
            pc = [PB[3][:, 0:386].rearrange("p (h c) -> p h c", h=2), PB[4][:, 0:386].rearrange("p (h c) -> p h c", h=2)]

            def cmp_post():
                for hb in range(2):
                    self.V(lambda e, hb=hb: e.tensor_scalar(out=rden[:, 0, hb * 2:hb * 2 + 2], in0=pc[hb][:, :, 64], scalar1=1e-30, scalar2=None, op0=ALU.max),
                           [('PB', 3 + hb)], ['rden0'])
                self.V(lambda e: e.reciprocal(out=rden[:, 0, :], in_=rden[:, 0, :]), ['rden0'], ['rden0'])
                for h in range(4):
                    src = pc[h // 2][:, h % 2, 65:193]
                    if h == 0:
                        self.V(lambda e, src=src: e.tensor_scalar(out=imp[:], in0=src, scalar1=rden[:, 0, 0:1], scalar2=None, op0=ALU.mult), [('PB', 3), 'rden0'], ['imp'])
                    else:
                        self.V(lambda e, src=src, h=h: e.scalar_tensor_tensor(out=imp[:], in0=src, scalar=rden[:, 0, h:h + 1], in1=imp[:], op0=ALU.mult, op1=ALU.add),
                               [('PB', 3 + h // 2), 'rden0', 'imp'], ['imp'])
                self.tt('vector', score[:], imp[:], Bt[:], ALU.add, ['imp', ('Btb', b2)], ['score'])
                self.V(lambda e: e.max(out=m8[:, 0:8], in_=score[:]), ['score'], ['m8'])
                self.V(lambda e: e.match_replace(out=sc2[:], in_to_replace=m8[:, 0:8], in_values=score[:], imm_value=-1e9), ['score', 'm8'], ['sc2'])
                self.V(lambda e: e.max(out=m8[:, 8:16], in_=sc2[:]), ['sc2'], ['m8'])
                nbk = 2 * (qbm + 1)
                self.V(lambda e: e.tensor_scalar(out=selx[:, 0:nbk, :], in0=score[:, 0:nbk].unsqueeze(2).to_broadcast([128, nbk, 64]), scalar1=m8[:, 15:16], scalar2=None, op0=ALU.is_ge),
                       ['score', 'm8'], ['selx'])
                for hb in range(2):
                    self.tt('vector', tmp[:, hb * 2:hb * 2 + 2, :], pc[hb][:, :, 0:64], rden[:, 0, hb * 2:hb * 2 + 2].unsqueeze(2).to_broadcast([128, 2, 64]), ALU.mult,
                            [('PB', 3 + hb), 'rden0'], ['etmp'])
                self.tt('gpsimd', acc[:], tmp[:], gz[:, 0, :].rearrange("p (h d) -> p h d", h=4), ALU.mult, ['etmp', ('gzb', b2)], ['eacc'])
                if self.dbg4 is not None and g == 0 and qb == self.dbg4:
                    self.V(lambda e: e.tensor_scalar(out=selb[:], in0=score[:], scalar1=m8[:, 15:16], scalar2=None, op0=ALU.is_ge), ['score', 'm8'], ['selb'])
                    self.DMA(self.d4['imp'], imp[:], ['imp'], ['dbg4'])
                    self.DMA(self.d4['sel'], selb[:], ['selb'], ['dbg4'])
                    self.DMA(self.d4['ocmp'], tmp[:], ['etmp'], ['dbg4'])

            for nt in range(ntmax + 1):
                it = {'mdep': False, 'M': None}

                def A(it=it, nt=nt):
                    it['E'], it['kE'] = qk_exp(self.kcmpT[:, g, nt * 128:(nt + 1) * 128], ['kcmpT'], qT, kq)

                def B(it=it, nt=nt):
                    E, kE = it['E'], it['kE']
                    self.tt('vector', E[:], E[:], cmk[b2][:, nt, :].unsqueeze(1).to_broadcast([128, 4, 128]), ALU.mult, [kE, ('cmk', b2)], [kE])
                    for h in range(4):
                        self.mm(pc[h // 2][:, h % 2, :], E[:, h, :], self.vcaug[:, g, nt, :], nt == 0 and h % 2 == 0, nt == ntmax and h % 2 == 1, [kE, 'vcaug'], [('PB', 3 + h // 2)])
                    if nt == ntmax:
                        cmp_post()
                it['A'], it['B'] = A, B
                items.append(it)

            for br in (1, 2):
                pacc = PB[4 + br][:, 0:260].rearrange("p (h c) -> p h c", h=4)
                kacc = ('PB', 4 + br)
                if br == 1:
                    kts = list(range(0, qbm + 1))
                    kT_, kkey, V_, vkey = kselT, 'kselT', vsel, 'vsel'
                else:
                    kts = [kt for kt in range(qbm - 5, qbm + 1) if kt >= 0]
                    kT_, kkey, V_, vkey = kwinT, 'kwinT', vwin, 'vwin'

                def br_post(br=br, pacc=pacc, kacc=kacc):
                    kr = 'rden%d' % br
                    self.V(lambda e: e.reciprocal(out=rden[:, br, :], in_=pacc[:, :, 64]), [kacc], [kr])
                    self.tt('vector', tmp[:], pacc[:, :, 0:64], rden[:, br, :].unsqueeze(2).to_broadcast([128, 4, 64]), ALU.mult, [kacc, kr], ['etmp'])
                    if self.dbg4 is not None and g == 0 and qb == self.dbg4:
                        self.DMA(self.d4['osel' if br == 1 else 'owin'], tmp[:], ['etmp'], ['dbg4'])
                    self.tt('gpsimd', tmp2[:], tmp[:], gz[:, br, :].rearrange("p (h d) -> p h d", h=4), ALU.mult, ['etmp', ('gzb', b2)], ['etmp2'])
                    if br == 1:
                        self.tt('gpsimd', acc[:], acc[:], tmp2[:], ALU.add, ['eacc', 'etmp2'], ['eacc'])
                    else:
                        og = ogt[b2]
                        self.tt('gpsimd', og[:].rearrange("p (h d) -> p h d", h=4), acc[:], tmp2[:], ALU.add, ['eacc', 'etmp2'], [('ogt', b2)])
                        self.DMA(self.og_d[t0:t0 + 128, g * 256:(g + 1) * 256], og[:], r=[('ogt', b2)], w=[('og_d', g, qb)])

                for kt in kts:
                    it = {'mdep': (br == 1 and kt == kts[0]), 'M': None}

                    def A(it=it, kt=kt, kT_=kT_, kkey=kkey):
                        it['E'], it['kE'] = qk_exp(kT_[:, kt * 128:(kt + 1) * 128], [kkey], qT, kq)

                    def M(it=it, kt=kt):
                        pM = PB[2 if kt % 2 == 0 else 7][:].bitcast(BF16)[:, 0:128]
                        kM = ('PB', 2 if kt % 2 == 0 else 7)
                        self.tr(pM, selx[:, 2 * kt:2 * kt + 2, :].rearrange("p a k -> p (a k)"), self.identb[:], ['selx', 'identb'], [kM])
                        it['pM'], it['kM'] = pM, kM

                    def B(it=it, kt=kt, br=br, kts=kts, pacc=pacc, kacc=kacc, V_=V_, vkey=vkey, br_post=br_post):
                        E, kE = it['E'], it['kE']
                        if br == 1:
                            ip = pcnt[0] % 2
                            pcnt[0] += 1
                            P = Pb_[ip]
                            kP = ('P4', ip)
                            self.tt('vector', P[:], E[:], it['pM'].unsqueeze(1).to_broadcast([128, 4, 128]), ALU.mult, [kE, it['kM']], [kP])
                            if kt >= qbm - 1:
                                self.tt('gpsimd', P[:], P[:], dmk[:, e2, kt - (qbm - 1), :].unsqueeze(1).to_broadcast([128, 4, 128]), ALU.mult, [kP, 'dmk'], [kP])
                        else:
                            P, kP = E, kE
                            wi = kt - (qbm - 5)
                            if wi not in (2, 3):
                                self.tt('gpsimd', P[:], P[:], wmk[:, e2, wi, :].unsqueeze(1).to_broadcast([128, 4, 128]), ALU.mult, [kP, 'wmk'], [kP])
                        for h in range(4):
                            self.mm(pacc[:, h, :], P[:, h, :], V_[:, kt, :], kt == kts[0] and h == 0, kt == kts[-1] and h == 3, [kP, vkey], [kacc])
                        if kt == kts[-1]:
                            br_post()
                    it['A'], it['B'] = A, B
                    if br == 1:
                        it['M'] = M
                    items.append(it)
            return items

        for g in range(4):
            self.DMA(kselT[:], self.kselT_d[g], w=['kselT'])
            self.DMA(kwinT[:], self.kwinT_d[g], w=['kwinT'])
            self.DMA(vsel[:], self.vsel_d[g], w=['vsel'])
            self.DMA(vwin[:], self.vwin_d[g], w=['vwin'])
            loads(g, 0)
            items = []
            for qb in range(NQB):
                if qb + 1 < NQB:
                    items.append({'load': (g, qb + 1)})
                items += make_items(g, qb)
            work = [it for it in items if 'load' not in it]
            pos = 0
            load_at = {}
            for it in items:
                if 'load' in it:
                    load_at.setdefault(pos, []).append(it['load'])
                else:
                    pos += 1
            n = len(work)
            done_loads = set()

            def do_loads(upto):
                for p_ in sorted(load_at):
                    if p_ <= upto and p_ not in done_loads:
                        done_loads.add(p_)
                        for l in load_at[p_]:
                            loads(*l)

            do_loads(0)
            for j in range(min(2, n)):
                work[j]['A']()
            if n > 0 and work[0]['M'] is not None:
                work[0]['M']()
            for i in range(n):
                do_loads(i)
                if i + 2 < n:
                    work[i + 2]['A']()
                nxt = work[i + 1] if i + 1 < n else None
                if nxt is not None and nxt['M'] is not None and not nxt['mdep']:
                    nxt['M']()
                work[i]['B']()
                if nxt is not None and nxt['M'] is not None and nxt['mdep']:
                    nxt['M']()

    def phase5(self):
        PB = self.PB
        NQB = T // 256 if self.nqb4 is None else self.nqb4
        bw_d = self.din("bw", [128, 2, 2]) if 'bw' not in self.inputs else self.inputs['bw'].ap()
        bw = self.sb("p5bw", [128, 2, 2], F32)
        self.DMA(bw[:], bw_d, w=['p5bw'])
        hB = [self.sb("p5hB%d" % i, [128, 1024], F32) for i in range(2)]
        b_w_out = self.din("b_w_out", [1024, 1024])
        b_ln_w = self.din("b_ln_w", [1, 1024])
        b_ln_b = self.din("b_ln_b", [1, 1024])
        out = self.dout("out", [T // 2, D], F32)
        w_out = self.sb("p5wout", [128, 8, 1024], BF16)
        self.DMA(w_out[:], b_w_out.rearrange("(c p) n -> p c n", p=128), w=['p5wout'], eng='gpsimd')
        lnw_b = self.sb("p5lnw", [128, 1024], F32)
        lnb_b = self.sb("p5lnb", [128, 1024], F32)
        self.DMA(lnw_b[:], b_ln_w.partition_broadcast(128), w=['p5lnw'])
        self.DMA(lnb_b[:], b_ln_b.partition_broadcast(128), w=['p5lnb'])
        ogs = [self.sb("p5og%d" % i, [128, 1024], BF16) for i in range(2)]
        h1s = [self.sb("p5h1%d" % i, [128, 1024], F32) for i in range(2)]
        ogT = self.sb("p5ogT", [128, 8, 128], BF16)
        rr = self.sb("p5rr", [128, 1024], F32)
        xo = [self.sb("p5xo%d" % i, [128, 1024], F32) for i in range(2)]
        bst = self.sb("p5bst", [128, 2, 6], F32)
        mv = self.sb("p5mv", [128, 2], F32)
        lrs = self.sb("p5lrs", [128, 1], F32)
        rstd = self.sb("p5rstd", [128, 1], F32)
        nb = self.sb("p5nb", [128, 1], F32)
        for qb in range(NQB):
            t0 = qb * 128
            b2 = qb % 2
            og = ogs[b2]
            h1 = h1s[b2]
            xn = xo[b2]
            self.DMA(og[:], self.og_d[t0:t0 + 128, :], w=[('p5og', b2)])
            e2 = qb % 2
            hb_ = hB[b2]
            self.DMA(h1[:], self.h1_d[(2 * qb) * 128:(2 * qb + 1) * 128, :], w=[('p5h1', b2)])
            self.DMA(hb_[:], self.h1_d[(2 * qb + 1) * 128:(2 * qb + 2) * 128, :], w=[('p5hB', b2)])
            self.V(lambda e, h1=h1, e2=e2: e.tensor_scalar(out=h1[:], in0=h1[:], scalar1=bw[:, e2, 0:1], scalar2=None, op0=ALU.mult), [('p5h1', b2), 'p5bw'], [('p5h1', b2)])
            self.V(lambda e, h1=h1, hb_=hb_, e2=e2: e.scalar_tensor_tensor(out=h1[:], in0=hb_[:], scalar=bw[:, e2, 1:2], in1=h1[:], op0=ALU.mult, op1=ALU.add),
                   [('p5hB', b2), 'p5bw', ('p5h1', b2)], [('p5h1', b2)])
            pTb = PB[2][:].bitcast(BF16).rearrange("p (c n) -> p c n", c=8)
            for kc in range(8):
                self.tr(pTb[:, kc, :], og[:, kc * 128:(kc + 1) * 128], self.identb[:], [('p5og', b2), 'identb'], [('PB', 2)])
            self.cp('scalar', ogT[:], pTb, [('PB', 2)], ['p5ogT'])
            pY = [PB[0][:], PB[1][:]]
            for half in range(2):
                for kc in range(8):
                    self.mm(pY[half], ogT[:, kc, :], w_out[:, kc, half * 512:(half + 1) * 512], kc == 0, kc == 7, ['p5ogT', 'p5wout'], [('PB', half)])
            for half in range(2):
                self.V(lambda e, half=half, h1=h1: e.scalar_tensor_tensor(out=rr[:, half * 512:(half + 1) * 512], in0=h1[:, half * 512:(half + 1) * 512], scalar=ALPHA, in1=pY[half], op0=ALU.mult, op1=ALU.add),
                       [('p5h1', b2), ('PB', half)], ['p5rr'])
                self.V(lambda e, half=half: e.bn_stats(out=bst[:, half, :], in_=rr[:, half * 512:(half + 1) * 512]), ['p5rr'], ['p5bst'])
            self.V(lambda e: e.bn_aggr(out=mv[:], in_=bst[:].rearrange("p a b -> p (a b)")), ['p5bst'], ['p5mv'])
            self.act(lrs[:], mv[:, 1:2], AF.Ln, ['p5mv'], ['p5lrs'], bias=EPS)
            self.act(rstd[:], lrs[:], AF.Exp, ['p5lrs'], ['p5rstd'], scale=-0.5)
            self.V(lambda e: e.tensor_scalar(out=nb[:], in0=mv[:, 0:1], scalar1=rstd[:, 0:1], scalar2=-1.0, op0=ALU.mult, op1=ALU.mult), ['p5mv', 'p5rstd'], ['p5nb'])
            self.act(xn[:], rr[:], AF.Identity, ['p5rr', 'p5rstd', 'p5nb'], [('p5xo', b2)], scale=rstd[:, 0:1], bias=nb[:, 0:1])
            self.tt('gpsimd', xn[:], xn[:], lnw_b[:], ALU.mult, [('p5xo', b2), 'p5lnw'], [('p5xo', b2)])
            self.tt('gpsimd', xn[:], xn[:], lnb_b[:], ALU.add, [('p5xo', b2), 'p5lnb'], [('p5xo', b2)])
            self.DMA(out[t0:t0 + 128, :], xn[:], r=[('p5xo', b2)], w=[('out', qb)])
            self.final_keys.append(('out', qb))


def _in_maps(b, inputs):
    hc = host_consts()
    maps = []
    for core in range(8):
        bi = core // 2
        m = dict(hc)
        m.update(core_consts(core % 2, hc))
        m['x'] = inputs['x'][bi]
        m['a_w_in'] = inputs['a_w_in'][0]
        m['a_conv_w'] = inputs['a_conv_w'][0]
        m['a_a_log'] = inputs['a_a_log'].reshape(1, 8)
        m['a_dt_bias'] = inputs['a_dt_bias'].reshape(1, 8)
        m['a_norm_w'] = inputs['a_norm_w'].reshape(128, 1)
        m['a_w_out'] = inputs['a_w_out'][0]
        m['a_ln_w'] = inputs['a_ln_w'].reshape(1, 1024)
        m['a_ln_b'] = inputs['a_ln_b'].reshape(1, 1024)
        for k in ('s_w_kv', 's_pe_k', 's_pe_v', 's_w1_k', 's_w2_k', 's_w1_v', 's_w2_v'):
            m[k] = inputs[k]
        m['b_w_in'] = inputs['b_w_in'][0]
        m['b_w_out'] = inputs['b_w_out'][0]
        m['b_ln_w'] = inputs['b_ln_w'].reshape(1, 1024)
        m['b_ln_b'] = inputs['b_ln_b'].reshape(1, 1024)
        maps.append({k: np.ascontiguousarray(v if k == 'cmask_c' else np.asarray(v, dtype=np.float32)) for k, v in m.items() if k in b.inputs})
    return maps


def kernel(**inputs):
    inputs = {k: np.asarray(v) for k, v in inputs.items()}
    import os
    ph = tuple(os.environ.get('KPHASES', 'p1,p2,p3,p4,p5').split(','))
    b = Builder(phases=ph)
    nc = b.build()
    maps = _in_maps(b, inputs)
    res = run_bass_kernel_spmd(nc, maps, core_ids=list(range(8)))
    if 'out' not in b.outputs:
        return np.zeros((4, T, D), np.float32)
    out = np.zeros((4, T, D), np.float32)
    for core in range(8):
        bi, p = core // 2, core % 2
        o = np.asarray(res.results[core]['out'], dtype=np.float32)
        for j in range(32):
            qb = slot_qb(p, j)
            out[bi, qb * 128:(qb + 1) * 128] = o[j * 128:(j + 1) * 128]
    return out
```

```python
import math
from contextlib import ExitStack

import numpy as np
import concourse.bass as bass
import concourse.mybir as mybir
from concourse.bass_utils import run_bass_kernel_spmd

F32 = mybir.dt.float32
BF16 = mybir.dt.bfloat16
AF = mybir.ActivationFunctionType
ALU = mybir.AluOpType
AX = mybir.AxisListType

ENGS = ('sync', 'gpsimd', 'scalar', 'vector', 'tensor')

T = 8192
D = 1024
NH = 8
EPS = 1e-6
ALPHA = 4.0 ** 0.25


class Sched:
    def __init__(self, nc, csems, dsems):
        self.nc = nc
        self.csem = csems
        self.dsems = dsems
        self.ops = {e: [] for e in ENGS}
        self.cnt = {e: 0 for e in ENGS}
        self.dcount = [0] * len(dsems)
        nd = len(dsems)
        self.dpool = {'sync': list(range(0, nd - 4)), 'gpsimd': list(range(nd - 4, nd))}
        self.dptr = {'sync': 0, 'gpsimd': 0}
        self.lastw = {}
        self.readers = {}
        self.waited = {e: {} for e in ENGS}
        self.nops = 0

    def _sem(self, sk):
        return self.csem[sk[1]] if sk[0] == 'c' else self.dsems[sk[1]]

    kmap = {}

    def _expand(self, keys):
        out = []
        for k in keys:
            if isinstance(k, str):
                k = self.kmap.get(k, k)
            if isinstance(k, tuple) and len(k) == 2 and k[0] == 'PB':
                out.append(('PB', k[1], 0))
                out.append(('PB', k[1], 1))
            else:
                out.append(k)
        return out

    def op(self, eng, fn, reads=(), writes=(), dma=False):
        need = {}
        reads = self._expand(reads)
        writes = self._expand(writes)

        def want(tok):
            sk, val, src = tok
            if sk[0] == 'c' and src == eng and eng == 'tensor':
                return
            if need.get(sk, 0) < val:
                need[sk] = val

        for k in reads:
            t = self.lastw.get(k)
            if t is not None:
                want(t)
        for k in writes:
            t = self.lastw.get(k)
            if t is not None:
                want(t)
            for t in self.readers.get(k, ()):
                want(t)
        if dma:
            pool = self.dpool[eng]
            i = pool[self.dptr[eng] % len(pool)]
            self.dptr[eng] += 1
            if self.dcount[i] > 0:
                want((('d', i), 16 * self.dcount[i], None))
            self.dcount[i] += 1
            tok = (('d', i), 16 * self.dcount[i], eng)
            inc = 16
        else:
            self.cnt[eng] += 1
            tok = (('c', eng), self.cnt[eng], eng)
            inc = 1
        w = self.waited[eng]
        waits = []
        for sk, val in need.items():
            if w.get(sk, 0) < val:
                w[sk] = val
                waits.append((self._sem(sk), val))
        self.ops[eng].append((waits, fn, self._sem(tok[0]), inc))
        for k in writes:
            self.lastw[k] = tok
            self.readers[k] = []
        for k in reads:
            lst = self.readers.setdefault(k, [])
            if len(lst) < 64:
                lst.append(tok)
            else:
                d = {}
                for t in lst + [tok]:
                    if d.get(t[0], (0,))[0] < t[1]:
                        d[t[0]] = (t[1], t[2])
                self.readers[k] = [(sk, v[0], v[1]) for sk, v in d.items()]
        self.nops += 1
        return tok

    def wait_all(self, eng, keys):
        need = {}
        for k in keys:
            t = self.lastw.get(k)
            if t is not None:
                sk, val, src = t
                if need.get(sk, 0) < val:
                    need[sk] = val
        waits = [(self._sem(sk), val) for sk, val in need.items()]
        self.ops[eng].append((waits, None, None, 0))

    def drain_dmas(self, eng):
        waits = [(self.dsems[i], 16 * self.dcount[i]) for i in range(len(self.dsems)) if self.dcount[i] > 0]
        self.ops[eng].append((waits, None, None, 0))
        for i in range(len(self.dsems)):
            self.waited[eng][('d', i)] = 16 * self.dcount[i]

    def emit(self):
        nc = self.nc
        with nc.Block() as block:
            for e in ENGS:
                ops = self.ops[e]
                if not ops:
                    continue

                def body(engine, ops=ops):
                    for waits, fn, sem, inc in ops:
                        for s, v in waits:
                            engine.wait_ge(s, v)
                        if fn is not None:
                            ins = fn(engine)
                            ins.then_inc(sem, inc)

                getattr(block, e)(body)
        self.ops = {e: [] for e in ENGS}


def host_consts():
    c = {}
    half = 32
    inv = (np.float32(10000.0) ** (-(np.arange(half, dtype=np.float32) / np.float32(half)))).astype(np.float32)
    pos = np.arange(T, dtype=np.float32)
    ang = (pos[:, None] * inv[None, :]).astype(np.float32)
    c['rope_cs'] = np.concatenate([np.cos(ang), np.sin(ang)], axis=1).astype(np.float32)
    pc = (np.arange(512, dtype=np.float32) * 16 + 31).astype(np.float32)
    angc = (pc[None, :] * inv[:, None]).astype(np.float32)
    cosF = np.concatenate([np.cos(angc), np.cos(angc)], axis=0)
    sinF = np.concatenate([-np.sin(angc), np.sin(angc)], axis=0)
    c['cmp_cs'] = np.concatenate([cosF, sinF], axis=1).astype(np.float32)
    st = np.arange(512)[:, None] * 16
    bs = np.arange(128)[None, :] * 64
    ov = np.clip(np.minimum(st + 32, bs + 64) - np.maximum(st, bs), 0, None) / 32.0
    ov[511] = 0.0
    c['ovm'] = ov.astype(np.float32)
    n = np.arange(128)[:, None, None]
    dl = np.arange(17)[None, :, None]
    i = np.arange(128)[None, None, :]
    c['cmpmask'] = (16 * n + 31 - i <= 128 * dl).astype(np.float32)
    kk = np.arange(128)[:, None]
    qq = np.arange(128)[None, :]
    c['cwmask'] = np.stack([(kk <= qq), (kk > qq)], axis=1).astype(np.float32)
    bt = np.zeros((64, 128, 128), np.float32)
    for qb in range(64):
        t = qb * 128 + np.arange(128)
        cur = t // 64
        blk = np.arange(128)[None, :]
        forced = (blk == 0) | (blk == cur[:, None]) | (blk == cur[:, None] - 1)
        vis = blk * 64 <= t[:, None]
        bt[qb] = np.where(vis, np.where(forced, 1.0e4, 0.0), -1.0)
    c['btab'] = bt
    c['ident'] = np.eye(128, dtype=np.float32)
    k = np.arange(64)
    tri = (k[:, None] <= k[None, :]).astype(np.float32)
    sup = (k[:, None] > k[None, :]).astype(np.float32)
    negT = np.where(k[None, :] < k[:, None], -1e30, 0.0).astype(np.float32)
    offd = (k[:, None] != k[None, :]).astype(np.float32)
    eye = np.eye(64, dtype=np.float32)
    c64 = np.concatenate([
        tri, -tri, sup, np.ones((64, 128), np.float32), -np.ones((64, 64), np.float32),
        np.tile(negT[:, None, :], (1, 8, 1)).reshape(64, 512), offd,
    ], axis=1)
    c['c64'] = np.ascontiguousarray(c64)
    return c


def slot_qb(p, j):
    first = (p == (j % 2))
    return 2 * j if first else 2 * j + 1


def core_consts(p, hc):
    import ml_dtypes
    c = {}
    bw = np.zeros((128, 2, 2), np.float32)
    for e in range(2):
        first = (p == e)
        bw[:, e, 0] = 1.0 if first else 0.0
        bw[:, e, 1] = 0.0 if first else 1.0
    c['bw'] = bw
    qbs = [slot_qb(p, j) for j in range(32)]
    c['rope_q'] = np.concatenate([hc['rope_cs'][qb * 128:(qb + 1) * 128] for qb in qbs], axis=0)
    c['btab_c'] = np.stack([hc['btab'][qb] for qb in qbs], axis=0)
    n = np.arange(128)[:, None, None, None]
    nt = np.arange(4)[None, None, :, None]
    i = np.arange(128)[None, None, None, :]
    qbv = np.array(qbs)[None, :, None, None]
    c['cmask_c'] = (16 * (128 * nt + n) + 31 <= 128 * qbv + i).astype(ml_dtypes.bfloat16)
    kk = np.arange(128)[:, None]
    qq = np.arange(128)[None, :]
    caus = (kk <= qq).astype(np.float32)
    win = (kk > qq).astype(np.float32)
    one = np.ones((128, 128), np.float32)
    zero = np.zeros((128, 128), np.float32)
    dm = np.zeros((128, 2, 2, 128), np.float32)
    wm = np.zeros((128, 2, 6, 128), np.float32)
    for e in range(2):
        first = (p == e)
        dl = [caus, zero] if first else [one, caus]
        wl = [win, one, one, one, caus, zero] if first else [zero, win, one, one, one, caus]
        for a, m_ in enumerate(dl):
            dm[:, e, a, :] = m_
        for a, m_ in enumerate(wl):
            wm[:, e, a, :] = m_
    c['dmask'] = dm
    c['wmask'] = wm
    return c


C64_OFF = {}
_o = 0
for _n, _w in [('tri', 64), ('ntri', 64), ('sup', 64), ('ones', 128), ('nones', 64),
               ('negT8', 512), ('offd', 64)]:
    C64_OFF[_n] = (_o, _o + _w)
    _o += _w
C64_W = _o


class Builder:
    def __init__(self, phases=('p1',), dbg=None, ntiles1=None, nqb4=None, dbg4=None):
        self.nqb4 = nqb4
        self.dbg4 = dbg4
        self.phases = phases
        self.dbg = dbg or {}
        self.ntiles1 = ntiles1
        self.nc = bass.Bass("TRN2", target_bir_lowering=False)
        self.es = ExitStack()
        self.inputs = {}
        self.outputs = {}

    def din(self, name, shape, dt=F32):
        t = self.nc.dram_tensor(name, list(shape), dt, kind="ExternalInput")
        self.inputs[name] = t
        return t.ap()

    def dscratch(self, name, shape, dt):
        if name in self.dbg:
            t = self.nc.dram_tensor(name, list(shape), dt, kind="ExternalOutput")
            self.outputs[name] = t
        else:
            t = self.nc.dram_tensor(name, list(shape), dt)
        return t.ap()

    def dout(self, name, shape, dt):
        t = self.nc.dram_tensor(name, list(shape), dt, kind="ExternalOutput")
        self.outputs[name] = t
        return t.ap()

    def sb(self, name, shape, dt):
        return self.pes.enter_context(self.nc.sbuf_tensor(name, list(shape), dt))

    def ps(self, name, shape, dt):
        return self.es.enter_context(self.nc.psum_tensor(name, list(shape), dt))

    def V(self, fn, r=(), w=()):
        self.S.op('vector', fn, r, w)

    def A(self, fn, r=(), w=()):
        self.S.op('scalar', fn, r, w)

    def G(self, fn, r=(), w=()):
        self.S.op('gpsimd', fn, r, w)

    def PE(self, fn, r=(), w=()):
        self.S.op('tensor', fn, r, w)

    def DMA(self, out, in_, r=(), w=(), eng='sync', **kw):
        self.S.op(eng, lambda e: e.dma_start(out=out, in_=in_, **kw), r, w, dma=True)

    def mm(self, out, lhsT, rhs, start, stop, r, w):
        self.S.op('tensor', lambda e: e.matmul(out, lhsT=lhsT, rhs=rhs, start=start, stop=stop), r, w)

    def tr(self, out, in_, ident, r, w):
        self.S.op('tensor', lambda e: e.transpose(out=out, in_=in_, identity=ident), r, w)

    def act(self, out, in_, func, r, w, **kw):
        self.S.op('scalar', lambda e: e.activation(out=out, in_=in_, func=func, **kw), r, w)

    def tt(self, eng, out, in0, in1, op, r, w):
        self.S.op(eng, lambda e: e.tensor_tensor(out=out, in0=in0, in1=in1, op=op), r, w)

    def cp(self, eng, out, in_, r, w):
        if eng == 'scalar':
            self.S.op(eng, lambda e: e.copy(out=out, in_=in_), r, w)
        else:
            self.S.op(eng, lambda e: e.tensor_copy(out=out, in_=in_), r, w)

    def build(self):
        nc = self.nc
        es = self.es
        with es:
            csems = {e: es.enter_context(nc.semaphore("c_" + e)) for e in ENGS}
            dsems = [es.enter_context(nc.semaphore("d%d" % i)) for i in range(12)]
            self.S = Sched(nc, csems, dsems)
            self.pes = es
            self.PB = [es.enter_context(nc.psum_tensor("pb%d" % i, [128, 512], F32)) for i in range(8)]
            self.setup_common()
            if 'p1' in self.phases:
                with ExitStack() as pes:
                    self.pes = pes
                    self.phase1()
                    self.S.drain_dmas('sync')
                    self.S.emit()
            for ph in ('p2', 'p3', 'p4', 'p5'):
                if ph in self.phases:
                    with ExitStack() as pes:
                        self.pes = pes
                        getattr(self, 'phase' + ph[1])()
                        self.S.drain_dmas('sync')
                        self.S.emit()
            self.pes = es
            self.finish()
            self.S.emit()
        return nc

    def finish(self):
        if not self.outputs:
            o = self.dout("out", [T // 2, D], F32)
            self.DMA(o[0:128, 0:128], self.ident[:], r=['ident'], w=['dummy_out'])
            self.final_keys.append('dummy_out')
        self.S.wait_all('sync', list(self.final_keys))

    def setup_common(self):
        self.final_keys = []
        x = self.din("x", [T, D])
        self.x = x
        self.c_ident = self.din("ident", [128, 128])
        self.c_c64 = self.din("c64", [64, C64_W])
        self.ident = self.sb("ident_sb", [128, 128], F32)
        self.identb = self.sb("identb_sb", [128, 128], BF16)
        self.c64 = self.sb("c64_sb", [64, C64_W], F32)
        self.DMA(self.ident[:], self.c_ident, w=['ident'])
        self.DMA(self.c64[:], self.c_c64, w=['c64'])
        self.cp('vector', self.identb[:], self.ident[:], ['ident'], ['identb'])
        self.ones128 = self.sb("ones128", [128, 128], F32)
        self.V(lambda e: e.memset(self.ones128[:], 1.0), w=['ones128'])
        if 'p1' in self.phases:
            self.h1_d = self.dscratch("h1_d", [T, D], F32)
            self.h1T_d = self.dscratch("h1T_d", [128, 8, T], BF16)
        else:
            self.h1_d = self.din("h1_d", [T, D], F32)
            self.h1T_d = self.din("h1T_d", [128, 8, T], BF16)
        self.kselT_d = self.dscratch("kselT_d", [4, 64, T], BF16)
        self.kwinT_d = self.dscratch("kwinT_d", [4, 64, T], BF16)
        self.kcsT_d = self.dscratch("kcsT_d", [4, 64, T], BF16)
        self.vcsT_d = self.dscratch("vcsT_d", [4, 64, T], BF16)
        self.vsel_d = self.dscratch("vsel_d", [4, 128, T // 128, 65], BF16)
        self.vwin_d = self.dscratch("vwin_d", [4, 128, T // 128, 65], BF16)
        self.qT_d = self.dscratch("qT_d", [4, 32, 64, 512], BF16)
        self.gz_d = self.dscratch("gz_d", [T // 2, 3, 1024], BF16)
        self.og_d = self.dscratch("og_d", [T // 2, 1024], BF16)
        if self.dbg4 is not None:
            self.d4 = {'imp': self.dout('dbg_imp', [128, 128], F32), 'sel': self.dout('dbg_sel', [128, 128], BF16),
                       'ocmp': self.dout('dbg_ocmp', [128, 4, 64], F32), 'osel': self.dout('dbg_osel', [128, 4, 64], F32),
                       'owin': self.dout('dbg_owin', [128, 4, 64], F32)}
            self.final_keys.append('dbg4')
        self.kcmpT = self.sb("kcmpT", [64, 4, 512], BF16)
        self.vcaug = self.sb("vcaug", [128, 4, 4, 193], BF16)

    def c64v(self, name, heads=False):
        a, b = C64_OFF[name]
        v = self.c64[:, a:b]
        if heads:
            v = v.rearrange("p (h i) -> p h i", h=8)
        return v

    def phase1(self):
        nc = self.nc
        S = self.S
        PB = self.PB
        TT = 256
        NCH = 4
        NT = T // TT if self.ntiles1 is None else self.ntiles1
        a_w_in = self.din("a_w_in", [1024, 4112])
        a_conv_w = self.din("a_conv_w", [4, 3072])
        a_a_log = self.din("a_a_log", [1, 8])
        a_dt_bias = self.din("a_dt_bias", [1, 8])
        a_norm_w = self.din("a_norm_w", [128, 1])
        a_w_out = self.din("a_w_out", [1024, 1024])
        a_ln_w = self.din("a_ln_w", [1, 1024])
        a_ln_b = self.din("a_ln_b", [1, 1024])

        w_in = self.sb("w_in_sb", [128, 8, 4112], BF16)
        qkT = self.sb("qkT", [128, 2, 8, TT], F32)
        qT = qkT[:, 0]
        kT = qkT[:, 1]
        w_out = qkT[:].rearrange("p a h t -> p (a h t)").bitcast(BF16).rearrange("p (c n) -> p c n", c=8)
        w_out_d = self.dscratch("w_out_bf_d", [128, 8, 1024], BF16)
        for kc in range(8):
            self.DMA(w_in[:, kc, :], a_w_in[kc * 128:(kc + 1) * 128, :], w=['w_in'], eng='gpsimd')
        self.DMA(w_out, a_w_out.rearrange("(c p) n -> p c n", p=128), w=['qT', 'kT'], eng='gpsimd')
        self.DMA(w_out_d, w_out, r=['qT', 'kT'], w=['w_out_d'])
        xs = [self.sb("xs0", [128, 2, 1024], F32)]
        vcur2 = [xs[0][0:64, i, :].rearrange("p (h d) -> p h d", h=8) for i in range(2)]
        cw4 = xs[0][0:4].rearrange("p s d -> p (s d)")
        convw = self.sb("convw", [128, 24, 4], F32)
        pcw = PB[0][:, 0:96].rearrange("p (b i) -> p b i", i=4)
        for part in range(2):
            nb_ = 16 if part == 0 else 8
            self.DMA(cw4[:, 0:nb_ * 128], a_conv_w[:, part * 2048:part * 2048 + nb_ * 128], w=['xs'])
            for b in range(nb_):
                self.tr(pcw[:, part * 16 + b, :], cw4[:, b * 128:(b + 1) * 128], self.ident[0:4, 0:4], ['xs', 'ident'], [('PB', 0)])
        self.cp('vector', convw[:], pcw, [('PB', 0)], ['convw'])
        normw = self.sb("normw", [128, 1], F32)
        self.DMA(normw[:], a_norm_w, w=['normw'])
        lnw_b = self.sb("lnw_b", [128, 1024], F32)
        lnb_b = self.sb("lnb_b", [128, 1024], F32)
        self.DMA(lnw_b[:], a_ln_w.partition_broadcast(128), w=['lnw_b'])
        self.DMA(lnb_b[:], a_ln_b.partition_broadcast(128), w=['lnb_b'])
        dtb = self.sb("dtb", [64, 8], F32)
        alog = self.sb("alog", [64, 8], F32)
        negA = self.sb("negA", [64, 8], F32)
        self.DMA(dtb[:], a_dt_bias.partition_broadcast(64), w=['dtb'])
        self.DMA(alog[:], a_a_log.partition_broadcast(64), w=['alog'])
        self.act(negA[:], alog[:], AF.Exp, ['alog'], ['negA'])
        self.V(lambda e: e.tensor_scalar(out=negA[:], in0=negA[:], scalar1=-1.0, scalar2=None, op0=ALU.mult), ['negA'], ['negA'])

        eye8b = self.sb("eye8b", [64, 8, 64], BF16)
        self.cp('vector', eye8b[:], self.ident[0:64, 0:64].unsqueeze(1).to_broadcast([64, 8, 64]), ['ident'], ['eye8b'])
        halo = self.sb("halo", [128, 24, 3], F32)
        self.V(lambda e: e.memset(halo[:], 0.0), w=['halo'])
        Sst = self.sb("Sst", [128, 8, 128], F32)
        self.V(lambda e: e.memset(Sst[:], 0.0), w=[('Sst', 0), ('Sst', 1)])

        xT = self.sb("xT", [128, 8, TT], BF16)
        pre = [self.sb("pre%d" % i, [128, TT + 3], F32) for i in range(2)]
        yb = [self.sb("yb%d" % i, [128, TT], F32) for i in range(2)]
        sb4 = [self.sb("s%d" % i, [128, TT], F32) for i in range(4)]
        sq = [self.sb("sq%d" % i, [128, TT], F32) for i in range(2)]
        lnb = [self.sb("lnt%d" % i, [128, TT], F32) for i in range(2)]
        ktokp = self.sb("ktok", [128, 2, 8, 128], F32)
        vtokp = self.sb("vtok", [128, 2, 8, 128], F32)

        def ktok_c(c):
            return ktokp[(c // 2) * 64:(c // 2) * 64 + 64, c % 2]

        def vtok_c(c):
            return vtokp[(c // 2) * 64:(c // 2) * 64 + 64, c % 2]
        szT = self.sb("szT", [128, 8, TT], BF16)
        ogT = self.sb("ogT", [128, 8, TT], BF16)
        beta = self.sb("beta", [64, NCH, 8], F32)
        nbeta = self.sb("nbeta", [64, NCH, 8], F32)
        gt = self.sb("gt", [64, NCH, 8], F32)
        xg = self.sb("xg", [64, NCH, 8], F32)

        decT = self.sb("decT", [64, 8, 64], F32)
        eg2 = [self.sb("eg%d" % i, [128, 24], F32) for i in range(2)]
        tmpA = self.sb("tmpA", [64, 8, 64], F32)
        G1 = tmpA
        Mb = [self.sb("Mb%d" % i, [64, 8, 64], BF16) for i in range(2)]
        MTb = [self.sb("MTb%d" % i, [64, 8, 64], BF16) for i in range(2)]
        Pb = [self.sb("Pb%d" % i, [64, 8, 64], BF16) for i in range(2)]
        XTf2 = [self.sb("XTf%d" % i, [64, 8, 64], F32) for i in range(2)]
        attT2 = [self.sb("attT%d" % i, [64, 8, 64], F32) for i in range(2)]
        kdec2 = [self.sb("kdec%d" % i, [64, 8, 128], F32) for i in range(2)]
        Rp = self.sb("Rp", [64, 4, 128], F32)
        qs = self.sb("qs", [64, 4, 128], F32)
        vnew = self.sb("vnew", [64, 4, 128], F32)
        oc = self.sb("oc", [64, 8, 128], F32)
        oss = self.sb("oss", [64, 8], F32)
        orr = self.sb("orr", [64, 8], F32)
        onb = self.sb("onb", [64, 8, 128], BF16)
        t2 = self.sb("t2", [128, 4, 128], F32)
        rr = self.sb("rr", [128, 1024], F32)
        h1s = rr
        xn = rr
        h1T = self.sb("h1T", [128, 8, 128], BF16)
        bst = self.sb("bst", [128, 2, 6], F32)
        mv = self.sb("mv", [128, 2], F32)
        lrs = self.sb("lrs", [128, 1], F32)
        rstd = self.sb("rstd", [128, 1], F32)
        nb = self.sb("nb", [128, 1], F32)

        tri = self.c64v('tri')
        ntri = self.c64v('ntri')
        sup = self.c64v('sup')
        ones64 = self.c64v('ones')
        negT8 = self.c64v('negT8')
        offd8 = self.c64v('offd').unsqueeze(1).to_broadcast([64, 8, 64])
        eye8 = self.ident[0:64, 0:64].unsqueeze(1).to_broadcast([64, 8, 64])
        tri8 = tri.unsqueeze(1).to_broadcast([64, 8, 64])
        id64 = self.ident[0:64, 0:64]
        idb64 = self.identb[0:64, 0:64]
        lnqs = math.log(128.0 ** -0.5)

        dbg = self.dbg
        if 'dbg_qT' in dbg:
            d_qT = self.dout('dbg_qT', [128, 8, TT], F32)
            d_kT = self.dout('dbg_kT', [128, 8, TT], F32)
            d_vtok = self.dout('dbg_vtok', [128, 2, 8, 128], F32)
            d_ktok = self.dout('dbg_ktok', [128, 2, 8, 128], F32)
            d_beta = self.dout('dbg_beta', [64, NCH, 8], F32)
            d_g = self.dout('dbg_g', [64, NCH, 8], F32)
            d_oc = self.dout('dbg_oc', [NT * NCH, 64, 8, 128], F32)
            d_decT = self.dout('dbg_decT', [64, 8, 64], F32)
            d_XT = self.dout('dbg_XT', [64, 8, 64], F32)
            d_M = self.dout('dbg_M', [64, 8, 64], F32)
            d_ogT = self.dout('dbg_ogT', [128, 8, TT], BF16)
            d_oss = self.dout('dbg_oss', [64, 8], F32)
            d_orr = self.dout('dbg_orr', [64, 8], F32)
            d_mv = self.dout('dbg_mv', [128, 2], F32)
            d_rstd = self.dout('dbg_rstd', [128, 1], F32)
            d_bst = self.dout('dbg_bst', [128, 12], F32)
            d_rr = self.dout('dbg_rr', [2, 128, 1024], F32)
            d_wout = self.dout('dbg_wout', [128, 8, 1024], BF16)

        def bank(i, shape=None, dt=None):
            v = PB[i][:]
            if dt is not None:
                v = v.bitcast(dt)
            return v

        for t in range(NT):
            t0 = t * TT
            xt = xs[0]
            kx = 'xs'
            self.DMA(xt[:], self.x[t0:t0 + TT, :].rearrange("(s p) d -> p s d", p=128), w=[kx])
            pTf = PB[2][:].rearrange("p (c n) -> p c n", c=4)
            for s in range(2):
                for g4 in range(2):
                    for k4 in range(4):
                        kc = g4 * 4 + k4
                        self.tr(pTf[:, k4, :], xt[:, s, kc * 128:(kc + 1) * 128], self.ident[:], [kx, 'ident'], [('PB', 2)])
                    self.cp('scalar', xT[:, g4 * 4:g4 * 4 + 4, s * 128:(s + 1) * 128], pTf, [('PB', 2)], ['xT'])
            ppb = [0, 1, 6, 7]

            def s1_mm(blk):
                pbk = ppb[blk % 4]
                pp = PB[pbk][:, 0:TT]
                for kc in range(8):
                    self.mm(pp, w_in[:, kc, blk * 128:(blk + 1) * 128], xT[:, kc, :], kc == 0, kc == 7, ['w_in', 'xT'], [('PB', pbk)])

            def bufs(blk):
                pb = blk % 2
                sp = blk % 4
                return pb, sp, pre[pb], ('pre', pb), yb[pb], ('y', pb), sb4[sp], ('s', sp)

            def a1(blk):
                pbk = ppb[blk % 4]
                pp = PB[pbk][:, 0:TT]
                kp = ('PB', pbk)
                if blk >= 24:
                    self.act(szT[:, blk - 24, :], pp, AF.Silu, [kp], ['szT'])
                    return
                pb, sp, pr, kpr, y, ky, s_, ks = bufs(blk)
                self.cp('scalar', pr[:, 3:TT + 3], pp, [kp], [kpr])
                self.cp('gpsimd', pr[:, 0:3], halo[:, blk, :], ['halo'], [kpr])

            def a2(blk):
                if blk >= 24:
                    return
                pb, sp, pr, kpr, y, ky, s_, ks = bufs(blk)
                self.V(lambda e, pr=pr, y=y, blk=blk: e.tensor_scalar(out=y[:], in0=pr[:, 0:TT], scalar1=convw[:, blk, 0:1], scalar2=None, op0=ALU.mult),
                       [kpr, 'convw'], [ky])
                for i in range(1, 4):
                    self.V(lambda e, pr=pr, y=y, blk=blk, i=i: e.scalar_tensor_tensor(out=y[:], in0=pr[:, i:i + TT], scalar=convw[:, blk, i:i + 1], in1=y[:], op0=ALU.mult, op1=ALU.add),
                           [kpr, 'convw', ky], [ky])
                self.cp('gpsimd', halo[:, blk, :], pr[:, TT:TT + 3], [kpr], ['halo'])

            def a3(blk):
                if blk >= 24:
                    return
                pb, sp, pr, kpr, y, ky, s_, ks = bufs(blk)
                self.act(s_[:], y[:], AF.Silu, [ky], [ks])

            def b45(blk):
                if blk >= 16:
                    return
                pb, sp, pr, kpr, y, ky, s_, ks = bufs(blk)
                q_ = sq[pb]
                ksq = ('sq', pb)
                self.tt('gpsimd', q_[:], s_[:], s_[:], ALU.mult, [ks], [ksq])
                psb = 3 if pb == 0 else 5
                self.mm(PB[psb][:, 0:TT], self.ones128[:], q_[:], True, True, ['ones128', ksq], [('PB', psb)])

            def b6(blk):
                if blk >= 16:
                    return
                pb = blk % 2
                psb = 3 if pb == 0 else 5
                l_ = lnb[pb]
                kl = ('lnt', pb)
                self.act(l_[:], PB[psb][:, 0:TT], AF.Ln, [('PB', psb)], [kl], bias=EPS)
                self.act(l_[:], l_[:], AF.Exp, [kl], [kl], scale=-0.5, bias=(lnqs if blk < 8 else 0.0))

            def b78(blk):
                if blk >= 24:
                    return
                pb, sp, pr, kpr, y, ky, s_, ks = bufs(blk)
                h = blk % 8
                pkb = 4 if pb == 0 else 2
                pk = PB[pkb][0:64, :].rearrange("p (c d) -> p c d", c=NCH)
                if blk < 16:
                    l_ = lnb[pb]
                    kl = ('lnt', pb)
                    dest = qT if blk < 8 else kT
                    kd = 'qT' if blk < 8 else 'kT'
                    self.tt('vector', dest[:, h, :], s_[:], l_[:], ALU.mult, [ks, kl], [kd])
                    if blk >= 8:
                        for c in range(NCH):
                            self.tr(pk[:, c, :], kT[:, h, c * 64:(c + 1) * 64], self.ident[:], ['kT', 'ident'], [('PB', pkb)])
                        for c2 in range(2):
                            self.cp('scalar', ktokp[c2 * 64:c2 * 64 + 64, :, h, :], pk[:, c2 * 2:c2 * 2 + 2, :], [('PB', pkb)], ['ktok'])
                else:
                    for c in range(NCH):
                        self.tr(pk[:, c, :], s_[:, c * 64:(c + 1) * 64], self.ident[:], [ks, 'ident'], [('PB', pkb)])
                    for c2 in range(2):
                        self.cp('scalar', vtokp[c2 * 64:c2 * 64 + 64, :, h, :], pk[:, c2 * 2:c2 * 2 + 2, :], [('PB', pkb)], ['vtok'])

            groups = [(2 * i, 2 * i + 1) for i in range(16)]
            NG = len(groups)

            def both(fn, g):
                fn(g[0])
                fn(g[1])

            both(s1_mm, groups[0])
            both(s1_mm, groups[1])
            both(a1, groups[0])
            both(a2, groups[0])
            both(a3, groups[0])
            for gi in range(NG):
                g = groups[gi]
                gn = groups[gi + 1] if gi + 1 < NG else None
                if gi + 2 < NG:
                    both(s1_mm, groups[gi + 2])
                if gn:
                    both(a1, gn)
                both(b45, g)
                if gn:
                    both(a2, gn)
                both(b6, g)
                if gn:
                    both(a3, gn)
                both(b78, g)
            pL = PB[5][0:64, 0:NCH * 16].rearrange("p (c n) -> p c n", c=NCH)
            for c in range(NCH):
                for kc in range(8):
                    self.mm(pL[:, c, :], xT[:, kc, c * 64:(c + 1) * 64], w_in[:, kc, 4096:4112], kc == 0, kc == 7, ['xT', 'w_in'], [('PB', 5)])
            self.act(beta[:], pL[:, :, 0:8], AF.Sigmoid, [('PB', 5)], ['beta'])
            self.tt('vector', xg[:], pL[:, :, 8:16], dtb[:].unsqueeze(1).to_broadcast([64, NCH, 8]), ALU.add, [('PB', 5), 'dtb'], ['xg'])
            self.act(xg[:], xg[:], AF.Exp, ['xg'], ['xg'])
            self.act(xg[:], xg[:], AF.Ln, ['xg'], ['xg'], bias=1.0)
            self.tt('vector', gt[:], xg[:], negA[:].unsqueeze(1).to_broadcast([64, NCH, 8]), ALU.mult, ['xg', 'negA'], ['gt'])
            self.V(lambda e: e.tensor_scalar(out=nbeta[:], in0=beta[:], scalar1=-1.0, scalar2=None, op0=ALU.mult), ['beta'], ['nbeta'])
            if 'dbg_qT' in dbg and t == 0:
                self.DMA(d_qT, qT[:], ['qT'], ['dbg'])
                self.DMA(d_kT, kT[:], ['kT'], ['dbg'])
                self.DMA(d_vtok, vtokp[:], ['vtok'], ['dbg'])
                self.DMA(d_ktok, ktokp[:], ['ktok'], ['dbg'])
                self.DMA(d_beta, beta[:], ['beta'], ['dbg'])
                self.DMA(d_g, gt[:], ['gt'], ['dbg'])
                self.final_keys.append('dbg')

            def prep(c):
                cp_ = c % 2
                cs = slice(c * 64, (c + 1) * 64)
                attT = attT2[cp_]
                G2 = attT
                eg = eg2[cp_]
                kdec = kdec2[cp_]
                XTf = XTf2[cp_]
                kat = ('attT', cp_)
                keg = ('eg', cp_)
                gcv = gt[:, c, :]
                gb = gcv.unsqueeze(2).to_broadcast([64, 8, 64])
                self.cp('vector', G1[:], gb, ['gt'], ['tmpA'])
                self.tt('vector', G2[:], tri8, gb, ALU.mult, ['gt', 'c64'], [kat])
                pD = PB[7][0:64, :]
                kD = ('PB', 7)
                self.mm(pD, ones64[:, 0:64], G2[:].rearrange("p h i -> p (h i)"), True, False, ['c64', kat], [kD])
                self.mm(pD, ntri, G1[:].rearrange("p h i -> p (h i)"), False, False, ['c64', 'tmpA'], [kD])
                self.mm(pD, id64, negT8, False, True, ['c64', 'ident'], [kD])
                self.act(decT[:].rearrange("p h i -> p (h i)"), pD, AF.Exp, [kD], ['decT'])
                pG = PB[3]
                kG = ('PB', 3)
                self.mm(pG[0:64, 0:8], tri, gcv, True, True, ['c64', 'gt'], [kG])
                self.mm(pG[0:64, 8:16], sup, gcv, True, True, ['c64', 'gt'], [kG])
                self.mm(pG[:, 16:24], ones64, gcv, True, True, ['c64', 'gt'], [kG])
                self.act(eg[0:64, 0:16], pG[0:64, 0:16], AF.Exp, [kG], [keg])
                self.act(eg[:, 16:24], pG[:, 16:24], AF.Exp, [kG], [keg])
                edec = eg[0:64, 8:16]
                yield
                pA = PB[4][0:64, :].rearrange("p (h i) -> p h i", h=8)
                pQK = PB[5][0:64, :].rearrange("p (h i) -> p h i", h=8)
                for h in range(8):
                    self.mm(pA[:, h, :], kT[:, h, cs], kT[:, h, cs], True, True, ['kT'], [('PB', 4)])
                for h in range(8):
                    self.mm(pQK[:, h, :], kT[:, h, cs], qT[:, h, cs], True, True, ['kT', 'qT'], [('PB', 5)])
                M = Mb[0]
                MT = MTb[0]
                P = Pb[0]
                self.tt('vector', tmpA[:], pA, decT[:], ALU.mult, [('PB', 4), 'decT'], ['tmpA'])
                self.tt('vector', tmpA[:], tmpA[:], beta[:, c, :].unsqueeze(2).to_broadcast([64, 8, 64]), ALU.mult, ['tmpA', 'beta'], ['tmpA'])
                self.tt('gpsimd', M[:], tmpA[:], offd8, ALU.mult, ['tmpA', 'c64'], [('M', 0, 0), ('M', 0, 1)])
                yield
                self.tt('vector', attT[:], pQK, decT[:], ALU.mult, [('PB', 5), 'decT'], [kat])
                pMT = PB[6][0:64, :].bitcast(BF16)[:, 0:512].rearrange("p (h i) -> p h i", h=8)
                for h in range(8):
                    self.tr(pMT[:, h, :], M[:, h, :], idb64, [('M', 0, 0), ('M', 0, 1), 'identb'], [('PB', 6)])
                self.cp('scalar', MT[:], pMT, [('PB', 6)], [('MT', 0, 0), ('MT', 0, 1)])
                self.tt('vector', P[:], eye8b[:], M[:], ALU.subtract, ['eye8b', ('M', 0, 0), ('M', 0, 1)], [('P', 0, 0), ('P', 0, 1)])
                if c < 2:
                    self.tt('gpsimd', kdec[:], ktok_c(c), edec.unsqueeze(2).to_broadcast([64, 8, 128]), ALU.mult, ['ktok', keg], [('kdec', cp_)])
                else:
                    self.cp('scalar', kdec[:], ktok_c(c), ['ktok'], [('kdec', cp_)])
                    self.tt('gpsimd', kdec[:], kdec[:], edec.unsqueeze(2).to_broadcast([64, 8, 128]), ALU.mult, [('kdec', cp_), keg], [('kdec', cp_)])
                    self.cp('scalar', vcur2[cp_], vtok_c(c), ['vtok'], ['xs'])
                yield
                cur = 0
                ibank = [(3, 4), (5, 6)]
                for lvl in range(1, 6):
                    nxt = 1 - cur
                    M, MT, P = Mb[cur], MTb[cur], Pb[cur]
                    M2, M2T, P2 = Mb[nxt], MTb[nxt], Pb[nxt]
                    for grp in range(2):
                        gs = slice(grp * 4, grp * 4 + 4)
                        b1, b2_ = ibank[grp]
                        pM2T = PB[b1][0:64, 0:256].rearrange("p (h i) -> p h i", h=4)
                        pM2 = PB[b2_][0:64, 0:256].rearrange("p (h i) -> p h i", h=4)
                        kin = [('M', cur, grp), ('MT', cur, grp)]
                        for hh in range(4):
                            h = grp * 4 + hh
                            self.mm(pM2T[:, hh, :], M[:, h, :], MT[:, h, :], True, True, kin, [('PB', b1)])
                        self.cp('scalar', M2T[:, gs, :], pM2T, [('PB', b1)], [('MT', nxt, grp)])
                        if lvl < 5:
                            for hh in range(4):
                                h = grp * 4 + hh
                                self.mm(pM2[:, hh, :], MT[:, h, :], M[:, h, :], True, True, kin, [('PB', b2_)])
                            self.cp('vector', M2[:, gs, :], pM2, [('PB', b2_)], [('M', nxt, grp)])
                    yield
                    for grp in range(2):
                        gs = slice(grp * 4, grp * 4 + 4)
                        b1, b2_ = ibank[grp]
                        pPP = PB[b1][0:64, 0:256].rearrange("p (h i) -> p h i", h=4)
                        for hh in range(4):
                            h = grp * 4 + hh
                            self.mm(pPP[:, hh, :], M2T[:, h, :], P[:, h, :], True, True, [('MT', nxt, grp), ('P', cur, grp)], [('PB', b1)])
                        if lvl == 5:
                            self.tt('vector', XTf[:, gs, :], P[:, gs, :], pPP, ALU.add, [('P', cur, grp), ('PB', b1)], [('XT', cp_, grp)])
                        else:
                            self.tt('vector', P2[:, gs, :], P[:, gs, :], pPP, ALU.add, [('P', cur, grp), ('PB', b1)], [('P', nxt, grp)])
                    cur = nxt
                    yield
                if 'dbg_qT' in dbg and t == 0 and c == 0:
                    self.DMA(d_decT, decT[:], ['decT'], ['dbg'])
                    self.DMA(d_XT, XTf[:], [('XT', cp_, 0), ('XT', cp_, 1)], ['dbg'])

            def scan_out(c):
                cp_ = c % 2
                cs = slice(c * 64, (c + 1) * 64)
                attT = attT2[cp_]
                eg = eg2[cp_]
                kdec = kdec2[cp_]
                osq = kdec
                XT = XTf2[cp_]
                vcur = vcur2[cp_] if c >= 2 else vtok_c(c)
                kvc = 'xs' if c >= 2 else 'vtok'
                kat = ('attT', cp_)
                keg = ('eg', cp_)
                egc = eg[0:64, 0:8]
                glb = eg[:, 16:24]
                pKS = PB[0][0:64, :].rearrange("p (h d) -> p h d", h=4)
                pQS = PB[1][0:64, :].rearrange("p (h d) -> p h d", h=4)
                pS = PB[2][:, :].rearrange("p (h d) -> p h d", h=4)
                kX, kY, kZ = ('PB', 0), ('PB', 1), ('PB', 2)
                for half in range(2):
                    hs = slice(half * 4, half * 4 + 4)
                    egb = egc[:, hs].unsqueeze(2).to_broadcast([64, 4, 128])
                    for hh in range(4):
                        h = half * 4 + hh
                        self.mm(pKS[:, hh, :], kT[:, h, cs], Sst[:, h, :], True, True, ['kT', ('Sst', half)], [kX])
                    for hh in range(4):
                        h = half * 4 + hh
                        self.mm(pQS[:, hh, :], qT[:, h, cs], Sst[:, h, :], True, True, ['qT', ('Sst', half)], [kY])
                    self.tt('vector', Rp[:], pKS, egb, ALU.mult, [kX, keg], ['Rp'])
                    self.tt('vector', Rp[:], Rp[:], vcur[:, hs, :], ALU.subtract, ['Rp', kvc], ['Rp'])
                    self.tt('vector', qs[:], pQS, egb, ALU.mult, [kY, keg], ['qs'])
                    yield
                    pVN = pKS
                    for hh in range(4):
                        h = half * 4 + hh
                        self.mm(pVN[:, hh, :], XT[:, h, :], Rp[:, hh, :], True, True, [('XT', cp_, half), 'Rp'], [kX])
                    self.tt('vector', vnew[:], pVN, nbeta[:, c, hs].unsqueeze(2).to_broadcast([64, 4, 128]), ALU.mult, [kX, 'nbeta'], ['vnew'])
                    yield
                    pAV = pQS
                    for hh in range(4):
                        h = half * 4 + hh
                        self.mm(pAV[:, hh, :], attT[:, h, :], vnew[:, hh, :], True, True, [kat, 'vnew'], [kY])
                    self.tt('vector', oc[:, hs, :], pAV, qs[:], ALU.add, [kY, 'qs'], [('oc', half)])
                    for hh in range(4):
                        h = half * 4 + hh
                        self.mm(pS[:, hh, :], kdec[:, h, :], vnew[:, hh, :], True, True, [('kdec', cp_), 'vnew'], [kZ])
                    self.tt('gpsimd', t2[:], Sst[:, hs, :], glb[:, hs].unsqueeze(2).to_broadcast([128, 4, 128]), ALU.mult, [('Sst', half), keg], ['t2'])
                    self.tt('vector', Sst[:, hs, :], t2[:], pS, ALU.add, ['t2', kZ], [('Sst', half)])
                    yield
                if 'dbg_qT' in dbg:
                    self.DMA(d_oc[t * NCH + c], oc[:], [('oc', 0), ('oc', 1)], ['dbg'])
                self.tt('gpsimd', osq[:], oc[:], oc[:], ALU.mult, [('oc', 0), ('oc', 1)], [('kdec', cp_)])
                self.V(lambda e: e.tensor_reduce(out=oss[:], in_=osq[:], axis=AX.X, op=ALU.add), [('kdec', cp_)], ['oss'])
                self.act(orr[:], oss[:], AF.Ln, ['oss'], ['orr'], scale=1.0 / 128.0, bias=EPS)
                self.act(orr[:], orr[:], AF.Exp, ['orr'], ['orr'], scale=-0.5)
                yield
                self.tt('vector', onb[:], oc[:], orr[:].unsqueeze(2).to_broadcast([64, 8, 128]), ALU.mult, [('oc', 0), ('oc', 1), 'orr'], ['onb'])
                pOT = PB[2][:].bitcast(BF16)[:, 0:512].rearrange("p (h i) -> p h i", h=8)
                for h in range(8):
                    self.tr(pOT[:, h, :], onb[:, h, :], idb64, ['onb', 'identb'], [('PB', 2)])
                self.V(lambda e, cs=cs, pOT=pOT: e.scalar_tensor_tensor(out=ogT[:, :, cs], in0=pOT, scalar=normw[:, 0:1], in1=szT[:, :, cs], op0=ALU.mult, op1=ALU.mult),
                       [('PB', 2), 'normw', 'szT'], ['ogT'])
                yield

            for c in range(NCH + 1):
                gens = []
                if c < NCH:
                    gens.append(prep(c))
                if c >= 1:
                    gens.append(scan_out(c - 1))
                while gens:
                    for g_ in list(gens):
                        try:
                            next(g_)
                        except StopIteration:
                            gens.remove(g_)
            self.DMA(w_out, w_out_d, r=['w_out_d'], w=['qT', 'kT'])
            if 'dbg_qT' in dbg and t == 0:
                self.DMA(d_ogT, ogT[:], ['ogT'], ['dbg'])
                self.DMA(d_wout, w_out, ['qT', 'kT'], ['dbg'])
            for s in range(2):
                pY = [PB[0][:], PB[1][:]]
                for half in range(2):
                    for h in range(8):
                        self.mm(pY[half], ogT[:, h, s * 128:(s + 1) * 128], w_out[:, h, half * 512:(half + 1) * 512], h == 0, h == 7, ['ogT', 'qT', 'kT'], [('PB', half)])
                self.DMA(rr[:], self.x[t0 + s * 128:t0 + (s + 1) * 128, :], w=['rr'])
                for half in range(2):
                    self.V(lambda e, half=half: e.scalar_tensor_tensor(out=rr[:, half * 512:(half + 1) * 512], in0=rr[:, half * 512:(half + 1) * 512], scalar=ALPHA, in1=pY[half], op0=ALU.mult, op1=ALU.add),
                           ['rr', ('PB', half)], ['rr'])
                    self.V(lambda e, half=half: e.bn_stats(out=bst[:, half, :], in_=rr[:, half * 512:(half + 1) * 512]), ['rr'], ['bst'])
                if 'dbg_qT' in dbg and t == 0:
                    self.DMA(d_rr[s], rr[:], ['rr'], ['dbg'])
                self.V(lambda e: e.bn_aggr(out=mv[:], in_=bst[:].rearrange("p a b -> p (a b)")), ['bst'], ['mv'])
                self.act(lrs[:], mv[:, 1:2], AF.Ln, ['mv'], ['lrs'], bias=EPS)
                self.act(rstd[:], lrs[:], AF.Exp, ['lrs'], ['rstd'], scale=-0.5)
                self.V(lambda e: e.tensor_scalar(out=nb[:], in0=mv[:, 0:1], scalar1=rstd[:, 0:1], scalar2=-1.0, op0=ALU.mult, op1=ALU.mult), ['mv', 'rstd'], ['nb'])
                if 'dbg_qT' in dbg and t == 0 and s == 0:
                    self.DMA(d_mv, mv[:], ['mv'], ['dbg'])
                    self.DMA(d_rstd, rstd[:], ['rstd'], ['dbg'])
                    self.DMA(d_bst, bst[:].rearrange("p a b -> p (a b)"), ['bst'], ['dbg'])
                self.act(xn[:], rr[:], AF.Identity, ['rr', 'rstd', 'nb'], ['rr'], scale=rstd[:, 0:1], bias=nb[:, 0:1])
                self.tt('gpsimd', xn[:], xn[:], lnw_b[:], ALU.mult, ['rr', 'lnw_b'], ['rr'])
                self.tt('gpsimd', h1s[:], xn[:], lnb_b[:], ALU.add, ['rr', 'lnb_b'], ['rr'])
                r0 = t0 + s * 128
                self.DMA(self.h1_d[r0:r0 + 128, :], h1s[:], ['rr'], [('h1_d', t, s)])
                self.final_keys.append(('h1_d', t, s))
                pTf = PB[2][:].rearrange("p (c n) -> p c n", c=4)
                for g4 in range(2):
                    for k4 in range(4):
                        kc = g4 * 4 + k4
                        self.tr(pTf[:, k4, :], h1s[:, kc * 128:(kc + 1) * 128], self.ident[:], ['rr', 'ident'], [('PB', 2)])
                    self.cp('scalar', h1T[:, g4 * 4:g4 * 4 + 4, :], pTf, [('PB', 2)], ['h1T'])
                self.DMA(self.h1T_d[:, :, r0:r0 + 128], h1T[:], ['h1T'], [('h1T_d', t, s)])
                self.final_keys.append(('h1T_d', t, s))

    def rope_tm(self, out4, x4, cs, nh, t1, t2, kx, kout):
        cosb = cs[:, 0:32].unsqueeze(1).unsqueeze(1).to_broadcast([128, nh, 2, 32])
        sinb = cs[:, 32:64].unsqueeze(1).to_broadcast([128, nh, 32])
        self.tt('vector', t1, x4, cosb, ALU.mult, [kx, 'cs'], ['rt1'])
        self.tt('gpsimd', t2[:, :, 0, :], x4[:, :, 1, :], sinb, ALU.mult, [kx, 'cs'], ['rt2'])
        self.tt('gpsimd', t2[:, :, 1, :], x4[:, :, 0, :], sinb, ALU.mult, [kx, 'cs'], ['rt2'])
        self.tt('vector', out4[:, :, 0, :], t1[:, :, 0, :], t2[:, :, 0, :], ALU.subtract, ['rt1', 'rt2'], [kout])
        self.tt('vector', out4[:, :, 1, :], t1[:, :, 1, :], t2[:, :, 1, :], ALU.add, ['rt1', 'rt2'], [kout])

    def phase2(self):
        PB = self.PB
        s_w_kv = self.din("s_w_kv", [1024, 1536])
        b_w_in = self.din("b_w_in", [1024, 4144])
        rope_cs = self.din("rope_cs", [T, 64])
        rope_q = self.din("rope_q", [T // 2, 64])
        bw_d = self.din("bw", [128, 2, 2])
        bw = self.sb("p2bw", [128, 2, 2], F32)
        self.DMA(bw[:], bw_d, w=['bw'])
        hA = self.sb("p2hA", [128, 8, 128], BF16)
        hB = self.sb("p2hB", [128, 8, 128], BF16)
        wkv = self.sb("wkv", [128, 8, 1536], BF16)
        wq = self.sb("wq", [128, 8, 1024], BF16)
        wz = self.sb("wz", [128, 8, 3072], BF16)
        wg = self.sb("wg", [128, 8, 48], BF16)
        for kc in range(8):
            rs = slice(kc * 128, (kc + 1) * 128)
            self.DMA(wkv[:, kc, :], s_w_kv[rs, :], w=['wkv'], eng='gpsimd')
            self.DMA(wq[:, kc, :], b_w_in[rs, 0:1024], w=['wq'], eng='gpsimd')
            self.DMA(wz[:, kc, :], b_w_in[rs, 1024:4096], w=['wz'], eng='gpsimd')
            self.DMA(wg[:, kc, :], b_w_in[rs, 4096:4144], w=['wg'], eng='gpsimd')
        h1T = [self.sb("p2h1T%d" % i, [128, 8, 128], BF16) for i in range(2)]
        cs = [self.sb("p2cs%d" % i, [128, 64], F32) for i in range(2)]
        kvs2 = [self.sb("kvs%d" % i, [128, 1536], F32) for i in range(2)]
        qs2_ = [self.sb("qs_%d" % i, [128, 1024], F32) for i in range(2)]
        t12 = [self.sb("rt1%d" % i, [128, 1024], F32) for i in range(2)]
        t22 = [self.sb("rt2%d" % i, [128, 1024], F32) for i in range(2)]
        krb2 = [self.sb("krb%d" % i, [128, 4, 256], BF16) for i in range(2)]
        qrb2 = [self.sb("qrb%d" % i, [128, 1024], BF16) for i in range(2)]
        vst2 = [self.sb("vst%d" % i, [128, 2, 4, 65], BF16) for i in range(2)]
        kT42 = [self.sb("kT4%d" % i, [64, 16, 128], BF16) for i in range(2)]
        qTt2 = [self.sb("qTt%d" % i, [64, 16, 128], BF16) for i in range(2)]
        zs3 = [self.sb("zs%d" % i, [128, 1024], F32) for i in range(3)]
        gts2 = [self.sb("gts%d" % i, [128, 48], F32) for i in range(2)]
        gz2 = [self.sb("gz%d" % i, [128, 3, 1024], BF16) for i in range(2)]
        for i in range(2):
            self.V(lambda e, i=i: e.memset(vst2[i][:], 1.0), w=[('vst', i)])
        P2KEYS = ['kvs', 'qs_', 'rt1', 'rt2', 'krb', 'qrb', 'vst', 'kT4', 'qTt', 'zs', 'gts', 'gz']

        pk = [PB[3][0:64, :].bitcast(BF16).rearrange("p (a t) -> p a t", a=8), PB[4][0:64, :].bitcast(BF16).rearrange("p (a t) -> p a t", a=8)]

        def kvA(qb):
            par = qb % 2
            self.S.kmap = {k: (k, par) for k in P2KEYS}
            kvs = kvs2[par]
            t0 = qb * 128
            hT = h1T[par]
            kh = ('p2h1T', par)
            self.DMA(hT[:], self.h1T_d[:, :, t0:t0 + 128], r=['h1T_all'], w=[kh])
            self.DMA(cs[par][:], rope_cs[t0:t0 + 128, :], w=[('cs', par)])
            for j in range(3):
                for kc in range(8):
                    self.mm(PB[j][:], hT[:, kc, :], wkv[:, kc, j * 512:(j + 1) * 512], kc == 0, kc == 7, [kh, 'wkv'], [('PB', j)])
                self.cp('scalar', kvs[:, j * 512:(j + 1) * 512], PB[j][:], [('PB', j)], ['kvs'])

        def kvB(qb):
            par = qb % 2
            self.S.kmap = {k: (k, par) for k in P2KEYS}
            self.S.kmap['cs'] = ('cs', par)
            kvs, t1, t2, krb, vst, kT4 = kvs2[par], t12[par], t22[par], krb2[par], vst2[par], kT42[par]
            t0 = qb * 128
            c_ = cs[par]
            for i, c0 in enumerate((512, 1024)):
                x4 = kvs[:, c0:c0 + 256].rearrange("p (g a d) -> p g a d", g=4, a=2)
                o4 = krb[:, i, :].rearrange("p (g a d) -> p g a d", g=4, a=2)
                self.rope_tm(o4, x4, c_, 4, t1[:, 0:256].rearrange("p (g a d) -> p g a d", g=4, a=2),
                             t2[:, 0:256].rearrange("p (g a d) -> p g a d", g=4, a=2), 'kvs', 'krb')
            self.cp('gpsimd', krb[:, 2, :], kvs[:, 0:256], ['kvs'], ['krb'])
            self.cp('gpsimd', krb[:, 3, :], kvs[:, 256:512], ['kvs'], ['krb'])
            self.cp('vector', vst[:, 0, :, 0:64], kvs[:, 768:1024].rearrange("p (g d) -> p g d", g=4), ['kvs'], ['vst'])
            self.cp('vector', vst[:, 1, :, 0:64], kvs[:, 1280:1536].rearrange("p (g d) -> p g d", g=4), ['kvs'], ['vst'])
            for i in range(4):
                for g in range(4):
                    a = i * 4 + g
                    self.tr(pk[a // 8][:, a % 8, :], krb[:, i, g * 64:(g + 1) * 64], self.identb[:], ['krb', 'identb'], [('PB', 3 + a // 8)])
            self.cp('scalar', kT4[:, 0:8, :], pk[0], [('PB', 3)], ['kT4'])
            self.cp('scalar', kT4[:, 8:16, :], pk[1], [('PB', 4)], ['kT4'])
            for i, dst in enumerate((self.kselT_d, self.kwinT_d, self.kcsT_d, self.vcsT_d)):
                self.DMA(dst[:, :, t0:t0 + 128].rearrange("g d t -> d g t"), kT4[:, i * 4:(i + 1) * 4, :], r=['kT4'], w=[('kvT_d', qb, i)])
            self.DMA(self.vsel_d[:, :, qb, :].rearrange("g p c -> p g c"), vst[:, 0], r=['vst'], w=[('vsel_d', qb)])
            self.DMA(self.vwin_d[:, :, qb, :].rearrange("g p c -> p g c"), vst[:, 1], r=['vst'], w=[('vwin_d', qb)])

        NKV = T // 128
        kvA(0)
        for qb in range(NKV):
            if qb + 1 < NKV:
                kvA(qb + 1)
            kvB(qb)

        def qA(slot):
            par = slot % 2
            self.S.kmap = {k: (k, par) for k in P2KEYS}
            qs_, gts, gz = qs2_[par], gts2[par], gz2[par]
            e2 = slot % 2
            t0 = slot * 128
            hT = h1T[par]
            kh = ('p2h1T', par)
            self.DMA(hA[:], self.h1T_d[:, :, (2 * slot) * 128:(2 * slot + 1) * 128], w=['p2hA'])
            self.DMA(hB[:], self.h1T_d[:, :, (2 * slot + 1) * 128:(2 * slot + 2) * 128], w=['p2hB'])
            self.DMA(cs[par][:], rope_q[t0:t0 + 128, :], w=[('cs', par)])
            self.V(lambda e, hT=hT, e2=e2: e.tensor_scalar(out=hT[:], in0=hA[:], scalar1=bw[:, e2, 0:1], scalar2=None, op0=ALU.mult), ['p2hA', 'bw'], [kh])
            self.V(lambda e, hT=hT, e2=e2: e.scalar_tensor_tensor(out=hT[:], in0=hB[:], scalar=bw[:, e2, 1:2], in1=hT[:], op0=ALU.mult, op1=ALU.add), ['p2hB', 'bw', kh], [kh])
            for j in range(2):
                for kc in range(8):
                    self.mm(PB[5 + j][:], hT[:, kc, :], wq[:, kc, j * 512:(j + 1) * 512], kc == 0, kc == 7, [kh, 'wq'], [('PB', 5 + j)])
                self.S.op('scalar', lambda e, j=j, qs_=qs_: e.mul(out=qs_[:, j * 512:(j + 1) * 512], in_=PB[5 + j][:], mul=0.125), [('PB', 5 + j)], ['qs_'])
            pg = PB[7][:, 0:48]
            for kc in range(8):
                self.mm(pg, hT[:, kc, :], wg[:, kc, :], kc == 0, kc == 7, [kh, 'wg'], [('PB', 7)])
            self.act(gts[:], pg, AF.Sigmoid, [('PB', 7)], ['gts'])
            for br in range(3):
                zs = zs3[br]
                for j in range(2):
                    pz = PB[j][:]
                    for kc in range(8):
                        self.mm(pz, hT[:, kc, :], wz[:, kc, br * 1024 + j * 512:br * 1024 + (j + 1) * 512], kc == 0, kc == 7, [kh, 'wz'], [('PB', j)])
                    self.act(zs[:, j * 512:(j + 1) * 512], pz, AF.Silu, [('PB', j)], [('zs3', br)])
                self.tt('vector' if br != 1 else 'gpsimd', gz[:, br, :].rearrange("p (h d) -> p h d", h=16), zs[:].rearrange("p (h d) -> p h d", h=16),
                        gts[:, br * 16:(br + 1) * 16].unsqueeze(2).to_broadcast([128, 16, 64]), ALU.mult, [('zs3', br), 'gts'], ['gz'])
            self.DMA(self.gz_d[t0:t0 + 128], gz[:], r=['gz'], w=[('gz_d', slot)])

        def qB(slot):
            par = slot % 2
            self.S.kmap = {k: (k, par) for k in P2KEYS}
            self.S.kmap['cs'] = ('cs', par)
            qs_, t1, t2, qrb, qTt = qs2_[par], t12[par], t22[par], qrb2[par], qTt2[par]
            c_ = cs[par]
            v16 = "p (g a d) -> p g a d"
            self.rope_tm(qrb[:].rearrange(v16, g=16, a=2), qs_[:].rearrange(v16, g=16, a=2), c_, 16,
                         t1[:].rearrange(v16, g=16, a=2), t2[:].rearrange(v16, g=16, a=2), 'qs_', 'qrb')
            for hh in range(16):
                self.tr(pk[hh // 8][:, hh % 8, :], qrb[:, hh * 64:(hh + 1) * 64], self.identb[:], ['qrb', 'identb'], [('PB', 3 + hh // 8)])
            self.cp('scalar', qTt[:, 0:8, :], pk[0], [('PB', 3)], ['qTt'])
            self.cp('scalar', qTt[:, 8:16, :], pk[1], [('PB', 4)], ['qTt'])
            self.DMA(self.qT_d[:, slot].rearrange("g d (h t) -> d g h t", h=4), qTt[:].rearrange("d (g h) t -> d g h t", g=4), r=['qTt'], w=[('qT_d', slot)])

        NSL = T // 256
        qA(0)
        for slot in range(NSL):
            if slot + 1 < NSL:
                qA(slot + 1)
            qB(slot)

        self.S.kmap = {}

    def phase3(self):
        PB = self.PB
        s_pe = [self.din("s_pe_k", [32, 64]), self.din("s_pe_v", [32, 64])]
        s_w1 = [self.din("s_w1_k", [32, 64, 128]), self.din("s_w1_v", [32, 64, 128])]
        s_w2 = [self.din("s_w2_k", [128, 64]), self.din("s_w2_v", [128, 64])]
        cmp_cs = self.din("cmp_cs", [64, 1024])
        ovm = self.din("ovm", [512, 128])
        w1 = [self.sb("w1_%d" % i, [64, 32, 128], BF16) for i in range(2)]
        w2 = [self.sb("w2_%d" % i, [128, 64], BF16) for i in range(2)]
        w2s = self.sb("w2s", [128, 64], BF16)
        pe32 = self.sb("pe32", [32, 2, 64], F32)
        peT = self.sb("peT", [64, 2, 32], BF16)
        bias = self.sb("cbias", [128, 2], F32)
        ccs = self.sb("ccs", [64, 1024], F32)
        src = self.sb("csrc", [64, T], BF16)
        hs = self.sb("chs", [128, 512], BF16)
        kx = self.sb("ckx", [64, 512], F32)
        kxs = self.sb("ckxs", [64, 512], F32)
        self.DMA(ccs[:], cmp_cs, w=['ccs'])
        for i in range(2):
            self.DMA(w1[i][:], s_w1[i].rearrange("c d h -> d c h"), w=[('w1', i)], eng='gpsimd')
            self.DMA(w2[i][:], s_w2[i], w=[('w2', i)], eng='gpsimd')
            self.DMA(pe32[:, i, :], s_pe[i], w=['pe32'])
        self.cp('vector', w2s[:, 0:32], w2[0][:, 32:64], [('w2', 0)], ['w2s'])
        self.cp('vector', w2s[:, 32:64], w2[0][:, 0:32], [('w2', 0)], ['w2s'])
        for i in range(2):
            pT = PB[0][0:64, i * 32:(i + 1) * 32]
            self.tr(pT, pe32[:, i, :], self.ident[0:32, 0:32], ['pe32', 'ident'], [('PB', 0)])
            self.cp('vector', peT[:, i, :], pT, [('PB', 0)], ['peT'])
        for i in range(2):
            pb_ = PB[1][:, i:i + 1]
            for c in range(32):
                self.mm(pb_, w1[i][:, c, :], peT[:, i, c:c + 1], c == 0, c == 31, [('w1', i), 'peT'], [('PB', 1)])
            self.cp('vector', bias[:, i:i + 1], pb_, [('PB', 1)], ['cbias'])
        self.V(lambda e: e.memset(self.vcaug[:, :, :, 64:65], 1.0), w=['vcaug'])
        for g in range(4):
            self.DMA(self.vcaug[:, g, :, 65:193], ovm.rearrange("(n p) s -> p n s", p=128), w=['vcaug'], eng='gpsimd')
        self.V(lambda e: e.memset(hs[:, 511:512], 0.0), w=['chs'])
        for i in range(2):
            srcd = self.kcsT_d if i == 0 else self.vcsT_d
            for g in range(4):
                self.DMA(src[:], srcd[g], r=['kvT_all'], w=['csrc'])
                s3 = src[:].rearrange("p (n r) -> p n r", r=16)
                ph = PB[2][:, 0:511]
                for c in range(32):
                    rhs = s3[:, 0:511, c] if c < 16 else s3[:, 1:512, c - 16]
                    self.mm(ph, w1[i][:, c, :], rhs, c == 0, c == 31, [('w1', i), 'csrc'], [('PB', 2)])
                self.act(hs[:, 0:511], ph, AF.Silu, [('PB', 2), 'cbias'], ['chs'], bias=bias[:, i:i + 1])
                if i == 0:
                    pk = PB[3][0:64, :]
                    pks = PB[4][0:64, :]
                    self.mm(pk, w2[0][:], hs[:], True, True, [('w2', 0), 'chs'], [('PB', 3)])
                    self.mm(pks, w2s[:], hs[:], True, True, ['w2s', 'chs'], [('PB', 4)])
                    self.tt('vector', kx[:], pk, ccs[:, 0:512], ALU.mult, [('PB', 3), 'ccs'], ['ckx'])
                    self.tt('vector', kxs[:], pks, ccs[:, 512:1024], ALU.mult, [('PB', 4), 'ccs'], ['ckxs'])
                    self.tt('vector', self.kcmpT[:, g, :], kx[:], kxs[:], ALU.add, ['ckx', 'ckxs'], ['kcmpT'])
                else:
                    pv = PB[5][:, 0:256].rearrange("p (n d) -> p n d", n=4)
                    for nt in range(4):
                        self.mm(pv[:, nt, :], hs[:, nt * 128:(nt + 1) * 128], w2[1][:], True, True, ['chs', ('w2', 1)], [('PB', 5)])
                    self.cp('vector', self.vcaug[:, g, :, 0:64], pv, [('PB', 5)], ['vcaug'])
        if 'dbg_kcmpT' in self.dbg:
            d1 = self.dout('dbg_kcmpT', [64, 4, 512], BF16)
            d2 = self.dout('dbg_vcaug', [128, 4, 4, 193], BF16)
            self.DMA(d1, self.kcmpT[:], ['kcmpT'], ['dbgk'])
            self.DMA(d2, self.vcaug[:], ['vcaug'], ['dbgk'])

    def phase4(self):
        PB = self.PB
        NQB = T // 256 if self.nqb4 is None else self.nqb4
        cmask_d = self.din("cmask_c", [128, 32, 4, 128], BF16)
        dmask_d = self.din("dmask", [128, 2, 2, 128])
        wmask_d = self.din("wmask", [128, 2, 6, 128])
        btab = self.din("btab_c", [32, 128, 128])
        cmk = [self.sb("cmk%d" % i, [128, 4, 128], BF16) for i in range(2)]
        dmk = self.sb("dmk", [128, 2, 2, 128], BF16)
        wmk = self.sb("wmk", [128, 2, 6, 128], BF16)
        self.DMA(dmk[:], dmask_d, w=['dmk'], eng='gpsimd')
        self.DMA(wmk[:], wmask_d, w=['wmk'], eng='gpsimd')
        kselT = self.sb("kselT", [64, T], BF16)
        kwinT = self.sb("kwinT", [64, T], BF16)
        vsel = self.sb("vsel", [128, 64, 65], BF16)
        vwin = self.sb("vwin", [128, 64, 65], BF16)
        qTb = [self.sb("qTb%d" % i, [64, 512], BF16) for i in range(2)]
        Btb = [self.sb("Btb%d" % i, [128, 128], F32) for i in range(2)]
        gzb = [self.sb("gzb%d" % i, [128, 3, 256], BF16) for i in range(2)]
        Eb = [self.sb("Eb%d" % i, [128, 4, 128], BF16) for i in range(3)]
        Pb_ = [self.sb("Pb_%d" % i, [128, 4, 128], BF16) for i in range(2)]
        rden = self.sb("rden", [128, 3, 4], F32)
        imp = self.sb("imp", [128, 128], F32)
        score = self.sb("score", [128, 128], F32)
        sc2 = self.sb("sc2", [128, 128], F32)
        m8 = self.sb("m8", [128, 16], F32)
        selb = self.sb("selb", [128, 128], BF16)
        selx = self.sb("selx", [128, 128, 64], BF16)
        tmp = self.sb("etmp", [128, 4, 64], F32)
        tmp2 = self.sb("etmp2", [128, 4, 64], F32)
        acc = self.sb("eacc", [128, 4, 64], F32)
        ogt = [self.sb("ogt%d" % i, [128, 256], BF16) for i in range(2)]
        ecnt = [0]
        pcnt = [0]
        scnt = [0]

        def qk_exp(kT_tile, kkeys, qT, kq):
            i = scnt[0] % 2
            scnt[0] += 1
            pS = PB[i][:]
            self.mm(pS, kT_tile, qT[:], True, True, list(kkeys) + [kq], [('PB', i)])
            j = ecnt[0] % 3
            ecnt[0] += 1
            E = Eb[j]
            self.act(E[:].rearrange("p h q -> p (h q)"), pS, AF.Exp, [('PB', i)], [('E', j)])
            return E, ('E', j)

        def loads(g, qb):
            t0 = qb * 128
            b2 = qb % 2
            self.DMA(qTb[b2][:], self.qT_d[g, qb], w=[('qTb', b2)])
            self.DMA(Btb[b2][:], btab[qb], w=[('Btb', b2)])
            self.DMA(gzb[b2][:], self.gz_d[t0:t0 + 128, :, g * 256:(g + 1) * 256], w=[('gzb', b2)])
            self.DMA(cmk[b2][:], cmask_d[:, qb], w=[('cmk', b2)])

        def make_items(g, qb):
            items = []
            t0 = qb * 128
            b2 = qb % 2
            qT = qTb[b2]
            kq = ('qTb', b2)
            Bt = Btb[b2]
            gz = gzb[b2]
            e2 = qb % 2
            qbm = 2 * qb + 1
            ntmax = (8 * qbm + 6) // 128
            pc = [PB[3][:, 0:386].rearrange("p (h c) -> p h c", h=2), PB[4][:, 0:386].rearrange("p (h c) -> p h c", h=2)]

            def cmp_post():
                for hb in range(2):
                    self.V(lambda e, hb=hb: e.tensor_scalar(out=rden[:, 0, hb * 2:hb * 2 + 2], in0=pc[hb][:, :, 64], scalar1=1e-30, scalar2=None, op0=ALU.max),
                           [('PB', 3 + hb)], ['rden0'])
                self.V(lambda e: e.reciprocal(out=rden[:, 0, :], in_=rden[:, 0, :]), ['rden0'], ['rden0'])
                for h in range(4):
                    src = pc[h // 2][:, h % 2, 65:193]
                    if h == 0:
                        self.V(lambda e, src=src: e.tensor_scalar(out=imp[:], in0=src, scalar1=rden[:, 0, 0:1], scalar2=None, op0=ALU.mult), [('PB', 3), 'rden0'], ['imp'])
                    else:
                        self.V(lambda e, src=src, h=h: e.scalar_tensor_tensor(out=imp[:], in0=src, scalar=rden[:, 0, h:h + 1], in1=imp[:], op0=ALU.mult, op1=ALU.add),
                               [('PB', 3 + h // 2), 'rden0', 'imp'], ['imp'])
                self.tt('vector', score[:], imp[:], Bt[:], ALU.add, ['imp', ('Btb', b2)], ['score'])
                self.V(lambda e: e.max(out=m8[:, 0:8], in_=score[:]), ['score'], ['m8'])
                self.V(lambda e: e.match_replace(out=sc2[:], in_to_replace=m8[:, 0:8], in_values=score[:], imm_value=-1e9), ['score', 'm8'], ['sc2'])
                self.V(lambda e: e.max(out=m8[:, 8:16], in_=sc2[:]), ['sc2'], ['m8'])
                nbk = 2 * (qbm + 1)
                self.V(lambda e: e.tensor_scalar(out=selx[:, 0:nbk, :], in0=score[:, 0:nbk].unsqueeze(2).to_broadcast([128, nbk, 64]), scalar1=m8[:, 15:16], scalar2=None, op0=ALU.is_ge),
                       ['score', 'm8'], ['selx'])
                for hb in range(2):
                    self.tt('vector', tmp[:, hb * 2:hb * 2 + 2, :], pc[hb][:, :, 0:64], rden[:, 0, hb * 2:hb * 2 + 2].unsqueeze(2).to_broadcast([128, 2, 64]), ALU.mult,
                            [('PB', 3 + hb), 'rden0'], ['etmp'])
                self.tt('gpsimd', acc[:], tmp[:], gz[:, 0, :].rearrange("p (h d) -> p h d", h=4), ALU.mult, ['etmp', ('gzb', b2)], ['eacc'])
                if self.dbg4 is not None and g == 0 and qb == self.dbg4:
                    self.V(lambda e: e.tensor_scalar(out=selb[:], in0=score[:], scalar1=m8[:, 15:16], scalar2=None, op0=ALU.is_ge), ['score', 'm8'], ['selb'])
                    self.DMA(self.d4['imp'], imp[:], ['imp'], ['dbg4'])
                    self.DMA(self.d4['sel'], selb[:], ['selb'], ['dbg4'])
                    self.DMA(self.d4['ocmp'], tmp[:], ['etmp'], ['dbg4'])

            for nt in range(ntmax + 1):
                it = {'mdep': False, 'M': None}

                def A(it=it, nt=nt):
                    it['E'], it['kE'] = qk_exp(self.kcmpT[:, g, nt * 128:(nt + 1) * 128], ['kcmpT'], qT, kq)

                def B(it=it, nt=nt):
                    E, kE = it['E'], it['kE']
                    self.tt('vector', E[:], E[:], cmk[b2][:, nt, :].unsqueeze(1).to_broadcast([128, 4, 128]), ALU.mult, [kE, ('cmk', b2)], [kE])
                    for h in range(4):
                        self.mm(pc[h // 2][:, h % 2, :], E[:, h, :], self.vcaug[:, g, nt, :], nt == 0 and h % 2 == 0, nt == ntmax and h % 2 == 1, [kE, 'vcaug'], [('PB', 3 + h // 2)])
                    if nt == ntmax:
                        cmp_post()
                it['A'], it['B'] = A, B
                items.append(it)

            for br in (1, 2):
                pacc = PB[4 + br][:, 0:260].rearrange("p (h c) -> p h c", h=4)
                kacc = ('PB', 4 + br)
                if br == 1:
                    kts = list(range(0, qbm + 1))
                    kT_, kkey, V_, vkey = kselT, 'kselT', vsel, 'vsel'
                else:
                    kts = [kt for kt in range(qbm - 5, qbm + 1) if kt >= 0]
                    kT_, kkey, V_, vkey = kwinT, 'kwinT', vwin, 'vwin'

                def br_post(br=br, pacc=pacc, kacc=kacc):
                    kr = 'rden%d' % br
                    self.V(lambda e: e.reciprocal(out=rden[:, br, :], in_=pacc[:, :, 64]), [kacc], [kr])
                    self.tt('vector', tmp[:], pacc[:, :, 0:64], rden[:, br, :].unsqueeze(2).to_broadcast([128, 4, 64]), ALU.mult, [kacc, kr], ['etmp'])
                    if self.dbg4 is not None and g == 0 and qb == self.dbg4:
                        self.DMA(self.d4['osel' if br == 1 else 'owin'], tmp[:], ['etmp'], ['dbg4'])
                    self.tt('gpsimd', tmp2[:], tmp[:], gz[:, br, :].rearrange("p (h d) -> p h d", h=4), ALU.mult, ['etmp', ('gzb', b2)], ['etmp2'])
                    if br == 1:
                        self.tt('gpsimd', acc[:], acc[:], tmp2[:], ALU.add, ['eacc', 'etmp2'], ['eacc'])
                    else:
                        og = ogt[b2]
                        self.tt('gpsimd', og[:].rearrange("p (h d) -> p h d", h=4), acc[:], tmp2[:], ALU.add, ['eacc', 'etmp2'], [('ogt', b2)])
                        self.DMA(self.og_d[t0:t0 + 128, g * 256:(g + 1) * 256], og[:], r=[('ogt', b2)], w=[('og_d', g, qb)])

                for kt in kts:
                    it = {'mdep': (br == 1 and kt == kts[0]), 'M': None}

                    def A(it=it, kt=kt, kT_=kT_, kkey=kkey):
                        it['E'], it['kE'] = qk_exp(kT_[:, kt * 128:(kt + 1) * 128], [kkey], qT, kq)

                    def M(it=it, kt=kt):
                        pM = PB[2 if kt % 2 == 0 else 7][:].bitcast(BF16)[:, 0:128]
                        kM = ('PB', 2 if kt % 2 == 0 else 7)
                        self.tr(pM, selx[:, 2 * kt:2 * kt + 2, :].rearrange("p a k -> p (a k)"), self.identb[:], ['selx', 'identb'], [kM])
                        it['pM'], it['kM'] = pM, kM

                    def B(it=it, kt=kt, br=br, kts=kts, pacc=pacc, kacc=kacc, V_=V_, vkey=vkey, br_post=br_post):
                        E, kE = it['E'], it['kE']
                        if br == 1:
                            ip = pcnt[0] % 2
                            pcnt[0] += 1
                            P = Pb_[ip]
                            kP = ('P4', ip)
                            self.tt('vector', P[:], E[:], it['pM'].unsqueeze(1).to_broadcast([128, 4, 128]), ALU.mult, [kE, it['kM']], [kP])
                            if kt >= qbm - 1:
                                self.tt('gpsimd', P[:], P[:], dmk[:, e2, kt - (qbm - 1), :].unsqueeze(1).to_broadcast([128, 4, 128]), ALU.mult, [kP, 'dmk'], [kP])
                        else:
                            P, kP = E, kE
                            wi = kt - (qbm - 5)
                            if wi not in (2, 3):
                                self.tt('gpsimd', P[:], P[:], wmk[:, e2, wi, :].unsqueeze(1).to_broadcast([128, 4, 128]), ALU.mult, [kP, 'wmk'], [kP])
                        for h in range(4):
                            self.mm(pacc[:, h, :], P[:, h, :], V_[:, kt, :], kt == kts[0] and h == 0, kt == kts[-1] and h == 3, [kP, vkey], [kacc])
                        if kt == kts[-1]:
                            br_post()
                    it['A'], it['B'] = A, B
                    if br == 1:
                        it['M'] = M
                    items.append(it)
            return items

        for g in range(4):
            self.DMA(kselT[:], self.kselT_d[g], w=['kselT'])
            self.DMA(kwinT[:], self.kwinT_d[g], w=['kwinT'])
            self.DMA(vsel[:], self.vsel_d[g], w=['vsel'])
            self.DMA(vwin[:], self.vwin_d[g], w=['vwin'])
            loads(g, 0)
            items = []
            for qb in range(NQB):
                if qb + 1 < NQB:
                    items.append({'load': (g, qb + 1)})
                items += make_items(g, qb)
            work = [it for it in items if 'load' not in it]
            pos = 0
            load_at = {}
            for it in items:
                if 'load' in it:
                    load_at.setdefault(pos, []).append(it['load'])
                else:
                    pos += 1
            n = len(work)
            done_loads = set()

            def do_loads(upto):
                for p_ in sorted(load_at):
                    if p_ <= upto and p_ not in done_loads:
                        done_loads.add(p_)
                        for l in load_at[p_]:
                            loads(*l)

            do_loads(0)
            for j in range(min(2, n)):
                work[j]['A']()
            if n > 0 and work[0]['M'] is not None:
                work[0]['M']()
            for i in range(n):
                do_loads(i)
                if i + 2 < n:
                    work[i + 2]['A']()
                nxt = work[i + 1] if i + 1 < n else None
                if nxt is not None and nxt['M'] is not None and not nxt['mdep']:
                    nxt['M']()
                work[i]['B']()
                if nxt is not None and nxt['M'] is not None and nxt['mdep']:
                    nxt['M']()

    def phase5(self):
        PB = self.PB
        NQB = T // 256 if self.nqb4 is None else self.nqb4
        bw_d = self.din("bw", [128, 2, 2]) if 'bw' not in self.inputs else self.inputs['bw'].ap()
        bw = self.sb("p5bw", [128, 2, 2], F32)
        self.DMA(bw[:], bw_d, w=['p5bw'])
        hB = [self.sb("p5hB%d" % i, [128, 1024], F32) for i in range(2)]
        b_w_out = self.din("b_w_out", [1024, 1024])
        b_ln_w = self.din("b_ln_w", [1, 1024])
        b_ln_b = self.din("b_ln_b", [1, 1024])
        out = self.dout("out", [T // 2, D], F32)
        w_out = self.sb("p5wout", [128, 8, 1024], BF16)
        self.DMA(w_out[:], b_w_out.rearrange("(c p) n -> p c n", p=128), w=['p5wout'], eng='gpsimd')
        lnw_b = self.sb("p5lnw", [128, 1024], F32)
        lnb_b = self.sb("p5lnb", [128, 1024], F32)
        self.DMA(lnw_b[:], b_ln_w.partition_broadcast(128), w=['p5lnw'])
        self.DMA(lnb_b[:], b_ln_b.partition_broadcast(128), w=['p5lnb'])
        ogs = [self.sb("p5og%d" % i, [128, 1024], BF16) for i in range(2)]
        h1s = [self.sb("p5h1%d" % i, [128, 1024], F32) for i in range(2)]
        ogT = self.sb("p5ogT", [128, 8, 128], BF16)
        rr = self.sb("p5rr", [128, 1024], F32)
        xo = [self.sb("p5xo%d" % i, [128, 1024], F32) for i in range(2)]
        bst = self.sb("p5bst", [128, 2, 6], F32)
        mv = self.sb("p5mv", [128, 2], F32)
        lrs = self.sb("p5lrs", [128, 1], F32)
        rstd = self.sb("p5rstd", [128, 1], F32)
        nb = self.sb("p5nb", [128, 1], F32)
        for qb in range(NQB):
            t0 = qb * 128
            b2 = qb % 2
            og = ogs[b2]
            h1 = h1s[b2]
            xn = xo[b2]
            self.DMA(og[:], self.og_d[t0:t0 + 128, :], w=[('p5og', b2)])
            e2 = qb % 2
            hb_ = hB[b2]
            self.DMA(h1[:], self.h1_d[(2 * qb) * 128:(2 * qb + 1) * 128, :], w=[('p5h1', b2)])
            self.DMA(hb_[:], self.h1_d[(2 * qb + 1) * 128:(2 * qb + 2) * 128, :], w=[('p5hB', b2)])
            self.V(lambda e, h1=h1, e2=e2: e.tensor_scalar(out=h1[:], in0=h1[:], scalar1=bw[:, e2, 0:1], scalar2=None, op0=ALU.mult), [('p5h1', b2), 'p5bw'], [('p5h1', b2)])
            self.V(lambda e, h1=h1, hb_=hb_, e2=e2: e.scalar_tensor_tensor(out=h1[:], in0=hb_[:], scalar=bw[:, e2, 1:2], in1=h1[:], op0=ALU.mult, op1=ALU.add),
                   [('p5hB', b2), 'p5bw', ('p5h1', b2)], [('p5h1', b2)])
            pTb = PB[2][:].bitcast(BF16).rearrange("p (c n) -> p c n", c=8)
            for kc in range(8):
                self.tr(pTb[:, kc, :], og[:, kc * 128:(kc + 1) * 128], self.identb[:], [('p5og', b2), 'identb'], [('PB', 2)])
            self.cp('scalar', ogT[:], pTb, [('PB', 2)], ['p5ogT'])
            pY = [PB[0][:], PB[1][:]]
            for half in range(2):
                for kc in range(8):
                    self.mm(pY[half], ogT[:, kc, :], w_out[:, kc, half * 512:(half + 1) * 512], kc == 0, kc == 7, ['p5ogT', 'p5wout'], [('PB', half)])
            for half in range(2):
                self.V(lambda e, half=half, h1=h1: e.scalar_tensor_tensor(out=rr[:, half * 512:(half + 1) * 512], in0=h1[:, half * 512:(half + 1) * 512], scalar=ALPHA, in1=pY[half], op0=ALU.mult, op1=ALU.add),
                       [('p5h1', b2), ('PB', half)], ['p5rr'])
                self.V(lambda e, half=half: e.bn_stats(out=bst[:, half, :], in_=rr[:, half * 512:(half + 1) * 512]), ['p5rr'], ['p5bst'])
            self.V(lambda e: e.bn_aggr(out=mv[:], in_=bst[:].rearrange("p a b -> p (a b)")), ['p5bst'], ['p5mv'])
            self.act(lrs[:], mv[:, 1:2], AF.Ln, ['p5mv'], ['p5lrs'], bias=EPS)
            self.act(rstd[:], lrs[:], AF.Exp, ['p5lrs'], ['p5rstd'], scale=-0.5)
            self.V(lambda e: e.tensor_scalar(out=nb[:], in0=mv[:, 0:1], scalar1=rstd[:, 0:1], scalar2=-1.0, op0=ALU.mult, op1=ALU.mult), ['p5mv', 'p5rstd'], ['p5nb'])
            self.act(xn[:], rr[:], AF.Identity, ['p5rr', 'p5rstd', 'p5nb'], [('p5xo', b2)], scale=rstd[:, 0:1], bias=nb[:, 0:1])
            self.tt('gpsimd', xn[:], xn[:], lnw_b[:], ALU.mult, [('p5xo', b2), 'p5lnw'], [('p5xo', b2)])
            self.tt('gpsimd', xn[:], xn[:], lnb_b[:], ALU.add, [('p5xo', b2), 'p5lnb'], [('p5xo', b2)])
            self.DMA(out[t0:t0 + 128, :], xn[:], r=[('p5xo', b2)], w=[('out', qb)])
            self.final_keys.append(('out', qb))


def _in_maps(b, inputs):
    hc = host_consts()
    maps = []
    for core in range(8):
        bi = core // 2
        m = dict(hc)
        m.update(core_consts(core % 2, hc))
        m['x'] = inputs['x'][bi]
        m['a_w_in'] = inputs['a_w_in'][0]
        m['a_conv_w'] = inputs['a_conv_w'][0]
        m['a_a_log'] = inputs['a_a_log'].reshape(1, 8)
        m['a_dt_bias'] = inputs['a_dt_bias'].reshape(1, 8)
        m['a_norm_w'] = inputs['a_norm_w'].reshape(128, 1)
        m['a_w_out'] = inputs['a_w_out'][0]
        m['a_ln_w'] = inputs['a_ln_w'].reshape(1, 1024)
        m['a_ln_b'] = inputs['a_ln_b'].reshape(1, 1024)
        for k in ('s_w_kv', 's_pe_k', 's_pe_v', 's_w1_k', 's_w2_k', 's_w1_v', 's_w2_v'):
            m[k] = inputs[k]
        m['b_w_in'] = inputs['b_w_in'][0]
        m['b_w_out'] = inputs['b_w_out'][0]
        m['b_ln_w'] = inputs['b_ln_w'].reshape(1, 1024)
        m['b_ln_b'] = inputs['b_ln_b'].reshape(1, 1024)
        maps.append({k: np.ascontiguousarray(v if k == 'cmask_c' else np.asarray(v, dtype=np.float32)) for k, v in m.items() if k in b.inputs})
    return maps


def kernel(**inputs):
    inputs = {k: np.asarray(v) for k, v in inputs.items()}
    import os
    ph = tuple(os.environ.get('KPHASES', 'p1,p2,p3,p4,p5').split(','))
    b = Builder(phases=ph)
    nc = b.build()
    maps = _in_maps(b, inputs)
    res = run_bass_kernel_spmd(nc, maps, core_ids=list(range(8)))
    if 'out' not in b.outputs:
        return np.zeros((4, T, D), np.float32)
    out = np.zeros((4, T, D), np.float32)
    for core in range(8):
        bi, p = core // 2, core % 2
        o = np.asarray(res.results[core]['out'], dtype=np.float32)
        for j in range(32):
            qb = slot_qb(p, j)
            out[bi, qb * 128:(qb + 1) * 128] = o[j * 128:(j + 1) * 128]
    return out
```

```python
import math
from contextlib import ExitStack

import numpy as np
import concourse.bass as bass
import concourse.mybir as mybir
from concourse.bass_utils import run_bass_kernel_spmd

F32 = mybir.dt.float32
BF16 = mybir.dt.bfloat16
AF = mybir.ActivationFunctionType
ALU = mybir.AluOpType
AX = mybir.AxisListType

ENGS = ('sync', 'gpsimd', 'scalar', 'vector', 'tensor')

T = 8192
D = 1024
NH = 8
EPS = 1e-6
ALPHA = 4.0 ** 0.25


class Sched:
    def __init__(self, nc, csems, dsems):
        self.nc = nc
        self.csem = csems
        self.dsems = dsems
        self.ops = {e: [] for e in ENGS}
        self.cnt = {e: 0 for e in ENGS}
        self.dcount = [0] * len(dsems)
        nd = len(dsems)
        self.dpool = {'sync': list(range(0, nd - 4)), 'gpsimd': list(range(nd - 4, nd))}
        self.dptr = {'sync': 0, 'gpsimd': 0}
        self.lastw = {}
        self.readers = {}
        self.waited = {e: {} for e in ENGS}
        self.nops = 0

    def _sem(self, sk):
        return self.csem[sk[1]] if sk[0] == 'c' else self.dsems[sk[1]]

    kmap = {}

    def _expand(self, keys):
        out = []
        for k in keys:
            if isinstance(k, str):
                k = self.kmap.get(k, k)
            if isinstance(k, tuple) and len(k) == 2 and k[0] == 'PB':
                out.append(('PB', k[1], 0))
                out.append(('PB', k[1], 1))
            else:
                out.append(k)
        return out

    def op(self, eng, fn, reads=(), writes=(), dma=False):
        need = {}
        reads = self._expand(reads)
        writes = self._expand(writes)

        def want(tok):
            sk, val, src = tok
            if sk[0] == 'c' and src == eng and eng == 'tensor':
                return
            if need.get(sk, 0) < val:
                need[sk] = val

        for k in reads:
            t = self.lastw.get(k)
            if t is not None:
                want(t)
        for k in writes:
            t = self.lastw.get(k)
            if t is not None:
                want(t)
            for t in self.readers.get(k, ()):
                want(t)
        if dma:
            pool = self.dpool[eng]
            i = pool[self.dptr[eng] % len(pool)]
            self.dptr[eng] += 1
            if self.dcount[i] > 0:
                want((('d', i), 16 * self.dcount[i], None))
            self.dcount[i] += 1
            tok = (('d', i), 16 * self.dcount[i], eng)
            inc = 16
        else:
            self.cnt[eng] += 1
            tok = (('c', eng), self.cnt[eng], eng)
            inc = 1
        w = self.waited[eng]
        waits = []
        for sk, val in need.items():
            if w.get(sk, 0) < val:
                w[sk] = val
                waits.append((self._sem(sk), val))
        self.ops[eng].append((waits, fn, self._sem(tok[0]), inc))
        for k in writes:
            self.lastw[k] = tok
            self.readers[k] = []
        for k in reads:
            lst = self.readers.setdefault(k, [])
            if len(lst) < 64:
                lst.append(tok)
            else:
                d = {}
                for t in lst + [tok]:
                    if d.get(t[0], (0,))[0] < t[1]:
                        d[t[0]] = (t[1], t[2])
                self.readers[k] = [(sk, v[0], v[1]) for sk, v in d.items()]
        self.nops += 1
        return tok

    def wait_all(self, eng, keys):
        need = {}
        for k in keys:
            t = self.lastw.get(k)
            if t is not None:
                sk, val, src = t
                if need.get(sk, 0) < val:
                    need[sk] = val
        waits = [(self._sem(sk), val) for sk, val in need.items()]
        self.ops[eng].append((waits, None, None, 0))

    def drain_dmas(self, eng):
        waits = [(self.dsems[i], 16 * self.dcount[i]) for i in range(len(self.dsems)) if self.dcount[i] > 0]
        self.ops[eng].append((waits, None, None, 0))
        for i in range(len(self.dsems)):
            self.waited[eng][('d', i)] = 16 * self.dcount[i]

    def emit(self):
        nc = self.nc
        with nc.Block() as block:
            for e in ENGS:
                ops = self.ops[e]
                if not ops:
                    continue

                def body(engine, ops=ops):
                    for waits, fn, sem, inc in ops:
                        for s, v in waits:
                            engine.wait_ge(s, v)
                        if fn is not None:
                            ins = fn(engine)
                            ins.then_inc(sem, inc)

                getattr(block, e)(body)
        self.ops = {e: [] for e in ENGS}


def host_consts():
    c = {}
    half = 32
    inv = (np.float32(10000.0) ** (-(np.arange(half, dtype=np.float32) / np.float32(half)))).astype(np.float32)
    pos = np.arange(T, dtype=np.float32)
    ang = (pos[:, None] * inv[None, :]).astype(np.float32)
    c['rope_cs'] = np.concatenate([np.cos(ang), np.sin(ang)], axis=1).astype(np.float32)
    pc = (np.arange(512, dtype=np.float32) * 16 + 31).astype(np.float32)
    angc = (pc[None, :] * inv[:, None]).astype(np.float32)
    cosF = np.concatenate([np.cos(angc), np.cos(angc)], axis=0)
    sinF = np.concatenate([-np.sin(angc), np.sin(angc)], axis=0)
    c['cmp_cs'] = np.concatenate([cosF, sinF], axis=1).astype(np.float32)
    st = np.arange(512)[:, None] * 16
    bs = np.arange(128)[None, :] * 64
    ov = np.clip(np.minimum(st + 32, bs + 64) - np.maximum(st, bs), 0, None) / 32.0
    ov[511] = 0.0
    c['ovm'] = ov.astype(np.float32)
    n = np.arange(128)[:, None, None]
    dl = np.arange(17)[None, :, None]
    i = np.arange(128)[None, None, :]
    c['cmpmask'] = (16 * n + 31 - i <= 128 * dl).astype(np.float32)
    kk = np.arange(128)[:, None]
    qq = np.arange(128)[None, :]
    c['cwmask'] = np.stack([(kk <= qq), (kk > qq)], axis=1).astype(np.float32)
    bt = np.zeros((64, 128, 128), np.float32)
    for qb in range(64):
        t = qb * 128 + np.arange(128)
        cur = t // 64
        blk = np.arange(128)[None, :]
        forced = (blk == 0) | (blk == cur[:, None]) | (blk == cur[:, None] - 1)
        vis = blk * 64 <= t[:, None]
        bt[qb] = np.where(vis, np.where(forced, 1.0e4, 0.0), -1.0)
    c['btab'] = bt
    c['ident'] = np.eye(128, dtype=np.float32)
    k = np.arange(64)
    tri = (k[:, None] <= k[None, :]).astype(np.float32)
    sup = (k[:, None] > k[None, :]).astype(np.float32)
    negT = np.where(k[None, :] < k[:, None], -1e30, 0.0).astype(np.float32)
    offd = (k[:, None] != k[None, :]).astype(np.float32)
    eye = np.eye(64, dtype=np.float32)
    c64 = np.concatenate([
        tri, -tri, sup, np.ones((64, 128), np.float32), -np.ones((64, 64), np.float32),
        np.tile(negT[:, None, :], (1, 8, 1)).reshape(64, 512), offd,
    ], axis=1)
    c['c64'] = np.ascontiguousarray(c64)
    return c


def slot_qb(p, j):
    first = (p == (j % 2))
    return 2 * j if first else 2 * j + 1


def core_consts(p, hc):
    import ml_dtypes
    c = {}
    bw = np.zeros((128, 2, 2), np.float32)
    for e in range(2):
        first = (p == e)
        bw[:, e, 0] = 1.0 if first else 0.0
        bw[:, e, 1] = 0.0 if first else 1.0
    c['bw'] = bw
    qbs = [slot_qb(p, j) for j in range(32)]
    c['rope_q'] = np.concatenate([hc['rope_cs'][qb * 128:(qb + 1) * 128] for qb in qbs], axis=0)
    c['btab_c'] = np.stack([hc['btab'][qb] for qb in qbs], axis=0)
    n = np.arange(128)[:, None, None, None]
    nt = np.arange(4)[None, None, :, None]
    i = np.arange(128)[None, None, None, :]
    qbv = np.array(qbs)[None, :, None, None]
    c['cmask_c'] = (16 * (128 * nt + n) + 31 <= 128 * qbv + i).astype(ml_dtypes.bfloat16)
    kk = np.arange(128)[:, None]
    qq = np.arange(128)[None, :]
    caus = (kk <= qq).astype(np.float32)
    win = (kk > qq).astype(np.float32)
    one = np.ones((128, 128), np.float32)
    zero = np.zeros((128, 128), np.float32)
    dm = np.zeros((128, 2, 2, 128), np.float32)
    wm = np.zeros((128, 2, 6, 128), np.float32)
    for e in range(2):
        first = (p == e)
        dl = [caus, zero] if first else [one, caus]
        wl = [win, one, one, one, caus, zero] if first else [zero, win, one, one, one, caus]
        for a, m_ in enumerate(dl):
            dm[:, e, a, :] = m_
        for a, m_ in enumerate(wl):
            wm[:, e, a, :] = m_
    c['dmask'] = dm
    c['wmask'] = wm
    return c


C64_OFF = {}
_o = 0
for _n, _w in [('tri', 64), ('ntri', 64), ('sup', 64), ('ones', 128), ('nones', 64),
               ('negT8', 512), ('offd', 64)]:
    C64_OFF[_n] = (_o, _o + _w)
    _o += _w
C64_W = _o


class Builder:
    def __init__(self, phases=('p1',), dbg=None, ntiles1=None, nqb4=None, dbg4=None):
        self.nqb4 = nqb4
        self.dbg4 = dbg4
        self.phases = phases
        self.dbg = dbg or {}
        self.ntiles1 = ntiles1
        self.nc = bass.Bass("TRN2", target_bir_lowering=False)
        self.es = ExitStack()
        self.inputs = {}
        self.outputs = {}

    def din(self, name, shape, dt=F32):
        t = self.nc.dram_tensor(name, list(shape), dt, kind="ExternalInput")
        self.inputs[name] = t
        return t.ap()

    def dscratch(self, name, shape, dt):
        if name in self.dbg:
            t = self.nc.dram_tensor(name, list(shape), dt, kind="ExternalOutput")
            self.outputs[name] = t
        else:
            t = self.nc.dram_tensor(name, list(shape), dt)
        return t.ap()

    def dout(self, name, shape, dt):
        t = self.nc.dram_tensor(name, list(shape), dt, kind="ExternalOutput")
        self.outputs[name] = t
        return t.ap()

    def sb(self, name, shape, dt):
        return self.pes.enter_context(self.nc.sbuf_tensor(name, list(shape), dt))

    def ps(self, name, shape, dt):
        return self.es.enter_context(self.nc.psum_tensor(name, list(shape), dt))

    def V(self, fn, r=(), w=()):
        self.S.op('vector', fn, r, w)

    def A(self, fn, r=(), w=()):
        self.S.op('scalar', fn, r, w)

    def G(self, fn, r=(), w=()):
        self.S.op('gpsimd', fn, r, w)

    def PE(self, fn, r=(), w=()):
        self.S.op('tensor', fn, r, w)

    def DMA(self, out, in_, r=(), w=(), eng='sync', **kw):
        self.S.op(eng, lambda e: e.dma_start(out=out, in_=in_, **kw), r, w, dma=True)

    def mm(self, out, lhsT, rhs, start, stop, r, w):
        self.S.op('tensor', lambda e: e.matmul(out, lhsT=lhsT, rhs=rhs, start=start, stop=stop), r, w)

    def tr(self, out, in_, ident, r, w):
        self.S.op('tensor', lambda e: e.transpose(out=out, in_=in_, identity=ident), r, w)

    def act(self, out, in_, func, r, w, **kw):
        self.S.op('scalar', lambda e: e.activation(out=out, in_=in_, func=func, **kw), r, w)

    def tt(self, eng, out, in0, in1, op, r, w):
        self.S.op(eng, lambda e: e.tensor_tensor(out=out, in0=in0, in1=in1, op=op), r, w)

    def cp(self, eng, out, in_, r, w):
        if eng == 'scalar':
            self.S.op(eng, lambda e: e.copy(out=out, in_=in_), r, w)
        else:
            self.S.op(eng, lambda e: e.tensor_copy(out=out, in_=in_), r, w)

    def build(self):
        nc = self.nc
        es = self.es
        with es:
            csems = {e: es.enter_context(nc.semaphore("c_" + e)) for e in ENGS}
            dsems = [es.enter_context(nc.semaphore("d%d" % i)) for i in range(12)]
            self.S = Sched(nc, csems, dsems)
            self.pes = es
            self.PB = [es.enter_context(nc.psum_tensor("pb%d" % i, [128, 512], F32)) for i in range(8)]
            self.setup_common()
            if 'p1' in self.phases:
                with ExitStack() as pes:
                    self.pes = pes
                    self.phase1()
                    self.S.drain_dmas('sync')
                    self.S.emit()
            for ph in ('p2', 'p3', 'p4', 'p5'):
                if ph in self.phases:
                    with ExitStack() as pes:
                        self.pes = pes
                        getattr(self, 'phase' + ph[1])()
                        self.S.drain_dmas('sync')
                        self.S.emit()
            self.pes = es
            self.finish()
            self.S.emit()
        return nc

    def finish(self):
        if not self.outputs:
            o = self.dout("out", [T // 2, D], F32)
            self.DMA(o[0:128, 0:128], self.ident[:], r=['ident'], w=['dummy_out'])
            self.final_keys.append('dummy_out')
        self.S.wait_all('sync', list(self.final_keys))

    def setup_common(self):
        self.final_keys = []
        x = self.din("x", [T, D])
        self.x = x
        self.c_ident = self.din("ident", [128, 128])
        self.c_c64 = self.din("c64", [64, C64_W])
        self.ident = self.sb("ident_sb", [128, 128], F32)
        self.identb = self.sb("identb_sb", [128, 128], BF16)
        self.c64 = self.sb("c64_sb", [64, C64_W], F32)
        self.DMA(self.ident[:], self.c_ident, w=['ident'])
        self.DMA(self.c64[:], self.c_c64, w=['c64'])
        self.cp('vector', self.identb[:], self.ident[:], ['ident'], ['identb'])
        self.ones128 = self.sb("ones128", [128, 128], F32)
        self.V(lambda e: e.memset(self.ones128[:], 1.0), w=['ones128'])
        if 'p1' in self.phases:
            self.h1_d = self.dscratch("h1_d", [T, D], F32)
            self.h1T_d = self.dscratch("h1T_d", [128, 8, T], BF16)
        else:
            self.h1_d = self.din("h1_d", [T, D], F32)
            self.h1T_d = self.din("h1T_d", [128, 8, T], BF16)
        self.kselT_d = self.dscratch("kselT_d", [4, 64, T], BF16)
        self.kwinT_d = self.dscratch("kwinT_d", [4, 64, T], BF16)
        self.kcsT_d = self.dscratch("kcsT_d", [4, 64, T], BF16)
        self.vcsT_d = self.dscratch("vcsT_d", [4, 64, T], BF16)
        self.vsel_d = self.dscratch("vsel_d", [4, 128, T // 128, 65], BF16)
        self.vwin_d = self.dscratch("vwin_d", [4, 128, T // 128, 65], BF16)
        self.qT_d = self.dscratch("qT_d", [4, 32, 64, 512], BF16)
        self.gz_d = self.dscratch("gz_d", [T // 2, 3, 1024], BF16)
        self.og_d = self.dscratch("og_d", [T // 2, 1024], BF16)
        if self.dbg4 is not None:
            self.d4 = {'imp': self.dout('dbg_imp', [128, 128], F32), 'sel': self.dout('dbg_sel', [128, 128], BF16),
                       'ocmp': self.dout('dbg_ocmp', [128, 4, 64], F32), 'osel': self.dout('dbg_osel', [128, 4, 64], F32),
                       'owin': self.dout('dbg_owin', [128, 4, 64], F32)}
            self.final_keys.append('dbg4')
        self.kcmpT = self.sb("kcmpT", [64, 4, 512], BF16)
        self.vcaug = self.sb("vcaug", [128, 4, 4, 193], BF16)

    def c64v(self, name, heads=False):
        a, b = C64_OFF[name]
        v = self.c64[:, a:b]
        if heads:
            v = v.rearrange("p (h i) -> p h i", h=8)
        return v

    def phase1(self):
        nc = self.nc
        S = self.S
        PB = self.PB
        TT = 256
        NCH = 4
        NT = T // TT if self.ntiles1 is None else self.ntiles1
        a_w_in = self.din("a_w_in", [1024, 4112])
        a_conv_w = self.din("a_conv_w", [4, 3072])
        a_a_log = self.din("a_a_log", [1, 8])
        a_dt_bias = self.din("a_dt_bias", [1, 8])
        a_norm_w = self.din("a_norm_w", [128, 1])
        a_w_out = self.din("a_w_out", [1024, 1024])
        a_ln_w = self.din("a_ln_w", [1, 1024])
        a_ln_b = self.din("a_ln_b", [1, 1024])

        w_in = self.sb("w_in_sb", [128, 8, 4112], BF16)
        qkT = self.sb("qkT", [128, 2, 8, TT], F32)
        qT = qkT[:, 0]
        kT = qkT[:, 1]
        w_out = qkT[:].rearrange("p a h t -> p (a h t)").bitcast(BF16).rearrange("p (c n) -> p c n", c=8)
        w_out_d = self.dscratch("w_out_bf_d", [128, 8, 1024], BF16)
        for kc in range(8):
            self.DMA(w_in[:, kc, :], a_w_in[kc * 128:(kc + 1) * 128, :], w=['w_in'], eng='gpsimd')
        self.DMA(w_out, a_w_out.rearrange("(c p) n -> p c n", p=128), w=['qT', 'kT'], eng='gpsimd')
        self.DMA(w_out_d, w_out, r=['qT', 'kT'], w=['w_out_d'])
        xs = [self.sb("xs0", [128, 2, 1024], F32)]
        vcur2 = [xs[0][0:64, i, :].rearrange("p (h d) -> p h d", h=8) for i in range(2)]
        cw4 = xs[0][0:4].rearrange("p s d -> p (s d)")
        convw = self.sb("convw", [128, 24, 4], F32)
        pcw = PB[0][:, 0:96].rearrange("p (b i) -> p b i", i=4)
        for part in range(2):
            nb_ = 16 if part == 0 else 8
            self.DMA(cw4[:, 0:nb_ * 128], a_conv_w[:, part * 2048:part * 2048 + nb_ * 128], w=['xs'])
            for b in range(nb_):
                self.tr(pcw[:, part * 16 + b, :], cw4[:, b * 128:(b + 1) * 128], self.ident[0:4, 0:4], ['xs', 'ident'], [('PB', 0)])
        self.cp('vector', convw[:], pcw, [('PB', 0)], ['convw'])
        normw = self.sb("normw", [128, 1], F32)
        self.DMA(normw[:], a_norm_w, w=['normw'])
        lnw_b = self.sb("lnw_b", [128, 1024], F32)
        lnb_b = self.sb("lnb_b", [128, 1024], F32)
        self.DMA(lnw_b[:], a_ln_w.partition_broadcast(128), w=['lnw_b'])
        self.DMA(lnb_b[:], a_ln_b.partition_broadcast(128), w=['lnb_b'])
        dtb = self.sb("dtb", [64, 8], F32)
        alog = self.sb("alog", [64, 8], F32)
        negA = self.sb("negA", [64, 8], F32)
        self.DMA(dtb[:], a_dt_bias.partition_broadcast(64), w=['dtb'])
        self.DMA(alog[:], a_a_log.partition_broadcast(64), w=['alog'])
        self.act(negA[:], alog[:], AF.Exp, ['alog'], ['negA'])
        self.V(lambda e: e.tensor_scalar(out=negA[:], in0=negA[:], scalar1=-1.0, scalar2=None, op0=ALU.mult), ['negA'], ['negA'])

        eye8b = self.sb("eye8b", [64, 8, 64], BF16)
        self.cp('vector', eye8b[:], self.ident[0:64, 0:64].unsqueeze(1).to_broadcast([64, 8, 64]), ['ident'], ['eye8b'])
        halo = self.sb("halo", [128, 24, 3], F32)
        self.V(lambda e: e.memset(halo[:], 0.0), w=['halo'])
        Sst = self.sb("Sst", [128, 8, 128], F32)
        self.V(lambda e: e.memset(Sst[:], 0.0), w=[('Sst', 0), ('Sst', 1)])

        xT = self.sb("xT", [128, 8, TT], BF16)
        pre = [self.sb("pre%d" % i, [128, TT + 3], F32) for i in range(2)]
        yb = [self.sb("yb%d" % i, [128, TT], F32) for i in range(2)]
        sb4 = [self.sb("s%d" % i, [128, TT], F32) for i in range(4)]
        sq = [self.sb("sq%d" % i, [128, TT], F32) for i in range(2)]
        lnb = [self.sb("lnt%d" % i, [128, TT], F32) for i in range(2)]
        ktokp = self.sb("ktok", [128, 2, 8, 128], F32)
        vtokp = self.sb("vtok", [128, 2, 8, 128], F32)

        def ktok_c(c):
            return ktokp[(c // 2) * 64:(c // 2) * 64 + 64, c % 2]

        def vtok_c(c):
            return vtokp[(c // 2) * 64:(c // 2) * 64 + 64, c % 2]
        szT = self.sb("szT", [128, 8, TT], BF16)
        ogT = self.sb("ogT", [128, 8, TT], BF16)
        beta = self.sb("beta", [64, NCH, 8], F32)
        nbeta = self.sb("nbeta", [64, NCH, 8], F32)
        gt = self.sb("gt", [64, NCH, 8], F32)
        xg = self.sb("xg", [64, NCH, 8], F32)

        decT = self.sb("decT", [64, 8, 64], F32)
        eg2 = [self.sb("eg%d" % i, [128, 24], F32) for i in range(2)]
        tmpA = self.sb("tmpA", [64, 8, 64], F32)
        G1 = tmpA
        Mb = [self.sb("Mb%d" % i, [64, 8, 64], BF16) for i in range(2)]
        MTb = [self.sb("MTb%d" % i, [64, 8, 64], BF16) for i in range(2)]
        Pb = [self.sb("Pb%d" % i, [64, 8, 64], BF16) for i in range(2)]
        XTf2 = [self.sb("XTf%d" % i, [64, 8, 64], F32) for i in range(2)]
        attT2 = [self.sb("attT%d" % i, [64, 8, 64], F32) for i in range(2)]
        kdec2 = [self.sb("kdec%d" % i, [64, 8, 128], F32) for i in range(2)]
        Rp = self.sb("Rp", [64, 4, 128], F32)
        qs = self.sb("qs", [64, 4, 128], F32)
        vnew = self.sb("vnew", [64, 4, 128], F32)
        oc = self.sb("oc", [64, 8, 128], F32)
        oss = self.sb("oss", [64, 8], F32)
        orr = self.sb("orr", [64, 8], F32)
        onb = self.sb("onb", [64, 8, 128], BF16)
        t2 = self.sb("t2", [128, 4, 128], F32)
        rr = self.sb("rr", [128, 1024], F32)
        h1s = rr
        xn = rr
        h1T = self.sb("h1T", [128, 8, 128], BF16)
        bst = self.sb("bst", [128, 2, 6], F32)
        mv = self.sb("mv", [128, 2], F32)
        lrs = self.sb("lrs", [128, 1], F32)
        rstd = self.sb("rstd", [128, 1], F32)
        nb = self.sb("nb", [128, 1], F32)

        tri = self.c64v('tri')
        ntri = self.c64v('ntri')
        sup = self.c64v('sup')
        ones64 = self.c64v('ones')
        negT8 = self.c64v('negT8')
        offd8 = self.c64v('offd').unsqueeze(1).to_broadcast([64, 8, 64])
        eye8 = self.ident[0:64, 0:64].unsqueeze(1).to_broadcast([64, 8, 64])
        tri8 = tri.unsqueeze(1).to_broadcast([64, 8, 64])
        id64 = self.ident[0:64, 0:64]
        idb64 = self.identb[0:64, 0:64]
        lnqs = math.log(128.0 ** -0.5)

        dbg = self.dbg
        if 'dbg_qT' in dbg:
            d_qT = self.dout('dbg_qT', [128, 8, TT], F32)
            d_kT = self.dout('dbg_kT', [128, 8, TT], F32)
            d_vtok = self.dout('dbg_vtok', [128, 2, 8, 128], F32)
            d_ktok = self.dout('dbg_ktok', [128, 2, 8, 128], F32)
            d_beta = self.dout('dbg_beta', [64, NCH, 8], F32)
            d_g = self.dout('dbg_g', [64, NCH, 8], F32)
            d_oc = self.dout('dbg_oc', [NT * NCH, 64, 8, 128], F32)
            d_decT = self.dout('dbg_decT', [64, 8, 64], F32)
            d_XT = self.dout('dbg_XT', [64, 8, 64], F32)
            d_M = self.dout('dbg_M', [64, 8, 64], F32)
            d_ogT = self.dout('dbg_ogT', [128, 8, TT], BF16)
            d_oss = self.dout('dbg_oss', [64, 8], F32)
            d_orr = self.dout('dbg_orr', [64, 8], F32)
            d_mv = self.dout('dbg_mv', [128, 2], F32)
            d_rstd = self.dout('dbg_rstd', [128, 1], F32)
            d_bst = self.dout('dbg_bst', [128, 12], F32)
            d_rr = self.dout('dbg_rr', [2, 128, 1024], F32)
            d_wout = self.dout('dbg_wout', [128, 8, 1024], BF16)

        def bank(i, shape=None, dt=None):
            v = PB[i][:]
            if dt is not None:
                v = v.bitcast(dt)
            return v

        def tile(t):
            t0 = t * TT
            xt = xs[0]
            kx = 'xs'
            self.DMA(xt[:], self.x[t0:t0 + TT, :].rearrange("(s p) d -> p s d", p=128), w=[kx])
            pTf = PB[2][:].rearrange("p (c n) -> p c n", c=4)
            for s in range(2):
                for g4 in range(2):
                    for k4 in range(4):
                        kc = g4 * 4 + k4
                        self.tr(pTf[:, k4, :], xt[:, s, kc * 128:(kc + 1) * 128], self.ident[:], [kx, 'ident'], [('PB', 2)])
                    self.cp('scalar', xT[:, g4 * 4:g4 * 4 + 4, s * 128:(s + 1) * 128], pTf, [('PB', 2)], ['xT'])
            ppb = [0, 1, 6, 7]

            def s1_mm(blk):
                pbk = ppb[blk % 4]
                pp = PB[pbk][:, 0:TT]
                for kc in range(8):
                    self.mm(pp, w_in[:, kc, blk * 128:(blk + 1) * 128], xT[:, kc, :], kc == 0, kc == 7, ['w_in', 'xT'], [('PB', pbk)])

            def bufs(blk):
                pb = blk % 2
                sp = blk % 4
                return pb, sp, pre[pb], ('pre', pb), yb[pb], ('y', pb), sb4[sp], ('s', sp)

            def a1(blk):
                pbk = ppb[blk % 4]
                pp = PB[pbk][:, 0:TT]
                kp = ('PB', pbk)
                if blk >= 24:
                    self.act(szT[:, blk - 24, :], pp, AF.Silu, [kp], ['szT'])
                    return
                pb, sp, pr, kpr, y, ky, s_, ks = bufs(blk)
                self.cp('scalar', pr[:, 3:TT + 3], pp, [kp], [kpr])
                self.cp('gpsimd', pr[:, 0:3], halo[:, blk, :], ['halo'], [kpr])

            def a2(blk):
                if blk >= 24:
                    return
                pb, sp, pr, kpr, y, ky, s_, ks = bufs(blk)
                self.V(lambda e, pr=pr, y=y, blk=blk: e.tensor_scalar(out=y[:], in0=pr[:, 0:TT], scalar1=convw[:, blk, 0:1], scalar2=None, op0=ALU.mult),
                       [kpr, 'convw'], [ky])
                for i in range(1, 4):
                    self.V(lambda e, pr=pr, y=y, blk=blk, i=i: e.scalar_tensor_tensor(out=y[:], in0=pr[:, i:i + TT], scalar=convw[:, blk, i:i + 1], in1=y[:], op0=ALU.mult, op1=ALU.add),
                           [kpr, 'convw', ky], [ky])
                self.cp('gpsimd', halo[:, blk, :], pr[:, TT:TT + 3], [kpr], ['halo'])

            def a3(blk):
                if blk >= 24:
                    return
                pb, sp, pr, kpr, y, ky, s_, ks = bufs(blk)
                self.act(s_[:], y[:], AF.Silu, [ky], [ks])

            def b45(blk):
                if blk >= 16:
                    return
                pb, sp, pr, kpr, y, ky, s_, ks = bufs(blk)
                q_ = sq[pb]
                ksq = ('sq', pb)
                self.tt('gpsimd', q_[:], s_[:], s_[:], ALU.mult, [ks], [ksq])
                psb = 3 if pb == 0 else 5
                self.mm(PB[psb][:, 0:TT], self.ones128[:], q_[:], True, True, ['ones128', ksq], [('PB', psb)])

            def b6(blk):
                if blk >= 16:
                    return
                pb = blk % 2
                psb = 3 if pb == 0 else 5
                l_ = lnb[pb]
                kl = ('lnt', pb)
                self.act(l_[:], PB[psb][:, 0:TT], AF.Ln, [('PB', psb)], [kl], bias=EPS)
                self.act(l_[:], l_[:], AF.Exp, [kl], [kl], scale=-0.5, bias=(lnqs if blk < 8 else 0.0))

            def b78(blk):
                if blk >= 24:
                    return
                pb, sp, pr, kpr, y, ky, s_, ks = bufs(blk)
                h = blk % 8
                pkb = 4 if pb == 0 else 2
                pk = PB[pkb][0:64, :].rearrange("p (c d) -> p c d", c=NCH)
                if blk < 16:
                    l_ = lnb[pb]
                    kl = ('lnt', pb)
                    dest = qT if blk < 8 else kT
                    kd = 'qT' if blk < 8 else 'kT'
                    self.tt('vector', dest[:, h, :], s_[:], l_[:], ALU.mult, [ks, kl], [kd])
                    if blk >= 8:
                        for c in range(NCH):
                            self.tr(pk[:, c, :], kT[:, h, c * 64:(c + 1) * 64], self.ident[:], ['kT', 'ident'], [('PB', pkb)])
                        for c2 in range(2):
                            self.cp('scalar', ktokp[c2 * 64:c2 * 64 + 64, :, h, :], pk[:, c2 * 2:c2 * 2 + 2, :], [('PB', pkb)], ['ktok'])
                else:
                    for c in range(NCH):
                        self.tr(pk[:, c, :], s_[:, c * 64:(c + 1) * 64], self.ident[:], [ks, 'ident'], [('PB', pkb)])
                    for c2 in range(2):
                        self.cp('scalar', vtokp[c2 * 64:c2 * 64 + 64, :, h, :], pk[:, c2 * 2:c2 * 2 + 2, :], [('PB', pkb)], ['vtok'])

            groups = [(2 * i, 2 * i + 1) for i in range(16)]
            NG = len(groups)

            def both(fn, g):
                fn(g[0])
                fn(g[1])

            both(s1_mm, groups[0])
            both(s1_mm, groups[1])
            both(a1, groups[0])
            both(a2, groups[0])
            both(a3, groups[0])
            yield
            for gi in range(NG):
                g = groups[gi]
                gn = groups[gi + 1] if gi + 1 < NG else None
                if gi + 2 < NG:
                    both(s1_mm, groups[gi + 2])
                if gn:
                    both(a1, gn)
                both(b45, g)
                if gn:
                    both(a2, gn)
                both(b6, g)
                if gn:
                    both(a3, gn)
                both(b78, g)
            pL = PB[5][0:64, 0:NCH * 16].rearrange("p (c n) -> p c n", c=NCH)
            for c in range(NCH):
                for kc in range(8):
                    self.mm(pL[:, c, :], xT[:, kc, c * 64:(c + 1) * 64], w_in[:, kc, 4096:4112], kc == 0, kc == 7, ['xT', 'w_in'], [('PB', 5)])
            self.act(beta[:], pL[:, :, 0:8], AF.Sigmoid, [('PB', 5)], ['beta'])
            self.tt('vector', xg[:], pL[:, :, 8:16], dtb[:].unsqueeze(1).to_broadcast([64, NCH, 8]), ALU.add, [('PB', 5), 'dtb'], ['xg'])
            self.act(xg[:], xg[:], AF.Exp, ['xg'], ['xg'])
            self.act(xg[:], xg[:], AF.Ln, ['xg'], ['xg'], bias=1.0)
            self.tt('vector', gt[:], xg[:], negA[:].unsqueeze(1).to_broadcast([64, NCH, 8]), ALU.mult, ['xg', 'negA'], ['gt'])
            self.V(lambda e: e.tensor_scalar(out=nbeta[:], in0=beta[:], scalar1=-1.0, scalar2=None, op0=ALU.mult), ['beta'], ['nbeta'])
            if 'dbg_qT' in dbg and t == 0:
                self.DMA(d_qT, qT[:], ['qT'], ['dbg'])
                self.DMA(d_kT, kT[:], ['kT'], ['dbg'])
                self.DMA(d_vtok, vtokp[:], ['vtok'], ['dbg'])
                self.DMA(d_ktok, ktokp[:], ['ktok'], ['dbg'])
                self.DMA(d_beta, beta[:], ['beta'], ['dbg'])
                self.DMA(d_g, gt[:], ['gt'], ['dbg'])
                self.final_keys.append('dbg')

            def prep(c):
                cp_ = c % 2
                cs = slice(c * 64, (c + 1) * 64)
                attT = attT2[cp_]
                G2 = attT
                eg = eg2[cp_]
                kdec = kdec2[cp_]
                XTf = XTf2[cp_]
                kat = ('attT', cp_)
                keg = ('eg', cp_)
                gcv = gt[:, c, :]
                gb = gcv.unsqueeze(2).to_broadcast([64, 8, 64])
                self.cp('vector', G1[:], gb, ['gt'], ['tmpA'])
                self.tt('vector', G2[:], tri8, gb, ALU.mult, ['gt', 'c64'], [kat])
                pD = PB[7][0:64, :]
                kD = ('PB', 7)
                self.mm(pD, ones64[:, 0:64], G2[:].rearrange("p h i -> p (h i)"), True, False, ['c64', kat], [kD])
                self.mm(pD, ntri, G1[:].rearrange("p h i -> p (h i)"), False, False, ['c64', 'tmpA'], [kD])
                self.mm(pD, id64, negT8, False, True, ['c64', 'ident'], [kD])
                self.act(decT[:].rearrange("p h i -> p (h i)"), pD, AF.Exp, [kD], ['decT'])
                pG = PB[3]
                kG = ('PB', 3)
                self.mm(pG[0:64, 0:8], tri, gcv, True, True, ['c64', 'gt'], [kG])
                self.mm(pG[0:64, 8:16], sup, gcv, True, True, ['c64', 'gt'], [kG])
                self.mm(pG[:, 16:24], ones64, gcv, True, True, ['c64', 'gt'], [kG])
                self.act(eg[0:64, 0:16], pG[0:64, 0:16], AF.Exp, [kG], [keg])
                self.act(eg[:, 16:24], pG[:, 16:24], AF.Exp, [kG], [keg])
                edec = eg[0:64, 8:16]
                yield
                pA = PB[4][0:64, :].rearrange("p (h i) -> p h i", h=8)
                pQK = PB[5][0:64, :].rearrange("p (h i) -> p h i", h=8)
                for h in range(8):
                    self.mm(pA[:, h, :], kT[:, h, cs], kT[:, h, cs], True, True, ['kT'], [('PB', 4)])
                for h in range(8):
                    self.mm(pQK[:, h, :], kT[:, h, cs], qT[:, h, cs], True, True, ['kT', 'qT'], [('PB', 5)])
                M = Mb[0]
                MT = MTb[0]
                P = Pb[0]
                self.tt('vector', tmpA[:], pA, decT[:], ALU.mult, [('PB', 4), 'decT'], ['tmpA'])
                self.tt('vector', tmpA[:], tmpA[:], beta[:, c, :].unsqueeze(2).to_broadcast([64, 8, 64]), ALU.mult, ['tmpA', 'beta'], ['tmpA'])
                self.tt('gpsimd', M[:], tmpA[:], offd8, ALU.mult, ['tmpA', 'c64'], [('M', 0, 0), ('M', 0, 1)])
                yield
                self.tt('vector', attT[:], pQK, decT[:], ALU.mult, [('PB', 5), 'decT'], [kat])
                pMT = PB[6][0:64, :].bitcast(BF16)[:, 0:512].rearrange("p (h i) -> p h i", h=8)
                for h in range(8):
                    self.tr(pMT[:, h, :], M[:, h, :], idb64, [('M', 0, 0), ('M', 0, 1), 'identb'], [('PB', 6)])
                self.cp('scalar', MT[:], pMT, [('PB', 6)], [('MT', 0, 0), ('MT', 0, 1)])
                self.tt('vector', P[:], eye8b[:], M[:], ALU.subtract, ['eye8b', ('M', 0, 0), ('M', 0, 1)], [('P', 0, 0), ('P', 0, 1)])
                if c < 2:
                    self.tt('gpsimd', kdec[:], ktok_c(c), edec.unsqueeze(2).to_broadcast([64, 8, 128]), ALU.mult, ['ktok', keg], [('kdec', cp_)])
                else:
                    self.cp('scalar', kdec[:], ktok_c(c), ['ktok'], [('kdec', cp_)])
                    self.tt('gpsimd', kdec[:], kdec[:], edec.unsqueeze(2).to_broadcast([64, 8, 128]), ALU.mult, [('kdec', cp_), keg], [('kdec', cp_)])
                    self.cp('scalar', vcur2[cp_], vtok_c(c), ['vtok'], ['xs'])
                yield
                cur = 0
                ibank = [(3, 4), (5, 6)]
                for lvl in range(1, 6):
                    nxt = 1 - cur
                    M, MT, P = Mb[cur], MTb[cur], Pb[cur]
                    M2, M2T, P2 = Mb[nxt], MTb[nxt], Pb[nxt]
                    for grp in range(2):
                        gs = slice(grp * 4, grp * 4 + 4)
                        b1, b2_ = ibank[grp]
                        pM2T = PB[b1][0:64, 0:256].rearrange("p (h i) -> p h i", h=4)
                        pM2 = PB[b2_][0:64, 0:256].rearrange("p (h i) -> p h i", h=4)
                        kin = [('M', cur, grp), ('MT', cur, grp)]
                        for hh in range(4):
                            h = grp * 4 + hh
                            self.mm(pM2T[:, hh, :], M[:, h, :], MT[:, h, :], True, True, kin, [('PB', b1)])
                        self.cp('scalar', M2T[:, gs, :], pM2T, [('PB', b1)], [('MT', nxt, grp)])
                        if lvl < 5:
                            for hh in range(4):
                                h = grp * 4 + hh
                                self.mm(pM2[:, hh, :], MT[:, h, :], M[:, h, :], True, True, kin, [('PB', b2_)])
                            self.cp('vector', M2[:, gs, :], pM2, [('PB', b2_)], [('M', nxt, grp)])
                    yield
                    for grp in range(2):
                        gs = slice(grp * 4, grp * 4 + 4)
                        b1, b2_ = ibank[grp]
                        pPP = PB[b1][0:64, 0:256].rearrange("p (h i) -> p h i", h=4)
                        for hh in range(4):
                            h = grp * 4 + hh
                            self.mm(pPP[:, hh, :], M2T[:, h, :], P[:, h, :], True, True, [('MT', nxt, grp), ('P', cur, grp)], [('PB', b1)])
                        if lvl == 5:
                            self.tt('vector', XTf[:, gs, :], P[:, gs, :], pPP, ALU.add, [('P', cur, grp), ('PB', b1)], [('XT', cp_, grp)])
                        else:
                            self.tt('vector', P2[:, gs, :], P[:, gs, :], pPP, ALU.add, [('P', cur, grp), ('PB', b1)], [('P', nxt, grp)])
                    cur = nxt
                    yield
                if 'dbg_qT' in dbg and t == 0 and c == 0:
                    self.DMA(d_decT, decT[:], ['decT'], ['dbg'])
                    self.DMA(d_XT, XTf[:], [('XT', cp_, 0), ('XT', cp_, 1)], ['dbg'])

            def scan_out(c):
                cp_ = c % 2
                cs = slice(c * 64, (c + 1) * 64)
                attT = attT2[cp_]
                eg = eg2[cp_]
                kdec = kdec2[cp_]
                osq = kdec
                XT = XTf2[cp_]
                vcur = vcur2[cp_] if c >= 2 else vtok_c(c)
                kvc = 'xs' if c >= 2 else 'vtok'
                kat = ('attT', cp_)
                keg = ('eg', cp_)
                egc = eg[0:64, 0:8]
                glb = eg[:, 16:24]
                pKS = PB[0][0:64, :].rearrange("p (h d) -> p h d", h=4)
                pQS = PB[1][0:64, :].rearrange("p (h d) -> p h d", h=4)
                pS = PB[2][:, :].rearrange("p (h d) -> p h d", h=4)
                kX, kY, kZ = ('PB', 0), ('PB', 1), ('PB', 2)
                for half in range(2):
                    hs = slice(half * 4, half * 4 + 4)
                    egb = egc[:, hs].unsqueeze(2).to_broadcast([64, 4, 128])
                    for hh in range(4):
                        h = half * 4 + hh
                        self.mm(pKS[:, hh, :], kT[:, h, cs], Sst[:, h, :], True, True, ['kT', ('Sst', half)], [kX])
                    for hh in range(4):
                        h = half * 4 + hh
                        self.mm(pQS[:, hh, :], qT[:, h, cs], Sst[:, h, :], True, True, ['qT', ('Sst', half)], [kY])
                    self.tt('vector', Rp[:], pKS, egb, ALU.mult, [kX, keg], ['Rp'])
                    self.tt('vector', Rp[:], Rp[:], vcur[:, hs, :], ALU.subtract, ['Rp', kvc], ['Rp'])
                    self.tt('vector', qs[:], pQS, egb, ALU.mult, [kY, keg], ['qs'])
                    yield
                    pVN = pKS
                    for hh in range(4):
                        h = half * 4 + hh
                        self.mm(pVN[:, hh, :], XT[:, h, :], Rp[:, hh, :], True, True, [('XT', cp_, half), 'Rp'], [kX])
                    self.tt('vector', vnew[:], pVN, nbeta[:, c, hs].unsqueeze(2).to_broadcast([64, 4, 128]), ALU.mult, [kX, 'nbeta'], ['vnew'])
                    yield
                    pAV = pQS
                    for hh in range(4):
                        h = half * 4 + hh
                        self.mm(pAV[:, hh, :], attT[:, h, :], vnew[:, hh, :], True, True, [kat, 'vnew'], [kY])
                    self.tt('vector', oc[:, hs, :], pAV, qs[:], ALU.add, [kY, 'qs'], [('oc', half)])
                    for hh in range(4):
                        h = half * 4 + hh
                        self.mm(pS[:, hh, :], kdec[:, h, :], vnew[:, hh, :], True, True, [('kdec', cp_), 'vnew'], [kZ])
                    self.tt('gpsimd', t2[:], Sst[:, hs, :], glb[:, hs].unsqueeze(2).to_broadcast([128, 4, 128]), ALU.mult, [('Sst', half), keg], ['t2'])
                    self.tt('vector', Sst[:, hs, :], t2[:], pS, ALU.add, ['t2', kZ], [('Sst', half)])
                    yield
                if 'dbg_qT' in dbg:
                    self.DMA(d_oc[t * NCH + c], oc[:], [('oc', 0), ('oc', 1)], ['dbg'])
                self.tt('gpsimd', osq[:], oc[:], oc[:], ALU.mult, [('oc', 0), ('oc', 1)], [('kdec', cp_)])
                self.V(lambda e: e.tensor_reduce(out=oss[:], in_=osq[:], axis=AX.X, op=ALU.add), [('kdec', cp_)], ['oss'])
                self.act(orr[:], oss[:], AF.Ln, ['oss'], ['orr'], scale=1.0 / 128.0, bias=EPS)
                self.act(orr[:], orr[:], AF.Exp, ['orr'], ['orr'], scale=-0.5)
                yield
                self.tt('vector', onb[:], oc[:], orr[:].unsqueeze(2).to_broadcast([64, 8, 128]), ALU.mult, [('oc', 0), ('oc', 1), 'orr'], ['onb'])
                pOT = PB[2][:].bitcast(BF16)[:, 0:512].rearrange("p (h i) -> p h i", h=8)
                for h in range(8):
                    self.tr(pOT[:, h, :], onb[:, h, :], idb64, ['onb', 'identb'], [('PB', 2)])
                self.V(lambda e, cs=cs, pOT=pOT: e.scalar_tensor_tensor(out=ogT[:, :, cs], in0=pOT, scalar=normw[:, 0:1], in1=szT[:, :, cs], op0=ALU.mult, op1=ALU.mult),
                       [('PB', 2), 'normw', 'szT'], ['ogT'])
                yield

            for c in range(NCH + 1):
                gens = []
                if c < NCH:
                    gens.append(prep(c))
                if c >= 1:
                    gens.append(scan_out(c - 1))
                while gens:
                    for g_ in list(gens):
                        try:
                            next(g_)
                        except StopIteration:
                            gens.remove(g_)
            yield
            self.DMA(w_out, w_out_d, r=['w_out_d'], w=['qT', 'kT'])
            if 'dbg_qT' in dbg and t == 0:
                self.DMA(d_ogT, ogT[:], ['ogT'], ['dbg'])
                self.DMA(d_wout, w_out, ['qT', 'kT'], ['dbg'])
            for s in range(2):
                pY = [PB[0][:], PB[1][:]]
                for half in range(2):
                    for h in range(8):
                        self.mm(pY[half], ogT[:, h, s * 128:(s + 1) * 128], w_out[:, h, half * 512:(half + 1) * 512], h == 0, h == 7, ['ogT', 'qT', 'kT'], [('PB', half)])
                self.DMA(rr[:], self.x[t0 + s * 128:t0 + (s + 1) * 128, :], w=['rr'])
                for half in range(2):
                    self.V(lambda e, half=half: e.scalar_tensor_tensor(out=rr[:, half * 512:(half + 1) * 512], in0=rr[:, half * 512:(half + 1) * 512], scalar=ALPHA, in1=pY[half], op0=ALU.mult, op1=ALU.add),
                           ['rr', ('PB', half)], ['rr'])
                    self.V(lambda e, half=half: e.bn_stats(out=bst[:, half, :], in_=rr[:, half * 512:(half + 1) * 512]), ['rr'], ['bst'])
                if 'dbg_qT' in dbg and t == 0:
                    self.DMA(d_rr[s], rr[:], ['rr'], ['dbg'])
                self.V(lambda e: e.bn_aggr(out=mv[:], in_=bst[:].rearrange("p a b -> p (a b)")), ['bst'], ['mv'])
                self.act(lrs[:], mv[:, 1:2], AF.Ln, ['mv'], ['lrs'], bias=EPS)
                self.act(rstd[:], lrs[:], AF.Exp, ['lrs'], ['rstd'], scale=-0.5)
                self.V(lambda e: e.tensor_scalar(out=nb[:], in0=mv[:, 0:1], scalar1=rstd[:, 0:1], scalar2=-1.0, op0=ALU.mult, op1=ALU.mult), ['mv', 'rstd'], ['nb'])
                if 'dbg_qT' in dbg and t == 0 and s == 0:
                    self.DMA(d_mv, mv[:], ['mv'], ['dbg'])
                    self.DMA(d_rstd, rstd[:], ['rstd'], ['dbg'])
                    self.DMA(d_bst, bst[:].rearrange("p a b -> p (a b)"), ['bst'], ['dbg'])
                self.act(xn[:], rr[:], AF.Identity, ['rr', 'rstd', 'nb'], ['rr'], scale=rstd[:, 0:1], bias=nb[:, 0:1])
                self.tt('gpsimd', xn[:], xn[:], lnw_b[:], ALU.mult, ['rr', 'lnw_b'], ['rr'])
                self.tt('gpsimd', h1s[:], xn[:], lnb_b[:], ALU.add, ['rr', 'lnb_b'], ['rr'])
                r0 = t0 + s * 128
                self.DMA(self.h1_d[r0:r0 + 128, :], h1s[:], ['rr'], [('h1_d', t, s)])
                self.final_keys.append(('h1_d', t, s))
                pTf = PB[2][:].rearrange("p (c n) -> p c n", c=4)
                for g4 in range(2):
                    for k4 in range(4):
                        kc = g4 * 4 + k4
                        self.tr(pTf[:, k4, :], h1s[:, kc * 128:(kc + 1) * 128], self.ident[:], ['rr', 'ident'], [('PB', 2)])
                    self.cp('scalar', h1T[:, g4 * 4:g4 * 4 + 4, :], pTf, [('PB', 2)], ['h1T'])
                self.DMA(self.h1T_d[:, :, r0:r0 + 128], h1T[:], ['h1T'], [('h1T_d', t, s)])
                self.final_keys.append(('h1T_d', t, s))

        tiles = [tile(t) for t in range(NT)]
        next(tiles[0])
        for t in range(NT):
            next(tiles[t])
            if t + 1 < NT:
                next(tiles[t + 1])
            for _ in tiles[t]:
                pass

    def rope_tm(self, out4, x4, cs, nh, t1, t2, kx, kout):
        cosb = cs[:, 0:32].unsqueeze(1).unsqueeze(1).to_broadcast([128, nh, 2, 32])
        sinb = cs[:, 32:64].unsqueeze(1).to_broadcast([128, nh, 32])
        self.tt('vector', t1, x4, cosb, ALU.mult, [kx, 'cs'], ['rt1'])
        self.tt('gpsimd', t2[:, :, 0, :], x4[:, :, 1, :], sinb, ALU.mult, [kx, 'cs'], ['rt2'])
        self.tt('gpsimd', t2[:, :, 1, :], x4[:, :, 0, :], sinb, ALU.mult, [kx, 'cs'], ['rt2'])
        self.tt('vector', out4[:, :, 0, :], t1[:, :, 0, :], t2[:, :, 0, :], ALU.subtract, ['rt1', 'rt2'], [kout])
        self.tt('vector', out4[:, :, 1, :], t1[:, :, 1, :], t2[:, :, 1, :], ALU.add, ['rt1', 'rt2'], [kout])

    def phase2(self):
        PB = self.PB
        s_w_kv = self.din("s_w_kv", [1024, 1536])
        b_w_in = self.din("b_w_in", [1024, 4144])
        rope_cs = self.din("rope_cs", [T, 64])
        rope_q = self.din("rope_q", [T // 2, 64])
        bw_d = self.din("bw", [128, 2, 2])
        bw = self.sb("p2bw", [128, 2, 2], F32)
        self.DMA(bw[:], bw_d, w=['bw'])
        hA = self.sb("p2hA", [128, 8, 128], BF16)
        hB = self.sb("p2hB", [128, 8, 128], BF16)
        wkv = self.sb("wkv", [128, 8, 1536], BF16)
        wq = self.sb("wq", [128, 8, 1024], BF16)
        wz = self.sb("wz", [128, 8, 3072], BF16)
        wg = self.sb("wg", [128, 8, 48], BF16)
        for kc in range(8):
            rs = slice(kc * 128, (kc + 1) * 128)
            self.DMA(wkv[:, kc, :], s_w_kv[rs, :], w=['wkv'], eng='gpsimd')
            self.DMA(wq[:, kc, :], b_w_in[rs, 0:1024], w=['wq'], eng='gpsimd')
            self.DMA(wz[:, kc, :], b_w_in[rs, 1024:4096], w=['wz'], eng='gpsimd')
            self.DMA(wg[:, kc, :], b_w_in[rs, 4096:4144], w=['wg'], eng='gpsimd')
        h1T = [self.sb("p2h1T%d" % i, [128, 8, 128], BF16) for i in range(2)]
        cs = [self.sb("p2cs%d" % i, [128, 64], F32) for i in range(2)]
        kvs2 = [self.sb("kvs%d" % i, [128, 1536], F32) for i in range(2)]
        qs2_ = [self.sb("qs_%d" % i, [128, 1024], F32) for i in range(2)]
        t12 = [self.sb("rt1%d" % i, [128, 1024], F32) for i in range(2)]
        t22 = [self.sb("rt2%d" % i, [128, 1024], F32) for i in range(2)]
        krb2 = [self.sb("krb%d" % i, [128, 4, 256], BF16) for i in range(2)]
        qrb2 = [self.sb("qrb%d" % i, [128, 1024], BF16) for i in range(2)]
        vst2 = [self.sb("vst%d" % i, [128, 2, 4, 65], BF16) for i in range(2)]
        kT42 = [self.sb("kT4%d" % i, [64, 16, 128], BF16) for i in range(2)]
        qTt2 = [self.sb("qTt%d" % i, [64, 16, 128], BF16) for i in range(2)]
        zs3 = [self.sb("zs%d" % i, [128, 1024], F32) for i in range(3)]
        gts2 = [self.sb("gts%d" % i, [128, 48], F32) for i in range(2)]
        gz2 = [self.sb("gz%d" % i, [128, 3, 1024], BF16) for i in range(2)]
        for i in range(2):
            self.V(lambda e, i=i: e.memset(vst2[i][:], 1.0), w=[('vst', i)])
        P2KEYS = ['kvs', 'qs_', 'rt1', 'rt2', 'krb', 'qrb', 'vst', 'kT4', 'qTt', 'zs', 'gts', 'gz']

        pk = [PB[3][0:64, :].bitcast(BF16).rearrange("p (a t) -> p a t", a=8), PB[4][0:64, :].bitcast(BF16).rearrange("p (a t) -> p a t", a=8)]

        def kvA(qb):
            par = qb % 2
            self.S.kmap = {k: (k, par) for k in P2KEYS}
            kvs = kvs2[par]
            t0 = qb * 128
            hT = h1T[par]
            kh = ('p2h1T', par)
            self.DMA(hT[:], self.h1T_d[:, :, t0:t0 + 128], r=['h1T_all'], w=[kh])
            self.DMA(cs[par][:], rope_cs[t0:t0 + 128, :], w=[('cs', par)])
            for j in range(3):
                for kc in range(8):
                    self.mm(PB[j][:], hT[:, kc, :], wkv[:, kc, j * 512:(j + 1) * 512], kc == 0, kc == 7, [kh, 'wkv'], [('PB', j)])
                self.cp('scalar', kvs[:, j * 512:(j + 1) * 512], PB[j][:], [('PB', j)], ['kvs'])

        def kvB(qb):
            par = qb % 2
            self.S.kmap = {k: (k, par) for k in P2KEYS}
            self.S.kmap['cs'] = ('cs', par)
            kvs, t1, t2, krb, vst, kT4 = kvs2[par], t12[par], t22[par], krb2[par], vst2[par], kT42[par]
            t0 = qb * 128
            c_ = cs[par]
            for i, c0 in enumerate((512, 1024)):
                x4 = kvs[:, c0:c0 + 256].rearrange("p (g a d) -> p g a d", g=4, a=2)
                o4 = krb[:, i, :].rearrange("p (g a d) -> p g a d", g=4, a=2)
                self.rope_tm(o4, x4, c_, 4, t1[:, 0:256].rearrange("p (g a d) -> p g a d", g=4, a=2),
                             t2[:, 0:256].rearrange("p (g a d) -> p g a d", g=4, a=2), 'kvs', 'krb')
            self.cp('gpsimd', krb[:, 2, :], kvs[:, 0:256], ['kvs'], ['krb'])
            self.cp('gpsimd', krb[:, 3, :], kvs[:, 256:512], ['kvs'], ['krb'])
            self.cp('vector', vst[:, 0, :, 0:64], kvs[:, 768:1024].rearrange("p (g d) -> p g d", g=4), ['kvs'], ['vst'])
            self.cp('vector', vst[:, 1, :, 0:64], kvs[:, 1280:1536].rearrange("p (g d) -> p g d", g=4), ['kvs'], ['vst'])
            for i in range(4):
                for g in range(4):
                    a = i * 4 + g
                    self.tr(pk[a // 8][:, a % 8, :], krb[:, i, g * 64:(g + 1) * 64], self.identb[:], ['krb', 'identb'], [('PB', 3 + a // 8)])
            self.cp('scalar', kT4[:, 0:8, :], pk[0], [('PB', 3)], ['kT4'])
            self.cp('scalar', kT4[:, 8:16, :], pk[1], [('PB', 4)], ['kT4'])
            for i, dst in enumerate((self.kselT_d, self.kwinT_d, self.kcsT_d, self.vcsT_d)):
                self.DMA(dst[:, :, t0:t0 + 128].rearrange("g d t -> d g t"), kT4[:, i * 4:(i + 1) * 4, :], r=['kT4'], w=[('kvT_d', qb, i)])
            self.DMA(self.vsel_d[:, :, qb, :].rearrange("g p c -> p g c"), vst[:, 0], r=['vst'], w=[('vsel_d', qb)])
            self.DMA(self.vwin_d[:, :, qb, :].rearrange("g p c -> p g c"), vst[:, 1], r=['vst'], w=[('vwin_d', qb)])

        NKV = T // 128
        kvA(0)
        for qb in range(NKV):
            if qb + 1 < NKV:
                kvA(qb + 1)
            kvB(qb)

        def qA(slot):
            par = slot % 2
            self.S.kmap = {k: (k, par) for k in P2KEYS}
            qs_, gts, gz = qs2_[par], gts2[par], gz2[par]
            e2 = slot % 2
            t0 = slot * 128
            hT = h1T[par]
            kh = ('p2h1T', par)
            self.DMA(hA[:], self.h1T_d[:, :, (2 * slot) * 128:(2 * slot + 1) * 128], w=['p2hA'])
            self.DMA(hB[:], self.h1T_d[:, :, (2 * slot + 1) * 128:(2 * slot + 2) * 128], w=['p2hB'])
            self.DMA(cs[par][:], rope_q[t0:t0 + 128, :], w=[('cs', par)])
            self.V(lambda e, hT=hT, e2=e2: e.tensor_scalar(out=hT[:], in0=hA[:], scalar1=bw[:, e2, 0:1], scalar2=None, op0=ALU.mult), ['p2hA', 'bw'], [kh])
            self.V(lambda e, hT=hT, e2=e2: e.scalar_tensor_tensor(out=hT[:], in0=hB[:], scalar=bw[:, e2, 1:2], in1=hT[:], op0=ALU.mult, op1=ALU.add), ['p2hB', 'bw', kh], [kh])
            for j in range(2):
                for kc in range(8):
                    self.mm(PB[5 + j][:], hT[:, kc, :], wq[:, kc, j * 512:(j + 1) * 512], kc == 0, kc == 7, [kh, 'wq'], [('PB', 5 + j)])
                self.S.op('scalar', lambda e, j=j, qs_=qs_: e.mul(out=qs_[:, j * 512:(j + 1) * 512], in_=PB[5 + j][:], mul=0.125), [('PB', 5 + j)], ['qs_'])
            pg = PB[7][:, 0:48]
            for kc in range(8):
                self.mm(pg, hT[:, kc, :], wg[:, kc, :], kc == 0, kc == 7, [kh, 'wg'], [('PB', 7)])
            self.act(gts[:], pg, AF.Sigmoid, [('PB', 7)], ['gts'])
            for br in range(3):
                zs = zs3[br]
                for j in range(2):
                    pz = PB[j][:]
                    for kc in range(8):
                        self.mm(pz, hT[:, kc, :], wz[:, kc, br * 1024 + j * 512:br * 1024 + (j + 1) * 512], kc == 0, kc == 7, [kh, 'wz'], [('PB', j)])
                    self.act(zs[:, j * 512:(j + 1) * 512], pz, AF.Silu, [('PB', j)], [('zs3', br)])
                self.tt('vector' if br != 1 else 'gpsimd', gz[:, br, :].rearrange("p (h d) -> p h d", h=16), zs[:].rearrange("p (h d) -> p h d", h=16),
                        gts[:, br * 16:(br + 1) * 16].unsqueeze(2).to_broadcast([128, 16, 64]), ALU.mult, [('zs3', br), 'gts'], ['gz'])
            self.DMA(self.gz_d[t0:t0 + 128], gz[:], r=['gz'], w=[('gz_d', slot)])

        def qB(slot):
            par = slot % 2
            self.S.kmap = {k: (k, par) for k in P2KEYS}
            self.S.kmap['cs'] = ('cs', par)
            qs_, t1, t2, qrb, qTt = qs2_[par], t12[par], t22[par], qrb2[par], qTt2[par]
            c_ = cs[par]
            v16 = "p (g a d) -> p g a d"
            self.rope_tm(qrb[:].rearrange(v16, g=16, a=2), qs_[:].rearrange(v16, g=16, a=2), c_, 16,
                         t1[:].rearrange(v16, g=16, a=2), t2[:].rearrange(v16, g=16, a=2), 'qs_', 'qrb')
            for hh in range(16):
                self.tr(pk[hh // 8][:, hh % 8, :], qrb[:, hh * 64:(hh + 1) * 64], self.identb[:], ['qrb', 'identb'], [('PB', 3 + hh // 8)])
            self.cp('scalar', qTt[:, 0:8, :], pk[0], [('PB', 3)], ['qTt'])
            self.cp('scalar', qTt[:, 8:16, :], pk[1], [('PB', 4)], ['qTt'])
            self.DMA(self.qT_d[:, slot].rearrange("g d (h t) -> d g h t", h=4), qTt[:].rearrange("d (g h) t -> d g h t", g=4), r=['qTt'], w=[('qT_d', slot)])

        NSL = T // 256
        qA(0)
        for slot in range(NSL):
            if slot + 1 < NSL:
                qA(slot + 1)
            qB(slot)

        self.S.kmap = {}

    def phase3(self):
        PB = self.PB
        s_pe = [self.din("s_pe_k", [32, 64]), self.din("s_pe_v", [32, 64])]
        s_w1 = [self.din("s_w1_k", [32, 64, 128]), self.din("s_w1_v", [32, 64, 128])]
        s_w2 = [self.din("s_w2_k", [128, 64]), self.din("s_w2_v", [128, 64])]
        cmp_cs = self.din("cmp_cs", [64, 1024])
        ovm = self.din("ovm", [512, 128])
        w1 = [self.sb("w1_%d" % i, [64, 32, 128], BF16) for i in range(2)]
        w2 = [self.sb("w2_%d" % i, [128, 64], BF16) for i in range(2)]
        w2s = self.sb("w2s", [128, 64], BF16)
        pe32 = self.sb("pe32", [32, 2, 64], F32)
        peT = self.sb("peT", [64, 2, 32], BF16)
        bias = self.sb("cbias", [128, 2], F32)
        ccs = self.sb("ccs", [64, 1024], F32)
        src = self.sb("csrc", [64, T], BF16)
        hs = self.sb("chs", [128, 512], BF16)
        kx = self.sb("ckx", [64, 512], F32)
        kxs = self.sb("ckxs", [64, 512], F32)
        self.DMA(ccs[:], cmp_cs, w=['ccs'])
        for i in range(2):
            self.DMA(w1[i][:], s_w1[i].rearrange("c d h -> d c h"), w=[('w1', i)], eng='gpsimd')
            self.DMA(w2[i][:], s_w2[i], w=[('w2', i)], eng='gpsimd')
            self.DMA(pe32[:, i, :], s_pe[i], w=['pe32'])
        self.cp('vector', w2s[:, 0:32], w2[0][:, 32:64], [('w2', 0)], ['w2s'])
        self.cp('vector', w2s[:, 32:64], w2[0][:, 0:32], [('w2', 0)], ['w2s'])
        for i in range(2):
            pT = PB[0][0:64, i * 32:(i + 1) * 32]
            self.tr(pT, pe32[:, i, :], self.ident[0:32, 0:32], ['pe32', 'ident'], [('PB', 0)])
            self.cp('vector', peT[:, i, :], pT, [('PB', 0)], ['peT'])
        for i in range(2):
            pb_ = PB[1][:, i:i + 1]
            for c in range(32):
                self.mm(pb_, w1[i][:, c, :], peT[:, i, c:c + 1], c == 0, c == 31, [('w1', i), 'peT'], [('PB', 1)])
            self.cp('vector', bias[:, i:i + 1], pb_, [('PB', 1)], ['cbias'])
        self.V(lambda e: e.memset(self.vcaug[:, :, :, 64:65], 1.0), w=['vcaug'])
        for g in range(4):
            self.DMA(self.vcaug[:, g, :, 65:193], ovm.rearrange("(n p) s -> p n s", p=128), w=['vcaug'], eng='gpsimd')
        self.V(lambda e: e.memset(hs[:, 511:512], 0.0), w=['chs'])
        for i in range(2):
            srcd = self.kcsT_d if i == 0 else self.vcsT_d
            for g in range(4):
                self.DMA(src[:], srcd[g], r=['kvT_all'], w=['csrc'])
                s3 = src[:].rearrange("p (n r) -> p n r", r=16)
                ph = PB[2][:, 0:511]
                for c in range(32):
                    rhs = s3[:, 0:511, c] if c < 16 else s3[:, 1:512, c - 16]
                    self.mm(ph, w1[i][:, c, :], rhs, c == 0, c == 31, [('w1', i), 'csrc'], [('PB', 2)])
                self.act(hs[:, 0:511], ph, AF.Silu, [('PB', 2), 'cbias'], ['chs'], bias=bias[:, i:i + 1])
                if i == 0:
                    pk = PB[3][0:64, :]
                    pks = PB[4][0:64, :]
                    self.mm(pk, w2[0][:], hs[:], True, True, [('w2', 0), 'chs'], [('PB', 3)])
                    self.mm(pks, w2s[:], hs[:], True, True, ['w2s', 'chs'], [('PB', 4)])
                    self.tt('vector', kx[:], pk, ccs[:, 0:512], ALU.mult, [('PB', 3), 'ccs'], ['ckx'])
                    self.tt('vector', kxs[:], pks, ccs[:, 512:1024], ALU.mult, [('PB', 4), 'ccs'], ['ckxs'])
                    self.tt('vector', self.kcmpT[:, g, :], kx[:], kxs[:], ALU.add, ['ckx', 'ckxs'], ['kcmpT'])
                else:
                    pv = PB[5][:, 0:256].rearrange("p (n d) -> p n d", n=4)
                    for nt in range(4):
                        self.mm(pv[:, nt, :], hs[:, nt * 128:(nt + 1) * 128], w2[1][:], True, True, ['chs', ('w2', 1)], [('PB', 5)])
                    self.cp('vector', self.vcaug[:, g, :, 0:64], pv, [('PB', 5)], ['vcaug'])
        if 'dbg_kcmpT' in self.dbg:
            d1 = self.dout('dbg_kcmpT', [64, 4, 512], BF16)
            d2 = self.dout('dbg_vcaug', [128, 4, 4, 193], BF16)
            self.DMA(d1, self.kcmpT[:], ['kcmpT'], ['dbgk'])
            self.DMA(d2, self.vcaug[:], ['vcaug'], ['dbgk'])

    def phase4(self):
        PB = self.PB
        NQB = T // 256 if self.nqb4 is None else self.nqb4
        cmask_d = self.din("cmask_c", [128, 32, 4, 128], BF16)
        dmask_d = self.din("dmask", [128, 2, 2, 128])
        wmask_d = self.din("wmask", [128, 2, 6, 128])
        btab = self.din("btab_c", [32, 128, 128])
        cmk = [self.sb("cmk%d" % i, [128, 4, 128], BF16) for i in range(2)]
        dmk = self.sb("dmk", [128, 2, 2, 128], BF16)
        wmk = self.sb("wmk", [128, 2, 6, 128], BF16)
        self.DMA(dmk[:], dmask_d, w=['dmk'], eng='gpsimd')
        self.DMA(wmk[:], wmask_d, w=['wmk'], eng='gpsimd')
        kselT = self.sb("kselT", [64, T], BF16)
        kwinT = self.sb("kwinT", [64, T], BF16)
        vsel = self.sb("vsel", [128, 64, 65], BF16)
        vwin = self.sb("vwin", [128, 64, 65], BF16)
        qTb = [self.sb("qTb%d" % i, [64, 512], BF16) for i in range(2)]
        Btb = [self.sb("Btb%d" % i, [128, 128], F32) for i in range(2)]
        gzb = [self.sb("gzb%d" % i, [128, 3, 256], BF16) for i in range(2)]
        Eb = [self.sb("Eb%d" % i, [128, 4, 128], BF16) for i in range(3)]
        Pb_ = [self.sb("Pb_%d" % i, [128, 4, 128], BF16) for i in range(2)]
        rden = self.sb("rden", [128, 3, 4], F32)
        imp = self.sb("imp", [128, 128], F32)
        score = self.sb("score", [128, 128], F32)
        sc2 = self.sb("sc2", [128, 128], F32)
        m8 = self.sb("m8", [128, 16], F32)
        selb = self.sb("selb", [128, 128], BF16)
        selx = self.sb("selx", [128, 128, 64], BF16)
        tmp = self.sb("etmp", [128, 4, 64], F32)
        tmp2 = self.sb("etmp2", [128, 4, 64], F32)
        acc = self.sb("eacc", [128, 4, 64], F32)
        ogt = [self.sb("ogt%d" % i, [128, 256], BF16) for i in range(2)]
        ecnt = [0]
        pcnt = [0]
        scnt = [0]

        def qk_exp(kT_tile, kkeys, qT, kq):
            i = scnt[0] % 2
            scnt[0] += 1
            pS = PB[i][:]
            self.mm(pS, kT_tile, qT[:], True, True, list(kkeys) + [kq], [('PB', i)])
            j = ecnt[0] % 3
            ecnt[0] += 1
            E = Eb[j]
            self.act(E[:].rearrange("p h q -> p (h q)"), pS, AF.Exp, [('PB', i)], [('E', j)])
            return E, ('E', j)

        def loads(g, qb):
            t0 = qb * 128
            b2 = qb % 2
            self.DMA(qTb[b2][:], self.qT_d[g, qb], w=[('qTb', b2)])
            self.DMA(Btb[b2][:], btab[qb], w=[('Btb', b2)])
            self.DMA(gzb[b2][:], self.gz_d[t0:t0 + 128, :, g * 256:(g + 1) * 256], w=[('gzb', b2)])
            self.DMA(cmk[b2][:], cmask_d[:, qb], w=[('cmk', b2)])

        def make_items(g, qb):
            items = []
            t0 = qb * 128
            b2 = qb % 2
            qT = qTb[b2]
            kq = ('qTb', b2)
            Bt = Btb[b2]
            gz = gzb[b2]
            e2 = qb % 2
            qbm = 2 * qb + 1
            ntmax = (8 * qbm + 6) // 128
            pc = [PB[3][:, 0:386].rearrange("p (h c) -> p h c", h=2), PB[4][:, 0:386].rearrange("p (h c) -> p h c", h=2)]

            def cmp_post():
                for hb in range(2):
                    self.V(lambda e, hb=hb: e.tensor_scalar(out=rden[:, 0, hb * 2:hb * 2 + 2], in0=pc[hb][:, :, 64], scalar1=1e-30, scalar2=None, op0=ALU.max),
                           [('PB', 3 + hb)], ['rden0'])
                self.V(lambda e: e.reciprocal(out=rden[:, 0, :], in_=rden[:, 0, :]), ['rden0'], ['rden0'])
                for h in range(4):
                    src = pc[h // 2][:, h % 2, 65:193]
                    if h == 0:
                        self.V(lambda e, src=src: e.tensor_scalar(out=imp[:], in0=src, scalar1=rden[:, 0, 0:1], scalar2=None, op0=ALU.mult), [('PB', 3), 'rden0'], ['imp'])
                    else:
                        self.V(lambda e, src=src, h=h: e.scalar_tensor_tensor(out=imp[:], in0=src, scalar=rden[:, 0, h:h + 1], in1=imp[:], op0=ALU.mult, op1=ALU.add),
                               [('PB', 3 + h // 2), 'rden0', 'imp'], ['imp'])
                self.tt('vector', score[:], imp[:], Bt[:], ALU.add, ['imp', ('Btb', b2)], ['score'])
                self.V(lambda e: e.max(out=m8[:, 0:8], in_=score[:]), ['score'], ['m8'])
                self.V(lambda e: e.match_replace(out=sc2[:], in_to_replace=m8[:, 0:8], in_values=score[:], imm_value=-1e9), ['score', 'm8'], ['sc2'])
                self.V(lambda e: e.max(out=m8[:, 8:16], in_=sc2[:]), ['sc2'], ['m8'])
                nbk = 2 * (qbm + 1)
                self.V(lambda e: e.tensor_scalar(out=selx[:, 0:nbk, :], in0=score[:, 0:nbk].unsqueeze(2).to_broadcast([128, nbk, 64]), scalar1=m8[:, 15:16], scalar2=None, op0=ALU.is_ge),
                       ['score', 'm8'], ['selx'])
                for hb in range(2):
                    self.tt('vector', tmp[:, hb * 2:hb * 2 + 2, :], pc[hb][:, :, 0:64], rden[:, 0, hb * 2:hb * 2 + 2].unsqueeze(2).to_broadcast([128, 2, 64]), ALU.mult,
                            [('PB', 3 + hb), 'rden0'], ['etmp'])
                self.tt('gpsimd', acc[:], tmp[:], gz[:, 0, :].rearrange("p (h d) -> p h d", h=4), ALU.mult, ['etmp', ('gzb', b2)], ['eacc'])
                if self.dbg4 is not None and g == 0 and qb == self.dbg4:
                    self.V(lambda e: e.tensor_scalar(out=selb[:], in0=score[:], scalar1=m8[:, 15:16], scalar2=None, op0=ALU.is_ge), ['score', 'm8'], ['selb'])
                    self.DMA(self.d4['imp'], imp[:], ['imp'], ['dbg4'])
                    self.DMA(self.d4['sel'], selb[:], ['selb'], ['dbg4'])
                    self.DMA(self.d4['ocmp'], tmp[:], ['etmp'], ['dbg4'])

            for nt in range(ntmax + 1):
                it = {'mdep': False, 'M': None}

                def A(it=it, nt=nt):
                    it['E'], it['kE'] = qk_exp(self.kcmpT[:, g, nt * 128:(nt + 1) * 128], ['kcmpT'], qT, kq)

                def B(it=it, nt=nt):
                    E, kE = it['E'], it['kE']
                    self.tt('vector', E[:], E[:], cmk[b2][:, nt, :].unsqueeze(1).to_broadcast([128, 4, 128]), ALU.mult, [kE, ('cmk', b2)], [kE])
                    for h in range(4):
                        self.mm(pc[h // 2][:, h % 2, :], E[:, h, :], self.vcaug[:, g, nt, :], nt == 0 and h % 2 == 0, nt == ntmax and h % 2 == 1, [kE, 'vcaug'], [('PB', 3 + h // 2)])
                    if nt == ntmax:
                        cmp_post()
                it['A'], it['B'] = A, B
                items.append(it)

            for br in (1, 2):
                pacc = PB[4 + br][:, 0:260].rearrange("p (h c) -> p h c", h=4)
                kacc = ('PB', 4 + br)
                if br == 1:
                    kts = list(range(0, qbm + 1))
                    kT_, kkey, V_, vkey = kselT, 'kselT', vsel, 'vsel'
                else:
                    kts = [kt for kt in range(qbm - 5, qbm + 1) if kt >= 0]
                    kT_, kkey, V_, vkey = kwinT, 'kwinT', vwin, 'vwin'

                def br_post(br=br, pacc=pacc, kacc=kacc):
                    kr = 'rden%d' % br
                    self.V(lambda e: e.reciprocal(out=rden[:, br, :], in_=pacc[:, :, 64]), [kacc], [kr])
                    self.tt('vector', tmp[:], pacc[:, :, 0:64], rden[:, br, :].unsqueeze(2).to_broadcast([128, 4, 64]), ALU.mult, [kacc, kr], ['etmp'])
                    if self.dbg4 is not None and g == 0 and qb == self.dbg4:
                        self.DMA(self.d4['osel' if br == 1 else 'owin'], tmp[:], ['etmp'], ['dbg4'])
                    self.tt('gpsimd', tmp2[:], tmp[:], gz[:, br, :].rearrange("p (h d) -> p h d", h=4), ALU.mult, ['etmp', ('gzb', b2)], ['etmp2'])
                    if br == 1:
                        self.tt('gpsimd', acc[:], acc[:], tmp2[:], ALU.add, ['eacc', 'etmp2'], ['eacc'])
                    else:
                        og = ogt[b2]
                        self.tt('gpsimd', og[:].rearrange("p (h d) -> p h d", h=4), acc[:], tmp2[:], ALU.add, ['eacc', 'etmp2'], [('ogt', b2)])
                        self.DMA(self.og_d[t0:t0 + 128, g * 256:(g + 1) * 256], og[:], r=[('ogt', b2)], w=[('og_d', g, qb)])

                for kt in kts:
                    it = {'mdep': (br == 1 and kt == kts[0]), 'M': None}

                    def A(it=it, kt=kt, kT_=kT_, kkey=kkey):
                        it['E'], it['kE'] = qk_exp(kT_[:, kt * 128:(kt + 1) * 128], [kkey], qT, kq)

                    def M(it=it, kt=kt):
                        pM = PB[2 if kt % 2 == 0 else 7][:].bitcast(BF16)[:, 0:128]
                        kM = ('PB', 2 if kt % 2 == 0 else 7)
                        self.tr(pM, selx[:, 2 * kt:2 * kt + 2, :].rearrange("p a k -> p (a k)"), self.identb[:], ['selx', 'identb'], [kM])
                        it['pM'], it['kM'] = pM, kM

                    def B(it=it, kt=kt, br=br, kts=kts, pacc=pacc, kacc=kacc, V_=V_, vkey=vkey, br_post=br_post):
                        E, kE = it['E'], it['kE']
                        if br == 1:
                            ip = pcnt[0] % 2
                            pcnt[0] += 1
                            P = Pb_[ip]
                            kP = ('P4', ip)
                            self.tt('vector', P[:], E[:], it['pM'].unsqueeze(1).to_broadcast([128, 4, 128]), ALU.mult, [kE, it['kM']], [kP])
                            if kt >= qbm - 1:
                                self.tt('gpsimd', P[:], P[:], dmk[:, e2, kt - (qbm - 1), :].unsqueeze(1).to_broadcast([128, 4, 128]), ALU.mult, [kP, 'dmk'], [kP])
                        else:
                            P, kP = E, kE
                            wi = kt - (qbm - 5)
                            if wi not in (2, 3):
                                self.tt('gpsimd', P[:], P[:], wmk[:, e2, wi, :].unsqueeze(1).to_broadcast([128, 4, 128]), ALU.mult, [kP, 'wmk'], [kP])
                        for h in range(4):
                            self.mm(pacc[:, h, :], P[:, h, :], V_[:, kt, :], kt == kts[0] and h == 0, kt == kts[-1] and h == 3, [kP, vkey], [kacc])
                        if kt == kts[-1]:
                            br_post()
                    it['A'], it['B'] = A, B
                    if br == 1:
                        it['M'] = M
                    items.append(it)
            return items

        for g in range(4):
            self.DMA(kselT[:], self.kselT_d[g], w=['kselT'])
            self.DMA(kwinT[:], self.kwinT_d[g], w=['kwinT'])
            self.DMA(vsel[:], self.vsel_d[g], w=['vsel'])
            self.DMA(vwin[:], self.vwin_d[g], w=['vwin'])
            loads(g, 0)
            items = []
            for qb in range(NQB):
                if qb + 1 < NQB:
                    items.append({'load': (g, qb + 1)})
                items += make_items(g, qb)
            work = [it for it in items if 'load' not in it]
            pos = 0
            load_at = {}
            for it in items:
                if 'load' in it:
                    load_at.setdefault(pos, []).append(it['load'])
                else:
                    pos += 1
            n = len(work)
            done_loads = set()

            def do_loads(upto):
                for p_ in sorted(load_at):
                    if p_ <= upto and p_ not in done_loads:
                        done_loads.add(p_)
                        for l in load_at[p_]:
                            loads(*l)

            do_loads(0)
            for j in range(min(2, n)):
                work[j]['A']()
            if n > 0 and work[0]['M'] is not None:
                work[0]['M']()
            for i in range(n):
                do_loads(i)
                if i + 2 < n:
                    work[i + 2]['A']()
                nxt = work[i + 1] if i + 1 < n else None
                if nxt is not None and nxt['M'] is not None and not nxt['mdep']:
                    nxt['M']()
                work[i]['B']()
                if nxt is not None and nxt['M'] is not None and nxt['mdep']:
                    nxt['M']()

    def phase5(self):
        PB = self.PB
        NQB = T // 256 if self.nqb4 is None else self.nqb4
        bw_d = self.din("bw", [128, 2, 2]) if 'bw' not in self.inputs else self.inputs['bw'].ap()
        bw = self.sb("p5bw", [128, 2, 2], F32)
        self.DMA(bw[:], bw_d, w=['p5bw'])
        hB = [self.sb("p5hB%d" % i, [128, 1024], F32) for i in range(2)]
        b_w_out = self.din("b_w_out", [1024, 1024])
        b_ln_w = self.din("b_ln_w", [1, 1024])
        b_ln_b = self.din("b_ln_b", [1, 1024])
        out = self.dout("out", [T // 2, D], F32)
        w_out = self.sb("p5wout", [128, 8, 1024], BF16)
        self.DMA(w_out[:], b_w_out.rearrange("(c p) n -> p c n", p=128), w=['p5wout'], eng='gpsimd')
        lnw_b = self.sb("p5lnw", [128, 1024], F32)
        lnb_b = self.sb("p5lnb", [128, 1024], F32)
        self.DMA(lnw_b[:], b_ln_w.partition_broadcast(128), w=['p5lnw'])
        self.DMA(lnb_b[:], b_ln_b.partition_broadcast(128), w=['p5lnb'])
        ogs = [self.sb("p5og%d" % i, [128, 1024], BF16) for i in range(2)]
        h1s = [self.sb("p5h1%d" % i, [128, 1024], F32) for i in range(2)]
        ogT = self.sb("p5ogT", [128, 8, 128], BF16)
        rr = self.sb("p5rr", [128, 1024], F32)
        xo = [self.sb("p5xo%d" % i, [128, 1024], F32) for i in range(2)]
        bst = self.sb("p5bst", [128, 2, 6], F32)
        mv = self.sb("p5mv", [128, 2], F32)
        lrs = self.sb("p5lrs", [128, 1], F32)
        rstd = self.sb("p5rstd", [128, 1], F32)
        nb = self.sb("p5nb", [128, 1], F32)
        for qb in range(NQB):
            t0 = qb * 128
            b2 = qb % 2
            og = ogs[b2]
            h1 = h1s[b2]
            xn = xo[b2]
            self.DMA(og[:], self.og_d[t0:t0 + 128, :], w=[('p5og', b2)])
            e2 = qb % 2
            hb_ = hB[b2]
            self.DMA(h1[:], self.h1_d[(2 * qb) * 128:(2 * qb + 1) * 128, :], w=[('p5h1', b2)])
            self.DMA(hb_[:], self.h1_d[(2 * qb + 1) * 128:(2 * qb + 2) * 128, :], w=[('p5hB', b2)])
            self.V(lambda e, h1=h1, e2=e2: e.tensor_scalar(out=h1[:], in0=h1[:], scalar1=bw[:, e2, 0:1], scalar2=None, op0=ALU.mult), [('p5h1', b2), 'p5bw'], [('p5h1', b2)])
            self.V(lambda e, h1=h1, hb_=hb_, e2=e2: e.scalar_tensor_tensor(out=h1[:], in0=hb_[:], scalar=bw[:, e2, 1:2], in1=h1[:], op0=ALU.mult, op1=ALU.add),
                   [('p5hB', b2), 'p5bw', ('p5h1', b2)], [('p5h1', b2)])
            pTb = PB[2][:].bitcast(BF16).rearrange("p (c n) -> p c n", c=8)
            for kc in range(8):
                self.tr(pTb[:, kc, :], og[:, kc * 128:(kc + 1) * 128], self.identb[:], [('p5og', b2), 'identb'], [('PB', 2)])
            self.cp('scalar', ogT[:], pTb, [('PB', 2)], ['p5ogT'])
            pY = [PB[0][:], PB[1][:]]
            for half in range(2):
                for kc in range(8):
                    self.mm(pY[half], ogT[:, kc, :], w_out[:, kc, half * 512:(half + 1) * 512], kc == 0, kc == 7, ['p5ogT', 'p5wout'], [('PB', half)])
            for half in range(2):
                self.V(lambda e, half=half, h1=h1: e.scalar_tensor_tensor(out=rr[:, half * 512:(half + 1) * 512], in0=h1[:, half * 512:(half + 1) * 512], scalar=ALPHA, in1=pY[half], op0=ALU.mult, op1=ALU.add),
                       [('p5h1', b2), ('PB', half)], ['p5rr'])
                self.V(lambda e, half=half: e.bn_stats(out=bst[:, half, :], in_=rr[:, half * 512:(half + 1) * 512]), ['p5rr'], ['p5bst'])
            self.V(lambda e: e.bn_aggr(out=mv[:], in_=bst[:].rearrange("p a b -> p (a b)")), ['p5bst'], ['p5mv'])
            self.act(lrs[:], mv[:, 1:2], AF.Ln, ['p5mv'], ['p5lrs'], bias=EPS)
            self.act(rstd[:], lrs[:], AF.Exp, ['p5lrs'], ['p5rstd'], scale=-0.5)
            self.V(lambda e: e.tensor_scalar(out=nb[:], in0=mv[:, 0:1], scalar1=rstd[:, 0:1], scalar2=-1.0, op0=ALU.mult, op1=ALU.mult), ['p5mv', 'p5rstd'], ['p5nb'])
            self.act(xn[:], rr[:], AF.Identity, ['p5rr', 'p5rstd', 'p5nb'], [('p5xo', b2)], scale=rstd[:, 0:1], bias=nb[:, 0:1])
            self.tt('gpsimd', xn[:], xn[:], lnw_b[:], ALU.mult, [('p5xo', b2), 'p5lnw'], [('p5xo', b2)])
            self.tt('gpsimd', xn[:], xn[:], lnb_b[:], ALU.add, [('p5xo', b2), 'p5lnb'], [('p5xo', b2)])
            self.DMA(out[t0:t0 + 128, :], xn[:], r=[('p5xo', b2)], w=[('out', qb)])
            self.final_keys.append(('out', qb))


def _in_maps(b, inputs):
    hc = host_consts()
    maps = []
    for core in range(8):
        bi = core // 2
        m = dict(hc)
        m.update(core_consts(core % 2, hc))
        m['x'] = inputs['x'][bi]
        m['a_w_in'] = inputs['a_w_in'][0]
        m['a_conv_w'] = inputs['a_conv_w'][0]
        m['a_a_log'] = inputs['a_a_log'].reshape(1, 8)
        m['a_dt_bias'] = inputs['a_dt_bias'].reshape(1, 8)
        m['a_norm_w'] = inputs['a_norm_w'].reshape(128, 1)
        m['a_w_out'] = inputs['a_w_out'][0]
        m['a_ln_w'] = inputs['a_ln_w'].reshape(1, 1024)
        m['a_ln_b'] = inputs['a_ln_b'].reshape(1, 1024)
        for k in ('s_w_kv', 's_pe_k', 's_pe_v', 's_w1_k', 's_w2_k', 's_w1_v', 's_w2_v'):
            m[k] = inputs[k]
        m['b_w_in'] = inputs['b_w_in'][0]
        m['b_w_out'] = inputs['b_w_out'][0]
        m['b_ln_w'] = inputs['b_ln_w'].reshape(1, 1024)
        m['b_ln_b'] = inputs['b_ln_b'].reshape(1, 1024)
        maps.append({k: np.ascontiguousarray(v if k == 'cmask_c' else np.asarray(v, dtype=np.float32)) for k, v in m.items() if k in b.inputs})
    return maps


def kernel(**inputs):
    inputs = {k: np.asarray(v) for k, v in inputs.items()}
    import os
    ph = tuple(os.environ.get('KPHASES', 'p1,p2,p3,p4,p5').split(','))
    b = Builder(phases=ph)
    nc = b.build()
    maps = _in_maps(b, inputs)
    res = run_bass_kernel_spmd(nc, maps, core_ids=list(range(8)))
    if 'out' not in b.outputs:
        return np.zeros((4, T, D), np.float32)
    out = np.zeros((4, T, D), np.float32)
    for core in range(8):
        bi, p = core // 2, core % 2
        o = np.asarray(res.results[core]['out'], dtype=np.float32)
        for j in range(32):
            qb = slot_qb(p, j)
            out[bi, qb * 128:(qb + 1) * 128] = o[j * 128:(j + 1) * 128]
    return out
```

```python
import math
from contextlib import ExitStack

import numpy as np
import concourse.bass as bass
import concourse.mybir as mybir
from concourse.bass_utils import run_bass_kernel_spmd

F32 = mybir.dt.float32
BF16 = mybir.dt.bfloat16
AF = mybir.ActivationFunctionType
ALU = mybir.AluOpType
AX = mybir.AxisListType

ENGS = ('sync', 'gpsimd', 'scalar', 'vector', 'tensor')

T = 8192
D = 1024
NH = 8
EPS = 1e-6
ALPHA = 4.0 ** 0.25


class Sched:
    def __init__(self, nc, csems, dsems):
        self.nc = nc
        self.csem = csems
        self.dsems = dsems
        self.ops = {e: [] for e in ENGS}
        self.cnt = {e: 0 for e in ENGS}
        self.dcount = [0] * len(dsems)
        nd = len(dsems)
        self.dpool = {'sync': list(range(0, nd - 4)), 'gpsimd': list(range(nd - 4, nd))}
        self.dptr = {'sync': 0, 'gpsimd': 0}
        self.lastw = {}
        self.readers = {}
        self.waited = {e: {} for e in ENGS}
        self.nops = 0

    def _sem(self, sk):
        return self.csem[sk[1]] if sk[0] == 'c' else self.dsems[sk[1]]

    kmap = {}

    def _expand(self, keys):
        out = []
        for k in keys:
            if isinstance(k, str):
                k = self.kmap.get(k, k)
            if isinstance(k, tuple) and len(k) == 2 and k[0] == 'PB':
                out.append(('PB', k[1], 0))
                out.append(('PB', k[1], 1))
            else:
                out.append(k)
        return out

    def op(self, eng, fn, reads=(), writes=(), dma=False):
        need = {}
        reads = self._expand(reads)
        writes = self._expand(writes)

        def want(tok):
            sk, val, src = tok
            if sk[0] == 'c' and src == eng and eng == 'tensor':
                return
            if need.get(sk, 0) < val:
                need[sk] = val

        for k in reads:
            t = self.lastw.get(k)
            if t is not None:
                want(t)
        for k in writes:
            t = self.lastw.get(k)
            if t is not None:
                want(t)
            for t in self.readers.get(k, ()):
                want(t)
        if dma:
            pool = self.dpool[eng]
            i = pool[self.dptr[eng] % len(pool)]
            self.dptr[eng] += 1
            if self.dcount[i] > 0:
                want((('d', i), 16 * self.dcount[i], None))
            self.dcount[i] += 1
            tok = (('d', i), 16 * self.dcount[i], eng)
            inc = 16
        else:
            self.cnt[eng] += 1
            tok = (('c', eng), self.cnt[eng], eng)
            inc = 1
        w = self.waited[eng]
        waits = []
        for sk, val in need.items():
            if w.get(sk, 0) < val:
                w[sk] = val
                waits.append((self._sem(sk), val))
        self.ops[eng].append((waits, fn, self._sem(tok[0]), inc))
        for k in writes:
            self.lastw[k] = tok
            self.readers[k] = []
        for k in reads:
            lst = self.readers.setdefault(k, [])
            if len(lst) < 64:
                lst.append(tok)
            else:
                d = {}
                for t in lst + [tok]:
                    if d.get(t[0], (0,))[0] < t[1]:
                        d[t[0]] = (t[1], t[2])
                self.readers[k] = [(sk, v[0], v[1]) for sk, v in d.items()]
        self.nops += 1
        return tok

    def wait_all(self, eng, keys):
        need = {}
        for k in keys:
            t = self.lastw.get(k)
            if t is not None:
                sk, val, src = t
                if need.get(sk, 0) < val:
                    need[sk] = val
        waits = [(self._sem(sk), val) for sk, val in need.items()]
        self.ops[eng].append((waits, None, None, 0))

    def drain_dmas(self, eng):
        waits = [(self.dsems[i], 16 * self.dcount[i]) for i in range(len(self.dsems)) if self.dcount[i] > 0]
        self.ops[eng].append((waits, None, None, 0))
        for i in range(len(self.dsems)):
            self.waited[eng][('d', i)] = 16 * self.dcount[i]

    def emit(self):
        nc = self.nc
        with nc.Block() as block:
            for e in ENGS:
                ops = self.ops[e]
                if not ops:
                    continue

                def body(engine, ops=ops):
                    for waits, fn, sem, inc in ops:
                        for s, v in waits:
                            engine.wait_ge(s, v)
                        if fn is not None:
                            ins = fn(engine)
                            ins.then_inc(sem, inc)

                getattr(block, e)(body)
        self.ops = {e: [] for e in ENGS}


def host_consts():
    c = {}
    half = 32
    inv = (np.float32(10000.0) ** (-(np.arange(half, dtype=np.float32) / np.float32(half)))).astype(np.float32)
    pos = np.arange(T, dtype=np.float32)
    ang = (pos[:, None] * inv[None, :]).astype(np.float32)
    c['rope_cs'] = np.concatenate([np.cos(ang), np.sin(ang)], axis=1).astype(np.float32)
    pc = (np.arange(512, dtype=np.float32) * 16 + 31).astype(np.float32)
    angc = (pc[None, :] * inv[:, None]).astype(np.float32)
    cosF = np.concatenate([np.cos(angc), np.cos(angc)], axis=0)
    sinF = np.concatenate([-np.sin(angc), np.sin(angc)], axis=0)
    c['cmp_cs'] = np.concatenate([cosF, sinF], axis=1).astype(np.float32)
    st = np.arange(512)[:, None] * 16
    bs = np.arange(128)[None, :] * 64
    ov = np.clip(np.minimum(st + 32, bs + 64) - np.maximum(st, bs), 0, None) / 32.0
    ov[511] = 0.0
    c['ovm'] = ov.astype(np.float32)
    n = np.arange(128)[:, None, None]
    dl = np.arange(17)[None, :, None]
    i = np.arange(128)[None, None, :]
    c['cmpmask'] = (16 * n + 31 - i <= 128 * dl).astype(np.float32)
    kk = np.arange(128)[:, None]
    qq = np.arange(128)[None, :]
    c['cwmask'] = np.stack([(kk <= qq), (kk > qq)], axis=1).astype(np.float32)
    bt = np.zeros((64, 128, 128), np.float32)
    for qb in range(64):
        t = qb * 128 + np.arange(128)
        cur = t // 64
        blk = np.arange(128)[None, :]
        forced = (blk == 0) | (blk == cur[:, None]) | (blk == cur[:, None] - 1)
        vis = blk * 64 <= t[:, None]
        bt[qb] = np.where(vis, np.where(forced, 1.0e4, 0.0), -1.0)
    c['btab'] = bt
    c['ident'] = np.eye(128, dtype=np.float32)
    k = np.arange(64)
    tri = (k[:, None] <= k[None, :]).astype(np.float32)
    sup = (k[:, None] > k[None, :]).astype(np.float32)
    negT = np.where(k[None, :] < k[:, None], -1e30, 0.0).astype(np.float32)
    offd = (k[:, None] != k[None, :]).astype(np.float32)
    eye = np.eye(64, dtype=np.float32)
    c64 = np.concatenate([
        tri, -tri, sup, np.ones((64, 128), np.float32), -np.ones((64, 64), np.float32),
        np.tile(negT[:, None, :], (1, 8, 1)).reshape(64, 512), offd,
    ], axis=1)
    c['c64'] = np.ascontiguousarray(c64)
    return c


def slot_qb(p, j):
    first = (p == (j % 2))
    return 2 * j if first else 2 * j + 1


def core_consts(p, hc):
    import ml_dtypes
    c = {}
    bw = np.zeros((128, 2, 2), np.float32)
    for e in range(2):
        first = (p == e)
        bw[:, e, 0] = 1.0 if first else 0.0
        bw[:, e, 1] = 0.0 if first else 1.0
    c['bw'] = bw
    qbs = [slot_qb(p, j) for j in range(32)]
    c['rope_q'] = np.concatenate([hc['rope_cs'][qb * 128:(qb + 1) * 128] for qb in qbs], axis=0)
    c['btab_c'] = np.stack([hc['btab'][qb] for qb in qbs], axis=0)
    n = np.arange(128)[:, None, None, None]
    nt = np.arange(4)[None, None, :, None]
    i = np.arange(128)[None, None, None, :]
    qbv = np.array(qbs)[None, :, None, None]
    c['cmask_c'] = (16 * (128 * nt + n) + 31 <= 128 * qbv + i).astype(ml_dtypes.bfloat16)
    kk = np.arange(128)[:, None]
    qq = np.arange(128)[None, :]
    caus = (kk <= qq).astype(np.float32)
    win = (kk > qq).astype(np.float32)
    one = np.ones((128, 128), np.float32)
    zero = np.zeros((128, 128), np.float32)
    dm = np.zeros((128, 2, 2, 128), np.float32)
    wm = np.zeros((128, 2, 6, 128), np.float32)
    for e in range(2):
        first = (p == e)
        dl = [caus, zero] if first else [one, caus]
        wl = [win, one, one, one, caus, zero] if first else [zero, win, one, one, one, caus]
        for a, m_ in enumerate(dl):
            dm[:, e, a, :] = m_
        for a, m_ in enumerate(wl):
            wm[:, e, a, :] = m_
    c['dmask'] = dm
    c['wmask'] = wm
    return c


C64_OFF = {}
_o = 0
for _n, _w in [('tri', 64), ('ntri', 64), ('sup', 64), ('ones', 128), ('nones', 64),
               ('negT8', 512), ('offd', 64)]:
    C64_OFF[_n] = (_o, _o + _w)
    _o += _w
C64_W = _o


class Builder:
    def __init__(self, phases=('p1',), dbg=None, ntiles1=None, nqb4=None, dbg4=None):
        self.nqb4 = nqb4
        self.dbg4 = dbg4
        self.phases = phases
        self.dbg = dbg or {}
        self.ntiles1 = ntiles1
        self.nc = bass.Bass("TRN2", target_bir_lowering=False)
        self.es = ExitStack()
        self.inputs = {}
        self.outputs = {}

    def din(self, name, shape, dt=F32):
        t = self.nc.dram_tensor(name, list(shape), dt, kind="ExternalInput")
        self.inputs[name] = t
        return t.ap()

    def dscratch(self, name, shape, dt):
        if name in self.dbg:
            t = self.nc.dram_tensor(name, list(shape), dt, kind="ExternalOutput")
            self.outputs[name] = t
        else:
            t = self.nc.dram_tensor(name, list(shape), dt)
        return t.ap()

    def dout(self, name, shape, dt):
        t = self.nc.dram_tensor(name, list(shape), dt, kind="ExternalOutput")
        self.outputs[name] = t
        return t.ap()

    def sb(self, name, shape, dt):
        return self.pes.enter_context(self.nc.sbuf_tensor(name, list(shape), dt))

    def ps(self, name, shape, dt):
        return self.es.enter_context(self.nc.psum_tensor(name, list(shape), dt))

    def V(self, fn, r=(), w=()):
        self.S.op('vector', fn, r, w)

    def A(self, fn, r=(), w=()):
        self.S.op('scalar', fn, r, w)

    def G(self, fn, r=(), w=()):
        self.S.op('gpsimd', fn, r, w)

    def PE(self, fn, r=(), w=()):
        self.S.op('tensor', fn, r, w)

    def DMA(self, out, in_, r=(), w=(), eng='sync', **kw):
        self.S.op(eng, lambda e: e.dma_start(out=out, in_=in_, **kw), r, w, dma=True)

    def mm(self, out, lhsT, rhs, start, stop, r, w):
        self.S.op('tensor', lambda e: e.matmul(out, lhsT=lhsT, rhs=rhs, start=start, stop=stop), r, w)

    def tr(self, out, in_, ident, r, w):
        self.S.op('tensor', lambda e: e.transpose(out=out, in_=in_, identity=ident), r, w)

    def act(self, out, in_, func, r, w, **kw):
        self.S.op('scalar', lambda e: e.activation(out=out, in_=in_, func=func, **kw), r, w)

    def tt(self, eng, out, in0, in1, op, r, w):
        self.S.op(eng, lambda e: e.tensor_tensor(out=out, in0=in0, in1=in1, op=op), r, w)

    def cp(self, eng, out, in_, r, w):
        if eng == 'scalar':
            self.S.op(eng, lambda e: e.copy(out=out, in_=in_), r, w)
        else:
            self.S.op(eng, lambda e: e.tensor_copy(out=out, in_=in_), r, w)

    def build(self):
        nc = self.nc
        es = self.es
        with es:
            csems = {e: es.enter_context(nc.semaphore("c_" + e)) for e in ENGS}
            dsems = [es.enter_context(nc.semaphore("d%d" % i)) for i in range(12)]
            self.S = Sched(nc, csems, dsems)
            self.pes = es
            self.PB = [es.enter_context(nc.psum_tensor("pb%d" % i, [128, 512], F32)) for i in range(8)]
            self.setup_common()
            if 'p1' in self.phases:
                with ExitStack() as pes:
                    self.pes = pes
                    self.phase1()
                    self.S.drain_dmas('sync')
                    self.S.emit()
            for ph in ('p2', 'p3', 'p4', 'p5'):
                if ph in self.phases:
                    with ExitStack() as pes:
                        self.pes = pes
                        getattr(self, 'phase' + ph[1])()
                        self.S.drain_dmas('sync')
                        self.S.emit()
            self.pes = es
            self.finish()
            self.S.emit()
        return nc

    def finish(self):
        if not self.outputs:
            o = self.dout("out", [T // 2, D], F32)
            self.DMA(o[0:128, 0:128], self.ident[:], r=['ident'], w=['dummy_out'])
            self.final_keys.append('dummy_out')
        self.S.wait_all('sync', list(self.final_keys))

    def setup_common(self):
        self.final_keys = []
        x = self.din("x", [T, D])
        self.x = x
        self.c_ident = self.din("ident", [128, 128])
        self.c_c64 = self.din("c64", [64, C64_W])
        self.ident = self.sb("ident_sb", [128, 128], F32)
        self.identb = self.sb("identb_sb", [128, 128], BF16)
        self.c64 = self.sb("c64_sb", [64, C64_W], F32)
        self.DMA(self.ident[:], self.c_ident, w=['ident'])
        self.DMA(self.c64[:], self.c_c64, w=['c64'])
        self.cp('vector', self.identb[:], self.ident[:], ['ident'], ['identb'])
        self.ones128 = self.sb("ones128", [128, 128], F32)
        self.V(lambda e: e.memset(self.ones128[:], 1.0), w=['ones128'])
        if 'p1' in self.phases:
            self.h1_d = self.dscratch("h1_d", [T, D], F32)
            self.h1T_d = self.dscratch("h1T_d", [128, 8, T], BF16)
        else:
            self.h1_d = self.din("h1_d", [T, D], F32)
            self.h1T_d = self.din("h1T_d", [128, 8, T], BF16)
        self.kselT_d = self.dscratch("kselT_d", [4, 64, T], BF16)
        self.kwinT_d = self.dscratch("kwinT_d", [4, 64, T], BF16)
        self.kcsT_d = self.dscratch("kcsT_d", [4, 64, T], BF16)
        self.vcsT_d = self.dscratch("vcsT_d", [4, 64, T], BF16)
        self.vsel_d = self.dscratch("vsel_d", [4, 128, T // 128, 65], BF16)
        self.vwin_d = self.dscratch("vwin_d", [4, 128, T // 128, 65], BF16)
        self.qT_d = self.dscratch("qT_d", [4, 32, 64, 512], BF16)
        self.gz_d = self.dscratch("gz_d", [T // 2, 3, 1024], BF16)
        self.og_d = self.dscratch("og_d", [T // 2, 1024], BF16)
        if self.dbg4 is not None:
            self.d4 = {'imp': self.dout('dbg_imp', [128, 128], F32), 'sel': self.dout('dbg_sel', [128, 128], BF16),
                       'ocmp': self.dout('dbg_ocmp', [128, 4, 64], F32), 'osel': self.dout('dbg_osel', [128, 4, 64], F32),
                       'owin': self.dout('dbg_owin', [128, 4, 64], F32)}
            self.final_keys.append('dbg4')
        self.kcmpT = self.sb("kcmpT", [64, 4, 512], BF16)
        self.vcaug = self.sb("vcaug", [128, 4, 4, 193], BF16)

    def c64v(self, name, heads=False):
        a, b = C64_OFF[name]
        v = self.c64[:, a:b]
        if heads:
            v = v.rearrange("p (h i) -> p h i", h=8)
        return v

    def phase1(self):
        nc = self.nc
        S = self.S
        PB = self.PB
        TT = 256
        NCH = 4
        NT = T // TT if self.ntiles1 is None else self.ntiles1
        a_w_in = self.din("a_w_in", [1024, 4112])
        a_conv_w = self.din("a_conv_w", [4, 3072])
        a_a_log = self.din("a_a_log", [1, 8])
        a_dt_bias = self.din("a_dt_bias", [1, 8])
        a_norm_w = self.din("a_norm_w", [128, 1])
        a_w_out = self.din("a_w_out", [1024, 1024])
        a_ln_w = self.din("a_ln_w", [1, 1024])
        a_ln_b = self.din("a_ln_b", [1, 1024])

        w_in = self.sb("w_in_sb", [128, 8, 4112], BF16)
        qkT = self.sb("qkT", [128, 2, 8, TT], F32)
        qT = qkT[:, 0]
        kT = qkT[:, 1]
        w_out = qkT[:].rearrange("p a h t -> p (a h t)").bitcast(BF16).rearrange("p (c n) -> p c n", c=8)
        w_out_d = self.dscratch("w_out_bf_d", [128, 8, 1024], BF16)
        for kc in range(8):
            self.DMA(w_in[:, kc, :], a_w_in[kc * 128:(kc + 1) * 128, :], w=['w_in'], eng='gpsimd')
        self.DMA(w_out, a_w_out.rearrange("(c p) n -> p c n", p=128), w=['qT', 'kT'], eng='gpsimd')
        self.DMA(w_out_d, w_out, r=['qT', 'kT'], w=['w_out_d'])
        xs = [self.sb("xs0", [128, 2, 1024], F32)]
        vcur2 = [xs[0][0:64, i, :].rearrange("p (h d) -> p h d", h=8) for i in range(2)]
        cw4 = xs[0][0:4].rearrange("p s d -> p (s d)")
        convw = self.sb("convw", [128, 24, 4], F32)
        pcw = PB[0][:, 0:96].rearrange("p (b i) -> p b i", i=4)
        for part in range(2):
            nb_ = 16 if part == 0 else 8
            self.DMA(cw4[:, 0:nb_ * 128], a_conv_w[:, part * 2048:part * 2048 + nb_ * 128], w=['xs'])
            for b in range(nb_):
                self.tr(pcw[:, part * 16 + b, :], cw4[:, b * 128:(b + 1) * 128], self.ident[0:4, 0:4], ['xs', 'ident'], [('PB', 0)])
        self.cp('vector', convw[:], pcw, [('PB', 0)], ['convw'])
        normw = self.sb("normw", [128, 1], F32)
        self.DMA(normw[:], a_norm_w, w=['normw'])
        lnw_b = self.sb("lnw_b", [128, 1024], F32)
        lnb_b = self.sb("lnb_b", [128, 1024], F32)
        self.DMA(lnw_b[:], a_ln_w.partition_broadcast(128), w=['lnw_b'])
        self.DMA(lnb_b[:], a_ln_b.partition_broadcast(128), w=['lnb_b'])
        dtb = self.sb("dtb", [64, 8], F32)
        alog = self.sb("alog", [64, 8], F32)
        negA = self.sb("negA", [64, 8], F32)
        self.DMA(dtb[:], a_dt_bias.partition_broadcast(64), w=['dtb'])
        self.DMA(alog[:], a_a_log.partition_broadcast(64), w=['alog'])
        self.act(negA[:], alog[:], AF.Exp, ['alog'], ['negA'])
        self.V(lambda e: e.tensor_scalar(out=negA[:], in0=negA[:], scalar1=-1.0, scalar2=None, op0=ALU.mult), ['negA'], ['negA'])

        eye8b = self.sb("eye8b", [64, 8, 64], BF16)
        self.cp('vector', eye8b[:], self.ident[0:64, 0:64].unsqueeze(1).to_broadcast([64, 8, 64]), ['ident'], ['eye8b'])
        halo = self.sb("halo", [128, 24, 3], F32)
        self.V(lambda e: e.memset(halo[:], 0.0), w=['halo'])
        Sst = self.sb("Sst", [128, 8, 128], F32)
        self.V(lambda e: e.memset(Sst[:], 0.0), w=[('Sst', 0), ('Sst', 1)])

        xT = self.sb("xT", [128, 8, TT], BF16)
        pre = [self.sb("pre%d" % i, [128, TT + 3], F32) for i in range(2)]
        yb = [self.sb("yb%d" % i, [128, TT], F32) for i in range(2)]
        sb4 = [self.sb("s%d" % i, [128, TT], F32) for i in range(4)]
        sq = [self.sb("sq%d" % i, [128, TT], F32) for i in range(2)]
        lnb = [self.sb("lnt%d" % i, [128, TT], F32) for i in range(2)]
        ktokp = self.sb("ktok", [128, 2, 8, 128], F32)
        vtokp = self.sb("vtok", [128, 2, 8, 128], F32)

        def ktok_c(c):
            return ktokp[(c // 2) * 64:(c // 2) * 64 + 64, c % 2]

        def vtok_c(c):
            return vtokp[(c // 2) * 64:(c // 2) * 64 + 64, c % 2]
        szT = self.sb("szT", [128, 8, TT], BF16)
        ogT = self.sb("ogT", [128, 8, TT], BF16)
        beta = self.sb("beta", [64, NCH, 8], F32)
        nbeta = self.sb("nbeta", [64, NCH, 8], F32)
        gt = self.sb("gt", [64, NCH, 8], F32)
        xg = self.sb("xg", [64, NCH, 8], F32)

        decT = self.sb("decT", [64, 8, 64], F32)
        eg2 = [self.sb("eg%d" % i, [128, 24], F32) for i in range(2)]
        tmpA = self.sb("tmpA", [64, 8, 64], F32)
        G1 = tmpA
        Mb = [self.sb("Mb%d" % i, [64, 8, 64], BF16) for i in range(2)]
        MTb = [self.sb("MTb%d" % i, [64, 8, 64], BF16) for i in range(2)]
        Pb = [self.sb("Pb%d" % i, [64, 8, 64], BF16) for i in range(2)]
        XTf2 = [self.sb("XTf%d" % i, [64, 8, 64], F32) for i in range(2)]
        attT2 = [self.sb("attT%d" % i, [64, 8, 64], F32) for i in range(2)]
        kdec2 = [self.sb("kdec%d" % i, [64, 8, 128], F32) for i in range(2)]
        Rp = self.sb("Rp", [64, 4, 128], F32)
        qs = self.sb("qs", [64, 4, 128], F32)
        vnew = self.sb("vnew", [64, 4, 128], F32)
        oc = self.sb("oc", [64, 8, 128], F32)
        oss = self.sb("oss", [64, 8], F32)
        orr = self.sb("orr", [64, 8], F32)
        onb = self.sb("onb", [64, 8, 128], BF16)
        t2 = self.sb("t2", [128, 4, 128], F32)
        rr = self.sb("rr", [128, 1024], F32)
        h1s = rr
        xn = rr
        h1T = self.sb("h1T", [128, 8, 128], BF16)
        bst = self.sb("bst", [128, 2, 6], F32)
        mv = self.sb("mv", [128, 2], F32)
        lrs = self.sb("lrs", [128, 1], F32)
        rstd = self.sb("rstd", [128, 1], F32)
        nb = self.sb("nb", [128, 1], F32)

        tri = self.c64v('tri')
        ntri = self.c64v('ntri')
        sup = self.c64v('sup')
        ones64 = self.c64v('ones')
        negT8 = self.c64v('negT8')
        offd8 = self.c64v('offd').unsqueeze(1).to_broadcast([64, 8, 64])
        eye8 = self.ident[0:64, 0:64].unsqueeze(1).to_broadcast([64, 8, 64])
        tri8 = tri.unsqueeze(1).to_broadcast([64, 8, 64])
        id64 = self.ident[0:64, 0:64]
        idb64 = self.identb[0:64, 0:64]
        lnqs = math.log(128.0 ** -0.5)

        dbg = self.dbg
        if 'dbg_qT' in dbg:
            d_qT = self.dout('dbg_qT', [128, 8, TT], F32)
            d_kT = self.dout('dbg_kT', [128, 8, TT], F32)
            d_vtok = self.dout('dbg_vtok', [128, 2, 8, 128], F32)
            d_ktok = self.dout('dbg_ktok', [128, 2, 8, 128], F32)
            d_beta = self.dout('dbg_beta', [64, NCH, 8], F32)
            d_g = self.dout('dbg_g', [64, NCH, 8], F32)
            d_oc = self.dout('dbg_oc', [NT * NCH, 64, 8, 128], F32)
            d_decT = self.dout('dbg_decT', [64, 8, 64], F32)
            d_XT = self.dout('dbg_XT', [64, 8, 64], F32)
            d_M = self.dout('dbg_M', [64, 8, 64], F32)
            d_ogT = self.dout('dbg_ogT', [128, 8, TT], BF16)
            d_oss = self.dout('dbg_oss', [64, 8], F32)
            d_orr = self.dout('dbg_orr', [64, 8], F32)
            d_mv = self.dout('dbg_mv', [128, 2], F32)
            d_rstd = self.dout('dbg_rstd', [128, 1], F32)
            d_bst = self.dout('dbg_bst', [128, 12], F32)
            d_rr = self.dout('dbg_rr', [2, 128, 1024], F32)
            d_wout = self.dout('dbg_wout', [128, 8, 1024], BF16)

        def bank(i, shape=None, dt=None):
            v = PB[i][:]
            if dt is not None:
                v = v.bitcast(dt)
            return v

        def tile(t):
            t0 = t * TT
            xt = xs[0]
            kx = 'xs'
            self.DMA(xt[:], self.x[t0:t0 + TT, :].rearrange("(s p) d -> p s d", p=128), w=[kx])
            pTf = PB[2][:].rearrange("p (c n) -> p c n", c=4)
            for s in range(2):
                for g4 in range(2):
                    for k4 in range(4):
                        kc = g4 * 4 + k4
                        self.tr(pTf[:, k4, :], xt[:, s, kc * 128:(kc + 1) * 128], self.ident[:], [kx, 'ident'], [('PB', 2)])
                    self.cp('scalar', xT[:, g4 * 4:g4 * 4 + 4, s * 128:(s + 1) * 128], pTf, [('PB', 2)], ['xT'])
            ppb = [0, 1, 6, 7]

            def s1_mm(blk):
                pbk = ppb[blk % 4]
                pp = PB[pbk][:, 0:TT]
                for kc in range(8):
                    self.mm(pp, w_in[:, kc, blk * 128:(blk + 1) * 128], xT[:, kc, :], kc == 0, kc == 7, ['w_in', 'xT'], [('PB', pbk)])

            def bufs(blk):
                pb = blk % 2
                sp = blk % 4
                return pb, sp, pre[pb], ('pre', pb), yb[pb], ('y', pb), sb4[sp], ('s', sp)

            def a1(blk):
                pbk = ppb[blk % 4]
                pp = PB[pbk][:, 0:TT]
                kp = ('PB', pbk)
                if blk >= 24:
                    self.act(szT[:, blk - 24, :], pp, AF.Silu, [kp], ['szT'])
                    return
                pb, sp, pr, kpr, y, ky, s_, ks = bufs(blk)
                self.cp('scalar', pr[:, 3:TT + 3], pp, [kp], [kpr])
                self.cp('gpsimd', pr[:, 0:3], halo[:, blk, :], ['halo'], [kpr])

            def a2(blk):
                if blk >= 24:
                    return
                pb, sp, pr, kpr, y, ky, s_, ks = bufs(blk)
                self.V(lambda e, pr=pr, y=y, blk=blk: e.tensor_scalar(out=y[:], in0=pr[:, 0:TT], scalar1=convw[:, blk, 0:1], scalar2=None, op0=ALU.mult),
                       [kpr, 'convw'], [ky])
                for i in range(1, 4):
                    self.V(lambda e, pr=pr, y=y, blk=blk, i=i: e.scalar_tensor_tensor(out=y[:], in0=pr[:, i:i + TT], scalar=convw[:, blk, i:i + 1], in1=y[:], op0=ALU.mult, op1=ALU.add),
                           [kpr, 'convw', ky], [ky])
                self.cp('gpsimd', halo[:, blk, :], pr[:, TT:TT + 3], [kpr], ['halo'])

            def a3(blk):
                if blk >= 24:
                    return
                pb, sp, pr, kpr, y, ky, s_, ks = bufs(blk)
                self.act(s_[:], y[:], AF.Silu, [ky], [ks])

            def b45(blk):
                if blk >= 16:
                    return
                pb, sp, pr, kpr, y, ky, s_, ks = bufs(blk)
                q_ = sq[pb]
                ksq = ('sq', pb)
                self.tt('gpsimd', q_[:], s_[:], s_[:], ALU.mult, [ks], [ksq])
                psb = 3 if pb == 0 else 5
                self.mm(PB[psb][:, 0:TT], self.ones128[:], q_[:], True, True, ['ones128', ksq], [('PB', psb)])

            def b6(blk):
                if blk >= 16:
                    return
                pb = blk % 2
                psb = 3 if pb == 0 else 5
                l_ = lnb[pb]
                kl = ('lnt', pb)
                self.act(l_[:], PB[psb][:, 0:TT], AF.Ln, [('PB', psb)], [kl], bias=EPS)
                self.act(l_[:], l_[:], AF.Exp, [kl], [kl], scale=-0.5, bias=(lnqs if blk < 8 else 0.0))

            def b78(blk):
                if blk >= 24:
                    return
                pb, sp, pr, kpr, y, ky, s_, ks = bufs(blk)
                h = blk % 8
                pkb = 4 if pb == 0 else 2
                pk = PB[pkb][0:64, :].rearrange("p (c d) -> p c d", c=NCH)
                if blk < 16:
                    l_ = lnb[pb]
                    kl = ('lnt', pb)
                    dest = qT if blk < 8 else kT
                    kd = 'qT' if blk < 8 else 'kT'
                    self.tt('vector', dest[:, h, :], s_[:], l_[:], ALU.mult, [ks, kl], [kd])
                    if blk >= 8:
                        for c in range(NCH):
                            self.tr(pk[:, c, :], kT[:, h, c * 64:(c + 1) * 64], self.ident[:], ['kT', 'ident'], [('PB', pkb)])
                        for c2 in range(2):
                            self.cp('scalar', ktokp[c2 * 64:c2 * 64 + 64, :, h, :], pk[:, c2 * 2:c2 * 2 + 2, :], [('PB', pkb)], ['ktok'])
                else:
                    for c in range(NCH):
                        self.tr(pk[:, c, :], s_[:, c * 64:(c + 1) * 64], self.ident[:], [ks, 'ident'], [('PB', pkb)])
                    for c2 in range(2):
                        self.cp('scalar', vtokp[c2 * 64:c2 * 64 + 64, :, h, :], pk[:, c2 * 2:c2 * 2 + 2, :], [('PB', pkb)], ['vtok'])

            groups = [(2 * i, 2 * i + 1) for i in range(16)]
            NG = len(groups)

            def both(fn, g):
                fn(g[0])
                fn(g[1])

            both(s1_mm, groups[0])
            both(s1_mm, groups[1])
            both(a1, groups[0])
            both(a2, groups[0])
            both(a3, groups[0])
            yield
            for gi in range(NG):
                g = groups[gi]
                gn = groups[gi + 1] if gi + 1 < NG else None
                if gi + 2 < NG:
                    both(s1_mm, groups[gi + 2])
                if gn:
                    both(a1, gn)
                both(b45, g)
                if gn:
                    both(a2, gn)
                both(b6, g)
                if gn:
                    both(a3, gn)
                both(b78, g)
            pL = PB[5][0:64, 0:NCH * 16].rearrange("p (c n) -> p c n", c=NCH)
            for c in range(NCH):
                for kc in range(8):
                    self.mm(pL[:, c, :], xT[:, kc, c * 64:(c + 1) * 64], w_in[:, kc, 4096:4112], kc == 0, kc == 7, ['xT', 'w_in'], [('PB', 5)])
            self.act(beta[:], pL[:, :, 0:8], AF.Sigmoid, [('PB', 5)], ['beta'])
            self.tt('vector', xg[:], pL[:, :, 8:16], dtb[:].unsqueeze(1).to_broadcast([64, NCH, 8]), ALU.add, [('PB', 5), 'dtb'], ['xg'])
            self.act(xg[:], xg[:], AF.Exp, ['xg'], ['xg'])
            self.act(xg[:], xg[:], AF.Ln, ['xg'], ['xg'], bias=1.0)
            self.tt('vector', gt[:], xg[:], negA[:].unsqueeze(1).to_broadcast([64, NCH, 8]), ALU.mult, ['xg', 'negA'], ['gt'])
            self.V(lambda e: e.tensor_scalar(out=nbeta[:], in0=beta[:], scalar1=-1.0, scalar2=None, op0=ALU.mult), ['beta'], ['nbeta'])
            if 'dbg_qT' in dbg and t == 0:
                self.DMA(d_qT, qT[:], ['qT'], ['dbg'])
                self.DMA(d_kT, kT[:], ['kT'], ['dbg'])
                self.DMA(d_vtok, vtokp[:], ['vtok'], ['dbg'])
                self.DMA(d_ktok, ktokp[:], ['ktok'], ['dbg'])
                self.DMA(d_beta, beta[:], ['beta'], ['dbg'])
                self.DMA(d_g, gt[:], ['gt'], ['dbg'])
                self.final_keys.append('dbg')

            def prep(c):
                cp_ = c % 2
                cs = slice(c * 64, (c + 1) * 64)
                attT = attT2[cp_]
                G2 = attT
                eg = eg2[cp_]
                kdec = kdec2[cp_]
                XTf = XTf2[cp_]
                kat = ('attT', cp_)
                keg = ('eg', cp_)
                gcv = gt[:, c, :]
                gb = gcv.unsqueeze(2).to_broadcast([64, 8, 64])
                self.cp('vector', G1[:], gb, ['gt'], ['tmpA'])
                self.tt('vector', G2[:], tri8, gb, ALU.mult, ['gt', 'c64'], [kat])
                pD = PB[7][0:64, :]
                kD = ('PB', 7)
                self.mm(pD, ones64[:, 0:64], G2[:].rearrange("p h i -> p (h i)"), True, False, ['c64', kat], [kD])
                self.mm(pD, ntri, G1[:].rearrange("p h i -> p (h i)"), False, False, ['c64', 'tmpA'], [kD])
                self.mm(pD, id64, negT8, False, True, ['c64', 'ident'], [kD])
                self.act(decT[:].rearrange("p h i -> p (h i)"), pD, AF.Exp, [kD], ['decT'])
                pG = PB[3]
                kG = ('PB', 3)
                self.mm(pG[0:64, 0:8], tri, gcv, True, True, ['c64', 'gt'], [kG])
                self.mm(pG[0:64, 8:16], sup, gcv, True, True, ['c64', 'gt'], [kG])
                self.mm(pG[:, 16:24], ones64, gcv, True, True, ['c64', 'gt'], [kG])
                self.act(eg[0:64, 0:16], pG[0:64, 0:16], AF.Exp, [kG], [keg])
                self.act(eg[:, 16:24], pG[:, 16:24], AF.Exp, [kG], [keg])
                edec = eg[0:64, 8:16]
                yield
                pA = PB[4][0:64, :].rearrange("p (h i) -> p h i", h=8)
                pQK = PB[5][0:64, :].rearrange("p (h i) -> p h i", h=8)
                for h in range(8):
                    self.mm(pA[:, h, :], kT[:, h, cs], kT[:, h, cs], True, True, ['kT'], [('PB', 4)])
                for h in range(8):
                    self.mm(pQK[:, h, :], kT[:, h, cs], qT[:, h, cs], True, True, ['kT', 'qT'], [('PB', 5)])
                M = Mb[0]
                MT = MTb[0]
                P = Pb[0]
                self.tt('vector', tmpA[:], pA, decT[:], ALU.mult, [('PB', 4), 'decT'], ['tmpA'])
                self.tt('vector', tmpA[:], tmpA[:], beta[:, c, :].unsqueeze(2).to_broadcast([64, 8, 64]), ALU.mult, ['tmpA', 'beta'], ['tmpA'])
                self.tt('gpsimd', M[:], tmpA[:], offd8, ALU.mult, ['tmpA', 'c64'], [('M', 0, 0), ('M', 0, 1)])
                yield
                self.tt('vector', attT[:], pQK, decT[:], ALU.mult, [('PB', 5), 'decT'], [kat])
                pMT = PB[6][0:64, :].bitcast(BF16)[:, 0:512].rearrange("p (h i) -> p h i", h=8)
                for h in range(8):
                    self.tr(pMT[:, h, :], M[:, h, :], idb64, [('M', 0, 0), ('M', 0, 1), 'identb'], [('PB', 6)])
                self.cp('scalar', MT[:], pMT, [('PB', 6)], [('MT', 0, 0), ('MT', 0, 1)])
                self.tt('vector', P[:], eye8b[:], M[:], ALU.subtract, ['eye8b', ('M', 0, 0), ('M', 0, 1)], [('P', 0, 0), ('P', 0, 1)])
                if c < 2:
                    self.tt('gpsimd', kdec[:], ktok_c(c), edec.unsqueeze(2).to_broadcast([64, 8, 128]), ALU.mult, ['ktok', keg], [('kdec', cp_)])
                else:
                    self.cp('scalar', kdec[:], ktok_c(c), ['ktok'], [('kdec', cp_)])
                    self.tt('gpsimd', kdec[:], kdec[:], edec.unsqueeze(2).to_broadcast([64, 8, 128]), ALU.mult, [('kdec', cp_), keg], [('kdec', cp_)])
                    self.cp('scalar', vcur2[cp_], vtok_c(c), ['vtok'], ['xs'])
                yield
                cur = 0
                ibank = [(3, 4), (5, 6)]
                for lvl in range(1, 6):
                    nxt = 1 - cur
                    M, MT, P = Mb[cur], MTb[cur], Pb[cur]
                    M2, M2T, P2 = Mb[nxt], MTb[nxt], Pb[nxt]
                    for grp in range(2):
                        gs = slice(grp * 4, grp * 4 + 4)
                        b1, b2_ = ibank[grp]
                        pM2T = PB[b1][0:64, 0:256].rearrange("p (h i) -> p h i", h=4)
                        pM2 = PB[b2_][0:64, 0:256].rearrange("p (h i) -> p h i", h=4)
                        kin = [('M', cur, grp), ('MT', cur, grp)]
                        for hh in range(4):
                            h = grp * 4 + hh
                            self.mm(pM2T[:, hh, :], M[:, h, :], MT[:, h, :], True, True, kin, [('PB', b1)])
                        self.cp('scalar', M2T[:, gs, :], pM2T, [('PB', b1)], [('MT', nxt, grp)])
                        if lvl < 5:
                            for hh in range(4):
                                h = grp * 4 + hh
                                self.mm(pM2[:, hh, :], MT[:, h, :], M[:, h, :], True, True, kin, [('PB', b2_)])
                            self.cp('vector', M2[:, gs, :], pM2, [('PB', b2_)], [('M', nxt, grp)])
                    yield
                    for grp in range(2):
                        gs = slice(grp * 4, grp * 4 + 4)
                        b1, b2_ = ibank[grp]
                        pPP = PB[b1][0:64, 0:256].rearrange("p (h i) -> p h i", h=4)
                        for hh in range(4):
                            h = grp * 4 + hh
                            self.mm(pPP[:, hh, :], M2T[:, h, :], P[:, h, :], True, True, [('MT', nxt, grp), ('P', cur, grp)], [('PB', b1)])
                        if lvl == 5:
                            self.tt('vector', XTf[:, gs, :], P[:, gs, :], pPP, ALU.add, [('P', cur, grp), ('PB', b1)], [('XT', cp_, grp)])
                        else:
                            self.tt('vector', P2[:, gs, :], P[:, gs, :], pPP, ALU.add, [('P', cur, grp), ('PB', b1)], [('P', nxt, grp)])
                    cur = nxt
                    yield
                if 'dbg_qT' in dbg and t == 0 and c == 0:
                    self.DMA(d_decT, decT[:], ['decT'], ['dbg'])
                    self.DMA(d_XT, XTf[:], [('XT', cp_, 0), ('XT', cp_, 1)], ['dbg'])

            def scan_out(c):
                cp_ = c % 2
                cs = slice(c * 64, (c + 1) * 64)
                attT = attT2[cp_]
                eg = eg2[cp_]
                kdec = kdec2[cp_]
                osq = kdec
                XT = XTf2[cp_]
                vcur = vcur2[cp_] if c >= 2 else vtok_c(c)
                kvc = 'xs' if c >= 2 else 'vtok'
                kat = ('attT', cp_)
                keg = ('eg', cp_)
                egc = eg[0:64, 0:8]
                glb = eg[:, 16:24]
                pKS = PB[0][0:64, :].rearrange("p (h d) -> p h d", h=4)
                pQS = PB[1][0:64, :].rearrange("p (h d) -> p h d", h=4)
                pS = PB[2][:, :].rearrange("p (h d) -> p h d", h=4)
                kX, kY, kZ = ('PB', 0), ('PB', 1), ('PB', 2)
                for half in range(2):
                    hs = slice(half * 4, half * 4 + 4)
                    egb = egc[:, hs].unsqueeze(2).to_broadcast([64, 4, 128])
                    for hh in range(4):
                        h = half * 4 + hh
                        self.mm(pKS[:, hh, :], kT[:, h, cs], Sst[:, h, :], True, True, ['kT', ('Sst', half)], [kX])
                    for hh in range(4):
                        h = half * 4 + hh
                        self.mm(pQS[:, hh, :], qT[:, h, cs], Sst[:, h, :], True, True, ['qT', ('Sst', half)], [kY])
                    self.tt('vector', Rp[:], pKS, egb, ALU.mult, [kX, keg], ['Rp'])
                    self.tt('vector', Rp[:], Rp[:], vcur[:, hs, :], ALU.subtract, ['Rp', kvc], ['Rp'])
                    self.tt('vector', qs[:], pQS, egb, ALU.mult, [kY, keg], ['qs'])
                    yield
                    pVN = pKS
                    for hh in range(4):
                        h = half * 4 + hh
                        self.mm(pVN[:, hh, :], XT[:, h, :], Rp[:, hh, :], True, True, [('XT', cp_, half), 'Rp'], [kX])
                    self.tt('vector', vnew[:], pVN, nbeta[:, c, hs].unsqueeze(2).to_broadcast([64, 4, 128]), ALU.mult, [kX, 'nbeta'], ['vnew'])
                    yield
                    pAV = pQS
                    for hh in range(4):
                        h = half * 4 + hh
                        self.mm(pAV[:, hh, :], attT[:, h, :], vnew[:, hh, :], True, True, [kat, 'vnew'], [kY])
                    self.tt('vector', oc[:, hs, :], pAV, qs[:], ALU.add, [kY, 'qs'], [('oc', half)])
                    for hh in range(4):
                        h = half * 4 + hh
                        self.mm(pS[:, hh, :], kdec[:, h, :], vnew[:, hh, :], True, True, [('kdec', cp_), 'vnew'], [kZ])
                    self.tt('gpsimd', t2[:], Sst[:, hs, :], glb[:, hs].unsqueeze(2).to_broadcast([128, 4, 128]), ALU.mult, [('Sst', half), keg], ['t2'])
                    self.tt('vector', Sst[:, hs, :], t2[:], pS, ALU.add, ['t2', kZ], [('Sst', half)])
                    yield
                if 'dbg_qT' in dbg:
                    self.DMA(d_oc[t * NCH + c], oc[:], [('oc', 0), ('oc', 1)], ['dbg'])
                self.tt('gpsimd', osq[:], oc[:], oc[:], ALU.mult, [('oc', 0), ('oc', 1)], [('kdec', cp_)])
                self.V(lambda e: e.tensor_reduce(out=oss[:], in_=osq[:], axis=AX.X, op=ALU.add), [('kdec', cp_)], ['oss'])
                self.act(orr[:], oss[:], AF.Ln, ['oss'], ['orr'], scale=1.0 / 128.0, bias=EPS)
                self.act(orr[:], orr[:], AF.Exp, ['orr'], ['orr'], scale=-0.5)
                yield
                self.tt('vector', onb[:], oc[:], orr[:].unsqueeze(2).to_broadcast([64, 8, 128]), ALU.mult, [('oc', 0), ('oc', 1), 'orr'], ['onb'])
                pOT = PB[2][:].bitcast(BF16)[:, 0:512].rearrange("p (h i) -> p h i", h=8)
                for h in range(8):
                    self.tr(pOT[:, h, :], onb[:, h, :], idb64, ['onb', 'identb'], [('PB', 2)])
                self.V(lambda e, cs=cs, pOT=pOT: e.scalar_tensor_tensor(out=ogT[:, :, cs], in0=pOT, scalar=normw[:, 0:1], in1=szT[:, :, cs], op0=ALU.mult, op1=ALU.mult),
                       [('PB', 2), 'normw', 'szT'], ['ogT'])
                yield

            for c in range(NCH + 1):
                gens = []
                if c < NCH:
                    gens.append(prep(c))
                if c >= 1:
                    gens.append(scan_out(c - 1))
                while gens:
                    for g_ in list(gens):
                        try:
                            next(g_)
                        except StopIteration:
                            gens.remove(g_)
            yield
            self.DMA(w_out, w_out_d, r=['w_out_d'], w=['qT', 'kT'])
            if 'dbg_qT' in dbg and t == 0:
                self.DMA(d_ogT, ogT[:], ['ogT'], ['dbg'])
                self.DMA(d_wout, w_out, ['qT', 'kT'], ['dbg'])
            for s in range(2):
                pY = [PB[0][:], PB[1][:]]
                for half in range(2):
                    for h in range(8):
                        self.mm(pY[half], ogT[:, h, s * 128:(s + 1) * 128], w_out[:, h, half * 512:(half + 1) * 512], h == 0, h == 7, ['ogT', 'qT', 'kT'], [('PB', half)])
                self.DMA(rr[:], self.x[t0 + s * 128:t0 + (s + 1) * 128, :], w=['rr'])
                for half in range(2):
                    self.V(lambda e, half=half: e.scalar_tensor_tensor(out=rr[:, half * 512:(half + 1) * 512], in0=rr[:, half * 512:(half + 1) * 512], scalar=ALPHA, in1=pY[half], op0=ALU.mult, op1=ALU.add),
                           ['rr', ('PB', half)], ['rr'])
                    self.V(lambda e, half=half: e.bn_stats(out=bst[:, half, :], in_=rr[:, half * 512:(half + 1) * 512]), ['rr'], ['bst'])
                if 'dbg_qT' in dbg and t == 0:
                    self.DMA(d_rr[s], rr[:], ['rr'], ['dbg'])
                self.V(lambda e: e.bn_aggr(out=mv[:], in_=bst[:].rearrange("p a b -> p (a b)")), ['bst'], ['mv'])
                self.act(lrs[:], mv[:, 1:2], AF.Ln, ['mv'], ['lrs'], bias=EPS)
                self.act(rstd[:], lrs[:], AF.Exp, ['lrs'], ['rstd'], scale=-0.5)
                self.V(lambda e: e.tensor_scalar(out=nb[:], in0=mv[:, 0:1], scalar1=rstd[:, 0:1], scalar2=-1.0, op0=ALU.mult, op1=ALU.mult), ['mv', 'rstd'], ['nb'])
                if 'dbg_qT' in dbg and t == 0 and s == 0:
                    self.DMA(d_mv, mv[:], ['mv'], ['dbg'])
                    self.DMA(d_rstd, rstd[:], ['rstd'], ['dbg'])
                    self.DMA(d_bst, bst[:].rearrange("p a b -> p (a b)"), ['bst'], ['dbg'])
                self.act(xn[:], rr[:], AF.Identity, ['rr', 'rstd', 'nb'], ['rr'], scale=rstd[:, 0:1], bias=nb[:, 0:1])
                self.tt('vector', xn[:], xn[:], lnw_b[:], ALU.mult, ['rr', 'lnw_b'], ['rr'])
                self.tt('gpsimd', h1s[:], xn[:], lnb_b[:], ALU.add, ['rr', 'lnb_b'], ['rr'])
                r0 = t0 + s * 128
                self.DMA(self.h1_d[r0:r0 + 128, :], h1s[:], ['rr'], [('h1_d', t, s)])
                self.final_keys.append(('h1_d', t, s))
                pTf = PB[2][:].rearrange("p (c n) -> p c n", c=4)
                for g4 in range(2):
                    for k4 in range(4):
                        kc = g4 * 4 + k4
                        self.tr(pTf[:, k4, :], h1s[:, kc * 128:(kc + 1) * 128], self.ident[:], ['rr', 'ident'], [('PB', 2)])
                    self.cp('scalar', h1T[:, g4 * 4:g4 * 4 + 4, :], pTf, [('PB', 2)], ['h1T'])
                self.DMA(self.h1T_d[:, :, r0:r0 + 128], h1T[:], ['h1T'], [('h1T_d', t, s)])
                self.final_keys.append(('h1T_d', t, s))

        tiles = [tile(t) for t in range(NT)]
        next(tiles[0])
        for t in range(NT):
            next(tiles[t])
            if t + 1 < NT:
                next(tiles[t + 1])
            for _ in tiles[t]:
                pass

    def rope_tm(self, out4, x4, cs, nh, t1, t2, kx, kout):
        cosb = cs[:, 0:32].unsqueeze(1).unsqueeze(1).to_broadcast([128, nh, 2, 32])
        sinb = cs[:, 32:64].unsqueeze(1).to_broadcast([128, nh, 32])
        self.tt('vector', t1, x4, cosb, ALU.mult, [kx, 'cs'], ['rt1'])
        self.tt('gpsimd', t2[:, :, 0, :], x4[:, :, 1, :], sinb, ALU.mult, [kx, 'cs'], ['rt2'])
        self.tt('gpsimd', t2[:, :, 1, :], x4[:, :, 0, :], sinb, ALU.mult, [kx, 'cs'], ['rt2'])
        self.tt('vector', out4[:, :, 0, :], t1[:, :, 0, :], t2[:, :, 0, :], ALU.subtract, ['rt1', 'rt2'], [kout])
        self.tt('vector', out4[:, :, 1, :], t1[:, :, 1, :], t2[:, :, 1, :], ALU.add, ['rt1', 'rt2'], [kout])

    def phase2(self):
        PB = self.PB
        s_w_kv = self.din("s_w_kv", [1024, 1536])
        b_w_in = self.din("b_w_in", [1024, 4144])
        rope_cs = self.din("rope_cs", [T, 64])
        rope_q = self.din("rope_q", [T // 2, 64])
        bw_d = self.din("bw", [128, 2, 2])
        bw = self.sb("p2bw", [128, 2, 2], F32)
        self.DMA(bw[:], bw_d, w=['bw'])
        hA = self.sb("p2hA", [128, 8, 128], BF16)
        hB = self.sb("p2hB", [128, 8, 128], BF16)
        wkv = self.sb("wkv", [128, 8, 1536], BF16)
        wq = self.sb("wq", [128, 8, 1024], BF16)
        wz = self.sb("wz", [128, 8, 3072], BF16)
        wg = self.sb("wg", [128, 8, 48], BF16)
        for kc in range(8):
            rs = slice(kc * 128, (kc + 1) * 128)
            self.DMA(wkv[:, kc, :], s_w_kv[rs, :], w=['wkv'], eng='gpsimd')
            self.DMA(wq[:, kc, :], b_w_in[rs, 0:1024], w=['wq'], eng='gpsimd')
            self.DMA(wz[:, kc, :], b_w_in[rs, 1024:4096], w=['wz'], eng='gpsimd')
            self.DMA(wg[:, kc, :], b_w_in[rs, 4096:4144], w=['wg'], eng='gpsimd')
        h1T = [self.sb("p2h1T%d" % i, [128, 8, 128], BF16) for i in range(2)]
        cs = [self.sb("p2cs%d" % i, [128, 64], F32) for i in range(2)]
        kvs2 = [self.sb("kvs%d" % i, [128, 1536], F32) for i in range(2)]
        qs2_ = [self.sb("qs_%d" % i, [128, 1024], F32) for i in range(2)]
        t12 = [self.sb("rt1%d" % i, [128, 1024], F32) for i in range(2)]
        t22 = [self.sb("rt2%d" % i, [128, 1024], F32) for i in range(2)]
        krb2 = [self.sb("krb%d" % i, [128, 4, 256], BF16) for i in range(2)]
        qrb2 = [self.sb("qrb%d" % i, [128, 1024], BF16) for i in range(2)]
        vst2 = [self.sb("vst%d" % i, [128, 2, 4, 65], BF16) for i in range(2)]
        kT42 = [self.sb("kT4%d" % i, [64, 16, 128], BF16) for i in range(2)]
        qTt2 = [self.sb("qTt%d" % i, [64, 16, 128], BF16) for i in range(2)]
        zs3 = [self.sb("zs%d" % i, [128, 1024], F32) for i in range(3)]
        gts2 = [self.sb("gts%d" % i, [128, 48], F32) for i in range(2)]
        gz2 = [self.sb("gz%d" % i, [128, 3, 1024], BF16) for i in range(2)]
        for i in range(2):
            self.V(lambda e, i=i: e.memset(vst2[i][:], 1.0), w=[('vst', i)])
        P2KEYS = ['kvs', 'qs_', 'rt1', 'rt2', 'krb', 'qrb', 'vst', 'kT4', 'qTt', 'zs', 'gts', 'gz']

        pk = [PB[3][0:64, :].bitcast(BF16).rearrange("p (a t) -> p a t", a=8), PB[4][0:64, :].bitcast(BF16).rearrange("p (a t) -> p a t", a=8)]

        def kvA(qb):
            par = qb % 2
            self.S.kmap = {k: (k, par) for k in P2KEYS}
            kvs = kvs2[par]
            t0 = qb * 128
            hT = h1T[par]
            kh = ('p2h1T', par)
            self.DMA(hT[:], self.h1T_d[:, :, t0:t0 + 128], r=['h1T_all'], w=[kh])
            self.DMA(cs[par][:], rope_cs[t0:t0 + 128, :], w=[('cs', par)])
            for j in range(3):
                for kc in range(8):
                    self.mm(PB[j][:], hT[:, kc, :], wkv[:, kc, j * 512:(j + 1) * 512], kc == 0, kc == 7, [kh, 'wkv'], [('PB', j)])
                self.cp('scalar', kvs[:, j * 512:(j + 1) * 512], PB[j][:], [('PB', j)], ['kvs'])

        def kvB(qb):
            par = qb % 2
            self.S.kmap = {k: (k, par) for k in P2KEYS}
            self.S.kmap['cs'] = ('cs', par)
            kvs, t1, t2, krb, vst, kT4 = kvs2[par], t12[par], t22[par], krb2[par], vst2[par], kT42[par]
            t0 = qb * 128
            c_ = cs[par]
            for i, c0 in enumerate((512, 1024)):
                x4 = kvs[:, c0:c0 + 256].rearrange("p (g a d) -> p g a d", g=4, a=2)
                o4 = krb[:, i, :].rearrange("p (g a d) -> p g a d", g=4, a=2)
                self.rope_tm(o4, x4, c_, 4, t1[:, 0:256].rearrange("p (g a d) -> p g a d", g=4, a=2),
                             t2[:, 0:256].rearrange("p (g a d) -> p g a d", g=4, a=2), 'kvs', 'krb')
            self.cp('gpsimd', krb[:, 2, :], kvs[:, 0:256], ['kvs'], ['krb'])
            self.cp('gpsimd', krb[:, 3, :], kvs[:, 256:512], ['kvs'], ['krb'])
            self.cp('vector', vst[:, 0, :, 0:64], kvs[:, 768:1024].rearrange("p (g d) -> p g d", g=4), ['kvs'], ['vst'])
            self.cp('vector', vst[:, 1, :, 0:64], kvs[:, 1280:1536].rearrange("p (g d) -> p g d", g=4), ['kvs'], ['vst'])
            for i in range(4):
                for g in range(4):
                    a = i * 4 + g
                    self.tr(pk[a // 8][:, a % 8, :], krb[:, i, g * 64:(g + 1) * 64], self.identb[:], ['krb', 'identb'], [('PB', 3 + a // 8)])
            self.cp('scalar', kT4[:, 0:8, :], pk[0], [('PB', 3)], ['kT4'])
            self.cp('scalar', kT4[:, 8:16, :], pk[1], [('PB', 4)], ['kT4'])
            for i, dst in enumerate((self.kselT_d, self.kwinT_d, self.kcsT_d, self.vcsT_d)):
                self.DMA(dst[:, :, t0:t0 + 128].rearrange("g d t -> d g t"), kT4[:, i * 4:(i + 1) * 4, :], r=['kT4'], w=[('kvT_d', qb, i)])
            self.DMA(self.vsel_d[:, :, qb, :].rearrange("g p c -> p g c"), vst[:, 0], r=['vst'], w=[('vsel_d', qb)])
            self.DMA(self.vwin_d[:, :, qb, :].rearrange("g p c -> p g c"), vst[:, 1], r=['vst'], w=[('vwin_d', qb)])

        NKV = T // 128
        kvA(0)
        for qb in range(NKV):
            if qb + 1 < NKV:
                kvA(qb + 1)
            kvB(qb)

        def qA(slot):
            par = slot % 2
            self.S.kmap = {k: (k, par) for k in P2KEYS}
            qs_, gts, gz = qs2_[par], gts2[par], gz2[par]
            e2 = slot % 2
            t0 = slot * 128
            hT = h1T[par]
            kh = ('p2h1T', par)
            self.DMA(hA[:], self.h1T_d[:, :, (2 * slot) * 128:(2 * slot + 1) * 128], w=['p2hA'])
            self.DMA(hB[:], self.h1T_d[:, :, (2 * slot + 1) * 128:(2 * slot + 2) * 128], w=['p2hB'])
            self.DMA(cs[par][:], rope_q[t0:t0 + 128, :], w=[('cs', par)])
            self.V(lambda e, hT=hT, e2=e2: e.tensor_scalar(out=hT[:], in0=hA[:], scalar1=bw[:, e2, 0:1], scalar2=None, op0=ALU.mult), ['p2hA', 'bw'], [kh])
            self.V(lambda e, hT=hT, e2=e2: e.scalar_tensor_tensor(out=hT[:], in0=hB[:], scalar=bw[:, e2, 1:2], in1=hT[:], op0=ALU.mult, op1=ALU.add), ['p2hB', 'bw', kh], [kh])
            for j in range(2):
                for kc in range(8):
                    self.mm(PB[5 + j][:], hT[:, kc, :], wq[:, kc, j * 512:(j + 1) * 512], kc == 0, kc == 7, [kh, 'wq'], [('PB', 5 + j)])
                self.S.op('scalar', lambda e, j=j, qs_=qs_: e.mul(out=qs_[:, j * 512:(j + 1) * 512], in_=PB[5 + j][:], mul=0.125), [('PB', 5 + j)], ['qs_'])
            pg = PB[7][:, 0:48]
            for kc in range(8):
                self.mm(pg, hT[:, kc, :], wg[:, kc, :], kc == 0, kc == 7, [kh, 'wg'], [('PB', 7)])
            self.act(gts[:], pg, AF.Sigmoid, [('PB', 7)], ['gts'])
            for br in range(3):
                zs = zs3[br]
                for j in range(2):
                    pz = PB[j][:]
                    for kc in range(8):
                        self.mm(pz, hT[:, kc, :], wz[:, kc, br * 1024 + j * 512:br * 1024 + (j + 1) * 512], kc == 0, kc == 7, [kh, 'wz'], [('PB', j)])
                    self.act(zs[:, j * 512:(j + 1) * 512], pz, AF.Silu, [('PB', j)], [('zs3', br)])
                self.tt('vector' if br != 1 else 'gpsimd', gz[:, br, :].rearrange("p (h d) -> p h d", h=16), zs[:].rearrange("p (h d) -> p h d", h=16),
                        gts[:, br * 16:(br + 1) * 16].unsqueeze(2).to_broadcast([128, 16, 64]), ALU.mult, [('zs3', br), 'gts'], ['gz'])
            self.DMA(self.gz_d[t0:t0 + 128], gz[:], r=['gz'], w=[('gz_d', slot)])

        def qB(slot):
            par = slot % 2
            self.S.kmap = {k: (k, par) for k in P2KEYS}
            self.S.kmap['cs'] = ('cs', par)
            qs_, t1, t2, qrb, qTt = qs2_[par], t12[par], t22[par], qrb2[par], qTt2[par]
            c_ = cs[par]
            v16 = "p (g a d) -> p g a d"
            self.rope_tm(qrb[:].rearrange(v16, g=16, a=2), qs_[:].rearrange(v16, g=16, a=2), c_, 16,
                         t1[:].rearrange(v16, g=16, a=2), t2[:].rearrange(v16, g=16, a=2), 'qs_', 'qrb')
            for hh in range(16):
                self.tr(pk[hh // 8][:, hh % 8, :], qrb[:, hh * 64:(hh + 1) * 64], self.identb[:], ['qrb', 'identb'], [('PB', 3 + hh // 8)])
            self.cp('scalar', qTt[:, 0:8, :], pk[0], [('PB', 3)], ['qTt'])
            self.cp('scalar', qTt[:, 8:16, :], pk[1], [('PB', 4)], ['qTt'])
            self.DMA(self.qT_d[:, slot].rearrange("g d (h t) -> d g h t", h=4), qTt[:].rearrange("d (g h) t -> d g h t", g=4), r=['qTt'], w=[('qT_d', slot)])

        NSL = T // 256
        qA(0)
        for slot in range(NSL):
            if slot + 1 < NSL:
                qA(slot + 1)
            qB(slot)

        self.S.kmap = {}

    def phase3(self):
        PB = self.PB
        s_pe = [self.din("s_pe_k", [32, 64]), self.din("s_pe_v", [32, 64])]
        s_w1 = [self.din("s_w1_k", [32, 64, 128]), self.din("s_w1_v", [32, 64, 128])]
        s_w2 = [self.din("s_w2_k", [128, 64]), self.din("s_w2_v", [128, 64])]
        cmp_cs = self.din("cmp_cs", [64, 1024])
        ovm = self.din("ovm", [512, 128])
        w1 = [self.sb("w1_%d" % i, [64, 32, 128], BF16) for i in range(2)]
        w2 = [self.sb("w2_%d" % i, [128, 64], BF16) for i in range(2)]
        w2s = self.sb("w2s", [128, 64], BF16)
        pe32 = self.sb("pe32", [32, 2, 64], F32)
        peT = self.sb("peT", [64, 2, 32], BF16)
        bias = self.sb("cbias", [128, 2], F32)
        ccs = self.sb("ccs", [64, 1024], F32)
        src = self.sb("csrc", [64, T], BF16)
        hs = self.sb("chs", [128, 512], BF16)
        kx = self.sb("ckx", [64, 512], F32)
        kxs = self.sb("ckxs", [64, 512], F32)
        self.DMA(ccs[:], cmp_cs, w=['ccs'])
        for i in range(2):
            self.DMA(w1[i][:], s_w1[i].rearrange("c d h -> d c h"), w=[('w1', i)], eng='gpsimd')
            self.DMA(w2[i][:], s_w2[i], w=[('w2', i)], eng='gpsimd')
            self.DMA(pe32[:, i, :], s_pe[i], w=['pe32'])
        self.cp('vector', w2s[:, 0:32], w2[0][:, 32:64], [('w2', 0)], ['w2s'])
        self.cp('vector', w2s[:, 32:64], w2[0][:, 0:32], [('w2', 0)], ['w2s'])
        for i in range(2):
            pT = PB[0][0:64, i * 32:(i + 1) * 32]
            self.tr(pT, pe32[:, i, :], self.ident[0:32, 0:32], ['pe32', 'ident'], [('PB', 0)])
            self.cp('vector', peT[:, i, :], pT, [('PB', 0)], ['peT'])
        for i in range(2):
            pb_ = PB[1][:, i:i + 1]
            for c in range(32):
                self.mm(pb_, w1[i][:, c, :], peT[:, i, c:c + 1], c == 0, c == 31, [('w1', i), 'peT'], [('PB', 1)])
            self.cp('vector', bias[:, i:i + 1], pb_, [('PB', 1)], ['cbias'])
        self.V(lambda e: e.memset(self.vcaug[:, :, :, 64:65], 1.0), w=['vcaug'])
        for g in range(4):
            self.DMA(self.vcaug[:, g, :, 65:193], ovm.rearrange("(n p) s -> p n s", p=128), w=['vcaug'], eng='gpsimd')
        self.V(lambda e: e.memset(hs[:, 511:512], 0.0), w=['chs'])
        for i in range(2):
            srcd = self.kcsT_d if i == 0 else self.vcsT_d
            for g in range(4):
                self.DMA(src[:], srcd[g], r=['kvT_all'], w=['csrc'])
                s3 = src[:].rearrange("p (n r) -> p n r", r=16)
                ph = PB[2][:, 0:511]
                for c in range(32):
                    rhs = s3[:, 0:511, c] if c < 16 else s3[:, 1:512, c - 16]
                    self.mm(ph, w1[i][:, c, :], rhs, c == 0, c == 31, [('w1', i), 'csrc'], [('PB', 2)])
                self.act(hs[:, 0:511], ph, AF.Silu, [('PB', 2), 'cbias'], ['chs'], bias=bias[:, i:i + 1])
                if i == 0:
                    pk = PB[3][0:64, :]
                    pks = PB[4][0:64, :]
                    self.mm(pk, w2[0][:], hs[:], True, True, [('w2', 0), 'chs'], [('PB', 3)])
                    self.mm(pks, w2s[:], hs[:], True, True, ['w2s', 'chs'], [('PB', 4)])
                    self.tt('vector', kx[:], pk, ccs[:, 0:512], ALU.mult, [('PB', 3), 'ccs'], ['ckx'])
                    self.tt('vector', kxs[:], pks, ccs[:, 512:1024], ALU.mult, [('PB', 4), 'ccs'], ['ckxs'])
                    self.tt('vector', self.kcmpT[:, g, :], kx[:], kxs[:], ALU.add, ['ckx', 'ckxs'], ['kcmpT'])
                else:
                    pv = PB[5][:, 0:256].rearrange("p (n d) -> p n d", n=4)
                    for nt in range(4):
                        self.mm(pv[:, nt, :], hs[:, nt * 128:(nt + 1) * 128], w2[1][:], True, True, ['chs', ('w2', 1)], [('PB', 5)])
                    self.cp('vector', self.vcaug[:, g, :, 0:64], pv, [('PB', 5)], ['vcaug'])
        if 'dbg_kcmpT' in self.dbg:
            d1 = self.dout('dbg_kcmpT', [64, 4, 512], BF16)
            d2 = self.dout('dbg_vcaug', [128, 4, 4, 193], BF16)
            self.DMA(d1, self.kcmpT[:], ['kcmpT'], ['dbgk'])
            self.DMA(d2, self.vcaug[:], ['vcaug'], ['dbgk'])

    def phase4(self):
        PB = self.PB
        NQB = T // 256 if self.nqb4 is None else self.nqb4
        cmask_d = self.din("cmask_c", [128, 32, 4, 128], BF16)
        dmask_d = self.din("dmask", [128, 2, 2, 128])
        wmask_d = self.din("wmask", [128, 2, 6, 128])
        btab = self.din("btab_c", [32, 128, 128])
        cmk = [self.sb("cmk%d" % i, [128, 4, 128], BF16) for i in range(2)]
        dmk = self.sb("dmk", [128, 2, 2, 128], BF16)
        wmk = self.sb("wmk", [128, 2, 6, 128], BF16)
        self.DMA(dmk[:], dmask_d, w=['dmk'], eng='gpsimd')
        self.DMA(wmk[:], wmask_d, w=['wmk'], eng='gpsimd')
        kselT = self.sb("kselT", [64, T], BF16)
        kwinT = self.sb("kwinT", [64, T], BF16)
        vsel = self.sb("vsel", [128, 64, 65], BF16)
        vwin = self.sb("vwin", [128, 64, 65], BF16)
        qTb = [self.sb("qTb%d" % i, [64, 512], BF16) for i in range(2)]
        Btb = [self.sb("Btb%d" % i, [128, 128], F32) for i in range(2)]
        gzb = [self.sb("gzb%d" % i, [128, 3, 256], BF16) for i in range(2)]
        Eb = [self.sb("Eb%d" % i, [128, 4, 128], BF16) for i in range(3)]
        Pb_ = [self.sb("Pb_%d" % i, [128, 4, 128], BF16) for i in range(2)]
        rden = self.sb("rden", [128, 3, 4], F32)
        imp = self.sb("imp", [128, 128], F32)
        score = self.sb("score", [128, 128], F32)
        sc2 = self.sb("sc2", [128, 128], F32)
        m8 = self.sb("m8", [128, 16], F32)
        selb = self.sb("selb", [128, 128], BF16)
        selx = self.sb("selx", [128, 128, 64], BF16)
        tmp = self.sb("etmp", [128, 4, 64], F32)
        tmp2 = self.sb("etmp2", [128, 4, 64], F32)
        acc = self.sb("eacc", [128, 4, 64], F32)
        ogt = [self.sb("ogt%d" % i, [128, 256], BF16) for i in range(2)]
        ecnt = [0]
        pcnt = [0]
        scnt = [0]

        def qk_exp(kT_tile, kkeys, qT, kq):
            i = scnt[0] % 2
            scnt[0] += 1
            pS = PB[i][:]
            self.mm(pS, kT_tile, qT[:], True, True, list(kkeys) + [kq], [('PB', i)])
            j = ecnt[0] % 3
            ecnt[0] += 1
            E = Eb[j]
            self.act(E[:].rearrange("p h q -> p (h q)"), pS, AF.Exp, [('PB', i)], [('E', j)])
            return E, ('E', j)

        def loads(g, qb):
            t0 = qb * 128
            b2 = qb % 2
            self.DMA(qTb[b2][:], self.qT_d[g, qb], w=[('qTb', b2)])
            self.DMA(Btb[b2][:], btab[qb], w=[('Btb', b2)])
            self.DMA(gzb[b2][:], self.gz_d[t0:t0 + 128, :, g * 256:(g + 1) * 256], w=[('gzb', b2)])
            self.DMA(cmk[b2][:], cmask_d[:, qb], w=[('cmk', b2)])

        def make_items(g, qb):
            items = []
            t0 = qb * 128
            b2 = qb % 2
            qT = qTb[b2]
            kq = ('qTb', b2)
            Bt = Btb[b2]
            gz = gzb[b2]
            e2 = qb % 2
            qbm = 2 * qb + 1
            ntmax = (8 * qbm + 6) // 128
            pc = [PB[3][:, 0:386].rearrange("p (h c) -> p h c", h=2), PB[4][:, 0:386].rearrange("p (h c) -> p h c", h=2)]

            def cmp_post():
                for hb in range(2):
                    self.V(lambda e, hb=hb: e.tensor_scalar(out=rden[:, 0, hb * 2:hb * 2 + 2], in0=pc[hb][:, :, 64], scalar1=1e-30, scalar2=None, op0=ALU.max),
                           [('PB', 3 + hb)], ['rden0'])
                self.V(lambda e: e.reciprocal(out=rden[:, 0, :], in_=rden[:, 0, :]), ['rden0'], ['rden0'])
                for h in range(4):
                    src = pc[h // 2][:, h % 2, 65:193]
                    if h == 0:
                        self.V(lambda e, src=src: e.tensor_scalar(out=imp[:], in0=src, scalar1=rden[:, 0, 0:1], scalar2=None, op0=ALU.mult), [('PB', 3), 'rden0'], ['imp'])
                    else:
                        self.V(lambda e, src=src, h=h: e.scalar_tensor_tensor(out=imp[:], in0=src, scalar=rden[:, 0, h:h + 1], in1=imp[:], op0=ALU.mult, op1=ALU.add),
                               [('PB', 3 + h // 2), 'rden0', 'imp'], ['imp'])
                self.tt('vector', score[:], imp[:], Bt[:], ALU.add, ['imp', ('Btb', b2)], ['score'])
                self.V(lambda e: e.max(out=m8[:, 0:8], in_=score[:]), ['score'], ['m8'])
                self.V(lambda e: e.match_replace(out=sc2[:], in_to_replace=m8[:, 0:8], in_values=score[:], imm_value=-1e9), ['score', 'm8'], ['sc2'])
                self.V(lambda e: e.max(out=m8[:, 8:16], in_=sc2[:]), ['sc2'], ['m8'])
                nbk = 2 * (qbm + 1)
                self.V(lambda e: e.tensor_scalar(out=selx[:, 0:nbk, :], in0=score[:, 0:nbk].unsqueeze(2).to_broadcast([128, nbk, 64]), scalar1=m8[:, 15:16], scalar2=None, op0=ALU.is_ge),
                       ['score', 'm8'], ['selx'])
                for hb in range(2):
                    self.tt('vector', tmp[:, hb * 2:hb * 2 + 2, :], pc[hb][:, :, 0:64], rden[:, 0, hb * 2:hb * 2 + 2].unsqueeze(2).to_broadcast([128, 2, 64]), ALU.mult,
                            [('PB', 3 + hb), 'rden0'], ['etmp'])
                self.tt('gpsimd', acc[:], tmp[:], gz[:, 0, :].rearrange("p (h d) -> p h d", h=4), ALU.mult, ['etmp', ('gzb', b2)], ['eacc'])
                if self.dbg4 is not None and g == 0 and qb == self.dbg4:
                    self.V(lambda e: e.tensor_scalar(out=selb[:], in0=score[:], scalar1=m8[:, 15:16], scalar2=None, op0=ALU.is_ge), ['score', 'm8'], ['selb'])
                    self.DMA(self.d4['imp'], imp[:], ['imp'], ['dbg4'])
                    self.DMA(self.d4['sel'], selb[:], ['selb'], ['dbg4'])
                    self.DMA(self.d4['ocmp'], tmp[:], ['etmp'], ['dbg4'])

            for nt in range(ntmax + 1):
                it = {'mdep': False, 'M': None}

                def A(it=it, nt=nt):
                    it['E'], it['kE'] = qk_exp(self.kcmpT[:, g, nt * 128:(nt + 1) * 128], ['kcmpT'], qT, kq)

                def B(it=it, nt=nt):
                    E, kE = it['E'], it['kE']
                    self.tt('vector', E[:], E[:], cmk[b2][:, nt, :].unsqueeze(1).to_broadcast([128, 4, 128]), ALU.mult, [kE, ('cmk', b2)], [kE])
                    for h in range(4):
                        self.mm(pc[h // 2][:, h % 2, :], E[:, h, :], self.vcaug[:, g, nt, :], nt == 0 and h % 2 == 0, nt == ntmax and h % 2 == 1, [kE, 'vcaug'], [('PB', 3 + h // 2)])
                    if nt == ntmax:
                        cmp_post()
                it['A'], it['B'] = A, B
                items.append(it)

            for br in (1, 2):
                pacc = PB[4 + br][:, 0:260].rearrange("p (h c) -> p h c", h=4)
                kacc = ('PB', 4 + br)
                if br == 1:
                    kts = list(range(0, qbm + 1))
                    kT_, kkey, V_, vkey = kselT, 'kselT', vsel, 'vsel'
                else:
                    kts = [kt for kt in range(qbm - 5, qbm + 1) if kt >= 0]
                    kT_, kkey, V_, vkey = kwinT, 'kwinT', vwin, 'vwin'

                def br_post(br=br, pacc=pacc, kacc=kacc):
                    kr = 'rden%d' % br
                    self.V(lambda e: e.reciprocal(out=rden[:, br, :], in_=pacc[:, :, 64]), [kacc], [kr])
                    self.tt('vector', tmp[:], pacc[:, :, 0:64], rden[:, br, :].unsqueeze(2).to_broadcast([128, 4, 64]), ALU.mult, [kacc, kr], ['etmp'])
                    if self.dbg4 is not None and g == 0 and qb == self.dbg4:
                        self.DMA(self.d4['osel' if br == 1 else 'owin'], tmp[:], ['etmp'], ['dbg4'])
                    self.tt('gpsimd', tmp2[:], tmp[:], gz[:, br, :].rearrange("p (h d) -> p h d", h=4), ALU.mult, ['etmp', ('gzb', b2)], ['etmp2'])
                    if br == 1:
                        self.tt('gpsimd', acc[:], acc[:], tmp2[:], ALU.add, ['eacc', 'etmp2'], ['eacc'])
                    else:
                        og = ogt[b2]
                        self.tt('gpsimd', og[:].rearrange("p (h d) -> p h d", h=4), acc[:], tmp2[:], ALU.add, ['eacc', 'etmp2'], [('ogt', b2)])
                        self.DMA(self.og_d[t0:t0 + 128, g * 256:(g + 1) * 256], og[:], r=[('ogt', b2)], w=[('og_d', g, qb)])

                for kt in kts:
                    it = {'mdep': (br == 1 and kt == kts[0]), 'M': None}

                    def A(it=it, kt=kt, kT_=kT_, kkey=kkey):
                        it['E'], it['kE'] = qk_exp(kT_[:, kt * 128:(kt + 1) * 128], [kkey], qT, kq)

                    def M(it=it, kt=kt):
                        pM = PB[2 if kt % 2 == 0 else 7][:].bitcast(BF16)[:, 0:128]
                        kM = ('PB', 2 if kt % 2 == 0 else 7)
                        self.tr(pM, selx[:, 2 * kt:2 * kt + 2, :].rearrange("p a k -> p (a k)"), self.identb[:], ['selx', 'identb'], [kM])
                        it['pM'], it['kM'] = pM, kM

                    def B(it=it, kt=kt, br=br, kts=kts, pacc=pacc, kacc=kacc, V_=V_, vkey=vkey, br_post=br_post):
                        E, kE = it['E'], it['kE']
                        if br == 1:
                            ip = pcnt[0] % 2
                            pcnt[0] += 1
                            P = Pb_[ip]
                            kP = ('P4', ip)
                            self.tt('vector', P[:], E[:], it['pM'].unsqueeze(1).to_broadcast([128, 4, 128]), ALU.mult, [kE, it['kM']], [kP])
                            if kt >= qbm - 1:
                                self.tt('gpsimd', P[:], P[:], dmk[:, e2, kt - (qbm - 1), :].unsqueeze(1).to_broadcast([128, 4, 128]), ALU.mult, [kP, 'dmk'], [kP])
                        else:
                            P, kP = E, kE
                            wi = kt - (qbm - 5)
                            if wi not in (2, 3):
                                self.tt('gpsimd', P[:], P[:], wmk[:, e2, wi, :].unsqueeze(1).to_broadcast([128, 4, 128]), ALU.mult, [kP, 'wmk'], [kP])
                        for h in range(4):
                            self.mm(pacc[:, h, :], P[:, h, :], V_[:, kt, :], kt == kts[0] and h == 0, kt == kts[-1] and h == 3, [kP, vkey], [kacc])
                        if kt == kts[-1]:
                            br_post()
                    it['A'], it['B'] = A, B
                    if br == 1:
                        it['M'] = M
                    items.append(it)
            return items

        for g in range(4):
            self.DMA(kselT[:], self.kselT_d[g], w=['kselT'])
            self.DMA(kwinT[:], self.kwinT_d[g], w=['kwinT'])
            self.DMA(vsel[:], self.vsel_d[g], w=['vsel'])
            self.DMA(vwin[:], self.vwin_d[g], w=['vwin'])
            loads(g, 0)
            items = []
            for qb in range(NQB):
                if qb + 1 < NQB:
                    items.append({'load': (g, qb + 1)})
                items += make_items(g, qb)
            work = [it for it in items if 'load' not in it]
            pos = 0
            load_at = {}
            for it in items:
                if 'load' in it:
                    load_at.setdefault(pos, []).append(it['load'])
                else:
                    pos += 1
            n = len(work)
            done_loads = set()

            def do_loads(upto):
                for p_ in sorted(load_at):
                    if p_ <= upto and p_ not in done_loads:
                        done_loads.add(p_)
                        for l in load_at[p_]:
                            loads(*l)

            do_loads(0)
            for j in range(min(2, n)):
                work[j]['A']()
            if n > 0 and work[0]['M'] is not None:
                work[0]['M']()
            for i in range(n):
                do_loads(i)
                if i + 2 < n:
                    work[i + 2]['A']()
                nxt = work[i + 1] if i + 1 < n else None
                if nxt is not None and nxt['M'] is not None and not nxt['mdep']:
                    nxt['M']()
                work[i]['B']()
                if nxt is not None and nxt['M'] is not None and nxt['mdep']:
                    nxt['M']()

    def phase5(self):
        PB = self.PB
        NQB = T // 256 if self.nqb4 is None else self.nqb4
        bw_d = self.din("bw", [128, 2, 2]) if 'bw' not in self.inputs else self.inputs['bw'].ap()
        bw = self.sb("p5bw", [128, 2, 2], F32)
        self.DMA(bw[:], bw_d, w=['p5bw'])
        hB = [self.sb("p5hB%d" % i, [128, 1024], F32) for i in range(2)]
        b_w_out = self.din("b_w_out", [1024, 1024])
        b_ln_w = self.din("b_ln_w", [1, 1024])
        b_ln_b = self.din("b_ln_b", [1, 1024])
        out = self.dout("out", [T // 2, D], F32)
        w_out = self.sb("p5wout", [128, 8, 1024], BF16)
        self.DMA(w_out[:], b_w_out.rearrange("(c p) n -> p c n", p=128), w=['p5wout'], eng='gpsimd')
        lnw_b = self.sb("p5lnw", [128, 1024], F32)
        lnb_b = self.sb("p5lnb", [128, 1024], F32)
        self.DMA(lnw_b[:], b_ln_w.partition_broadcast(128), w=['p5lnw'])
        self.DMA(lnb_b[:], b_ln_b.partition_broadcast(128), w=['p5lnb'])
        ogs = [self.sb("p5og%d" % i, [128, 1024], BF16) for i in range(2)]
        h1s = [self.sb("p5h1%d" % i, [128, 1024], F32) for i in range(2)]
        ogT = self.sb("p5ogT", [128, 8, 128], BF16)
        rr = self.sb("p5rr", [128, 1024], F32)
        xo = [self.sb("p5xo%d" % i, [128, 1024], F32) for i in range(2)]
        bst = self.sb("p5bst", [128, 2, 6], F32)
        mv = self.sb("p5mv", [128, 2], F32)
        lrs = self.sb("p5lrs", [128, 1], F32)
        rstd = self.sb("p5rstd", [128, 1], F32)
        nb = self.sb("p5nb", [128, 1], F32)
        for qb in range(NQB):
            t0 = qb * 128
            b2 = qb % 2
            og = ogs[b2]
            h1 = h1s[b2]
            xn = xo[b2]
            self.DMA(og[:], self.og_d[t0:t0 + 128, :], w=[('p5og', b2)])
            e2 = qb % 2
            hb_ = hB[b2]
            self.DMA(h1[:], self.h1_d[(2 * qb) * 128:(2 * qb + 1) * 128, :], w=[('p5h1', b2)])
            self.DMA(hb_[:], self.h1_d[(2 * qb + 1) * 128:(2 * qb + 2) * 128, :], w=[('p5hB', b2)])
            self.V(lambda e, h1=h1, e2=e2: e.tensor_scalar(out=h1[:], in0=h1[:], scalar1=bw[:, e2, 0:1], scalar2=None, op0=ALU.mult), [('p5h1', b2), 'p5bw'], [('p5h1', b2)])
            self.V(lambda e, h1=h1, hb_=hb_, e2=e2: e.scalar_tensor_tensor(out=h1[:], in0=hb_[:], scalar=bw[:, e2, 1:2], in1=h1[:], op0=ALU.mult, op1=ALU.add),
                   [('p5hB', b2), 'p5bw', ('p5h1', b2)], [('p5h1', b2)])
            pTb = PB[2][:].bitcast(BF16).rearrange("p (c n) -> p c n", c=8)
            for kc in range(8):
                self.tr(pTb[:, kc, :], og[:, kc * 128:(kc + 1) * 128], self.identb[:], [('p5og', b2), 'identb'], [('PB', 2)])
            self.cp('scalar', ogT[:], pTb, [('PB', 2)], ['p5ogT'])
            pY = [PB[0][:], PB[1][:]]
            for half in range(2):
                for kc in range(8):
                    self.mm(pY[half], ogT[:, kc, :], w_out[:, kc, half * 512:(half + 1) * 512], kc == 0, kc == 7, ['p5ogT', 'p5wout'], [('PB', half)])
            for half in range(2):
                self.V(lambda e, half=half, h1=h1: e.scalar_tensor_tensor(out=rr[:, half * 512:(half + 1) * 512], in0=h1[:, half * 512:(half + 1) * 512], scalar=ALPHA, in1=pY[half], op0=ALU.mult, op1=ALU.add),
                       [('p5h1', b2), ('PB', half)], ['p5rr'])
                self.V(lambda e, half=half: e.bn_stats(out=bst[:, half, :], in_=rr[:, half * 512:(half + 1) * 512]), ['p5rr'], ['p5bst'])
            self.V(lambda e: e.bn_aggr(out=mv[:], in_=bst[:].rearrange("p a b -> p (a b)")), ['p5bst'], ['p5mv'])
            self.act(lrs[:], mv[:, 1:2], AF.Ln, ['p5mv'], ['p5lrs'], bias=EPS)
            self.act(rstd[:], lrs[:], AF.Exp, ['p5lrs'], ['p5rstd'], scale=-0.5)
            self.V(lambda e: e.tensor_scalar(out=nb[:], in0=mv[:, 0:1], scalar1=rstd[:, 0:1], scalar2=-1.0, op0=ALU.mult, op1=ALU.mult), ['p5mv', 'p5rstd'], ['p5nb'])
            self.act(xn[:], rr[:], AF.Identity, ['p5rr', 'p5rstd', 'p5nb'], [('p5xo', b2)], scale=rstd[:, 0:1], bias=nb[:, 0:1])
            self.tt('vector', xn[:], xn[:], lnw_b[:], ALU.mult, [('p5xo', b2), 'p5lnw'], [('p5xo', b2)])
            self.tt('gpsimd', xn[:], xn[:], lnb_b[:], ALU.add, [('p5xo', b2), 'p5lnb'], [('p5xo', b2)])
            self.DMA(out[t0:t0 + 128, :], xn[:], r=[('p5xo', b2)], w=[('out', qb)])
            self.final_keys.append(('out', qb))


def _in_maps(b, inputs):
    hc = host_consts()
    maps = []
    for core in range(8):
        bi = core // 2
        m = dict(hc)
        m.update(core_consts(core % 2, hc))
        m['x'] = inputs['x'][bi]
        m['a_w_in'] = inputs['a_w_in'][0]
        m['a_conv_w'] = inputs['a_conv_w'][0]
        m['a_a_log'] = inputs['a_a_log'].reshape(1, 8)
        m['a_dt_bias'] = inputs['a_dt_bias'].reshape(1, 8)
        m['a_norm_w'] = inputs['a_norm_w'].reshape(128, 1)
        m['a_w_out'] = inputs['a_w_out'][0]
        m['a_ln_w'] = inputs['a_ln_w'].reshape(1, 1024)
        m['a_ln_b'] = inputs['a_ln_b'].reshape(1, 1024)
        for k in ('s_w_kv', 's_pe_k', 's_pe_v', 's_w1_k', 's_w2_k', 's_w1_v', 's_w2_v'):
            m[k] = inputs[k]
        m['b_w_in'] = inputs['b_w_in'][0]
        m['b_w_out'] = inputs['b_w_out'][0]
        m['b_ln_w'] = inputs['b_ln_w'].reshape(1, 1024)
        m['b_ln_b'] = inputs['b_ln_b'].reshape(1, 1024)
        maps.append({k: np.ascontiguousarray(v if k == 'cmask_c' else np.asarray(v, dtype=np.float32)) for k, v in m.items() if k in b.inputs})
    return maps


def kernel(**inputs):
    inputs = {k: np.asarray(v) for k, v in inputs.items()}
    import os
    ph = tuple(os.environ.get('KPHASES', 'p1,p2,p3,p4,p5').split(','))
    b = Builder(phases=ph)
    nc = b.build()
    maps = _in_maps(b, inputs)
    res = run_bass_kernel_spmd(nc, maps, core_ids=list(range(8)))
    if 'out' not in b.outputs:
        return np.zeros((4, T, D), np.float32)
    out = np.zeros((4, T, D), np.float32)
    for core in range(8):
        bi, p = core // 2, core % 2
        o = np.asarray(res.results[core]['out'], dtype=np.float32)
        for j in range(32):
            qb = slot_qb(p, j)
            out[bi, qb * 128:(qb + 1) * 128] = o[j * 128:(j + 1) * 128]
    return out
```
